# Optimizing a Trainium2 kernel written in Bass

```python
import math
import jax, jax.numpy as jnp
from jax import lax
import numpy as np

D_MODEL = 1024
BATCH = 32
SEQ = 256
DEPTH = 4
DEC_BATCH = 8
DEC_SEQ = 1024
PAST_LEN = 512

GRID_W = 64
Q_BLOCK = 128
ROPE_BASE = 10000.0
EPS = 1e-6
A_HEADS = 4
A_KV_HEADS = 2
A_HEAD_DIM = 64
B_HEADS = 4
B_KEY_DIM = 32
B_VAL_DIM = 64
B_GATE_RANK = 16
B_GATE_NORM = 16.0
B_CHUNK = 64
C_HEADS = 4
C_HEAD_DIM = 32
D_HEADS = 4
D_NOPE_DIM = 64
D_ROPE_DIM = 32
D_V_DIM = 64
D_Q_RANK = 192
D_KV_RANK = 128
N_BRANCH = 4
BRANCH_WIDTH = 256
D_FF = 2816
CONV_W = 3

IN_SPLITS = (
    ('a_q', A_HEADS * A_HEAD_DIM), ('a_k', A_KV_HEADS * A_HEAD_DIM), ('a_v', A_KV_HEADS * A_HEAD_DIM),
    ('b_q', B_HEADS * B_KEY_DIM), ('b_k', B_HEADS * B_KEY_DIM), ('b_v', B_HEADS * B_VAL_DIM),
    ('b_r', B_HEADS * B_VAL_DIM), ('b_gf', B_GATE_RANK), ('b_gb', B_GATE_RANK),
    ('c_q', C_HEADS * 2 * C_HEAD_DIM), ('c_k', C_HEADS * 2 * C_HEAD_DIM), ('c_v', C_HEADS * 2 * C_HEAD_DIM),
    ('d_q', D_Q_RANK), ('d_kv', D_KV_RANK), ('d_kr', D_ROPE_DIM),
    ('gates', N_BRANCH * D_MODEL),
)

kernel_name = 'hybrid_diffusion_prefix_trunk_step'


def split_cols(z):
    out = {}
    off = 0
    for name, w in IN_SPLITS:
        out[name] = z[..., off:off + w]
        off += w
    return out


def rmsnorm(x, g):
    xf = x.astype(jnp.float32)
    y = xf * lax.rsqrt(jnp.mean(xf * xf, axis=-1, keepdims=True) + EPS)
    return (y * g.astype(jnp.float32)).astype(x.dtype)


def grid_positions(n):
    rows = n // GRID_W
    t = jnp.arange(rows * GRID_W)
    return (t // GRID_W).astype(jnp.float32), (t % GRID_W).astype(jnp.float32)


def rope_1d(x, pos):
    half = x.shape[-1] // 2
    inv = ROPE_BASE ** (-jnp.arange(half, dtype=jnp.float32) / half)
    ang = pos[:, None] * inv[None, :]
    shape = (1, pos.shape[0]) + (1,) * (x.ndim - 3) + (half,)
    cos = jnp.cos(ang).reshape(shape)
    sin = jnp.sin(ang).reshape(shape)
    xf = x.astype(jnp.float32)
    x1, x2 = xf[..., :half], xf[..., half:]
    return jnp.concatenate([x1 * cos - x2 * sin, x2 * cos + x1 * sin], axis=-1).astype(x.dtype)


def rope_2d(x, row, col):
    r = x.shape[-1] // 2
    return jnp.concatenate([rope_1d(x[..., :r], row), rope_1d(x[..., r:], col)], axis=-1)


def sweep_q_blocks(fn, qs):
    b, sq = qs[0].shape[:2]
    nb = sq // Q_BLOCK
    blocks = tuple(jnp.moveaxis(q.reshape((b, nb, Q_BLOCK) + q.shape[2:]), 1, 0) for q in qs)
    out = jnp.moveaxis(lax.map(fn, blocks), 0, 1)
    return out.reshape((b, sq) + out.shape[3:])


def gqa_attention(q, k, v, scale):
    def block(qb):
        (qb,) = qb
        s = jnp.einsum('bqhgd,bkhd->bhgqk', qb, k).astype(jnp.float32) * scale
        p = jax.nn.softmax(s, axis=-1).astype(v.dtype)
        return jnp.einsum('bhgqk,bkhd->bqhgd', p, v)
    return sweep_q_blocks(block, (q,))


def diff_attention(q, k, v, lam, scale):
    def block(qb):
        (qb,) = qb
        s = jnp.einsum('bqhjd,bkhjd->bhjqk', qb, k).astype(jnp.float32) * scale
        p = jax.nn.softmax(s, axis=-1)
        a = (p[:, :, 0] - lam * p[:, :, 1]).astype(v.dtype)
        return jnp.einsum('bhqk,bkhe->bqhe', a, v)
    return sweep_q_blocks(block, (q,))


def gla_scan(q, k, v, logg, s0):
    f32 = jnp.float32
    bsz, n, h, dk = q.shape
    dv = v.shape[-1]
    nc = n // B_CHUNK
    ch = lambda t: t.reshape((bsz, nc, B_CHUNK) + t.shape[2:]).astype(f32)
    qc, kc, vc = ch(q), ch(k), ch(v)
    bc = jnp.cumsum(ch(logg), axis=2)
    b_last = bc[:, :, -1]
    causal = jnp.tril(jnp.ones((B_CHUNK, B_CHUNK), bool))[None, None, :, :, None, None]
    expo = bc[:, :, :, None] - bc[:, :, None, :]
    decay = jnp.exp(jnp.where(causal, expo, -jnp.inf))
    attn = jnp.einsum('bcthd,bcshd,bctshd->bchts', qc, kc, decay)
    o_intra = jnp.einsum('bchts,bcshe->bcthe', attn, vc)
    q_dec = qc * jnp.exp(bc)
    k_dec = kc * jnp.exp(b_last[:, :, None] - bc)
    ds = jnp.einsum('bcshd,bcshe->bchde', k_dec, vc)
    g_last = jnp.exp(b_last)

    def step(s, inp):
        g, d = inp
        return g[..., None] * s + d, s

    s_fin, s_in = lax.scan(step, s0.astype(f32), (jnp.moveaxis(g_last, 1, 0), jnp.moveaxis(ds, 1, 0)))
    s_in = jnp.moveaxis(s_in, 0, 1)
    o_inter = jnp.einsum('bcthd,bchde->bcthe', q_dec, s_in)
    o = (o_intra + o_inter).reshape(bsz, n, h, dv)
    return o.astype(v.dtype), s_fin


def gla_bidir(q, k, v, lg_f, lg_b, s0_f, s0_b):
    o_f, s_f = gla_scan(q, k, v, lg_f, s0_f)
    flip = lambda t: jnp.flip(t, axis=1)
    o_b, s_b = gla_scan(flip(q), flip(k), flip(v), flip(lg_b), s0_b)
    return o_f + flip(o_b), s_f, s_b


def mixing_block(h, p, lam_init, pos, ctx):
    f32 = jnp.float32
    bsz, n, _ = h.shape
    z = split_cols(h @ p['w_in'])
    aq = rmsnorm(z['a_q'].reshape(bsz, n, A_KV_HEADS, A_HEADS // A_KV_HEADS, A_HEAD_DIM), p['a_qnorm_g'])
    ak = rmsnorm(z['a_k'].reshape(bsz, n, A_KV_HEADS, A_HEAD_DIM), p['a_knorm_g'])
    av = z['a_v'].reshape(bsz, n, A_KV_HEADS, A_HEAD_DIM)
    bq = z['b_q'].reshape(bsz, n, B_HEADS, B_KEY_DIM) * (B_KEY_DIM ** -0.5)
    bk = z['b_k'].reshape(bsz, n, B_HEADS, B_KEY_DIM)
    bv = z['b_v'].reshape(bsz, n, B_HEADS, B_VAL_DIM)
    lg_f = (jax.nn.log_sigmoid((z['b_gf'] @ p['b_gate_w_fwd'] + p['b_gate_b_fwd']).astype(f32)) / B_GATE_NORM).reshape(bsz, n, B_HEADS, B_KEY_DIM)
    lg_b = (jax.nn.log_sigmoid((z['b_gb'] @ p['b_gate_w_bwd'] + p['b_gate_b_bwd']).astype(f32)) / B_GATE_NORM).reshape(bsz, n, B_HEADS, B_KEY_DIM)
    cq = z['c_q'].reshape(bsz, n, C_HEADS, 2, C_HEAD_DIM)
    ck = z['c_k'].reshape(bsz, n, C_HEADS, 2, C_HEAD_DIM)
    cv = z['c_v'].reshape(bsz, n, C_HEADS, 2 * C_HEAD_DIM)
    dq = (rmsnorm(z['d_q'], p['d_qnorm_g']) @ p['d_w_uq']).reshape(bsz, n, D_HEADS, D_NOPE_DIM + D_ROPE_DIM)
    dckv = rmsnorm(z['d_kv'], p['d_kvnorm_g'])
    dkr = z['d_kr']
    if pos is not None:
        row, col = pos
        aq = rope_2d(aq, row, col)
        ak = rope_2d(ak, row, col)
        cq = rope_2d(cq, row, col)
        ck = rope_2d(ck, row, col)
        dq = jnp.concatenate([dq[..., :D_NOPE_DIM], rope_2d(dq[..., D_NOPE_DIM:], row, col)], axis=-1)
        dkr = rope_2d(dkr, row, col)
    if ctx is None:
        ak_all, av_all, ck_all, cv_all, ckv_all, kr_all = ak, av, ck, cv, dckv, dkr
        s0_f = jnp.zeros((bsz, B_HEADS, B_KEY_DIM, B_VAL_DIM), f32)
        s0_b = s0_f
    else:
        c_ak, c_av, s0_f, s0_b, c_ck, c_cv, c_ckv, c_kr = ctx
        cat = lambda a, b: jnp.concatenate([a.astype(b.dtype), b], axis=1)
        ak_all, av_all = cat(c_ak, ak), cat(c_av, av)
        ck_all, cv_all = cat(c_ck, ck), cat(c_cv, cv)
        ckv_all, kr_all = cat(c_ckv, dckv), cat(c_kr, dkr)
    oa = gqa_attention(aq, ak_all, av_all, A_HEAD_DIM ** -0.5).reshape(bsz, n, BRANCH_WIDTH)
    ob_raw, s_f, s_b = gla_bidir(bq, bk, bv, lg_f, lg_b, s0_f, s0_b)
    ob = (rmsnorm(ob_raw, p['b_onorm_g']) * jax.nn.silu(z['b_r'].reshape(bsz, n, B_HEADS, B_VAL_DIM))).reshape(bsz, n, BRANCH_WIDTH)
    lam = (jnp.exp(jnp.sum(p['c_lq1'].astype(f32) * p['c_lk1'].astype(f32)))
           - jnp.exp(jnp.sum(p['c_lq2'].astype(f32) * p['c_lk2'].astype(f32))) + lam_init)
    oc_raw = diff_attention(cq, ck_all, cv_all, lam, C_HEAD_DIM ** -0.5)
    oc = (rmsnorm(oc_raw, p['c_onorm_g']) * (1.0 - lam_init)).reshape(bsz, n, BRANCH_WIDTH)
    kv_up = (ckv_all @ p['d_w_ukv']).reshape(bsz, -1, D_HEADS, D_NOPE_DIM + D_V_DIM)
    sk = kv_up.shape[1]
    dk_full = jnp.concatenate([kv_up[..., :D_NOPE_DIM],
                               jnp.broadcast_to(kr_all[:, :, None].astype(kv_up.dtype), (bsz, sk, D_HEADS, D_ROPE_DIM))], axis=-1)
    dv_all = kv_up[..., D_NOPE_DIM:]
    od = gqa_attention(dq[:, :, :, None], dk_full, dv_all, (D_NOPE_DIM + D_ROPE_DIM) ** -0.5).reshape(bsz, n, BRANCH_WIDTH)
    branches = jnp.stack([oa, ob, oc, od], axis=2)
    proj = jnp.einsum('bnjw,jwd->bnjd', branches, p['w_branch'])
    gates = jax.nn.sigmoid(z['gates'].reshape(bsz, n, N_BRANCH, D_MODEL).astype(f32)).astype(proj.dtype)
    out = jnp.sum(gates * proj, axis=2) @ p['w_out']
    return out, (ak, av, s_f, s_b, ck, cv, dckv, dkr)


def conv_ffn(h, p):
    u = h @ p['w_ffu']
    g = h @ p['w_ffg']
    up = jnp.pad(u, ((0, 0), (1, 1), (0, 0)))
    w = p['conv_w']
    u = up[:, :-2] * w[0] + up[:, 1:-1] * w[1] + up[:, 2:] * w[2] + p['conv_b']
    return (jax.nn.gelu(u) * g) @ p['w_ffd']


def trunk_layer(x, cvec, p, lam_init, pos, ctx):
    mod = jax.nn.silu(cvec) @ p['w_mod'] + p['b_mod']
    sh1, sc1, g1, sh2, sc2, g2 = jnp.split(mod[:, None, :], 6, axis=-1)
    h = rmsnorm(x, p['norm1_g']) * (1.0 + sc1) + sh1
    mix, cache = mixing_block(h, p, lam_init, pos, ctx)
    x = x + g1 * mix
    h = rmsnorm(x, p['norm2_g']) * (1.0 + sc2) + sh2
    x = x + g2 * conv_ffn(h, p)
    return x, cache


def setup_inputs(seed: int = 0) -> dict:
    key = jax.random.key(seed)
    keys = jax.random.split(key, 64)
    cnt = [0]

    def nrm(shape, scale=1.0):
        k = keys[cnt[0]]
        cnt[0] += 1
        return jax.random.normal(k, shape, jnp.float32) * scale

    def gain(shape):
        return 1.0 + nrm(shape, 0.02)

    in_width = sum(w for _, w in IN_SPLITS)
    L = DEPTH
    return {
        'x_prompt': nrm((BATCH, SEQ, D_MODEL)),
        'x_sample': nrm((DEC_BATCH, DEC_SEQ, D_MODEL)),
        'c': nrm((DEC_BATCH, D_MODEL)),
        'cache_a_k': nrm((DEC_BATCH, L, PAST_LEN, A_KV_HEADS, A_HEAD_DIM)),
        'cache_a_v': nrm((DEC_BATCH, L, PAST_LEN, A_KV_HEADS, A_HEAD_DIM)),
        'state_b_fwd': nrm((DEC_BATCH, L, B_HEADS, B_KEY_DIM, B_VAL_DIM), 0.5),
        'state_b_bwd': nrm((DEC_BATCH, L, B_HEADS, B_KEY_DIM, B_VAL_DIM), 0.5),
        'cache_c_k': nrm((DEC_BATCH, L, PAST_LEN, C_HEADS, 2, C_HEAD_DIM)),
        'cache_c_v': nrm((DEC_BATCH, L, PAST_LEN, C_HEADS, 2 * C_HEAD_DIM)),
        'cache_d_ckv': nrm((DEC_BATCH, L, PAST_LEN, D_KV_RANK)),
        'cache_d_krope': nrm((DEC_BATCH, L, PAST_LEN, D_ROPE_DIM)),
        'c_ctx': nrm((D_MODEL,)),
        'w_mod': nrm((L, D_MODEL, 6 * D_MODEL), 0.5 * D_MODEL ** -0.5),
        'b_mod': nrm((L, 6 * D_MODEL), 0.02),
        'norm1_g': gain((L, D_MODEL)),
        'norm2_g': gain((L, D_MODEL)),
        'w_in': nrm((L, D_MODEL, in_width), D_MODEL ** -0.5),
        'a_qnorm_g': gain((L, A_HEAD_DIM)),
        'a_knorm_g': gain((L, A_HEAD_DIM)),
        'b_gate_w_fwd': nrm((L, B_GATE_RANK, B_HEADS * B_KEY_DIM), B_GATE_RANK ** -0.5),
        'b_gate_b_fwd': nrm((L, B_HEADS * B_KEY_DIM), 0.1),
        'b_gate_w_bwd': nrm((L, B_GATE_RANK, B_HEADS * B_KEY_DIM), B_GATE_RANK ** -0.5),
        'b_gate_b_bwd': nrm((L, B_HEADS * B_KEY_DIM), 0.1),
        'b_onorm_g': gain((L, B_VAL_DIM)),
        'c_lq1': nrm((L, C_HEAD_DIM), 0.1),
        'c_lk1': nrm((L, C_HEAD_DIM), 0.1),
        'c_lq2': nrm((L, C_HEAD_DIM), 0.1),
        'c_lk2': nrm((L, C_HEAD_DIM), 0.1),
        'c_onorm_g': gain((L, 2 * C_HEAD_DIM)),
        'd_qnorm_g': gain((L, D_Q_RANK)),
        'd_w_uq': nrm((L, D_Q_RANK, D_HEADS * (D_NOPE_DIM + D_ROPE_DIM)), D_Q_RANK ** -0.5),
        'd_kvnorm_g': gain((L, D_KV_RANK)),
        'd_w_ukv': nrm((L, D_KV_RANK, D_HEADS * (D_NOPE_DIM + D_V_DIM)), D_KV_RANK ** -0.5),
        'w_branch': nrm((L, N_BRANCH, BRANCH_WIDTH, D_MODEL), BRANCH_WIDTH ** -0.5),
        'w_out': nrm((L, D_MODEL, D_MODEL), D_MODEL ** -0.5),
        'w_ffu': nrm((L, D_MODEL, D_FF), D_MODEL ** -0.5),
        'w_ffg': nrm((L, D_MODEL, D_FF), D_MODEL ** -0.5),
        'conv_w': nrm((L, CONV_W, D_FF), CONV_W ** -0.5),
        'conv_b': nrm((L, D_FF), 0.02),
        'w_ffd': nrm((L, D_FF, D_MODEL), D_FF ** -0.5),
        'final_g': gain((D_MODEL,)),
    }


def reference(x_prompt, x_sample, c, cache_a_k, cache_a_v, state_b_fwd, state_b_bwd, cache_c_k, cache_c_v,
              cache_d_ckv, cache_d_krope, c_ctx, w_mod, b_mod, norm1_g, norm2_g, w_in, a_qnorm_g, a_knorm_g,
              b_gate_w_fwd, b_gate_b_fwd, b_gate_w_bwd, b_gate_b_bwd, b_onorm_g, c_lq1, c_lk1, c_lq2, c_lk2,
              c_onorm_g, d_qnorm_g, d_w_uq, d_kvnorm_g, d_w_ukv, w_branch, w_out, w_ffu, w_ffg, conv_w, conv_b,
              w_ffd, final_g):
    pos = grid_positions(x_sample.shape[1])
    xp, xs = x_prompt, x_sample
    ctx_vec = c_ctx[None, :]
    caches = []
    for l in range(DEPTH):
        p = {
            'w_mod': w_mod[l], 'b_mod': b_mod[l], 'norm1_g': norm1_g[l], 'norm2_g': norm2_g[l],
            'w_in': w_in[l], 'a_qnorm_g': a_qnorm_g[l], 'a_knorm_g': a_knorm_g[l],
            'b_gate_w_fwd': b_gate_w_fwd[l], 'b_gate_b_fwd': b_gate_b_fwd[l],
            'b_gate_w_bwd': b_gate_w_bwd[l], 'b_gate_b_bwd': b_gate_b_bwd[l], 'b_onorm_g': b_onorm_g[l],
            'c_lq1': c_lq1[l], 'c_lk1': c_lk1[l], 'c_lq2': c_lq2[l], 'c_lk2': c_lk2[l], 'c_onorm_g': c_onorm_g[l],
            'd_qnorm_g': d_qnorm_g[l], 'd_w_uq': d_w_uq[l], 'd_kvnorm_g': d_kvnorm_g[l], 'd_w_ukv': d_w_ukv[l],
            'w_branch': w_branch[l], 'w_out': w_out[l], 'w_ffu': w_ffu[l], 'w_ffg': w_ffg[l],
            'conv_w': conv_w[l], 'conv_b': conv_b[l], 'w_ffd': w_ffd[l],
        }
        lam_init = 0.8 - 0.6 * math.exp(-0.3 * l)
        xp, cache_l = trunk_layer(xp, ctx_vec, p, lam_init, None, None)
        caches.append(cache_l)
        ctx_l = (cache_a_k[:, l], cache_a_v[:, l], state_b_fwd[:, l], state_b_bwd[:, l],
                 cache_c_k[:, l], cache_c_v[:, l], cache_d_ckv[:, l], cache_d_krope[:, l])
        xs, _ = trunk_layer(xs, c, p, lam_init, pos, ctx_l)
    y_prompt = rmsnorm(xp, final_g)
    y_sample = rmsnorm(xs, final_g)
    stack = lambda i: jnp.stack([cl[i] for cl in caches], axis=1)
    new_a_k = stack(0)
    new_a_v = stack(1)
    new_b_fwd = stack(2)
    new_b_bwd = stack(3)
    new_c_k = stack(4)
    new_c_v = stack(5)
    new_d_ckv = stack(6)
    new_d_krope = stack(7)
    return (y_prompt, y_sample, new_a_k, new_a_v, new_b_fwd, new_b_bwd, new_c_k, new_c_v, new_d_ckv, new_d_krope)
```

```python
from contextlib import ExitStack
import numpy as np
import concourse.bass as bass
import concourse.mybir as mybir

F32 = mybir.dt.float32
BF16 = mybir.dt.bfloat16
AF = mybir.ActivationFunctionType
ALU = mybir.AluOpType
AX = mybir.AxisListType

EPOCH = 30000
NDMA_SEM = 20


class Res:
    __slots__ = ("name", "w", "r", "excl")

    def __init__(self, name="", excl=False):
        self.name = name
        self.excl = excl
        self.w = []
        self.r = []


class EngState:
    def __init__(self, name, handle_name, is_compute):
        self.name = name
        self.handle_name = handle_name
        self.is_compute = is_compute
        self.prog = []
        self.sem = None
        self.cnt = 0
        self.known = {}
        self.dma_ring = []
        self.dma_i = 0
        self.nops = 0


class KB:
    def __init__(self):
        self.nc = bass.Bass("TRN2", target_bir_lowering=False)
        self.es = ExitStack()
        self.E = {
            "pe": EngState("pe", "tensor", True),
            "act": EngState("act", "scalar", True),
            "dve": EngState("dve", "vector", True),
            "pool": EngState("pool", "gpsimd", True),
            "sp": EngState("sp", "sync", False),
        }
        self.nsem = 0
        self.all_dma_events = []

    def new_sem(self, name):
        self.nsem += 1
        return self.es.enter_context(self.nc.semaphore(f"{name}_{self.nsem}"))

    def sbuf(self, name, shape, dtype=F32):
        return self.es.enter_context(self.nc.sbuf_tensor(name, list(shape), dtype))

    def psum(self, name, shape, dtype=F32):
        return self.es.enter_context(self.nc.psum_tensor(name, list(shape), dtype))

    def dram(self, name, shape, dtype, kind):
        return self.nc.dram_tensor(name, list(shape), dtype, kind=kind)

    def _wait(self, st, ev):
        sem, val, _ = ev
        k = id(sem)
        if st.known.get(k, 0) >= val:
            return
        st.known[k] = val
        st.prog.append(("wait", sem, val))

    def _deps(self, st, reads, writes, is_dma):
        for r in reads:
            for ev in r.w:
                self._wait(st, ev)
            if r.excl:
                for ev in r.r:
                    if ev[2] != st.name:
                        self._wait(st, ev)
        for w in writes:
            for ev in w.w:
                if is_dma or ev[2] != st.name or not st.is_compute:
                    self._wait(st, ev)
            for ev in w.r:
                if is_dma or ev[2] != st.name or not st.is_compute:
                    self._wait(st, ev)

    def _commit(self, ev, reads, writes):
        for r in reads:
            if r in writes:
                continue
            if ev[2] in ("pe", "act", "dve", "pool"):
                r.r = [e for e in r.r if e[2] != ev[2]]
            r.r.append(ev)
        for w in writes:
            w.w = [ev]
            w.r = []

    def op(self, eng, fn, reads=(), writes=()):
        st = self.E[eng]
        assert st.is_compute
        self._deps(st, reads, writes, False)
        if st.sem is None or st.cnt >= EPOCH:
            st.sem = self.new_sem(f"s_{eng}")
            st.cnt = 0
        st.cnt += 1
        ev = (st.sem, st.cnt, eng)
        st.prog.append(("op", fn, st.sem))
        st.nops += 1
        self._commit(ev, reads, writes)
        return ev

    def dma(self, q, out, in_, reads=(), writes=(), **kw):
        st = self.E[q]
        kw.setdefault("allow_slow_non_contiguous", True)
        self._deps(st, reads, writes, True)
        if not st.dma_ring:
            st.dma_ring = [[self.new_sem(f"d_{q}"), 0] for _ in range(NDMA_SEM)]
        slot = st.dma_ring[st.dma_i % NDMA_SEM]
        st.dma_i += 1
        if slot[1] > 0:
            self._wait(st, (slot[0], slot[1], "dma"))
        slot[1] += 16
        ev = (slot[0], slot[1], "dma_" + q)
        st.prog.append(("dma", out, in_, slot[0], kw))
        self._commit(ev, reads, writes)
        self.all_dma_events.append(ev)
        return ev

    def finish(self):
        nc = self.nc
        sp = self.E["sp"]
        for st in self.E.values():
            for sem, val in st.dma_ring:
                if val > 0:
                    self._wait(sp, (sem, val, "dma"))
        with nc.Block() as block:
            for st in self.E.values():
                if not st.prog:
                    continue

                def body(eng, st=st):
                    for item in st.prog:
                        if item[0] == "wait":
                            eng.wait_ge(item[1], item[2])
                        elif item[0] == "op":
                            ins = item[1](eng)
                            ins.then_inc(item[2], 1)
                        else:
                            _, out, in_, sem, kw = item
                            eng.dma_start(out=out, in_=in_, **kw).then_inc(sem, 16)

                getattr(block, st.handle_name)(body)
        self.es.close()
        return nc

import math
from concourse.bass_utils import run_bass_kernel_spmd

L = 4
EPS = 1e-6
MUL, ADD, SUB = ALU.mult, ALU.add, ALU.subtract
VOFFS = [0, 64, 192, 256]
VDST = [(0, 64, 0, 64), (128, 256, 64, 192), (320, 384, 192, 256)]


def barrier(kb):
    comp = ["pe", "act", "dve", "pool"]
    for e in comp:
        st = kb.E[e]
        for o in comp:
            so = kb.E[o]
            if o != e and so.sem is not None and so.cnt > 0:
                kb._wait(st, (so.sem, so.cnt, o))
        for q in kb.E.values():
            for sem, val in q.dma_ring:
                if val > 0:
                    kb._wait(st, (sem, val, "dma"))


def build(NLAYERS=L, GROUPS=(0, 1), STOP='', DEBUG=False):
    kb = KB()
    nc = kb.nc
    BCUT = int(STOP.split(':')[1]) if ':' in STOP else 0

    def din(name, shape):
        return kb.dram(name, shape, F32, "ExternalInput").ap()

    def dout(name, shape):
        return kb.dram(name, shape, F32, "ExternalOutput").ap()

    x_d = din("x", [2, 1024, 1024])
    cvec_d = din("cvec", [2, 1024])
    cak_d = din("ca_k", [L, 512, 128]); cav_d = din("ca_v", [L, 512, 128])
    sbf_d = din("sb_f", [L, 128, 64]); sbb_d = din("sb_b", [L, 128, 64])
    cck_d = din("cc_k", [L, 512, 256]); ccv_d = din("cc_v", [L, 512, 256])
    cdc_d = din("cd_ckv", [L, 512, 128]); cdr_d = din("cd_kr", [L, 512, 32])
    wmod_d = din("w_mod", [L, 1024, 6144]); bmod_d = din("b_mod", [L, 6144])
    n1_d = din("norm1_g", [L, 1024]); n2_d = din("norm2_g", [L, 1024])
    win_d = din("w_in", [L, 1024, 6528])
    aqg_d = din("a_qnorm_g", [L, 64]); akg_d = din("a_knorm_g", [L, 64])
    gwf_d = din("b_gate_w_fwd", [L, 16, 128]); gbf_d = din("b_gate_b_fwd", [L, 128])
    gwb_d = din("b_gate_w_bwd", [L, 16, 128]); gbb_d = din("b_gate_b_bwd", [L, 128])
    bon_d = din("b_onorm_g", [L, 64])
    lq1_d = din("c_lq1", [L, 32]); lk1_d = din("c_lk1", [L, 32]); lq2_d = din("c_lq2", [L, 32]); lk2_d = din("c_lk2", [L, 32])
    con_d = din("c_onorm_g", [L, 64])
    dqg_d = din("d_qnorm_g", [L, 192]); wuq_d = din("d_w_uq", [L, 192, 384])
    dkg_d = din("d_kvnorm_g", [L, 128]); wukv_d = din("d_w_ukv", [L, 128, 512])
    wbr_d = din("w_branch", [L, 4, 256, 1024]); wout_d = din("w_out", [L, 1024, 1024])
    wfu_d = din("w_ffu", [L, 1024, 2816]); wfg_d = din("w_ffg", [L, 1024, 2816])
    cw_d = din("conv_w", [L, 3, 2816]); cb_d = din("conv_b", [L, 2816]); wfd_d = din("w_ffd", [L, 2816, 1024])
    fg_d = din("final_g", [1024])
    k_ident = din("k_ident", [128, 128]); k_bd64 = din("k_bd64", [128, 128]); k_tri = din("k_tri", [4, 128, 128])
    k_hm = din("k_hm", [128, 4]); k_pm = din("k_pm", [128, 2])
    k_cosA = din("k_cosA", [1024, 32]); k_sinA = din("k_sinA", [1024, 32])
    k_cosC = din("k_cosC", [1024, 16]); k_sinC = din("k_sinC", [1024, 16])
    k_cosD = din("k_cosD", [32, 1024]); k_sinD = din("k_sinD", [32, 1024])

    y_d = dout("y", [2, 1024, 1024])
    nak_d = dout("nak", [4, L, 256, 128]); nav_d = dout("nav", [4, L, 256, 128])
    nbf_d = dout("nbf", [4, L, 128, 64]); nbb_d = dout("nbb", [4, L, 128, 64])
    nck_d = dout("nck", [4, L, 256, 256]); ncv_d = dout("ncv", [4, L, 256, 256])
    nckv_d = dout("nckv", [4, L, 256, 128]); nkr_d = dout("nkr", [4, L, 256, 32])

    if DEBUG:
        dbg_br = dout('dbg_br', [128, 8, 1024]); dbg_x1 = dout('dbg_x1', [128, 8, 1024]); dbg_x2 = dout('dbg_x2', [128, 8, 1024]); dbg_mg = dout('dbg_mg', [128, 8, 1024])
    xT = kb.sbuf("xT", [128, 8, 1024]); xT_R = [[Res(f"xT{k}_{h}") for h in range(2)] for k in range(8)]
    hT = kb.sbuf("hT", [128, 8, 1024], BF16); hT_R = [Res("hT0"), Res("hT1")]
    NSLOT = 2
    ring = [kb.sbuf(f"ring{i}", [128, 8192], BF16) for i in range(NSLOT)]
    ring_R = [Res(f"ring{i}") for i in range(NSLOT)]
    ring_i = [0]
    arena = kb.sbuf("arena", [128, 30720], BF16)
    PS = kb.psum("ps", [128, 4096])
    pb = [PS[:, i * 512:(i + 1) * 512] for i in range(8)]
    pb_R = [Res(f"pb{i}", excl=True) for i in range(8)]
    PSB = PS[:, 7 * 512:8 * 512].bitcast(BF16)

    def new_slot():
        i = ring_i[0] % NSLOT
        ring_i[0] += 1
        return ring[i], ring_R[i]

    def av(off, n):
        return arena[:, off:off + n]

    def vap(off, stride):
        return bass.AP(arena, off, [[30720, 128], [stride, 2], [1, 64]])

    identf = kb.sbuf("identf", [128, 128]); identb = kb.sbuf("identb", [128, 128], BF16)
    onesb = kb.sbuf("onesb", [128, 128], BF16); bd64 = kb.sbuf("bd64", [128, 128], BF16)
    tri = kb.sbuf("tri", [128, 4, 128]); hm = kb.sbuf("hm", [128, 4]); pm = kb.sbuf("pm", [128, 2])
    onescol = kb.sbuf("onescol", [128, 1]); onesrow = kb.sbuf("onesrow", [1, 128]); epsT = kb.sbuf("epsT", [128, 1])
    onesf = kb.sbuf("onesf", [128, 128])
    cosA = kb.sbuf("cosA", [128, 8, 32]); sinA = kb.sbuf("sinA", [128, 8, 32])
    cosC = kb.sbuf("cosC", [128, 8, 16]); sinC = kb.sbuf("sinC", [128, 8, 16])
    cosD = kb.sbuf("cosD", [128, 1024]); sinD = kb.sbuf("sinD", [128, 1024])
    CR = Res("consts")
    kb.dma("sp", identf[:], k_ident, writes=[CR])
    kb.dma("pool", identb[:], k_ident, writes=[CR])
    kb.dma("pool", bd64[:], k_bd64, writes=[CR])
    kb.dma("sp", tri[:], k_tri.rearrange("m s t -> s m t"), writes=[CR])
    kb.dma("sp", hm[:], k_hm, writes=[CR]); kb.dma("sp", pm[:], k_pm, writes=[CR])
    kb.dma("sp", cosA[:], k_cosA.rearrange("(t p) d -> p t d", p=128), writes=[CR])
    kb.dma("sp", sinA[:], k_sinA.rearrange("(t p) d -> p t d", p=128), writes=[CR])
    kb.dma("sp", cosC[:], k_cosC.rearrange("(t p) d -> p t d", p=128), writes=[CR])
    kb.dma("sp", sinC[:], k_sinC.rearrange("(t p) d -> p t d", p=128), writes=[CR])
    kb.dma("sp", cosD[64:96, :], k_cosD, writes=[CR]); kb.dma("sp", sinD[64:96, :], k_sinD, writes=[CR])
    kb.op("dve", lambda e: e.memset(onesb[:], 1.0), writes=[CR])
    kb.op("dve", lambda e: e.memset(onescol[:], 1.0), writes=[CR])
    kb.op("dve", lambda e: e.memset(onesrow[:], 1.0), writes=[CR])
    kb.op("dve", lambda e: e.memset(onesf[:], 1.0), writes=[CR])
    kb.op("dve", lambda e: e.memset(epsT[:], EPS), writes=[CR])

    def mm(out, lhsT, rhs, start, stop, reads, writes):
        kb.op("pe", lambda e: e.matmul(out, lhsT=lhsT, rhs=rhs, start=start, stop=stop), reads=reads, writes=writes)

    def tr(out, in_, ident, reads, writes):
        kb.op("pe", lambda e: e.transpose(out=out, in_=in_, identity=ident), reads=reads + [CR], writes=writes)

    def act(out, in_, func, reads, writes, scale=1.0, bias=None, accum_out=None):
        kw = {}
        if bias is not None:
            kw["bias"] = bias
        if accum_out is not None:
            kw["accum_out"] = accum_out
        kb.op("act", lambda e: e.activation(out=out, in_=in_, func=func, scale=scale, **kw), reads=reads, writes=writes)

    def tt(out, in0, in1, op, reads, writes, eng="dve"):
        kb.op(eng, lambda e: e.tensor_tensor(out=out, in0=in0, in1=in1, op=op), reads=reads, writes=writes)

    def stt(out, in0, scalar, in1, op0, op1, reads, writes):
        kb.op("dve", lambda e: e.scalar_tensor_tensor(out=out, in0=in0, scalar=scalar, in1=in1, op0=op0, op1=op1), reads=reads, writes=writes)

    def ts(out, in0, s1, op0, reads, writes, s2=None, op1=None, eng="dve"):
        if op1 is None:
            kb.op(eng, lambda e: e.tensor_scalar(out=out, in0=in0, scalar1=s1, scalar2=None, op0=op0), reads=reads, writes=writes)
        else:
            kb.op(eng, lambda e: e.tensor_scalar(out=out, in0=in0, scalar1=s1, scalar2=s2, op0=op0, op1=op1), reads=reads, writes=writes)

    def cp(out, in_, reads, writes, eng="dve"):
        if eng == "act":
            kb.op("act", lambda e: e.copy(out=out, in_=in_), reads=reads, writes=writes)
        else:
            kb.op(eng, lambda e: e.tensor_copy(out=out, in_=in_), reads=reads, writes=writes)

    def recip(out, in_, reads, writes):
        kb.op("dve", lambda e: e.reciprocal(out=out, in_=in_), reads=reads, writes=writes)

    def bc(ap, shape):
        return ap.broadcast_to(list(shape))

    pbi = [0]

    def nb():
        i = pbi[0] % 7
        pbi[0] += 1
        return i

    scT = kb.sbuf("scT", [128, 8, 2]); modT = kb.sbuf("modT", [128, L, 48, 2]); bmT = kb.sbuf("bmT", [128, L, 48])
    n1T = kb.sbuf("n1T", [128, L, 8]); n2T = kb.sbuf("n2T", [128, L, 8])
    A1 = kb.sbuf("A1", [128, L, 8, 2]); A2 = kb.sbuf("A2", [128, L, 8, 2])
    MR = Res("mod")
    with nc.allow_non_contiguous_dma(reason="tiny transposed vector loads"):
        for g_ in range(2):
            kb.dma("sp", scT[:, :, g_], cvec_d[g_].rearrange("(k p) -> p k", p=128), writes=[MR])
        for l_ in range(L):
            kb.dma("sp", bmT[:, l_, :], bmod_d[l_].rearrange("(j p) -> p j", p=128), writes=[MR])
            kb.dma("sp", n1T[:, l_, :], n1_d[l_].rearrange("(k p) -> p k", p=128), writes=[MR])
            kb.dma("sp", n2T[:, l_, :], n2_d[l_].rearrange("(k p) -> p k", p=128), writes=[MR])
    act(scT[:], scT[:], AF.Silu, [MR], [MR])
    wmv = [arena[:, i * 12288:(i + 1) * 12288].bitcast(F32).rearrange("p (k c) -> p k c", k=8) for i in range(2)]
    wm_R = [Res("wm0"), Res("wm1")]
    ti = 0
    for l in range(L):
        for cb in range(8):
            w, wr = wmv[ti % 2], wm_R[ti % 2]
            ti += 1
            kb.dma("sp", w, wmod_d[l].rearrange("(k p) c -> p k c", p=128)[:, :, cb * 768:(cb + 1) * 768], writes=[wr])
            b = nb()
            for jj in range(6):
                for kc in range(8):
                    mm(pb[b][:, jj * 2:(jj + 1) * 2], w[:, kc, jj * 128:(jj + 1) * 128], scT[:, kc, :], kc == 0, kc == 7, [wr, MR], [pb_R[b]])
            tt(modT[:, l, cb * 6:(cb + 1) * 6, :], pb[b][:, 0:12].rearrange("p (j g) -> p j g", g=2),
               bc(bmT[:, l, cb * 6:(cb + 1) * 6].unsqueeze(2), [128, 6, 2]), ADD, [pb_R[b], MR], [MR])
    for l in range(L):
        stt(A1[:, l], modT[:, l, 8:16, :], 1.0, bc(n1T[:, l, :].unsqueeze(2), [128, 8, 2]), ADD, MUL, [MR], [MR])
        stt(A2[:, l], modT[:, l, 32:40, :], 1.0, bc(n2T[:, l, :].unsqueeze(2), [128, 8, 2]), ADD, MUL, [MR], [MR])

    lqk = kb.sbuf("lqk", [32, 4, L]); lamT = kb.sbuf("lamT", [128, L]); neglam = kb.sbuf("neglam", [128, L])
    ocs = kb.sbuf("ocs", [128, L]); lam2 = kb.sbuf("lam2", [128, 2, L])
    with nc.allow_non_contiguous_dma(reason="tiny transposed vector loads"):
        for i, d in enumerate([lq1_d, lk1_d, lq2_d, lk2_d]):
            kb.dma("sp", lqk[:, i, :], d.rearrange("l d -> d l"), writes=[MR])
        kb.dma("sp", ocs[0:64, :], con_d.rearrange("l d -> d l"), writes=[MR])
        kb.dma("sp", ocs[64:128, :], con_d.rearrange("l d -> d l"), writes=[MR])
    tt(lqk[:, 0, :], lqk[:, 0, :], lqk[:, 1, :], MUL, [MR], [MR])
    tt(lqk[:, 2, :], lqk[:, 2, :], lqk[:, 3, :], MUL, [MR], [MR])
    b = nb()
    mm(pb[b][:, 0:L], onesf[0:32, :], lqk[:, 0, :], True, True, [MR, CR], [pb_R[b]])
    mm(pb[b][:, L:2 * L], onesf[0:32, :], lqk[:, 2, :], True, True, [MR, CR], [pb_R[b]])
    act(lam2[:].rearrange("p a l -> p (a l)"), pb[b][:, 0:2 * L], AF.Exp, [pb_R[b]], [MR])
    tt(lamT[:], lam2[:, 0, :], lam2[:, 1, :], SUB, [MR], [MR])
    lam_init = [0.8 - 0.6 * math.exp(-0.3 * l) for l in range(L)]
    for l in range(L):
        ts(lamT[:, l:l + 1], lamT[:, l:l + 1], lam_init[l], ADD, [MR], [MR])
        ts(ocs[:, l:l + 1], ocs[:, l:l + 1], 1.0 - lam_init[l], MUL, [MR], [MR])
    ts(neglam[:], lamT[:], -1.0, MUL, [MR], [MR])

    gA = kb.sbuf("gA", [128, 384]); gDQ = kb.sbuf("gDQ", [128, 192]); gKV = kb.sbuf("gKV", [128, 128]); gBO = kb.sbuf("gBO", [128, 256])
    GW = kb.sbuf("GW", [16, 256]); GBias = kb.sbuf("GBias", [1, 256]); cwT = kb.sbuf("cwT", [128, 3, 22]); cbT = kb.sbuf("cbT", [128, 22])
    fgB = kb.sbuf("fgB", [128, 1024])
    LP = Res("layer_params")
    kb.dma("sp", fgB[:], fg_d.partition_broadcast(128), writes=[CR])

    def load_layer_params(l):
        def bsrc(d, n, rep):
            return bass.AP(d.tensor, d[l].offset, [[0, 128], [0, rep], [1, n]])
        kb.dma("sp", gA[:, 0:256].rearrange("p (r d) -> p r d", r=4), bsrc(aqg_d, 64, 4), writes=[LP])
        kb.dma("sp", gA[:, 256:384].rearrange("p (r d) -> p r d", r=2), bsrc(akg_d, 64, 2), writes=[LP])
        kb.dma("sp", gDQ[:], dqg_d[l].partition_broadcast(128), writes=[LP])
        kb.dma("sp", gKV[:], dkg_d[l].partition_broadcast(128), writes=[LP])
        kb.dma("sp", gBO[:].rearrange("p (r d) -> p r d", r=4), bsrc(bon_d, 64, 4), writes=[LP])
        kb.dma("sp", GW[0:16, 0:128], gwf_d[l], writes=[LP]); kb.dma("sp", GW[0:16, 128:256], gwb_d[l], writes=[LP])
        kb.dma("sp", GBias[0:1, 0:128], gbf_d[l:l + 1, :], writes=[LP]); kb.dma("sp", GBias[0:1, 128:256], gbb_d[l:l + 1, :], writes=[LP])
        with nc.allow_non_contiguous_dma(reason="tiny transposed vector loads"):
            for w_ in range(3):
                kb.dma("sp", cwT[:, w_, :], cw_d[l, w_].rearrange("(f p) -> p f", p=128), writes=[LP])
            kb.dma("sp", cbT[:], cb_d[l].rearrange("(f p) -> p f", p=128), writes=[LP])

    stg = [kb.sbuf(f"stg{i}", [128, 1024]) for i in range(2)]; stg_R = [Res("stg0"), Res("stg1")]
    stg_i = [0]

    def nstg():
        i = stg_i[0] % 2
        stg_i[0] += 1
        return stg[i], stg_R[i]

    tA = kb.sbuf("tA", [128, 1024]); tA_R = Res("tA")
    tB = kb.sbuf("tB", [128, 1024]); tB_R = Res("tB")
    tC = kb.sbuf("tC", [128, 1024]); tC_R = Res("tC")
    sm = kb.sbuf("sm", [128, 64]); sm_R = Res("sm")
    sqb = kb.sbuf("sqb", [128, 8, 512], BF16); sqb_R = Res("sqb")
    rstd = kb.sbuf("rstd", [128, 512]); rstd_R = Res("rstd")
    PT = [kb.sbuf(f"PT{i}", [128, 512], BF16) for i in range(3)]; PT_R = [Res(f"PT{i}") for i in range(3)]
    pt_i = [0]
    qkb = kb.sbuf("qkb", [128, 768], BF16); qkb_R = Res("qkb")
    krpad = kb.sbuf("krpad", [128, 96], BF16); krpad_R = Res("krpad")
    kb.op("dve", lambda e: e.memset(krpad[:], 0.0), writes=[krpad_R])
    stgkr = kb.sbuf("stgkr", [128, 4, 96]); stgkr_R = Res("stgkr")
    kb.op("dve", lambda e: e.memset(stgkr[:], 0.0), writes=[stgkr_R])
    Sst = [kb.sbuf(f"Sst{i}", [128, 64]) for i in range(2)]; Sst_R = [Res("Sf"), Res("Sb")]
    GL = kb.sbuf("GL", [128, 8, 2]); GL_R = Res("GL")
    zgT = kb.sbuf("zgT", [16, 256]); zgT_R = Res("zgT")

    def norm_mod(Asc, shift_c0, l, g):
        for h in range(2):
            hs = slice(h * 512, (h + 1) * 512)
            xr = [xT_R[k][h] for k in range(8)]
            act(sqb[:], xT[:, :, hs], AF.Square, xr, [sqb_R])
            b = nb()
            for kc in range(8):
                mm(pb[b], onesb[:], sqb[:, kc, :], kc == 0, kc == 7, [sqb_R, CR], [pb_R[b]])
            act(rstd[:], pb[b], AF.Sqrt, [pb_R[b], CR], [rstd_R], scale=1.0 / 1024, bias=epsT[:])
            recip(rstd[:], rstd[:], [rstd_R], [rstd_R])
            for kc in range(8):
                tmp, tr_ = (tA, tA_R) if kc % 2 == 0 else (tB, tB_R)
                tt(tmp[:, 0:512], xT[:, kc, hs], rstd[:], MUL, [xT_R[kc][h], rstd_R], [tr_])
                act(hT[:, kc, hs], tmp[:, 0:512], AF.Identity, [tr_, MR], [hT_R[h]],
                    scale=Asc[:, l, kc, g:g + 1], bias=modT[:, l, shift_c0 + kc, g:g + 1])

    def load_w(dst, src, sr):
        kb.dma("pool", dst, src, writes=[sr])

    def proj_tm(bank0, ncols, W, wr, t):
        nbk = (ncols + 511) // 512
        for bi in range(nbk):
            c0 = bi * 512
            c1 = min(ncols, c0 + 512)
            for kc in range(8):
                mm(pb[bank0 + bi][:, 0:c1 - c0], hT[:, kc, t * 128:(t + 1) * 128], W[:, kc, c0:c1], kc == 0, kc == 7,
                   [hT_R[t // 4], wr], [pb_R[bank0 + bi]])

    def rope_tm(src, src_R, U2, hs, cos_t, sin_t, out):
        xv = src.rearrange("p (u r h i) -> p u r h i", u=U2, r=2, h=2)
        ov = out.rearrange("p (u r h i) -> p u r h i", u=U2, r=2, h=2)
        cv = bc(cos_t.rearrange("p (r i) -> p r i", r=2).unsqueeze(1), [128, U2, 2, hs])
        sv = bc(sin_t.rearrange("p (r i) -> p r i", r=2).unsqueeze(1), [128, U2, 2, hs])
        n = U2 * 2 * hs
        a = tA[:, 0:n].rearrange("p (u r i) -> p u r i", u=U2, r=2)
        b2 = tB[:, 0:n].rearrange("p (u r i) -> p u r i", u=U2, r=2)
        x0 = xv[:, :, :, 0, :]; x1 = xv[:, :, :, 1, :]
        tt(a, x0, cv, MUL, [src_R, CR], [tA_R]); tt(b2, x1, sv, MUL, [src_R, CR], [tB_R])
        tt(ov[:, :, :, 0, :], a, b2, SUB, [tA_R, tB_R], [qkb_R])
        tt(a, x1, cv, MUL, [src_R, CR], [tA_R]); tt(b2, x0, sv, MUL, [src_R, CR], [tB_R])
        tt(ov[:, :, :, 1, :], a, b2, ADD, [tA_R, tB_R], [qkb_R])

    def attend(kT, qT, vT, kts, q0, N, scale, reads, ob, avoid=()):
        sb = [None] * len(kts)

        def S(i):
            b = nb()
            while b == ob or b in avoid:
                b = nb()
            sb[i] = b
            mm(pb[b][:, 0:N], kT(kts[i]), qT, True, True, reads, [pb_R[b]])
        S(0)
        for i in range(len(kts)):
            if i + 1 < len(kts):
                S(i + 1)
            p = pt_i[0] % 3
            pt_i[0] += 1
            act(PT[p][:, 0:N], pb[sb[i]][:, 0:N], AF.Exp, [pb_R[sb[i]]], [PT_R[p]], scale=scale)
            mm(pb[ob][:, 0:N], vT(kts[i]), PT[p][:, 0:N], i == 0, i == len(kts) - 1, reads + [PT_R[p]], [pb_R[ob]])

    def obank():
        b = nb()
        return b

    for g in GROUPS:
        S_GRP = (g == 1)
        NKT = 12 if S_GRP else 8
        KOFF = 4 if S_GRP else 0
        NK = NKT * 128
        if S_GRP:
            seqs = [(0, 1024, list(range(12)))]
        else:
            seqs = [(s * 256, 256, [2 * s, 2 * s + 1]) for s in range(4)]
        for t in range(8):
            st_, sr = nstg()
            kb.dma("sp", st_[:], x_d[g, t * 128:(t + 1) * 128, :], writes=[sr])
            for hh in range(2):
                b = nb()
                for kk in range(4):
                    kc = hh * 4 + kk
                    tr(pb[b][:, kk * 128:(kk + 1) * 128], st_[:, kc * 128:(kc + 1) * 128], identf[:], [sr], [pb_R[b]])
                cp(xT[:, hh * 4:(hh + 1) * 4, t * 128:(t + 1) * 128], pb[b].rearrange("p (k c) -> p k c", k=4), [pb_R[b]],
                   [xT_R[k][t // 4] for k in range(hh * 4, hh * 4 + 4)], eng=("act" if hh else "dve"))

        for l in range(NLAYERS):
            load_layer_params(l)
            norm_mod(A1, 0, l, g)
            if STOP == 'N':
                continue
            BR_OFF = 22528
            BR = arena[:, BR_OFF:BR_OFF + 8192].rearrange("p (j b n) -> p j b n", j=4, b=2)
            BR_R = [Res(f"BR{j}") for j in range(4)]
            OUTS = not S_GRP

            barrier(kb)
            QT = av(0, 2048).rearrange("p (b n) -> p b n", b=2); QT_R = Res("QT_A")
            KT = av(2048, 1536); KT_R = Res("KT_A")
            VA = av(3584, 4608).rearrange("p (k c) -> p k c", c=384); VA_R = Res("VA")
            slot, sr = new_slot()
            W = slot[:, 0:4096].rearrange("p (k c) -> p k c", k=8)
            wsrc = win_d[l].rearrange("(k p) c -> p k c", p=128)
            for g2_ in range(2):
                for kv_ in range(2):
                    load_w(W[:, :, g2_ * 128 + kv_ * 64:g2_ * 128 + kv_ * 64 + 64], wsrc[:, :, (kv_ * 2 + g2_) * 64:(kv_ * 2 + g2_) * 64 + 64], sr)
            load_w(W[:, :, 256:512], wsrc[:, :, 256:512], sr)
            kb.op("dve", lambda e: e.memset(VA[:, 0:NKT, :], 1.0), writes=[VA_R])
            if S_GRP:
                st_, sr2 = nstg()
                kb.dma("sp", st_[:, 0:512].rearrange("p (k f) -> p k f", k=4), cak_d[l].rearrange("(k p) f -> p k f", p=128), writes=[sr2])
                b = nb()
                for kt in range(4):
                    tr(pb[b][:, kt * 128:(kt + 1) * 128], st_[:, kt * 128:(kt + 1) * 128], identf[:], [sr2], [pb_R[b]])
                cp(KT[:, 0:512], pb[b], [pb_R[b]], [KT_R])
                kb.dma("pool", VA[:, 0:4, 64:128], cav_d[l].rearrange("(k p) f -> p k f", p=128)[:, :, 0:64], writes=[VA_R])
                kb.dma("pool", VA[:, 0:4, 256:320], cav_d[l].rearrange("(k p) f -> p k f", p=128)[:, :, 64:128], writes=[VA_R])
            if STOP == 'A0':
                continue
            for t in range(8):
                zb = nb()
                if zb == 6:
                    zb = nb()
                z = pb[zb]; zr = pb_R[zb]
                for kc in range(8):
                    mm(z, hT[:, kc, t * 128:(t + 1) * 128], W[:, kc, :], kc == 0, kc == 7, [hT_R[t // 4], sr], [zr])
                if STOP == 'A2a':
                    continue
                act(tC[:, 0:384], z[:, 0:384], AF.Square, [zr], [tC_R])
                kb.op("dve", lambda e: e.reduce_sum(out=sm[:, 0:6], in_=tC[:, 0:384].rearrange("p (h d) -> p h d", h=6), axis=AX.X), reads=[tC_R], writes=[sm_R])
                if STOP == 'A2b':
                    continue
                act(sm[:, 0:6], sm[:, 0:6], AF.Sqrt, [sm_R, CR], [sm_R], scale=1.0 / 64, bias=epsT[:])
                recip(sm[:, 0:6], sm[:, 0:6], [sm_R], [sm_R])
                if STOP == 'A2c':
                    continue
                tt(tC[:, 0:384].rearrange("p (h d) -> p h d", h=6), z[:, 0:384].rearrange("p (h d) -> p h d", h=6),
                   bc(sm[:, 0:6].unsqueeze(2), [128, 6, 64]), MUL, [zr, sm_R], [tC_R])
                tt(tC[:, 0:384], tC[:, 0:384], gA[:], MUL, [tC_R, LP], [tC_R])
                kt = KOFF + t
                if STOP == 'A2':
                    continue
                cp(VA[:, kt, 64:128], z[:, 384:448], [zr], [VA_R], eng="act")
                cp(VA[:, kt, 256:320], z[:, 448:512], [zr], [VA_R], eng="act")
                if S_GRP:
                    rope_tm(tC[:, 0:384], tC_R, 6, 16, cosA[:, t, :], sinA[:, t, :], qkb[:, 0:384])
                else:
                    cp(qkb[:, 0:384], tC[:, 0:384], [tC_R], [qkb_R])
                    o_, or_ = nstg()
                    cp(o_[:, 0:128], tC[:, 256:384], [tC_R], [or_], eng="act")
                    cp(o_[:, 128:256], z[:, 384:512], [zr], [or_], eng="act")
                    s_, tt_ = t // 2, (t % 2) * 128
                    kb.dma("sp", nak_d[s_, l, tt_:tt_ + 128, :], o_[:, 0:128], reads=[or_])
                    kb.dma("sp", nav_d[s_, l, tt_:tt_ + 128, :], o_[:, 128:256], reads=[or_])
                if STOP == 'A3':
                    continue
                for g2 in range(2):
                    tr(PSB[:, g2 * 128:(g2 + 1) * 128], qkb[:, g2 * 128:(g2 + 1) * 128], identb[:], [qkb_R], [pb_R[7]])
                tr(PSB[:, 256:384], qkb[:, 256:384], identb[:], [qkb_R], [pb_R[7]])
                if STOP == 'A4':
                    continue
                if STOP != 'A6':
                    cp(QT[:, :, t * 128:(t + 1) * 128], PSB[:, 0:256].rearrange("p (b n) -> p b n", b=2), [pb_R[7]], [QT_R])
                if STOP != 'A5':
                    cp(KT[:, kt * 128:(kt + 1) * 128], PSB[:, 256:384], [pb_R[7], QT_R], [KT_R], eng="act")
            if STOP in ('A1', 'A2', 'A2a', 'A2b', 'A2c', 'A3', 'A4', 'A5', 'A6'):
                continue
            for (q0, qlen, kts) in seqs:
                for qc in range(0, qlen, 512):
                    N = min(512, qlen - qc)
                    for kv in range(2):
                        for g2 in range(2):
                            ob = obank()
                            rows = slice(kv * 64, (kv + 1) * 64)
                            orow = g2 * 64
                            srow = 64 - orow
                            if g2 == 0:
                                vfn = lambda kt, kv=kv: VA[:, kt, kv * 192 + 64:kv * 192 + 192]
                            else:
                                vfn = lambda kt, kv=kv: VA[:, kt, kv * 192:kv * 192 + 128]
                            attend(lambda kt, rows=rows: KT[rows, kt * 128:(kt + 1) * 128], QT[rows, g2, q0 + qc:q0 + qc + N], vfn, kts, q0 + qc, N,
                                   0.125, [KT_R, QT_R, VA_R], ob)
                            recip(tA[srow:srow + 64, 0:N], pb[ob][srow:srow + 64, 0:N], [pb_R[ob]], [tA_R])
                            tt(BR[orow:orow + 64, 0, kv, q0 + qc:q0 + qc + N], pb[ob][orow:orow + 64, 0:N], tA[srow:srow + 64, 0:N], MUL,
                               [pb_R[ob], tA_R], [BR_R[0]])
            if STOP == 'A':
                continue
            barrier(kb)
            QT = av(0, 2048).rearrange("p (b n) -> p b n", b=2); QT_R = Res("QT_C")
            KC = av(2048, 6144).rearrange("p (v b n) -> p v b n", v=2, b=2); KC_R = Res("KT_C")
            VC_OFF = 8192
            VC = av(VC_OFF, 4608).rearrange("p (k c) -> p k c", c=384); VC_R = Res("VC")
            slot, sr = new_slot()
            W = slot[:, 0:6144].rearrange("p (k c) -> p k c", k=8)
            load_w(W, win_d[l].rearrange("(k p) c -> p k c", p=128)[:, :, 1312:2080], sr)
            kb.op("dve", lambda e: e.memset(VC[:, 0:NKT, :], 1.0), writes=[VC_R])

            def kc_store(src, kcols):
                for v in range(2):
                    for bb in range(2):
                        ts(KC[:, v, bb, kcols], src[:, bb, :], pm[:, v:v + 1], MUL, [pb_R[7], pb_R[6], CR], [KC_R])
            if S_GRP:
                for bb in range(2):
                    st_, sr2 = nstg()
                    kb.dma("sp", st_[:, 0:512].rearrange("p (k f) -> p k f", k=4), cck_d[l].rearrange("(k p) f -> p k f", p=128)[:, :, bb * 128:(bb + 1) * 128], writes=[sr2])
                    for kt in range(4):
                        tr(pb[6][:, kt * 128:(kt + 1) * 128], st_[:, kt * 128:(kt + 1) * 128], identf[:], [sr2], [pb_R[6]])
                    for v in range(2):
                        ts(KC[:, v, bb, 0:512], pb[6], pm[:, v:v + 1], MUL, [pb_R[6], CR], [KC_R])
                for (d0, d1, s0, s1) in VDST:
                    kb.dma("pool", VC[:, 0:4, d0:d1], ccv_d[l].rearrange("(k p) f -> p k f", p=128)[:, :, s0:s1], writes=[VC_R])
            for t in range(8):
                zb = nb()
                while zb >= 5:
                    zb = nb()
                zr = [pb_R[zb], pb_R[zb + 1]]
                z = PS[:, zb * 512:zb * 512 + 768]
                for bi in range(2):
                    c0, c1 = bi * 512, min(768, bi * 512 + 512)
                    for kc in range(8):
                        mm(PS[:, (zb + bi) * 512:(zb + bi) * 512 + c1 - c0], hT[:, kc, t * 128:(t + 1) * 128], W[:, kc, c0:c1], kc == 0, kc == 7,
                           [hT_R[t // 4], sr], [pb_R[zb + bi]])
                kt = KOFF + t
                for (d0, d1, s0, s1) in VDST:
                    cp(VC[:, kt, d0:d1], z[:, 512 + s0:512 + s1], zr, [VC_R], eng="act")
                if S_GRP:
                    rope_tm(z[:, 0:512], zr[0], 16, 8, cosC[:, t, :], sinC[:, t, :], qkb[:, 0:512])
                else:
                    cp(qkb[:, 0:512], z[:, 0:512], zr, [qkb_R])
                    o_, or_ = nstg()
                    cp(o_[:, 0:512], z[:, 256:768], zr, [or_], eng="act")
                    s_, tt_ = t // 2, (t % 2) * 128
                    kb.dma("sp", nck_d[s_, l, tt_:tt_ + 128, :], o_[:, 0:256], reads=[or_])
                    kb.dma("sp", ncv_d[s_, l, tt_:tt_ + 128, :], o_[:, 256:512], reads=[or_])
                for i in range(4):
                    tr(PSB[:, i * 128:(i + 1) * 128], qkb[:, i * 128:(i + 1) * 128], identb[:], [qkb_R], [pb_R[7]])
                cp(QT[:, :, t * 128:(t + 1) * 128], PSB[:, 0:256].rearrange("p (b n) -> p b n", b=2), [pb_R[7]], [QT_R], eng="act")
                kc_store(PSB[:, 256:512].rearrange("p (b n) -> p b n", b=2), slice(kt * 128, (kt + 1) * 128))
            OCR = tC; OCR_R = tC_R
            for (q0, qlen, kts) in seqs:
                for qc in range(0, qlen, 512):
                    N = min(512, qlen - qc)
                    for bb in range(2):
                        for hf in range(2):
                            h = bb * 2 + hf
                            rows = slice(hf * 64, (hf + 1) * 64)
                            orow = hf * 64
                            srow = 64 - orow
                            if True:
                                vfn = lambda kt, h=h: VC[:, kt, VOFFS[h]:VOFFS[h] + 128]
                            obs = []
                            for j in range(2):
                                ob = obank()
                                while ob in obs:
                                    ob = obank()
                                obs.append(ob)
                                attend(lambda kt, rows=rows, j=j, bb=bb: KC[rows, j, bb, kt * 128:(kt + 1) * 128], QT[rows, bb, q0 + qc:q0 + qc + N], vfn, kts,
                                       q0 + qc, N, 32 ** -0.5, [KC_R, QT_R, VC_R], ob, avoid=tuple(obs))
                            o1, o2 = obs
                            recip(tA[srow:srow + 64, 0:N], pb[o1][srow:srow + 64, 0:N], [pb_R[o1]], [tA_R])
                            recip(tB[srow:srow + 64, 0:N], pb[o2][srow:srow + 64, 0:N], [pb_R[o2]], [tB_R])
                            tt(tA[orow:orow + 64, 512:512 + N], pb[o1][orow:orow + 64, 0:N], tA[srow:srow + 64, 0:N], MUL, [pb_R[o1], tA_R], [tA_R])
                            tt(tB[orow:orow + 64, 512:512 + N], pb[o2][orow:orow + 64, 0:N], tB[srow:srow + 64, 0:N], MUL, [pb_R[o2], tB_R], [tB_R])
                            stt(OCR[orow:orow + 64, 0:N], tB[orow:orow + 64, 512:512 + N], neglam[orow:orow + 64, l:l + 1], tA[orow:orow + 64, 512:512 + N], MUL, ADD,
                                [tA_R, tB_R, MR], [OCR_R])
                        act(PT[0][:, 0:N], OCR[:, 0:N], AF.Square, [OCR_R], [PT_R[0]])
                        b = nb()
                        mm(pb[b][:, 0:N], bd64[:], PT[0][:, 0:N], True, True, [PT_R[0], CR], [pb_R[b]])
                        act(rstd[:, 0:N], pb[b][:, 0:N], AF.Sqrt, [pb_R[b], CR], [rstd_R], scale=1.0 / 64, bias=epsT[:])
                        recip(rstd[:, 0:N], rstd[:, 0:N], [rstd_R], [rstd_R])
                        tt(OCR[:, 0:N], OCR[:, 0:N], rstd[:, 0:N], MUL, [OCR_R, rstd_R], [OCR_R])
                        act(BR[:, 2, bb, q0 + qc:q0 + qc + N], OCR[:, 0:N], AF.Copy, [OCR_R, MR], [BR_R[2]], scale=ocs[:, l:l + 1])

            if STOP == 'C':
                continue
            barrier(kb)
            DQN = av(0, 2048).rearrange("p (b n) -> p b n", b=2); DQN_R = Res("DQN")
            CKT = av(2048, 1536); CKT_R = Res("CKT")
            KD_ = av(3584, 6144).rearrange("p (h n) -> p h n", h=4); KD_R = Res("KT_D")
            DQT = av(9728, 4096).rearrange("p (h n) -> p h n", h=4); DQT_R = Res("DQT")
            VD_OFF = 13824
            VD = av(VD_OFF, 4608).rearrange("p (k c) -> p k c", c=384); VD_R = Res("VD")
            slot, sr = new_slot()
            W = slot[:, 0:2816].rearrange("p (k c) -> p k c", k=8)
            load_w(W, win_d[l].rearrange("(k p) c -> p k c", p=128)[:, :, 2080:2432], sr)
            UQ = slot[:, 2816:3584].rearrange("p (k c) -> p k c", k=2)
            UQS = slot[:, 3584:4352].rearrange("p (k c) -> p k c", k=2)
            UKV = slot[:, 4352:4864]
            load_w(UQ[:, 0, :], wuq_d[l, 0:128, :], sr); load_w(UQ[0:64, 1, :], wuq_d[l, 128:192, :], sr)
            load_w(UKV, wukv_d[l], sr)
            if S_GRP:
                with nc.allow_non_contiguous_dma(reason="rope column swap"):
                    for (kc_, r0, r1, pr) in ((0, 0, 128, 128), (1, 128, 192, 64)):
                        src = wuq_d[l, r0:r1, :].rearrange("p (h c) -> p h c", h=4)[:, :, 64:96].rearrange("p h (r f i) -> p h r f i", r=2, f=2)
                        dst = UQS[0:pr, kc_, :].rearrange("p (h c) -> p h c", h=4)[:, :, 64:96].rearrange("p h (r f i) -> p h r f i", r=2, f=2)
                        for f in range(2):
                            for r_ in range(2):
                                load_w(dst[:, :, r_, f, :], src[:, :, r_, 1 - f, :], sr)
            kb.op("dve", lambda e: e.memset(VD[:, 0:NKT, :], 1.0), writes=[VD_R])
            if S_GRP:
                st_, sr2 = nstg()
                kb.dma("sp", st_[:, 0:512].rearrange("p (k f) -> p k f", k=4), cdc_d[l].rearrange("(k p) f -> p k f", p=128), writes=[sr2])
                for kt in range(4):
                    tr(pb[6][:, kt * 128:(kt + 1) * 128], st_[:, kt * 128:(kt + 1) * 128], identf[:], [sr2], [pb_R[6]])
                cp(CKT[:, 0:512], pb[6], [pb_R[6]], [CKT_R])
                kb.dma("sp", stgkr[:, :, 64:96], cdr_d[l].rearrange("(k p) f -> p k f", p=128), writes=[stgkr_R])
                b = nb()
                for kt in range(4):
                    mm(pb[b][0:96, kt * 128:(kt + 1) * 128], stgkr[:, kt, :], identf[:], True, True, [stgkr_R, CR], [pb_R[b]])
                cp(KD_[64:96, :, 0:512], bc(pb[b][64:96, :].unsqueeze(1), [32, 4, 512]), [pb_R[b]], [KD_R])
            for t in range(8):
                zb = nb()
                z = pb[zb]; zr = pb_R[zb]
                for kc in range(8):
                    mm(z[:, 0:352], hT[:, kc, t * 128:(t + 1) * 128], W[:, kc, :], kc == 0, kc == 7, [hT_R[t // 4], sr], [zr])
                act(tA[:, 0:192], z[:, 0:192], AF.Square, [zr], [tA_R], accum_out=sm[:, 8:9])
                act(tA[:, 192:320], z[:, 192:320], AF.Square, [zr], [tA_R], accum_out=sm[:, 9:10])
                act(sm[:, 8:9], sm[:, 8:9], AF.Sqrt, [tA_R, CR], [sm_R], scale=1.0 / 192, bias=epsT[:])
                act(sm[:, 9:10], sm[:, 9:10], AF.Sqrt, [tA_R, CR], [sm_R], scale=1.0 / 128, bias=epsT[:])
                recip(sm[:, 8:10], sm[:, 8:10], [sm_R], [sm_R])
                stt(qkb[:, 0:192], z[:, 0:192], sm[:, 8:9], gDQ[:], MUL, MUL, [zr, sm_R, LP], [qkb_R])
                stt(tB[:, 0:128], z[:, 192:320], sm[:, 9:10], gKV[:], MUL, MUL, [zr, sm_R, LP], [tB_R])
                cp(qkb[:, 192:320], tB[:, 0:128], [tB_R], [qkb_R])
                kt = KOFF + t
                if S_GRP:
                    xv = z[:, 320:352].rearrange("p (r h i) -> p r h i", r=2, h=2)
                    ov = krpad[:, 64:96].rearrange("p (r h i) -> p r h i", r=2, h=2)
                    cv = cosC[:, t, :].rearrange("p (r i) -> p r i", r=2); sv = sinC[:, t, :].rearrange("p (r i) -> p r i", r=2)
                    a = tC[:, 0:16].rearrange("p (r i) -> p r i", r=2); b2 = tC[:, 16:32].rearrange("p (r i) -> p r i", r=2)
                    tt(a, xv[:, :, 0, :], cv, MUL, [zr, CR], [tC_R]); tt(b2, xv[:, :, 1, :], sv, MUL, [zr, CR], [tC_R])
                    tt(ov[:, :, 0, :], a, b2, SUB, [tC_R], [krpad_R])
                    tt(a, xv[:, :, 1, :], cv, MUL, [zr, CR], [tC_R]); tt(b2, xv[:, :, 0, :], sv, MUL, [zr, CR], [tC_R])
                    tt(ov[:, :, 1, :], a, b2, ADD, [tC_R], [krpad_R])
                else:
                    cp(krpad[:, 64:96], z[:, 320:352], [zr], [krpad_R])
                    o_, or_ = nstg()
                    cp(o_[:, 0:128], tB[:, 0:128], [tB_R], [or_], eng="act")
                    cp(o_[:, 128:160], z[:, 320:352], [zr], [or_], eng="act")
                    s_, tt_ = t // 2, (t % 2) * 128
                    kb.dma("sp", nckv_d[s_, l, tt_:tt_ + 128, :], o_[:, 0:128], reads=[or_])
                    kb.dma("sp", nkr_d[s_, l, tt_:tt_ + 128, :], o_[:, 128:160], reads=[or_])
                tr(PSB[:, 0:128], qkb[:, 0:128], identb[:], [qkb_R], [pb_R[7]])
                tr(PSB[0:64, 128:256], qkb[:, 128:192], identb[:], [qkb_R], [pb_R[7]])
                tr(PSB[:, 256:384], qkb[:, 192:320], identb[:], [qkb_R], [pb_R[7]])
                cp(DQN[:, 0, t * 128:(t + 1) * 128], PSB[:, 0:128], [pb_R[7]], [DQN_R])
                cp(DQN[0:64, 1, t * 128:(t + 1) * 128], PSB[0:64, 128:256], [pb_R[7]], [DQN_R])
                cp(CKT[:, kt * 128:(kt + 1) * 128], PSB[:, 256:384], [pb_R[7]], [CKT_R], eng="act")
                b = nb()
                mm(pb[b][0:96, 0:128], krpad[:], identb[:], True, True, [krpad_R, CR], [pb_R[b]])
                cp(KD_[64:96, :, kt * 128:(kt + 1) * 128], bc(pb[b][64:96, 0:128].unsqueeze(1), [32, 4, 128]), [pb_R[b]], [KD_R])
            for h in range(4):
                for c0 in range(0, NK, 512):
                    b = nb()
                    mm(pb[b][0:64, :], UKV[:, h * 128:h * 128 + 64], CKT[:, c0:c0 + 512], True, True, [sr, CKT_R], [pb_R[b]])
                    cp(KD_[0:64, h, c0:c0 + 512], pb[b][0:64, :], [pb_R[b]], [KD_R], eng=("act" if h % 2 else "dve"))
            for kt in range(NKT):
                b = nb()
                mm(pb[b][:, 0:256].rearrange("p (h e) -> p h e", h=4), CKT[:, kt * 128:(kt + 1) * 128], UKV.rearrange("p (h c) -> p h c", h=4)[:, :, 64:128], True, True,
                   [sr, CKT_R], [pb_R[b]])
                for (d0, d1, s0, s1) in VDST:
                    cp(VD[:, kt, d0:d1], pb[b][:, s0:s1], [pb_R[b]], [VD_R], eng=("act" if kt % 2 else "dve"))
            for h in range(4):
                for qc in range(0, 1024, 512):
                    b = nb()
                    mm(pb[b][0:96, :], UQ[:, 0, h * 96:(h + 1) * 96], DQN[:, 0, qc:qc + 512], True, False, [sr, DQN_R], [pb_R[b]])
                    mm(pb[b][0:96, :], UQ[0:64, 1, h * 96:(h + 1) * 96], DQN[0:64, 1, qc:qc + 512], False, True, [sr, DQN_R], [pb_R[b]])
                    if S_GRP:
                        b2_ = nb()
                        mm(pb[b2_][0:96, :], UQS[:, 0, h * 96:(h + 1) * 96], DQN[:, 0, qc:qc + 512], True, False, [sr, DQN_R], [pb_R[b2_]])
                        mm(pb[b2_][0:96, :], UQS[0:64, 1, h * 96:(h + 1) * 96], DQN[0:64, 1, qc:qc + 512], False, True, [sr, DQN_R], [pb_R[b2_]])
                        cp(DQT[0:64, h, qc:qc + 512], pb[b][0:64, :], [pb_R[b]], [DQT_R], eng="act")
                        tt(tA[64:96, 0:512], pb[b][64:96, :], cosD[64:96, qc:qc + 512], MUL, [pb_R[b], CR], [tA_R])
                        tt(tB[64:96, 0:512], pb[b2_][64:96, :], sinD[64:96, qc:qc + 512], MUL, [pb_R[b2_], CR], [tB_R])
                        tt(DQT[64:96, h, qc:qc + 512], tA[64:96, 0:512], tB[64:96, 0:512], ADD, [tA_R, tB_R], [DQT_R])
                    else:
                        cp(DQT[0:96, h, qc:qc + 512], pb[b][0:96, :], [pb_R[b]], [DQT_R], eng=("act" if h % 2 else "dve"))
            for (q0, qlen, kts) in seqs:
                for qc in range(0, qlen, 512):
                    N = min(512, qlen - qc)
                    for h in range(4):
                        bb, hf = h // 2, h % 2
                        orow = hf * 64
                        srow = 64 - orow
                        if True:
                            vfn = lambda kt, h=h: VD[:, kt, VOFFS[h]:VOFFS[h] + 128]
                        ob = obank()
                        attend(lambda kt, h=h: KD_[0:96, h, kt * 128:(kt + 1) * 128], DQT[0:96, h, q0 + qc:q0 + qc + N], vfn, kts, q0 + qc, N,
                               96 ** -0.5, [KD_R, DQT_R, VD_R], ob)
                        recip(tA[srow:srow + 64, 0:N], pb[ob][srow:srow + 64, 0:N], [pb_R[ob]], [tA_R])
                        tt(BR[orow:orow + 64, 3, bb, q0 + qc:q0 + qc + N], pb[ob][orow:orow + 64, 0:N], tA[srow:srow + 64, 0:N], MUL,
                           [pb_R[ob], tA_R], [BR_R[3]])
            if STOP == 'D':
                continue
            barrier(kb)
            BT = av(0, 12288).rearrange("p (t d c) -> p t d c", t=8, d=2); BT_R = Res("BT")
            KDc = av(12288, 2048).rearrange("p (t d c) -> p t d c", t=8, d=2); KDc_R = Res("KDc")
            VB = av(14336, 2048).rearrange("p (t c) -> p t c", t=8); VB_R = Res("VB")
            GR = av(16384, 2048).rearrange("p (t c) -> p t c", t=8); GR_R = Res("GR")
            SIN = av(18432, 4096).rearrange("p (t d c) -> p t d c", t=8, d=2); SIN_R = Res("SIN")
            slot, sr = new_slot()
            W = slot[:, 0:6144].rearrange("p (k c) -> p k c", k=8)
            G = slot[:, 6144:6656].rearrange("p (k c) -> p k c", k=8)
            kb.op("pool", lambda e: e.memset(G, 0.0), writes=[sr])
            load_w(W, win_d[l].rearrange("(k p) c -> p k c", p=128)[:, :, 512:1280], sr)
            load_w(G[:, :, 0:16], win_d[l].rearrange("(k p) c -> p k c", p=128)[:, :, 1280:1296], sr)
            load_w(G[:, :, 32:48], win_d[l].rearrange("(k p) c -> p k c", p=128)[:, :, 1296:1312], sr)
            Lsp = tC; Lsp_R = tC_R
            for t in range(8):
                zb = nb()
                while zb >= 4:
                    zb = nb()
                zr = [pb_R[zb], pb_R[zb + 1]]
                z = PS[:, zb * 512:zb * 512 + 768]
                for bi in range(2):
                    c0, c1 = bi * 512, min(768, bi * 512 + 512)
                    for kc in range(8):
                        mm(PS[:, (zb + bi) * 512:(zb + bi) * 512 + c1 - c0], hT[:, kc, t * 128:(t + 1) * 128], W[:, kc, c0:c1], kc == 0, kc == 7,
                           [hT_R[t // 4], sr], [pb_R[zb + bi]])
                if BCUT == 1:
                    continue
                for d in range(2):
                    for kc in range(8):
                        mm(pb[5][0:16, d * 128:(d + 1) * 128], G[:, kc, d * 32:d * 32 + 16], hT[:, kc, t * 128:(t + 1) * 128], kc == 0, kc == 7, [hT_R[t // 4], sr], [pb_R[5]])
                cp(zgT[:], pb[5][0:16, 0:256], [pb_R[5]], [zgT_R])
                if BCUT == 2:
                    continue
                for d in range(2):
                    mm(pb[5][:, 256 + d * 128:384 + d * 128], zgT[0:16, d * 128:(d + 1) * 128], GW[0:16, d * 128:(d + 1) * 128], True, False, [zgT_R, LP], [pb_R[5]])
                    mm(pb[5][:, 256 + d * 128:384 + d * 128], onesrow[0:1, :], GBias[0:1, d * 128:(d + 1) * 128], False, True, [CR, LP], [pb_R[5]])
                if BCUT == 3:
                    continue
                act(tA[:, 0:256], pb[5][:, 256:512], AF.Exp, [pb_R[5]], [tA_R], scale=-1.0)
                act(Lsp[:, 0:256], tA[:, 0:256], AF.Ln, [tA_R, CR], [Lsp_R], bias=onescol[:])
                if BCUT == 4:
                    continue
                for i, (m_, d) in enumerate(((0, 0), (1, 0), (2, 1), (3, 1))):
                    mm(pb[6][:, i * 128:(i + 1) * 128], tri[:, m_, :], Lsp[:, d * 128:(d + 1) * 128], True, True, [Lsp_R, CR], [pb_R[6]])
                if BCUT == 5:
                    continue
                for d in range(2):
                    mm(pb[5][:, 2 * d:2 + 2 * d], Lsp[:, d * 128:(d + 1) * 128], onesf[:, 0:2], True, True, [Lsp_R, CR], [pb_R[5]])
                act(GL[:, t, :], pb[5][:, 0:4].rearrange("p (d two) -> p d two", two=2)[:, :, 0], AF.Exp, [pb_R[5]], [GL_R], scale=-1.0 / 16)
                if BCUT == 6:
                    continue
                Ea = tA; Eb = tB
                act(Ea[:, 0:512], pb[6], AF.Exp, [pb_R[6]], [tA_R], scale=-1.0 / 16)
                act(Eb[:, 0:256].rearrange("p (a c) -> p a c", a=2), pb[6].rearrange("p (a b c) -> p a b c", a=2, b=2)[:, :, 0, :], AF.Exp, [pb_R[6]], [tB_R], scale=1.0 / 16)
                if BCUT == 7:
                    continue
                zq, zk = z[:, 0:128], z[:, 128:256]
                stt(qkb[:, 0:128], zq, 32 ** -0.5, Ea[:, 0:128], MUL, MUL, zr + [tA_R], [qkb_R])
                stt(qkb[:, 128:256], zq, 32 ** -0.5, Ea[:, 256:384], MUL, MUL, zr + [tA_R], [qkb_R])
                tt(qkb[:, 256:384], zk, Eb[:, 0:128], MUL, zr + [tB_R], [qkb_R])
                tt(qkb[:, 384:512], zk, Eb[:, 128:256], MUL, zr + [tB_R], [qkb_R])
                tt(KDc[:, t, 0, :], zk, Ea[:, 128:256], MUL, zr + [tA_R], [KDc_R])
                tt(KDc[:, t, 1, :], zk, Ea[:, 384:512], MUL, zr + [tA_R], [KDc_R])
                if BCUT == 8:
                    continue
                cp(VB[:, t, :], z[:, 256:512], zr, [VB_R], eng="act")
                act(GR[:, t, :], z[:, 512:768], AF.Silu, zr, [GR_R])
                if BCUT == 9:
                    continue
                for i in range(4):
                    tr(PSB[:, i * 128:(i + 1) * 128], qkb[:, i * 128:(i + 1) * 128], identb[:], [qkb_R], [pb_R[7]])
                if BCUT == 10:
                    continue
                for d in range(2):
                    cp(BT[:, t, d, 0:128], PSB[:, 256 + d * 128:384 + d * 128], [pb_R[7]], [BT_R], eng="act")
                    cp(BT[:, t, d, 128:256], PSB[:, d * 128:(d + 1) * 128], [pb_R[7]], [BT_R], eng="act")
                    tt(BT[:, t, d, 256:768].rearrange("p (h n) -> p h n", h=4), bc(PSB[:, d * 128:(d + 1) * 128].unsqueeze(1), [128, 4, 128]),
                       bc(hm[:].unsqueeze(2), [128, 4, 128]), MUL, [pb_R[7], CR], [BT_R])
            if STOP.startswith('B1'):
                continue
            hm3 = bc(hm[:].unsqueeze(2), [128, 4, 64])
            for si, (q0, qlen, kts) in enumerate(seqs):
                t0, nt = q0 // 128, qlen // 128
                for d in range(2):
                    Sf, Sr = Sst[d], Sst_R[d]
                    if S_GRP:
                        kb.dma("sp", Sf[:], (sbf_d if d == 0 else sbb_d)[l], writes=[Sr])
                    else:
                        kb.op("dve", lambda e, Sf=Sf: e.memset(Sf[:], 0.0), writes=[Sr])
                    order = range(t0, t0 + nt) if d == 0 else range(t0 + nt - 1, t0 - 1, -1)
                    for t in order:
                        tt(SIN[:, t, d, :].rearrange("p (h e) -> p h e", h=4), bc(Sf[:].unsqueeze(1), [128, 4, 64]), hm3, MUL, [Sr, CR], [SIN_R])
                        b = nb()
                        mm(pb[b][:, 0:256], KDc[:, t, d, :], VB[:, t, :], True, True, [KDc_R, VB_R], [pb_R[b]])
                        tt(tA[:, 0:256].rearrange("p (h e) -> p h e", h=4), pb[b][:, 0:256].rearrange("p (h e) -> p h e", h=4), hm3, MUL, [pb_R[b], CR], [tA_R])
                        kb.op("dve", lambda e: e.reduce_sum(out=tB[:, 0:64], in_=tA[:, 0:256].rearrange("p (h e) -> p e h", h=4), axis=AX.X), reads=[tA_R], writes=[tB_R])
                        stt(Sf[:], Sf[:], GL[:, t, d:d + 1], tB[:, 0:64], MUL, ADD, [Sr, GL_R, tB_R], [Sr])
                    if not S_GRP:
                        kb.dma("sp", (nbf_d if d == 0 else nbb_d)[si, l], Sf[:], reads=[Sr])
            if STOP == 'B2':
                continue
            for t in range(8):
                ob = nb()
                mm(pb[ob][:, 0:256], BT[:, t, 0, 128:256], SIN[:, t, 0, :], True, False, [BT_R, SIN_R], [pb_R[ob]])
                mm(pb[ob][:, 0:256], BT[:, t, 1, 128:256], SIN[:, t, 1, :], False, False, [BT_R, SIN_R], [pb_R[ob]])
                for d in range(2):
                    ab = nb()
                    while ab == ob:
                        ab = nb()
                    mm(pb[ab], BT[:, t, d, 0:128], BT[:, t, d, 256:768], True, True, [BT_R], [pb_R[ab]])
                    p = pt_i[0] % 3
                    pt_i[0] += 1
                    tt(PT[p][:].rearrange("p (h n) -> p h n", h=4), pb[ab].rearrange("p (h n) -> p h n", h=4),
                       bc(tri[:, (0 if d == 0 else 2), :].unsqueeze(1), [128, 4, 128]), MUL, [pb_R[ab], CR], [PT_R[p]])
                    for h in range(4):
                        mm(pb[ob][:, h * 64:(h + 1) * 64], PT[p][:, h * 128:(h + 1) * 128], VB[:, t, h * 64:(h + 1) * 64], False, (d == 1 and h == 3),
                           [PT_R[p], VB_R], [pb_R[ob]])
                o = pb[ob][:, 0:256]
                act(tA[:, 0:256], o, AF.Square, [pb_R[ob]], [tA_R])
                kb.op("dve", lambda e: e.reduce_sum(out=sm[:, 16:20], in_=tA[:, 0:256].rearrange("p (h d) -> p h d", h=4), axis=AX.X), reads=[tA_R], writes=[sm_R])
                act(sm[:, 16:20], sm[:, 16:20], AF.Sqrt, [sm_R, CR], [sm_R], scale=1.0 / 64, bias=epsT[:])
                recip(sm[:, 16:20], sm[:, 16:20], [sm_R], [sm_R])
                tt(tA[:, 0:256].rearrange("p (h d) -> p h d", h=4), o.rearrange("p (h d) -> p h d", h=4), bc(sm[:, 16:20].unsqueeze(2), [128, 4, 64]), MUL,
                   [pb_R[ob], sm_R], [tA_R])
                tt(tA[:, 0:256], tA[:, 0:256], gBO[:], MUL, [tA_R, LP], [tA_R])
                tt(qkb[:, 0:256], tA[:, 0:256], GR[:, t, :], MUL, [tA_R, GR_R], [qkb_R])
                for i in range(2):
                    tr(PSB[:, i * 128:(i + 1) * 128], qkb[:, i * 128:(i + 1) * 128], identb[:], [qkb_R], [pb_R[7]])
                cp(BR[:, 1, :, t * 128:(t + 1) * 128], PSB[:, 0:256].rearrange("p (b n) -> p b n", b=2), [pb_R[7]], [BR_R[1]])

            if STOP == 'B':
                continue
            barrier(kb)
            if DEBUG and l == 0 and g == GROUPS[0]:
                for jb in range(8):
                    o_, or_ = nstg()
                    cp(o_[:], BR[:, jb // 2, jb % 2, :], BR_R, [or_])
                    kb.dma("sp", dbg_br[:, jb, :], o_[:], reads=[or_])
            MG = av(0, 8192).rearrange("p (k n) -> p k n", k=8); MG_R = [Res("MG0"), Res("MG1")]
            for m in range(8):
                slot, sr = new_slot()
                GWT = slot[:, 0:4096].rearrange("p (j k c) -> p j k c", j=4, k=8)
                BWT = slot[:, 4096:5120].rearrange("p (j k c) -> p j k c", j=4, k=2)
                for j in range(4):
                    c0 = 2432 + j * 1024 + m * 128
                    load_w(GWT[:, j], win_d[l].rearrange("(k p) c -> p k c", p=128)[:, :, c0:c0 + 128], sr)
                    load_w(BWT[:, j], wbr_d[l, j].rearrange("(k p) c -> p k c", p=128)[:, :, m * 128:(m + 1) * 128], sr)
                for h in range(2):
                    hs = slice(h * 512, (h + 1) * 512)
                    for j in range(4):
                        b1 = nb(); b2_ = nb()
                        for kc in range(8):
                            mm(pb[b1], GWT[:, j, kc, :], hT[:, kc, hs], kc == 0, kc == 7, [sr, hT_R[h]], [pb_R[b1]])
                        for kc in range(2):
                            mm(pb[b2_], BWT[:, j, kc, :], BR[:, j, kc, hs], kc == 0, kc == 1, [sr, BR_R[j]], [pb_R[b2_]])
                        act(tA[:, 0:512], pb[b1], AF.Sigmoid, [pb_R[b1]], [tA_R])
                        if j == 0:
                            tt(tC[:, 0:512], pb[b2_], tA[:, 0:512], MUL, [pb_R[b2_], tA_R], [tC_R])
                        else:
                            tt(tB[:, 0:512], pb[b2_], tA[:, 0:512], MUL, [pb_R[b2_], tA_R], [tB_R])
                            if j < 3:
                                tt(tC[:, 0:512], tC[:, 0:512], tB[:, 0:512], ADD, [tC_R, tB_R], [tC_R])
                            else:
                                tt(MG[:, m, hs], tC[:, 0:512], tB[:, 0:512], ADD, [tC_R, tB_R], [MG_R[h]])
            if STOP == 'M':
                continue
            slot, sr = new_slot()
            WO = slot[:, 0:8192].rearrange("p (k c) -> p k c", k=8)
            load_w(WO, wout_d[l].rearrange("(k p) c -> p k c", p=128), sr)
            for m in range(8):
                for h in range(2):
                    hs = slice(h * 512, (h + 1) * 512)
                    b = nb()
                    for kc in range(8):
                        mm(pb[b], WO[:, kc, m * 128:(m + 1) * 128], MG[:, kc, hs], kc == 0, kc == 7, [sr, MG_R[h]], [pb_R[b]])
                    stt(xT[:, m, hs], pb[b], modT[:, l, 16 + m, g:g + 1], xT[:, m, hs], MUL, ADD, [pb_R[b], MR, xT_R[m][h]], [xT_R[m][h]])
            if DEBUG and l == 0 and g == GROUPS[0]:
                for kc in range(8):
                    kb.dma("sp", dbg_x1[:, kc, :], xT[:, kc, :], reads=[xT_R[kc][0], xT_R[kc][1]])
                    o_, or_ = nstg()
                    cp(o_[:], MG[:, kc, :], MG_R, [or_])
                    kb.dma("sp", dbg_mg[:, kc, :], o_[:], reads=[or_])
            norm_mod(A2, 24, l, g)
            AT = av(8192, 22528).rearrange("p (f n) -> p f n", f=22); AT_R = [Res("AT0"), Res("AT1")]
            U_ = stg[0]; U_R = stg_R[0]; Cc = stg[1]; Cc_R = stg_R[1]
            nsq = 1 if S_GRP else 4
            sl = 1024 // nsq
            for f0 in range(0, 22, 4):
                nf = min(4, 22 - f0)
                slot, sr = new_slot()
                UW = slot[:, 0:4096].rearrange("p (k c) -> p k c", k=8)
                GW2 = slot[:, 4096:8192].rearrange("p (k c) -> p k c", k=8)
                load_w(UW[:, :, 0:nf * 128], wfu_d[l].rearrange("(k p) c -> p k c", p=128)[:, :, f0 * 128:(f0 + nf) * 128], sr)
                load_w(GW2[:, :, 0:nf * 128], wfg_d[l].rearrange("(k p) c -> p k c", p=128)[:, :, f0 * 128:(f0 + nf) * 128], sr)
                for fi in range(nf):
                    f = f0 + fi
                    gb_ = []
                    for h in range(2):
                        hs = slice(h * 512, (h + 1) * 512)
                        bu = nb(); bg = nb()
                        gb_.append(bg)
                        for kc in range(8):
                            mm(pb[bu], UW[:, kc, fi * 128:(fi + 1) * 128], hT[:, kc, hs], kc == 0, kc == 7, [sr, hT_R[h]], [pb_R[bu]])
                        for kc in range(8):
                            mm(pb[bg], GW2[:, kc, fi * 128:(fi + 1) * 128], hT[:, kc, hs], kc == 0, kc == 7, [sr, hT_R[h]], [pb_R[bg]])
                        cp(U_[:, hs], pb[bu], [pb_R[bu]], [U_R], eng="act")
                    act(Cc[:], U_[:], AF.Identity, [U_R, LP], [Cc_R], scale=cwT[:, 1, f:f + 1], bias=cbT[:, f:f + 1])
                    Uv = U_[:].rearrange("p (s n) -> p s n", s=nsq); Cv = Cc[:].rearrange("p (s n) -> p s n", s=nsq)
                    stt(Cv[:, :, 1:sl], Uv[:, :, 0:sl - 1], cwT[:, 0, f:f + 1], Cv[:, :, 1:sl], MUL, ADD, [U_R, Cc_R, LP], [Cc_R])
                    stt(Cv[:, :, 0:sl - 1], Uv[:, :, 1:sl], cwT[:, 2, f:f + 1], Cv[:, :, 0:sl - 1], MUL, ADD, [U_R, Cc_R, LP], [Cc_R])
                    act(tA[:], Cc[:], AF.Gelu_apprx_tanh, [Cc_R], [tA_R])
                    for h in range(2):
                        hs = slice(h * 512, (h + 1) * 512)
                        tt(AT[:, f, hs], tA[:, hs], pb[gb_[h]], MUL, [tA_R, pb_R[gb_[h]]], [AT_R[h]])
            for m0 in range(0, 8, 2):
                slot, sr = new_slot()
                WD = slot[:, 0:5632].rearrange("p (f c) -> p f c", f=22)
                load_w(WD, wfd_d[l].rearrange("(f p) c -> p f c", p=128)[:, :, m0 * 128:(m0 + 2) * 128], sr)
                for mi in range(2):
                    m = m0 + mi
                    for h in range(2):
                        hs = slice(h * 512, (h + 1) * 512)
                        b = nb()
                        for f in range(22):
                            mm(pb[b], WD[:, f, mi * 128:(mi + 1) * 128], AT[:, f, hs], f == 0, f == 21, [sr, AT_R[h]], [pb_R[b]])
                        stt(xT[:, m, hs], pb[b], modT[:, l, 40 + m, g:g + 1], xT[:, m, hs], MUL, ADD, [pb_R[b], MR, xT_R[m][h]], [xT_R[m][h]])

            if DEBUG and l == 0 and g == GROUPS[0]:
                for kc in range(8):
                    kb.dma("sp", dbg_x2[:, kc, :], xT[:, kc, :], reads=[xT_R[kc][0], xT_R[kc][1]])
        barrier(kb)
        for t in range(8):
            b0 = nb()
            while b0 >= 5:
                b0 = nb()
            for kc in range(8):
                bi = b0 + kc // 4
                tr(PS[:, bi * 512 + (kc % 4) * 128: bi * 512 + (kc % 4 + 1) * 128], xT[:, kc, t * 128:(t + 1) * 128], identf[:], [xT_R[kc][t // 4]], [pb_R[bi]])
            yp = PS[:, b0 * 512:b0 * 512 + 1024]
            yr = [pb_R[b0], pb_R[b0 + 1]]
            act(tA[:], yp, AF.Square, yr, [tA_R], accum_out=sm[:, 24:25])
            act(sm[:, 24:25], sm[:, 24:25], AF.Sqrt, [tA_R, CR], [sm_R], scale=1.0 / 1024, bias=epsT[:])
            recip(sm[:, 24:25], sm[:, 24:25], [sm_R], [sm_R])
            o_, or_ = nstg()
            stt(o_[:], yp, sm[:, 24:25], fgB[:], MUL, MUL, yr + [sm_R, CR], [or_])
            kb.dma("sp", y_d[g, t * 128:(t + 1) * 128, :], o_[:], reads=[or_])
    return kb


def _consts():
    c = {}
    c["k_ident"] = np.eye(128, dtype=np.float32)
    bd = np.zeros((128, 128), np.float32); bd[:64, :64] = 1; bd[64:, 64:] = 1
    c["k_bd64"] = bd
    s = np.arange(128)[:, None]; t = np.arange(128)[None, :]
    c["k_tri"] = np.stack([(s <= t), (s > t), (s >= t), (s < t)]).astype(np.float32)
    hmk = np.zeros((128, 4), np.float32)
    for h in range(4):
        hmk[h * 32:(h + 1) * 32, h] = 1
    c["k_hm"] = hmk
    pmk = np.zeros((128, 2), np.float32)
    for p in range(128):
        pmk[p, (p // 32) % 2] = 1
    c["k_pm"] = pmk
    tok = np.arange(1024)
    row = (tok // 64).astype(np.float32); col = (tok % 64).astype(np.float32)

    def tab(half):
        inv = (10000.0 ** (-np.arange(half, dtype=np.float32) / half)).astype(np.float32)
        ar = row[:, None] * inv[None, :]; ac = col[:, None] * inv[None, :]
        return (np.concatenate([np.cos(ar), np.cos(ac)], 1).astype(np.float32), np.concatenate([np.sin(ar), np.sin(ac)], 1).astype(np.float32))
    c["k_cosA"], c["k_sinA"] = tab(16)
    cC, sC = tab(8)
    c["k_cosC"], c["k_sinC"] = cC, sC
    cr, cc_ = cC[:, 0:8], cC[:, 8:16]; sr_, sc_ = sC[:, 0:8], sC[:, 8:16]
    c["k_cosD"] = np.ascontiguousarray(np.concatenate([cr, cr, cc_, cc_], 1).T)
    c["k_sinD"] = np.ascontiguousarray(np.concatenate([-sr_, sr_, -sc_, sc_], 1).T)
    return c


WNAMES = ["w_mod", "b_mod", "norm1_g", "norm2_g", "w_in", "a_qnorm_g", "a_knorm_g", "b_gate_w_fwd", "b_gate_b_fwd", "b_gate_w_bwd",
          "b_gate_b_bwd", "b_onorm_g", "c_lq1", "c_lk1", "c_lq2", "c_lk2", "c_onorm_g", "d_qnorm_g", "d_w_uq", "d_kvnorm_g", "d_w_ukv",
          "w_branch", "w_out", "w_ffu", "w_ffg", "conv_w", "conv_b", "w_ffd", "final_g"]


def in_map(inp, c, consts, wts):
    m = dict(wts)
    m.update(consts)
    f = lambda a: np.ascontiguousarray(a, dtype=np.float32)
    m["x"] = f(np.stack([inp["x_prompt"][4 * c:4 * c + 4].reshape(1024, 1024), inp["x_sample"][c]]))
    m["cvec"] = f(np.stack([inp["c_ctx"], inp["c"][c]]))
    m["ca_k"] = f(inp["cache_a_k"][c].reshape(4, 512, 128)); m["ca_v"] = f(inp["cache_a_v"][c].reshape(4, 512, 128))
    m["sb_f"] = f(inp["state_b_fwd"][c].reshape(4, 128, 64)); m["sb_b"] = f(inp["state_b_bwd"][c].reshape(4, 128, 64))
    m["cc_k"] = f(inp["cache_c_k"][c].reshape(4, 512, 256)); m["cc_v"] = f(inp["cache_c_v"][c].reshape(4, 512, 256))
    m["cd_ckv"] = f(inp["cache_d_ckv"][c]); m["cd_kr"] = f(inp["cache_d_krope"][c])
    return m


def assemble(R):
    y_p = np.concatenate([r["y"][0].reshape(4, 256, 1024) for r in R], 0)
    y_s = np.stack([r["y"][1] for r in R], 0)

    def cat(name, shp):
        return np.concatenate([r[name].reshape((4,) + shp) for r in R], 0)
    return (y_p, y_s, cat("nak", (4, 256, 2, 64)), cat("nav", (4, 256, 2, 64)), cat("nbf", (4, 4, 32, 64)), cat("nbb", (4, 4, 32, 64)),
            cat("nck", (4, 256, 4, 2, 32)), cat("ncv", (4, 256, 4, 64)), cat("nckv", (4, 256, 128)), cat("nkr", (4, 256, 32)))


def kernel(**inp):
    inp = {k: np.asarray(v) for k, v in inp.items()}
    kb = build()
    nc = kb.finish()
    consts = _consts()
    wts = {k: np.ascontiguousarray(inp[k], dtype=np.float32) for k in WNAMES}
    in_maps = [in_map(inp, c, consts, wts) for c in range(8)]
    res = run_bass_kernel_spmd(nc, in_maps, core_ids=list(range(8)))
    return assemble(res.results)
```

```python
from contextlib import ExitStack
import numpy as np
import concourse.bass as bass
import concourse.mybir as mybir

F32 = mybir.dt.float32
BF16 = mybir.dt.bfloat16
AF = mybir.ActivationFunctionType
ALU = mybir.AluOpType
AX = mybir.AxisListType

EPOCH = 30000
NDMA_SEM = 20


class Res:
    __slots__ = ("name", "w", "r", "excl")

    def __init__(self, name="", excl=False):
        self.name = name
        self.excl = excl
        self.w = []
        self.r = []


class EngState:
    def __init__(self, name, handle_name, is_compute):
        self.name = name
        self.handle_name = handle_name
        self.is_compute = is_compute
        self.prog = []
        self.sem = None
        self.cnt = 0
        self.known = {}
        self.dma_ring = []
        self.dma_i = 0
        self.nops = 0


class KB:
    def __init__(self):
        self.nc = bass.Bass("TRN2", target_bir_lowering=False)
        self.es = ExitStack()
        self.E = {
            "pe": EngState("pe", "tensor", True),
            "act": EngState("act", "scalar", True),
            "dve": EngState("dve", "vector", True),
            "pool": EngState("pool", "gpsimd", True),
            "sp": EngState("sp", "sync", False),
        }
        self.nsem = 0
        self.all_dma_events = []

    def new_sem(self, name):
        self.nsem += 1
        return self.es.enter_context(self.nc.semaphore(f"{name}_{self.nsem}"))

    def sbuf(self, name, shape, dtype=F32):
        return self.es.enter_context(self.nc.sbuf_tensor(name, list(shape), dtype))

    def psum(self, name, shape, dtype=F32):
        return self.es.enter_context(self.nc.psum_tensor(name, list(shape), dtype))

    def dram(self, name, shape, dtype, kind):
        return self.nc.dram_tensor(name, list(shape), dtype, kind=kind)

    def _wait(self, st, ev):
        sem, val, _ = ev
        k = id(sem)
        if st.known.get(k, 0) >= val:
            return
        st.known[k] = val
        st.prog.append(("wait", sem, val))

    def _deps(self, st, reads, writes, is_dma):
        for r in reads:
            for ev in r.w:
                self._wait(st, ev)
            if r.excl:
                for ev in r.r:
                    if ev[2] != st.name:
                        self._wait(st, ev)
        for w in writes:
            for ev in w.w:
                if is_dma or ev[2] != st.name or not st.is_compute:
                    self._wait(st, ev)
            for ev in w.r:
                if is_dma or ev[2] != st.name or not st.is_compute:
                    self._wait(st, ev)

    def _commit(self, ev, reads, writes):
        for r in reads:
            if r in writes:
                continue
            if ev[2] in ("pe", "act", "dve", "pool"):
                r.r = [e for e in r.r if e[2] != ev[2]]
            r.r.append(ev)
        for w in writes:
            w.w = [ev]
            w.r = []

    def op(self, eng, fn, reads=(), writes=()):
        st = self.E[eng]
        assert st.is_compute
        self._deps(st, reads, writes, False)
        if st.sem is None or st.cnt >= EPOCH:
            st.sem = self.new_sem(f"s_{eng}")
            st.cnt = 0
        st.cnt += 1
        ev = (st.sem, st.cnt, eng)
        st.prog.append(("op", fn, st.sem))
        st.nops += 1
        self._commit(ev, reads, writes)
        return ev

    def dma(self, q, out, in_, reads=(), writes=(), **kw):
        st = self.E[q]
        kw.setdefault("allow_slow_non_contiguous", True)
        for r in reads:
            for ev in r.w:
                self._wait(st, ev)
        for w in writes:
            for ev in w.w:
                if ev[2].startswith("dma") and not w.r:
                    continue
                self._wait(st, ev)
            for ev in w.r:
                self._wait(st, ev)
        if not st.dma_ring:
            st.dma_ring = [[self.new_sem(f"d_{q}"), 0] for _ in range(NDMA_SEM)]
        slot = st.dma_ring[st.dma_i % NDMA_SEM]
        st.dma_i += 1
        if slot[1] > 0:
            self._wait(st, (slot[0], slot[1], "dma"))
        slot[1] += 16
        ev = (slot[0], slot[1], "dma_" + q)
        st.prog.append(("dma", out, in_, slot[0], kw))
        keep = {id(w): list(w.w) for w in writes if w.w and not w.r and all(e[2].startswith("dma") for e in w.w)}
        self._commit(ev, reads, writes)
        for w in writes:
            if id(w) in keep:
                w.w = keep[id(w)] + [ev]
        self.all_dma_events.append(ev)
        return ev

    def finish(self):
        nc = self.nc
        sp = self.E["sp"]
        for st in self.E.values():
            for sem, val in st.dma_ring:
                if val > 0:
                    self._wait(sp, (sem, val, "dma"))
        with nc.Block() as block:
            for st in self.E.values():
                if not st.prog:
                    continue

                def body(eng, st=st):
                    for item in st.prog:
                        if item[0] == "wait":
                            eng.wait_ge(item[1], item[2])
                        elif item[0] == "op":
                            ins = item[1](eng)
                            ins.then_inc(item[2], 1)
                        else:
                            _, out, in_, sem, kw = item
                            eng.dma_start(out=out, in_=in_, **kw).then_inc(sem, 16)

                getattr(block, st.handle_name)(body)
        self.es.close()
        return nc

import math
from concourse.bass_utils import run_bass_kernel_spmd

L = 4
EPS = 1e-6
MUL, ADD, SUB = ALU.mult, ALU.add, ALU.subtract
VOFFS = [0, 64, 192, 256]
VDST = [(0, 64, 0, 64), (128, 256, 64, 192), (320, 384, 192, 256)]


def barrier(kb):
    comp = ["pe", "act", "dve", "pool"]
    for e in comp + ["sp"]:
        st = kb.E[e]
        for o in comp:
            so = kb.E[o]
            if o != e and so.sem is not None and so.cnt > 0:
                kb._wait(st, (so.sem, so.cnt, o))
        for q in kb.E.values():
            for sem, val in q.dma_ring:
                if val > 0:
                    kb._wait(st, (sem, val, "dma"))


def build(NLAYERS=L, GROUPS=(0, 1), STOP='', DEBUG=False):
    kb = KB()
    nc = kb.nc
    BCUT = int(STOP.split(':')[1]) if ':' in STOP else 0

    def din(name, shape):
        return kb.dram(name, shape, F32, "ExternalInput").ap()

    def dout(name, shape):
        return kb.dram(name, shape, F32, "ExternalOutput").ap()

    x_d = din("x", [2, 1024, 1024])
    cvec_d = din("cvec", [2, 1024])
    cak_d = din("ca_k", [L, 512, 128]); cav_d = din("ca_v", [L, 512, 128])
    sbf_d = din("sb_f", [L, 128, 64]); sbb_d = din("sb_b", [L, 128, 64])
    cck_d = din("cc_k", [L, 512, 256]); ccv_d = din("cc_v", [L, 512, 256])
    cdc_d = din("cd_ckv", [L, 512, 128]); cdr_d = din("cd_kr", [L, 512, 32])
    wmod_d = din("w_mod", [L, 1024, 6144]); bmod_d = din("b_mod", [L, 6144])
    n1_d = din("norm1_g", [L, 1024]); n2_d = din("norm2_g", [L, 1024])
    win_d = din("w_in", [L, 1024, 6528])
    aqg_d = din("a_qnorm_g", [L, 64]); akg_d = din("a_knorm_g", [L, 64])
    gwf_d = din("b_gate_w_fwd", [L, 16, 128]); gbf_d = din("b_gate_b_fwd", [L, 128])
    gwb_d = din("b_gate_w_bwd", [L, 16, 128]); gbb_d = din("b_gate_b_bwd", [L, 128])
    bon_d = din("b_onorm_g", [L, 64])
    lq1_d = din("c_lq1", [L, 32]); lk1_d = din("c_lk1", [L, 32]); lq2_d = din("c_lq2", [L, 32]); lk2_d = din("c_lk2", [L, 32])
    con_d = din("c_onorm_g", [L, 64])
    dqg_d = din("d_qnorm_g", [L, 192]); wuq_d = din("d_w_uq", [L, 192, 384])
    dkg_d = din("d_kvnorm_g", [L, 128]); wukv_d = din("d_w_ukv", [L, 128, 512])
    wbr_d = din("w_branch", [L, 4, 256, 1024]); wout_d = din("w_out", [L, 1024, 1024])
    wfu_d = din("w_ffu", [L, 1024, 2816]); wfg_d = din("w_ffg", [L, 1024, 2816])
    cw_d = din("conv_w", [L, 3, 2816]); cb_d = din("conv_b", [L, 2816]); wfd_d = din("w_ffd", [L, 2816, 1024])
    fg_d = din("final_g", [1024])
    k_ident = din("k_ident", [128, 128]); k_bd64 = din("k_bd64", [128, 128]); k_tri = din("k_tri", [4, 128, 128])
    k_hm = din("k_hm", [128, 4]); k_pm = din("k_pm", [128, 2])
    k_cosA = din("k_cosA", [1024, 32]); k_sinA = din("k_sinA", [1024, 32])
    k_cosC = din("k_cosC", [1024, 16]); k_sinC = din("k_sinC", [1024, 16])
    k_cosD = din("k_cosD", [32, 1024]); k_sinD = din("k_sinD", [32, 1024])

    y_d = dout("y", [2, 1024, 1024])
    nak_d = dout("nak", [4, L, 256, 128]); nav_d = dout("nav", [4, L, 256, 128])
    nbf_d = dout("nbf", [4, L, 128, 64]); nbb_d = dout("nbb", [4, L, 128, 64])
    nck_d = dout("nck", [4, L, 256, 256]); ncv_d = dout("ncv", [4, L, 256, 256])
    nckv_d = dout("nckv", [4, L, 256, 128]); nkr_d = dout("nkr", [4, L, 256, 32])

    if DEBUG:
        dbg_br = dout('dbg_br', [128, 8, 1024]); dbg_x1 = dout('dbg_x1', [128, 8, 1024]); dbg_x2 = dout('dbg_x2', [128, 8, 1024]); dbg_mg = dout('dbg_mg', [128, 8, 1024])
    xT = kb.sbuf("xT", [128, 8, 1024]); xT_R = [[Res(f"xT{k}_{h}") for h in range(2)] for k in range(8)]
    hT = kb.sbuf("hT", [128, 8, 1024], BF16); hT_R = [Res("hT0"), Res("hT1")]
    NSLOT = 3
    ring = [kb.sbuf(f"ring{i}", [128, 8192], BF16) for i in range(NSLOT)]
    ring_R = [Res(f"ring{i}") for i in range(NSLOT)]
    ring_i = [0]
    arena = kb.sbuf("arena", [128, 30720], BF16)
    PS = kb.psum("ps", [128, 4096])
    pb = [PS[:, i * 512:(i + 1) * 512] for i in range(8)]
    pb_R = [Res(f"pb{i}", excl=True) for i in range(8)]
    PSB = PS[:, 7 * 512:8 * 512].bitcast(BF16)

    def new_slot(avoid=None):
        i = ring_i[0] % NSLOT
        ring_i[0] += 1
        while avoid is not None and ring[i] is avoid:
            i = ring_i[0] % NSLOT
            ring_i[0] += 1
        return ring[i], ring_R[i]

    def av(off, n):
        return arena[:, off:off + n]

    def vap(off, stride):
        return bass.AP(arena, off, [[30720, 128], [stride, 2], [1, 64]])

    identf = kb.sbuf("identf", [128, 128]); identb = kb.sbuf("identb", [128, 128], BF16)
    onesb = kb.sbuf("onesb", [128, 128], BF16); bd64 = kb.sbuf("bd64", [128, 128], BF16)
    tri = kb.sbuf("tri", [128, 4, 128]); hm = kb.sbuf("hm", [128, 4]); pm = kb.sbuf("pm", [128, 2])
    onescol = kb.sbuf("onescol", [128, 1]); onesrow = kb.sbuf("onesrow", [1, 128]); epsT = kb.sbuf("epsT", [128, 1])
    onesf = kb.sbuf("onesf", [128, 128])
    cosA = kb.sbuf("cosA", [128, 8, 32]); sinA = kb.sbuf("sinA", [128, 8, 32])
    cosC = kb.sbuf("cosC", [128, 8, 16]); sinC = kb.sbuf("sinC", [128, 8, 16])
    cosD = kb.sbuf("cosD", [128, 1024]); sinD = kb.sbuf("sinD", [128, 1024])
    CR = Res("consts")
    kb.dma("sp", identf[:], k_ident, writes=[CR])
    kb.dma("pool", identb[:], k_ident, writes=[CR])
    kb.dma("pool", bd64[:], k_bd64, writes=[CR])
    kb.dma("sp", tri[:], k_tri.rearrange("m s t -> s m t"), writes=[CR])
    kb.dma("sp", hm[:], k_hm, writes=[CR]); kb.dma("sp", pm[:], k_pm, writes=[CR])
    kb.dma("sp", cosA[:], k_cosA.rearrange("(t p) d -> p t d", p=128), writes=[CR])
    kb.dma("sp", sinA[:], k_sinA.rearrange("(t p) d -> p t d", p=128), writes=[CR])
    kb.dma("sp", cosC[:], k_cosC.rearrange("(t p) d -> p t d", p=128), writes=[CR])
    kb.dma("sp", sinC[:], k_sinC.rearrange("(t p) d -> p t d", p=128), writes=[CR])
    kb.dma("sp", cosD[64:96, :], k_cosD, writes=[CR]); kb.dma("sp", sinD[64:96, :], k_sinD, writes=[CR])
    kb.op("dve", lambda e: e.memset(onesb[:], 1.0), writes=[CR])
    kb.op("dve", lambda e: e.memset(onescol[:], 1.0), writes=[CR])
    kb.op("dve", lambda e: e.memset(onesrow[:], 1.0), writes=[CR])
    kb.op("dve", lambda e: e.memset(onesf[:], 1.0), writes=[CR])
    kb.op("dve", lambda e: e.memset(epsT[:], EPS), writes=[CR])

    def mm(out, lhsT, rhs, start, stop, reads, writes):
        kb.op("pe", lambda e: e.matmul(out, lhsT=lhsT, rhs=rhs, start=start, stop=stop), reads=reads, writes=writes)

    def tr(out, in_, ident, reads, writes):
        kb.op("pe", lambda e: e.transpose(out=out, in_=in_, identity=ident), reads=reads + [CR], writes=writes)

    def act(out, in_, func, reads, writes, scale=1.0, bias=None, accum_out=None):
        kw = {}
        if bias is not None:
            kw["bias"] = bias
        if accum_out is not None:
            kw["accum_out"] = accum_out
        kb.op("act", lambda e: e.activation(out=out, in_=in_, func=func, scale=scale, **kw), reads=reads, writes=writes)

    def tt(out, in0, in1, op, reads, writes, eng="dve"):
        kb.op(eng, lambda e: e.tensor_tensor(out=out, in0=in0, in1=in1, op=op), reads=reads, writes=writes)

    def stt(out, in0, scalar, in1, op0, op1, reads, writes):
        kb.op("dve", lambda e: e.scalar_tensor_tensor(out=out, in0=in0, scalar=scalar, in1=in1, op0=op0, op1=op1), reads=reads, writes=writes)

    def ts(out, in0, s1, op0, reads, writes, s2=None, op1=None, eng="dve"):
        if op1 is None:
            kb.op(eng, lambda e: e.tensor_scalar(out=out, in0=in0, scalar1=s1, scalar2=None, op0=op0), reads=reads, writes=writes)
        else:
            kb.op(eng, lambda e: e.tensor_scalar(out=out, in0=in0, scalar1=s1, scalar2=s2, op0=op0, op1=op1), reads=reads, writes=writes)

    def cp(out, in_, reads, writes, eng="dve"):
        if eng == "act":
            kb.op("act", lambda e: e.copy(out=out, in_=in_), reads=reads, writes=writes)
        else:
            kb.op(eng, lambda e: e.tensor_copy(out=out, in_=in_), reads=reads, writes=writes)

    def recip(out, in_, reads, writes):
        kb.op("dve", lambda e: e.reciprocal(out=out, in_=in_), reads=reads, writes=writes)

    def bc(ap, shape):
        return ap.broadcast_to(list(shape))

    pbi = [0]

    def nb():
        i = pbi[0] % 7
        pbi[0] += 1
        return i

    scT = kb.sbuf("scT", [128, 8, 2]); modT = kb.sbuf("modT", [128, L, 48, 2]); bmT = kb.sbuf("bmT", [128, L, 48])
    n1T = kb.sbuf("n1T", [128, L, 8]); n2T = kb.sbuf("n2T", [128, L, 8])
    A1 = kb.sbuf("A1", [128, L, 8, 2]); A2 = kb.sbuf("A2", [128, L, 8, 2])
    MR = Res("mod")
    with nc.allow_non_contiguous_dma(reason="tiny transposed vector loads"):
        for g_ in range(2):
            kb.dma("sp", scT[:, :, g_], cvec_d[g_].rearrange("(k p) -> p k", p=128), writes=[MR])
        for l_ in range(L):
            kb.dma("sp", bmT[:, l_, :], bmod_d[l_].rearrange("(j p) -> p j", p=128), writes=[MR])
            kb.dma("sp", n1T[:, l_, :], n1_d[l_].rearrange("(k p) -> p k", p=128), writes=[MR])
            kb.dma("sp", n2T[:, l_, :], n2_d[l_].rearrange("(k p) -> p k", p=128), writes=[MR])
    act(scT[:], scT[:], AF.Silu, [MR], [MR])
    scb = kb.sbuf("scb", [128, 8, 2], BF16)
    cp(scb[:], scT[:], [MR], [MR])
    for l in range(L):
        for cb in range(8):
            slot, wr = new_slot()
            w = slot[:, 0:6144].rearrange("p (k c) -> p k c", k=8)
            kb.dma("pool", w, wmod_d[l].rearrange("(k p) c -> p k c", p=128)[:, :, cb * 768:(cb + 1) * 768], writes=[wr])
            b = nb()
            for jj in range(6):
                for kc in range(8):
                    mm(pb[b][:, jj * 2:(jj + 1) * 2], w[:, kc, jj * 128:(jj + 1) * 128], scb[:, kc, :], kc == 0, kc == 7, [wr, MR], [pb_R[b]])
            tt(modT[:, l, cb * 6:(cb + 1) * 6, :], pb[b][:, 0:12].rearrange("p (j g) -> p j g", g=2),
               bc(bmT[:, l, cb * 6:(cb + 1) * 6].unsqueeze(2), [128, 6, 2]), ADD, [pb_R[b], MR], [MR])
    for l in range(L):
        stt(A1[:, l], modT[:, l, 8:16, :], 1.0, bc(n1T[:, l, :].unsqueeze(2), [128, 8, 2]), ADD, MUL, [MR], [MR])
        stt(A2[:, l], modT[:, l, 32:40, :], 1.0, bc(n2T[:, l, :].unsqueeze(2), [128, 8, 2]), ADD, MUL, [MR], [MR])

    lqk = kb.sbuf("lqk", [32, 4, L]); lamT = kb.sbuf("lamT", [128, L]); neglam = kb.sbuf("neglam", [128, L])
    ocs = kb.sbuf("ocs", [128, L]); lam2 = kb.sbuf("lam2", [128, 2, L])
    with nc.allow_non_contiguous_dma(reason="tiny transposed vector loads"):
        for i, d in enumerate([lq1_d, lk1_d, lq2_d, lk2_d]):
            kb.dma("sp", lqk[:, i, :], d.rearrange("l d -> d l"), writes=[MR])
        kb.dma("sp", ocs[0:64, :], con_d.rearrange("l d -> d l"), writes=[MR])
        kb.dma("sp", ocs[64:128, :], con_d.rearrange("l d -> d l"), writes=[MR])
    tt(lqk[:, 0, :], lqk[:, 0, :], lqk[:, 1, :], MUL, [MR], [MR])
    tt(lqk[:, 2, :], lqk[:, 2, :], lqk[:, 3, :], MUL, [MR], [MR])
    b = nb()
    mm(pb[b][:, 0:L], onesf[0:32, :], lqk[:, 0, :], True, True, [MR, CR], [pb_R[b]])
    mm(pb[b][:, L:2 * L], onesf[0:32, :], lqk[:, 2, :], True, True, [MR, CR], [pb_R[b]])
    act(lam2[:].rearrange("p a l -> p (a l)"), pb[b][:, 0:2 * L], AF.Exp, [pb_R[b]], [MR])
    tt(lamT[:], lam2[:, 0, :], lam2[:, 1, :], SUB, [MR], [MR])
    lam_init = [0.8 - 0.6 * math.exp(-0.3 * l) for l in range(L)]
    for l in range(L):
        ts(lamT[:, l:l + 1], lamT[:, l:l + 1], lam_init[l], ADD, [MR], [MR])
        ts(ocs[:, l:l + 1], ocs[:, l:l + 1], 1.0 - lam_init[l], MUL, [MR], [MR])
    ts(neglam[:], lamT[:], -1.0, MUL, [MR], [MR])

    gA = kb.sbuf("gA", [128, 384]); gDQ = kb.sbuf("gDQ", [128, 192]); gKV = kb.sbuf("gKV", [128, 128]); gBO = kb.sbuf("gBO", [128, 256])
    GW = kb.sbuf("GW", [16, 256]); GBias = kb.sbuf("GBias", [1, 256]); cwT = kb.sbuf("cwT", [128, 3, 22]); cbT = kb.sbuf("cbT", [128, 22])
    fgB = arena[:, 0:2048].bitcast(F32); fgB_R = Res("fgB")
    LP = Res("layer_params")

    def load_layer_params(l):
        def bsrc(d, n, rep):
            return bass.AP(d.tensor, d[l].offset, [[0, 128], [0, rep], [1, n]])
        kb.dma("sp", gA[:, 0:256].rearrange("p (r d) -> p r d", r=4), bsrc(aqg_d, 64, 4), writes=[LP])
        kb.dma("sp", gA[:, 256:384].rearrange("p (r d) -> p r d", r=2), bsrc(akg_d, 64, 2), writes=[LP])
        kb.dma("sp", gDQ[:], dqg_d[l].partition_broadcast(128), writes=[LP])
        kb.dma("sp", gKV[:], dkg_d[l].partition_broadcast(128), writes=[LP])
        kb.dma("sp", gBO[:].rearrange("p (r d) -> p r d", r=4), bsrc(bon_d, 64, 4), writes=[LP])
        kb.dma("sp", GW[0:16, 0:128], gwf_d[l], writes=[LP]); kb.dma("sp", GW[0:16, 128:256], gwb_d[l], writes=[LP])
        kb.dma("sp", GBias[0:1, 0:128], gbf_d[l:l + 1, :], writes=[LP]); kb.dma("sp", GBias[0:1, 128:256], gbb_d[l:l + 1, :], writes=[LP])
        with nc.allow_non_contiguous_dma(reason="tiny transposed vector loads"):
            for w_ in range(3):
                kb.dma("sp", cwT[:, w_, :], cw_d[l, w_].rearrange("(f p) -> p f", p=128), writes=[LP])
            kb.dma("sp", cbT[:], cb_d[l].rearrange("(f p) -> p f", p=128), writes=[LP])

    stg = [kb.sbuf(f"stg{i}", [128, 1024]) for i in range(2)]; stg_R = [Res("stg0"), Res("stg1")]
    stg_i = [0]

    def nstg():
        i = stg_i[0] % 2
        stg_i[0] += 1
        return stg[i], stg_R[i]

    tA = kb.sbuf("tA", [128, 1024]); tA_R = Res("tA")
    tB = kb.sbuf("tB", [128, 1024]); tB_R = Res("tB")
    tC = kb.sbuf("tC", [128, 512]); tC_R = Res("tC")
    sm = kb.sbuf("sm", [128, 64]); sm_R = Res("sm")
    sqb = arena[:, 0:4096].rearrange("p (k n) -> p k n", k=8); sqb_R = Res("sqb")
    MG_R = [Res("MG0"), Res("MG1")]
    rstd = kb.sbuf("rstd", [128, 512]); rstd_R = Res("rstd")
    PT = [kb.sbuf(f"PT{i}", [128, 512], BF16) for i in range(3)]; PT_R = [Res(f"PT{i}") for i in range(3)]
    pt_i = [0]
    qkb = kb.sbuf("qkb", [128, 512], BF16); qkb_R = Res("qkb")
    krpad = kb.sbuf("krpad", [128, 96], BF16); krpad_R = Res("krpad")
    kb.op("dve", lambda e: e.memset(krpad[:], 0.0), writes=[krpad_R])
    Sst = [kb.sbuf(f"Sst{i}", [128, 64]) for i in range(2)]; Sst_R = [Res("Sf"), Res("Sb")]
    GL = kb.sbuf("GL", [128, 8, 2]); GL_R = Res("GL")
    zgT = kb.sbuf("zgT", [16, 256]); zgT_R = Res("zgT")

    def norm_mod(Asc, shift_c0, l, g):
        for h in range(2):
            hs = slice(h * 512, (h + 1) * 512)
            xr = [xT_R[k][h] for k in range(8)]
            act(sqb, xT[:, :, hs], AF.Square, xr, [sqb_R, MG_R[0], MG_R[1]])
            b = nb()
            for kc in range(8):
                mm(pb[b], onesb[:], sqb[:, kc, :], kc == 0, kc == 7, [sqb_R, CR], [pb_R[b]])
            act(rstd[:], pb[b], AF.Sqrt, [pb_R[b], CR], [rstd_R], scale=1.0 / 1024, bias=epsT[:])
            recip(rstd[:], rstd[:], [rstd_R], [rstd_R])
            for kc in range(8):
                tmp, tr_ = (tA, tA_R) if kc % 2 == 0 else (tB, tB_R)
                tt(tmp[:, 0:512], xT[:, kc, hs], rstd[:], MUL, [xT_R[kc][h], rstd_R], [tr_])
                act(hT[:, kc, hs], tmp[:, 0:512], AF.Identity, [tr_, MR], [hT_R[h]],
                    scale=Asc[:, l, kc, g:g + 1], bias=modT[:, l, shift_c0 + kc, g:g + 1])

    def load_w(dst, src, sr):
        kb.dma("pool", dst, src, writes=[sr])

    def proj_tm(bank0, ncols, W, wr, t):
        nbk = (ncols + 511) // 512
        for bi in range(nbk):
            c0 = bi * 512
            c1 = min(ncols, c0 + 512)
            for kc in range(8):
                mm(pb[bank0 + bi][:, 0:c1 - c0], hT[:, kc, t * 128:(t + 1) * 128], W[:, kc, c0:c1], kc == 0, kc == 7,
                   [hT_R[t // 4], wr], [pb_R[bank0 + bi]])

    def rope_tm(src, src_R, U2, hs, cos_t, sin_t, out):
        xv = src.rearrange("p (u r h i) -> p u r h i", u=U2, r=2, h=2)
        ov = out.rearrange("p (u r h i) -> p u r h i", u=U2, r=2, h=2)
        cv = bc(cos_t.rearrange("p (r i) -> p r i", r=2).unsqueeze(1), [128, U2, 2, hs])
        sv = bc(sin_t.rearrange("p (r i) -> p r i", r=2).unsqueeze(1), [128, U2, 2, hs])
        n = U2 * 2 * hs
        a = tA[:, 0:n].rearrange("p (u r i) -> p u r i", u=U2, r=2)
        b2 = tB[:, 0:n].rearrange("p (u r i) -> p u r i", u=U2, r=2)
        x0 = xv[:, :, :, 0, :]; x1 = xv[:, :, :, 1, :]
        tt(a, x0, cv, MUL, [src_R, CR], [tA_R]); tt(b2, x1, sv, MUL, [src_R, CR], [tB_R])
        tt(ov[:, :, :, 0, :], a, b2, SUB, [tA_R, tB_R], [qkb_R])
        tt(a, x1, cv, MUL, [src_R, CR], [tA_R]); tt(b2, x0, sv, MUL, [src_R, CR], [tB_R])
        tt(ov[:, :, :, 1, :], a, b2, ADD, [tA_R, tB_R], [qkb_R])

    def attend(kT, qT, vT, kts, q0, N, scale, reads, ob, avoid=()):
        sb = [None] * len(kts)

        def S(i):
            b = nb()
            while b == ob or b in avoid:
                b = nb()
            sb[i] = b
            mm(pb[b][:, 0:N], kT(kts[i]), qT, True, True, reads, [pb_R[b]])
        S(0)
        for i in range(len(kts)):
            if i + 1 < len(kts):
                S(i + 1)
            p = pt_i[0] % 3
            pt_i[0] += 1
            act(PT[p][:, 0:N], pb[sb[i]][:, 0:N], AF.Exp, [pb_R[sb[i]]], [PT_R[p]], scale=scale)
            mm(pb[ob][:, 0:N], vT(kts[i]), PT[p][:, 0:N], i == 0, i == len(kts) - 1, reads + [PT_R[p]], [pb_R[ob]])

    def obank():
        b = nb()
        return b

    for g in GROUPS:
        S_GRP = (g == 1)
        NKT = 12 if S_GRP else 8
        KOFF = 4 if S_GRP else 0
        NK = NKT * 128
        if S_GRP:
            seqs = [(0, 1024, list(range(12)))]
        else:
            seqs = [(s * 256, 256, [2 * s, 2 * s + 1]) for s in range(4)]
        for t in range(8):
            st_, sr = nstg()
            kb.dma("sp", st_[:], x_d[g, t * 128:(t + 1) * 128, :], writes=[sr])
            for hh in range(2):
                b = nb()
                for kk in range(4):
                    kc = hh * 4 + kk
                    tr(pb[b][:, kk * 128:(kk + 1) * 128], st_[:, kc * 128:(kc + 1) * 128], identf[:], [sr], [pb_R[b]])
                cp(xT[:, hh * 4:(hh + 1) * 4, t * 128:(t + 1) * 128], pb[b].rearrange("p (k c) -> p k c", k=4), [pb_R[b]],
                   [xT_R[k][t // 4] for k in range(hh * 4, hh * 4 + 4)], eng=("act" if hh else "dve"))

        for l in range(NLAYERS):
            load_layer_params(l)
            norm_mod(A1, 0, l, g)
            if STOP == 'N':
                continue
            BR_OFF = 22528
            BR = arena[:, BR_OFF:BR_OFF + 8192].rearrange("p (j b n) -> p j b n", j=4, b=2)
            BR_R = [Res(f"BR{j}") for j in range(4)]
            OUTS = not S_GRP

            slot, sr = new_slot()
            W = slot[:, 0:4096].rearrange("p (k c) -> p k c", k=8)
            wsrc = win_d[l].rearrange("(k p) c -> p k c", p=128)
            for g2_ in range(2):
                for kv_ in range(2):
                    load_w(W[:, :, g2_ * 128 + kv_ * 64:g2_ * 128 + kv_ * 64 + 64], wsrc[:, :, (kv_ * 2 + g2_) * 64:(kv_ * 2 + g2_) * 64 + 64], sr)
            load_w(W[:, :, 256:512], wsrc[:, :, 256:512], sr)
            barrier(kb)
            QT = av(0, 2048).rearrange("p (b n) -> p b n", b=2); QT_R = Res("QT_A")
            KT = av(2048, 1536); KT_R = Res("KT_A")
            VA = av(3584, 4608).rearrange("p (k c) -> p k c", c=384); VA_R = Res("VA")
            kb.op("dve", lambda e: e.memset(VA[:, 0:NKT, :], 1.0), writes=[VA_R])
            if S_GRP:
                st_, sr2 = nstg()
                kb.dma("sp", st_[:, 0:512].rearrange("p (k f) -> p k f", k=4), cak_d[l].rearrange("(k p) f -> p k f", p=128), writes=[sr2])
                b = nb()
                for kt in range(4):
                    tr(pb[b][:, kt * 128:(kt + 1) * 128], st_[:, kt * 128:(kt + 1) * 128], identf[:], [sr2], [pb_R[b]])
                cp(KT[:, 0:512], pb[b], [pb_R[b]], [KT_R])
                kb.dma("pool", VA[:, 0:4, 64:128], cav_d[l].rearrange("(k p) f -> p k f", p=128)[:, :, 0:64], writes=[VA_R])
                kb.dma("pool", VA[:, 0:4, 256:320], cav_d[l].rearrange("(k p) f -> p k f", p=128)[:, :, 64:128], writes=[VA_R])
            if STOP == 'A0':
                continue
            for t in range(8):
                zb = nb()
                if zb == 6:
                    zb = nb()
                z = pb[zb]; zr = pb_R[zb]
                for kc in range(8):
                    mm(z, hT[:, kc, t * 128:(t + 1) * 128], W[:, kc, :], kc == 0, kc == 7, [hT_R[t // 4], sr], [zr])
                if STOP == 'A2a':
                    continue
                act(tC[:, 0:384], z[:, 0:384], AF.Square, [zr], [tC_R])
                kb.op("dve", lambda e: e.reduce_sum(out=sm[:, 0:6], in_=tC[:, 0:384].rearrange("p (h d) -> p h d", h=6), axis=AX.X), reads=[tC_R], writes=[sm_R])
                if STOP == 'A2b':
                    continue
                act(sm[:, 0:6], sm[:, 0:6], AF.Sqrt, [sm_R, CR], [sm_R], scale=1.0 / 64, bias=epsT[:])
                recip(sm[:, 0:6], sm[:, 0:6], [sm_R], [sm_R])
                if STOP == 'A2c':
                    continue
                tt(tC[:, 0:384].rearrange("p (h d) -> p h d", h=6), z[:, 0:384].rearrange("p (h d) -> p h d", h=6),
                   bc(sm[:, 0:6].unsqueeze(2), [128, 6, 64]), MUL, [zr, sm_R], [tC_R])
                tt(tC[:, 0:384], tC[:, 0:384], gA[:], MUL, [tC_R, LP], [tC_R])
                kt = KOFF + t
                if STOP == 'A2':
                    continue
                cp(VA[:, kt, 64:128], z[:, 384:448], [zr], [VA_R], eng="act")
                cp(VA[:, kt, 256:320], z[:, 448:512], [zr], [VA_R], eng="act")
                if S_GRP:
                    rope_tm(tC[:, 0:384], tC_R, 6, 16, cosA[:, t, :], sinA[:, t, :], qkb[:, 0:384])
                else:
                    cp(qkb[:, 0:384], tC[:, 0:384], [tC_R], [qkb_R])
                    o_, or_ = nstg()
                    cp(o_[:, 0:128], tC[:, 256:384], [tC_R], [or_], eng="act")
                    cp(o_[:, 128:256], z[:, 384:512], [zr], [or_], eng="act")
                    s_, tt_ = t // 2, (t % 2) * 128
                    kb.dma("sp", nak_d[s_, l, tt_:tt_ + 128, :], o_[:, 0:128], reads=[or_])
                    kb.dma("sp", nav_d[s_, l, tt_:tt_ + 128, :], o_[:, 128:256], reads=[or_])
                if STOP == 'A3':
                    continue
                for g2 in range(2):
                    tr(PSB[:, g2 * 128:(g2 + 1) * 128], qkb[:, g2 * 128:(g2 + 1) * 128], identb[:], [qkb_R], [pb_R[7]])
                tr(PSB[:, 256:384], qkb[:, 256:384], identb[:], [qkb_R], [pb_R[7]])
                if STOP == 'A4':
                    continue
                if STOP != 'A6':
                    cp(QT[:, :, t * 128:(t + 1) * 128], PSB[:, 0:256].rearrange("p (b n) -> p b n", b=2), [pb_R[7]], [QT_R])
                if STOP != 'A5':
                    cp(KT[:, kt * 128:(kt + 1) * 128], PSB[:, 256:384], [pb_R[7], QT_R], [KT_R], eng="act")
            if STOP in ('A1', 'A2', 'A2a', 'A2b', 'A2c', 'A3', 'A4', 'A5', 'A6'):
                continue
            for (q0, qlen, kts) in seqs:
                for qc in range(0, qlen, 512):
                    N = min(512, qlen - qc)
                    for kv in range(2):
                        for g2 in range(2):
                            ob = obank()
                            rows = slice(kv * 64, (kv + 1) * 64)
                            orow = g2 * 64
                            srow = 64 - orow
                            if g2 == 0:
                                vfn = lambda kt, kv=kv: VA[:, kt, kv * 192 + 64:kv * 192 + 192]
                            else:
                                vfn = lambda kt, kv=kv: VA[:, kt, kv * 192:kv * 192 + 128]
                            attend(lambda kt, rows=rows: KT[rows, kt * 128:(kt + 1) * 128], QT[rows, g2, q0 + qc:q0 + qc + N], vfn, kts, q0 + qc, N,
                                   0.125, [KT_R, QT_R, VA_R], ob)
                            recip(tA[srow:srow + 64, 0:N], pb[ob][srow:srow + 64, 0:N], [pb_R[ob]], [tA_R])
                            tt(BR[orow:orow + 64, 0, kv, q0 + qc:q0 + qc + N], pb[ob][orow:orow + 64, 0:N], tA[srow:srow + 64, 0:N], MUL,
                               [pb_R[ob], tA_R], [BR_R[0]])
            if STOP == 'A':
                continue
            slot, sr = new_slot()
            W = slot[:, 0:6144].rearrange("p (k c) -> p k c", k=8)
            load_w(W, win_d[l].rearrange("(k p) c -> p k c", p=128)[:, :, 1312:2080], sr)
            barrier(kb)
            QT = av(0, 2048).rearrange("p (b n) -> p b n", b=2); QT_R = Res("QT_C")
            KC = av(2048, 6144).rearrange("p (v b n) -> p v b n", v=2, b=2); KC_R = Res("KT_C")
            VC_OFF = 8192
            VC = av(VC_OFF, 4608).rearrange("p (k c) -> p k c", c=384); VC_R = Res("VC")
            kb.op("dve", lambda e: e.memset(VC[:, 0:NKT, :], 1.0), writes=[VC_R])

            def kc_store(src, kcols):
                for v in range(2):
                    for bb in range(2):
                        ts(KC[:, v, bb, kcols], src[:, bb, :], pm[:, v:v + 1], MUL, [pb_R[7], pb_R[6], CR], [KC_R])
            if S_GRP:
                for bb in range(2):
                    st_, sr2 = nstg()
                    kb.dma("sp", st_[:, 0:512].rearrange("p (k f) -> p k f", k=4), cck_d[l].rearrange("(k p) f -> p k f", p=128)[:, :, bb * 128:(bb + 1) * 128], writes=[sr2])
                    for kt in range(4):
                        tr(pb[6][:, kt * 128:(kt + 1) * 128], st_[:, kt * 128:(kt + 1) * 128], identf[:], [sr2], [pb_R[6]])
                    for v in range(2):
                        ts(KC[:, v, bb, 0:512], pb[6], pm[:, v:v + 1], MUL, [pb_R[6], CR], [KC_R])
                for (d0, d1, s0, s1) in VDST:
                    kb.dma("pool", VC[:, 0:4, d0:d1], ccv_d[l].rearrange("(k p) f -> p k f", p=128)[:, :, s0:s1], writes=[VC_R])
            for t in range(8):
                zb = nb()
                while zb >= 5:
                    zb = nb()
                zr = [pb_R[zb], pb_R[zb + 1]]
                z = PS[:, zb * 512:zb * 512 + 768]
                for bi in range(2):
                    c0, c1 = bi * 512, min(768, bi * 512 + 512)
                    for kc in range(8):
                        mm(PS[:, (zb + bi) * 512:(zb + bi) * 512 + c1 - c0], hT[:, kc, t * 128:(t + 1) * 128], W[:, kc, c0:c1], kc == 0, kc == 7,
                           [hT_R[t // 4], sr], [pb_R[zb + bi]])
                kt = KOFF + t
                for (d0, d1, s0, s1) in VDST:
                    cp(VC[:, kt, d0:d1], z[:, 512 + s0:512 + s1], zr, [VC_R], eng="act")
                if S_GRP:
                    rope_tm(z[:, 0:512], zr[0], 16, 8, cosC[:, t, :], sinC[:, t, :], qkb[:, 0:512])
                else:
                    cp(qkb[:, 0:512], z[:, 0:512], zr, [qkb_R])
                    o_, or_ = nstg()
                    cp(o_[:, 0:512], z[:, 256:768], zr, [or_], eng="act")
                    s_, tt_ = t // 2, (t % 2) * 128
                    kb.dma("sp", nck_d[s_, l, tt_:tt_ + 128, :], o_[:, 0:256], reads=[or_])
                    kb.dma("sp", ncv_d[s_, l, tt_:tt_ + 128, :], o_[:, 256:512], reads=[or_])
                for i in range(4):
                    tr(PSB[:, i * 128:(i + 1) * 128], qkb[:, i * 128:(i + 1) * 128], identb[:], [qkb_R], [pb_R[7]])
                cp(QT[:, :, t * 128:(t + 1) * 128], PSB[:, 0:256].rearrange("p (b n) -> p b n", b=2), [pb_R[7]], [QT_R], eng="act")
                kc_store(PSB[:, 256:512].rearrange("p (b n) -> p b n", b=2), slice(kt * 128, (kt + 1) * 128))
            OCR = tC; OCR_R = tC_R
            for (q0, qlen, kts) in seqs:
                for qc in range(0, qlen, 512):
                    N = min(512, qlen - qc)
                    for bb in range(2):
                        for hf in range(2):
                            h = bb * 2 + hf
                            rows = slice(hf * 64, (hf + 1) * 64)
                            orow = hf * 64
                            srow = 64 - orow
                            if True:
                                vfn = lambda kt, h=h: VC[:, kt, VOFFS[h]:VOFFS[h] + 128]
                            obs = []
                            for j in range(2):
                                ob = obank()
                                while ob in obs:
                                    ob = obank()
                                obs.append(ob)
                                attend(lambda kt, rows=rows, j=j, bb=bb: KC[rows, j, bb, kt * 128:(kt + 1) * 128], QT[rows, bb, q0 + qc:q0 + qc + N], vfn, kts,
                                       q0 + qc, N, 32 ** -0.5, [KC_R, QT_R, VC_R], ob, avoid=tuple(obs))
                            o1, o2 = obs
                            recip(tA[srow:srow + 64, 0:N], pb[o1][srow:srow + 64, 0:N], [pb_R[o1]], [tA_R])
                            recip(tB[srow:srow + 64, 0:N], pb[o2][srow:srow + 64, 0:N], [pb_R[o2]], [tB_R])
                            tt(tA[orow:orow + 64, 512:512 + N], pb[o1][orow:orow + 64, 0:N], tA[srow:srow + 64, 0:N], MUL, [pb_R[o1], tA_R], [tA_R])
                            tt(tB[orow:orow + 64, 512:512 + N], pb[o2][orow:orow + 64, 0:N], tB[srow:srow + 64, 0:N], MUL, [pb_R[o2], tB_R], [tB_R])
                            stt(OCR[orow:orow + 64, 0:N], tB[orow:orow + 64, 512:512 + N], neglam[orow:orow + 64, l:l + 1], tA[orow:orow + 64, 512:512 + N], MUL, ADD,
                                [tA_R, tB_R, MR], [OCR_R])
                        act(PT[0][:, 0:N], OCR[:, 0:N], AF.Square, [OCR_R], [PT_R[0]])
                        b = nb()
                        mm(pb[b][:, 0:N], bd64[:], PT[0][:, 0:N], True, True, [PT_R[0], CR], [pb_R[b]])
                        act(rstd[:, 0:N], pb[b][:, 0:N], AF.Sqrt, [pb_R[b], CR], [rstd_R], scale=1.0 / 64, bias=epsT[:])
                        recip(rstd[:, 0:N], rstd[:, 0:N], [rstd_R], [rstd_R])
                        tt(OCR[:, 0:N], OCR[:, 0:N], rstd[:, 0:N], MUL, [OCR_R, rstd_R], [OCR_R])
                        act(BR[:, 2, bb, q0 + qc:q0 + qc + N], OCR[:, 0:N], AF.Copy, [OCR_R, MR], [BR_R[2]], scale=ocs[:, l:l + 1])

            if STOP == 'C':
                continue
            slot, sr = new_slot()
            W = slot[:, 0:2816].rearrange("p (k c) -> p k c", k=8)
            load_w(W, win_d[l].rearrange("(k p) c -> p k c", p=128)[:, :, 2080:2432], sr)
            UQ = slot[:, 2816:3584].rearrange("p (k c) -> p k c", k=2)
            UQS = slot[:, 3584:4352].rearrange("p (k c) -> p k c", k=2)
            UKV = slot[:, 4352:4864]
            load_w(UQ[:, 0, :], wuq_d[l, 0:128, :], sr); load_w(UQ[0:64, 1, :], wuq_d[l, 128:192, :], sr)
            load_w(UKV, wukv_d[l], sr)
            if S_GRP:
                with nc.allow_non_contiguous_dma(reason="rope column swap"):
                    for (kc_, r0, r1, pr) in ((0, 0, 128, 128), (1, 128, 192, 64)):
                        src = wuq_d[l, r0:r1, :].rearrange("p (h c) -> p h c", h=4)[:, :, 64:96].rearrange("p h (r f i) -> p h r f i", r=2, f=2)
                        dst = UQS[0:pr, kc_, :].rearrange("p (h c) -> p h c", h=4)[:, :, 64:96].rearrange("p h (r f i) -> p h r f i", r=2, f=2)
                        for f in range(2):
                            for r_ in range(2):
                                load_w(dst[:, :, r_, f, :], src[:, :, r_, 1 - f, :], sr)
            barrier(kb)
            DQN = av(0, 2048).rearrange("p (b n) -> p b n", b=2); DQN_R = Res("DQN")
            CKT = av(2048, 1536); CKT_R = Res("CKT")
            KD_ = av(3584, 6144).rearrange("p (h n) -> p h n", h=4); KD_R = Res("KT_D")
            DQT = av(9728, 4096).rearrange("p (h n) -> p h n", h=4); DQT_R = Res("DQT")
            VD_OFF = 13824
            VD = av(VD_OFF, 4608).rearrange("p (k c) -> p k c", c=384); VD_R = Res("VD")
            kb.op("dve", lambda e: e.memset(VD[:, 0:NKT, :], 1.0), writes=[VD_R])
            if S_GRP:
                st_, sr2 = nstg()
                kb.dma("sp", st_[:, 0:512].rearrange("p (k f) -> p k f", k=4), cdc_d[l].rearrange("(k p) f -> p k f", p=128), writes=[sr2])
                for kt in range(4):
                    tr(pb[6][:, kt * 128:(kt + 1) * 128], st_[:, kt * 128:(kt + 1) * 128], identf[:], [sr2], [pb_R[6]])
                cp(CKT[:, 0:512], pb[6], [pb_R[6]], [CKT_R])
                stgkr = st_[:, 512:896].rearrange("p (k f) -> p k f", k=4)
                kb.op("dve", lambda e, stgkr=stgkr: e.memset(stgkr[:, :, 0:64], 0.0), writes=[sr2])
                kb.dma("sp", stgkr[:, :, 64:96], cdr_d[l].rearrange("(k p) f -> p k f", p=128), writes=[sr2])
                b = nb()
                for kt in range(4):
                    mm(pb[b][0:96, kt * 128:(kt + 1) * 128], stgkr[:, kt, :], identf[:], True, True, [sr2, CR], [pb_R[b]])
                cp(KD_[64:96, :, 0:512], bc(pb[b][64:96, :].unsqueeze(1), [32, 4, 512]), [pb_R[b]], [KD_R])
            for t in range(8):
                zb = nb()
                z = pb[zb]; zr = pb_R[zb]
                for kc in range(8):
                    mm(z[:, 0:352], hT[:, kc, t * 128:(t + 1) * 128], W[:, kc, :], kc == 0, kc == 7, [hT_R[t // 4], sr], [zr])
                act(tA[:, 0:192], z[:, 0:192], AF.Square, [zr], [tA_R], accum_out=sm[:, 8:9])
                act(tA[:, 192:320], z[:, 192:320], AF.Square, [zr], [tA_R], accum_out=sm[:, 9:10])
                act(sm[:, 8:9], sm[:, 8:9], AF.Sqrt, [tA_R, CR], [sm_R], scale=1.0 / 192, bias=epsT[:])
                act(sm[:, 9:10], sm[:, 9:10], AF.Sqrt, [tA_R, CR], [sm_R], scale=1.0 / 128, bias=epsT[:])
                recip(sm[:, 8:10], sm[:, 8:10], [sm_R], [sm_R])
                stt(qkb[:, 0:192], z[:, 0:192], sm[:, 8:9], gDQ[:], MUL, MUL, [zr, sm_R, LP], [qkb_R])
                stt(tB[:, 0:128], z[:, 192:320], sm[:, 9:10], gKV[:], MUL, MUL, [zr, sm_R, LP], [tB_R])
                cp(qkb[:, 192:320], tB[:, 0:128], [tB_R], [qkb_R])
                kt = KOFF + t
                if S_GRP:
                    xv = z[:, 320:352].rearrange("p (r h i) -> p r h i", r=2, h=2)
                    ov = krpad[:, 64:96].rearrange("p (r h i) -> p r h i", r=2, h=2)
                    cv = cosC[:, t, :].rearrange("p (r i) -> p r i", r=2); sv = sinC[:, t, :].rearrange("p (r i) -> p r i", r=2)
                    a = tC[:, 0:16].rearrange("p (r i) -> p r i", r=2); b2 = tC[:, 16:32].rearrange("p (r i) -> p r i", r=2)
                    tt(a, xv[:, :, 0, :], cv, MUL, [zr, CR], [tC_R]); tt(b2, xv[:, :, 1, :], sv, MUL, [zr, CR], [tC_R])
                    tt(ov[:, :, 0, :], a, b2, SUB, [tC_R], [krpad_R])
                    tt(a, xv[:, :, 1, :], cv, MUL, [zr, CR], [tC_R]); tt(b2, xv[:, :, 0, :], sv, MUL, [zr, CR], [tC_R])
                    tt(ov[:, :, 1, :], a, b2, ADD, [tC_R], [krpad_R])
                else:
                    cp(krpad[:, 64:96], z[:, 320:352], [zr], [krpad_R])
                    o_, or_ = nstg()
                    cp(o_[:, 0:128], tB[:, 0:128], [tB_R], [or_], eng="act")
                    cp(o_[:, 128:160], z[:, 320:352], [zr], [or_], eng="act")
                    s_, tt_ = t // 2, (t % 2) * 128
                    kb.dma("sp", nckv_d[s_, l, tt_:tt_ + 128, :], o_[:, 0:128], reads=[or_])
                    kb.dma("sp", nkr_d[s_, l, tt_:tt_ + 128, :], o_[:, 128:160], reads=[or_])
                tr(PSB[:, 0:128], qkb[:, 0:128], identb[:], [qkb_R], [pb_R[7]])
                tr(PSB[0:64, 128:256], qkb[:, 128:192], identb[:], [qkb_R], [pb_R[7]])
                tr(PSB[:, 256:384], qkb[:, 192:320], identb[:], [qkb_R], [pb_R[7]])
                cp(DQN[:, 0, t * 128:(t + 1) * 128], PSB[:, 0:128], [pb_R[7]], [DQN_R])
                cp(DQN[0:64, 1, t * 128:(t + 1) * 128], PSB[0:64, 128:256], [pb_R[7]], [DQN_R])
                cp(CKT[:, kt * 128:(kt + 1) * 128], PSB[:, 256:384], [pb_R[7]], [CKT_R], eng="act")
                b = nb()
                mm(pb[b][0:96, 0:128], krpad[:], identb[:], True, True, [krpad_R, CR], [pb_R[b]])
                cp(KD_[64:96, :, kt * 128:(kt + 1) * 128], bc(pb[b][64:96, 0:128].unsqueeze(1), [32, 4, 128]), [pb_R[b]], [KD_R])
            for h in range(4):
                for c0 in range(0, NK, 512):
                    b = nb()
                    mm(pb[b][0:64, :], UKV[:, h * 128:h * 128 + 64], CKT[:, c0:c0 + 512], True, True, [sr, CKT_R], [pb_R[b]])
                    cp(KD_[0:64, h, c0:c0 + 512], pb[b][0:64, :], [pb_R[b]], [KD_R], eng=("act" if h % 2 else "dve"))
            for kt in range(NKT):
                b = nb()
                mm(pb[b][:, 0:256].rearrange("p (h e) -> p h e", h=4), CKT[:, kt * 128:(kt + 1) * 128], UKV.rearrange("p (h c) -> p h c", h=4)[:, :, 64:128], True, True,
                   [sr, CKT_R], [pb_R[b]])
                for (d0, d1, s0, s1) in VDST:
                    cp(VD[:, kt, d0:d1], pb[b][:, s0:s1], [pb_R[b]], [VD_R], eng=("act" if kt % 2 else "dve"))
            for h in range(4):
                for qc in range(0, 1024, 512):
                    b = nb()
                    mm(pb[b][0:96, :], UQ[:, 0, h * 96:(h + 1) * 96], DQN[:, 0, qc:qc + 512], True, False, [sr, DQN_R], [pb_R[b]])
                    mm(pb[b][0:96, :], UQ[0:64, 1, h * 96:(h + 1) * 96], DQN[0:64, 1, qc:qc + 512], False, True, [sr, DQN_R], [pb_R[b]])
                    if S_GRP:
                        b2_ = nb()
                        mm(pb[b2_][0:96, :], UQS[:, 0, h * 96:(h + 1) * 96], DQN[:, 0, qc:qc + 512], True, False, [sr, DQN_R], [pb_R[b2_]])
                        mm(pb[b2_][0:96, :], UQS[0:64, 1, h * 96:(h + 1) * 96], DQN[0:64, 1, qc:qc + 512], False, True, [sr, DQN_R], [pb_R[b2_]])
                        cp(DQT[0:64, h, qc:qc + 512], pb[b][0:64, :], [pb_R[b]], [DQT_R], eng="act")
                        tt(tA[64:96, 0:512], pb[b][64:96, :], cosD[64:96, qc:qc + 512], MUL, [pb_R[b], CR], [tA_R])
                        tt(tB[64:96, 0:512], pb[b2_][64:96, :], sinD[64:96, qc:qc + 512], MUL, [pb_R[b2_], CR], [tB_R])
                        tt(DQT[64:96, h, qc:qc + 512], tA[64:96, 0:512], tB[64:96, 0:512], ADD, [tA_R, tB_R], [DQT_R])
                    else:
                        cp(DQT[0:96, h, qc:qc + 512], pb[b][0:96, :], [pb_R[b]], [DQT_R], eng=("act" if h % 2 else "dve"))
            for (q0, qlen, kts) in seqs:
                for qc in range(0, qlen, 512):
                    N = min(512, qlen - qc)
                    for h in range(4):
                        bb, hf = h // 2, h % 2
                        orow = hf * 64
                        srow = 64 - orow
                        if True:
                            vfn = lambda kt, h=h: VD[:, kt, VOFFS[h]:VOFFS[h] + 128]
                        ob = obank()
                        attend(lambda kt, h=h: KD_[0:96, h, kt * 128:(kt + 1) * 128], DQT[0:96, h, q0 + qc:q0 + qc + N], vfn, kts, q0 + qc, N,
                               96 ** -0.5, [KD_R, DQT_R, VD_R], ob)
                        recip(tA[srow:srow + 64, 0:N], pb[ob][srow:srow + 64, 0:N], [pb_R[ob]], [tA_R])
                        tt(BR[orow:orow + 64, 3, bb, q0 + qc:q0 + qc + N], pb[ob][orow:orow + 64, 0:N], tA[srow:srow + 64, 0:N], MUL,
                           [pb_R[ob], tA_R], [BR_R[3]])
            if STOP == 'D':
                continue
            slot, sr = new_slot()
            W = slot[:, 0:6144].rearrange("p (k c) -> p k c", k=8)
            G = slot[:, 6144:6656].rearrange("p (k c) -> p k c", k=8)
            kb.op("pool", lambda e: e.memset(G, 0.0), writes=[sr])
            load_w(W, win_d[l].rearrange("(k p) c -> p k c", p=128)[:, :, 512:1280], sr)
            load_w(G[:, :, 0:16], win_d[l].rearrange("(k p) c -> p k c", p=128)[:, :, 1280:1296], sr)
            load_w(G[:, :, 32:48], win_d[l].rearrange("(k p) c -> p k c", p=128)[:, :, 1296:1312], sr)
            barrier(kb)
            BT = av(0, 12288).rearrange("p (t d c) -> p t d c", t=8, d=2); BT_R = Res("BT")
            KDc = av(12288, 2048).rearrange("p (t d c) -> p t d c", t=8, d=2); KDc_R = Res("KDc")
            VB = av(14336, 2048).rearrange("p (t c) -> p t c", t=8); VB_R = Res("VB")
            GR = av(16384, 2048).rearrange("p (t c) -> p t c", t=8); GR_R = Res("GR")
            SIN = av(18432, 4096).rearrange("p (t d c) -> p t d c", t=8, d=2); SIN_R = Res("SIN")
            Lsp = tC; Lsp_R = tC_R
            for t in range(8):
                zb = nb()
                while zb >= 4:
                    zb = nb()
                zr = [pb_R[zb], pb_R[zb + 1]]
                z = PS[:, zb * 512:zb * 512 + 768]
                for bi in range(2):
                    c0, c1 = bi * 512, min(768, bi * 512 + 512)
                    for kc in range(8):
                        mm(PS[:, (zb + bi) * 512:(zb + bi) * 512 + c1 - c0], hT[:, kc, t * 128:(t + 1) * 128], W[:, kc, c0:c1], kc == 0, kc == 7,
                           [hT_R[t // 4], sr], [pb_R[zb + bi]])
                if BCUT == 1:
                    continue
                for d in range(2):
                    for kc in range(8):
                        mm(pb[5][0:16, d * 128:(d + 1) * 128], G[:, kc, d * 32:d * 32 + 16], hT[:, kc, t * 128:(t + 1) * 128], kc == 0, kc == 7, [hT_R[t // 4], sr], [pb_R[5]])
                cp(zgT[:], pb[5][0:16, 0:256], [pb_R[5]], [zgT_R])
                if BCUT == 2:
                    continue
                for d in range(2):
                    mm(pb[5][:, 256 + d * 128:384 + d * 128], zgT[0:16, d * 128:(d + 1) * 128], GW[0:16, d * 128:(d + 1) * 128], True, False, [zgT_R, LP], [pb_R[5]])
                    mm(pb[5][:, 256 + d * 128:384 + d * 128], onesrow[0:1, :], GBias[0:1, d * 128:(d + 1) * 128], False, True, [CR, LP], [pb_R[5]])
                if BCUT == 3:
                    continue
                act(tA[:, 0:256], pb[5][:, 256:512], AF.Exp, [pb_R[5]], [tA_R], scale=-1.0)
                act(Lsp[:, 0:256], tA[:, 0:256], AF.Ln, [tA_R, CR], [Lsp_R], bias=onescol[:])
                if BCUT == 4:
                    continue
                for i, (m_, d) in enumerate(((0, 0), (1, 0), (2, 1), (3, 1))):
                    mm(pb[6][:, i * 128:(i + 1) * 128], tri[:, m_, :], Lsp[:, d * 128:(d + 1) * 128], True, True, [Lsp_R, CR], [pb_R[6]])
                if BCUT == 5:
                    continue
                for d in range(2):
                    mm(pb[5][:, 2 * d:2 + 2 * d], Lsp[:, d * 128:(d + 1) * 128], onesf[:, 0:2], True, True, [Lsp_R, CR], [pb_R[5]])
                act(GL[:, t, :], pb[5][:, 0:4].rearrange("p (d two) -> p d two", two=2)[:, :, 0], AF.Exp, [pb_R[5]], [GL_R], scale=-1.0 / 16)
                if BCUT == 6:
                    continue
                Ea = tA; Eb = tB
                act(Ea[:, 0:512], pb[6], AF.Exp, [pb_R[6]], [tA_R], scale=-1.0 / 16)
                act(Eb[:, 0:256].rearrange("p (a c) -> p a c", a=2), pb[6].rearrange("p (a b c) -> p a b c", a=2, b=2)[:, :, 0, :], AF.Exp, [pb_R[6]], [tB_R], scale=1.0 / 16)
                if BCUT == 7:
                    continue
                zq, zk = z[:, 0:128], z[:, 128:256]
                stt(qkb[:, 0:128], zq, 32 ** -0.5, Ea[:, 0:128], MUL, MUL, zr + [tA_R], [qkb_R])
                stt(qkb[:, 128:256], zq, 32 ** -0.5, Ea[:, 256:384], MUL, MUL, zr + [tA_R], [qkb_R])
                tt(qkb[:, 256:384], zk, Eb[:, 0:128], MUL, zr + [tB_R], [qkb_R])
                tt(qkb[:, 384:512], zk, Eb[:, 128:256], MUL, zr + [tB_R], [qkb_R])
                tt(KDc[:, t, 0, :], zk, Ea[:, 128:256], MUL, zr + [tA_R], [KDc_R])
                tt(KDc[:, t, 1, :], zk, Ea[:, 384:512], MUL, zr + [tA_R], [KDc_R])
                if BCUT == 8:
                    continue
                cp(VB[:, t, :], z[:, 256:512], zr, [VB_R], eng="act")
                act(GR[:, t, :], z[:, 512:768], AF.Silu, zr, [GR_R])
                if BCUT == 9:
                    continue
                for i in range(4):
                    tr(PSB[:, i * 128:(i + 1) * 128], qkb[:, i * 128:(i + 1) * 128], identb[:], [qkb_R], [pb_R[7]])
                if BCUT == 10:
                    continue
                for d in range(2):
                    cp(BT[:, t, d, 0:128], PSB[:, 256 + d * 128:384 + d * 128], [pb_R[7]], [BT_R], eng="act")
                    cp(BT[:, t, d, 128:256], PSB[:, d * 128:(d + 1) * 128], [pb_R[7]], [BT_R], eng="act")
                    tt(BT[:, t, d, 256:768].rearrange("p (h n) -> p h n", h=4), bc(PSB[:, d * 128:(d + 1) * 128].unsqueeze(1), [128, 4, 128]),
                       bc(hm[:].unsqueeze(2), [128, 4, 128]), MUL, [pb_R[7], CR], [BT_R])
            if STOP.startswith('B1'):
                continue
            hm3 = bc(hm[:].unsqueeze(2), [128, 4, 64])
            for si, (q0, qlen, kts) in enumerate(seqs):
                t0, nt = q0 // 128, qlen // 128
                for d in range(2):
                    Sf, Sr = Sst[d], Sst_R[d]
                    if S_GRP:
                        kb.dma("sp", Sf[:], (sbf_d if d == 0 else sbb_d)[l], writes=[Sr])
                    else:
                        kb.op("dve", lambda e, Sf=Sf: e.memset(Sf[:], 0.0), writes=[Sr])
                    order = range(t0, t0 + nt) if d == 0 else range(t0 + nt - 1, t0 - 1, -1)
                    for t in order:
                        tt(SIN[:, t, d, :].rearrange("p (h e) -> p h e", h=4), bc(Sf[:].unsqueeze(1), [128, 4, 64]), hm3, MUL, [Sr, CR], [SIN_R])
                        b = nb()
                        mm(pb[b][:, 0:256], KDc[:, t, d, :], VB[:, t, :], True, True, [KDc_R, VB_R], [pb_R[b]])
                        tt(tA[:, 0:256].rearrange("p (h e) -> p h e", h=4), pb[b][:, 0:256].rearrange("p (h e) -> p h e", h=4), hm3, MUL, [pb_R[b], CR], [tA_R])
                        kb.op("dve", lambda e: e.reduce_sum(out=tB[:, 0:64], in_=tA[:, 0:256].rearrange("p (h e) -> p e h", h=4), axis=AX.X), reads=[tA_R], writes=[tB_R])
                        stt(Sf[:], Sf[:], GL[:, t, d:d + 1], tB[:, 0:64], MUL, ADD, [Sr, GL_R, tB_R], [Sr])
                    if not S_GRP:
                        kb.dma("sp", (nbf_d if d == 0 else nbb_d)[si, l], Sf[:], reads=[Sr])
            if STOP == 'B2':
                continue
            for t in range(8):
                ob = nb()
                mm(pb[ob][:, 0:256], BT[:, t, 0, 128:256], SIN[:, t, 0, :], True, False, [BT_R, SIN_R], [pb_R[ob]])
                mm(pb[ob][:, 0:256], BT[:, t, 1, 128:256], SIN[:, t, 1, :], False, False, [BT_R, SIN_R], [pb_R[ob]])
                for d in range(2):
                    ab = nb()
                    while ab == ob:
                        ab = nb()
                    mm(pb[ab], BT[:, t, d, 0:128], BT[:, t, d, 256:768], True, True, [BT_R], [pb_R[ab]])
                    p = pt_i[0] % 3
                    pt_i[0] += 1
                    tt(PT[p][:].rearrange("p (h n) -> p h n", h=4), pb[ab].rearrange("p (h n) -> p h n", h=4),
                       bc(tri[:, (0 if d == 0 else 2), :].unsqueeze(1), [128, 4, 128]), MUL, [pb_R[ab], CR], [PT_R[p]])
                    for h in range(4):
                        mm(pb[ob][:, h * 64:(h + 1) * 64], PT[p][:, h * 128:(h + 1) * 128], VB[:, t, h * 64:(h + 1) * 64], False, (d == 1 and h == 3),
                           [PT_R[p], VB_R], [pb_R[ob]])
                o = pb[ob][:, 0:256]
                act(tA[:, 0:256], o, AF.Square, [pb_R[ob]], [tA_R])
                kb.op("dve", lambda e: e.reduce_sum(out=sm[:, 16:20], in_=tA[:, 0:256].rearrange("p (h d) -> p h d", h=4), axis=AX.X), reads=[tA_R], writes=[sm_R])
                act(sm[:, 16:20], sm[:, 16:20], AF.Sqrt, [sm_R, CR], [sm_R], scale=1.0 / 64, bias=epsT[:])
                recip(sm[:, 16:20], sm[:, 16:20], [sm_R], [sm_R])
                tt(tA[:, 0:256].rearrange("p (h d) -> p h d", h=4), o.rearrange("p (h d) -> p h d", h=4), bc(sm[:, 16:20].unsqueeze(2), [128, 4, 64]), MUL,
                   [pb_R[ob], sm_R], [tA_R])
                tt(tA[:, 0:256], tA[:, 0:256], gBO[:], MUL, [tA_R, LP], [tA_R])
                tt(qkb[:, 0:256], tA[:, 0:256], GR[:, t, :], MUL, [tA_R, GR_R], [qkb_R])
                for i in range(2):
                    tr(PSB[:, i * 128:(i + 1) * 128], qkb[:, i * 128:(i + 1) * 128], identb[:], [qkb_R], [pb_R[7]])
                cp(BR[:, 1, :, t * 128:(t + 1) * 128], PSB[:, 0:256].rearrange("p (b n) -> p b n", b=2), [pb_R[7]], [BR_R[1]])

            if STOP == 'B':
                continue
            barrier(kb)
            if DEBUG and l == 0 and g == GROUPS[0]:
                for jb in range(8):
                    o_, or_ = nstg()
                    cp(o_[:], BR[:, jb // 2, jb % 2, :], BR_R, [or_])
                    kb.dma("sp", dbg_br[:, jb, :], o_[:], reads=[or_])
            MG = av(0, 8192).rearrange("p (k n) -> p k n", k=8)
            bslot, bsr = new_slot()
            BW = bslot[:, 0:8192].rearrange("p (j k c) -> p j k c", j=4, k=2)
            for j in range(4):
                load_w(BW[:, j], wbr_d[l, j].rearrange("(k p) c -> p k c", p=128), bsr)
            for mp in range(4):
                slot, sr = new_slot(avoid=bslot)
                GWT = slot[:, 0:8192].rearrange("p (j k c) -> p j k c", j=4, k=8)
                for j in range(4):
                    c0 = 2432 + j * 1024 + mp * 256
                    load_w(GWT[:, j], win_d[l].rearrange("(k p) c -> p k c", p=128)[:, :, c0:c0 + 256], sr)
                for mi in range(2):
                    m = mp * 2 + mi
                    for h in range(2):
                        hs = slice(h * 512, (h + 1) * 512)
                        for j in range(4):
                            b1 = nb(); b2_ = nb()
                            for kc in range(8):
                                mm(pb[b1], GWT[:, j, kc, mi * 128:(mi + 1) * 128], hT[:, kc, hs], kc == 0, kc == 7, [sr, hT_R[h]], [pb_R[b1]])
                            for kc in range(2):
                                mm(pb[b2_], BW[:, j, kc, m * 128:(m + 1) * 128], BR[:, j, kc, hs], kc == 0, kc == 1, [bsr, BR_R[j]], [pb_R[b2_]])
                            act(tA[:, 0:512], pb[b1], AF.Sigmoid, [pb_R[b1]], [tA_R])
                            if j == 0:
                                tt(tC[:, 0:512], pb[b2_], tA[:, 0:512], MUL, [pb_R[b2_], tA_R], [tC_R])
                            else:
                                tt(tB[:, 0:512], pb[b2_], tA[:, 0:512], MUL, [pb_R[b2_], tA_R], [tB_R])
                                if j < 3:
                                    tt(tC[:, 0:512], tC[:, 0:512], tB[:, 0:512], ADD, [tC_R, tB_R], [tC_R])
                                else:
                                    tt(MG[:, m, hs], tC[:, 0:512], tB[:, 0:512], ADD, [tC_R, tB_R], [MG_R[h]])
            if STOP == 'M':
                continue
            slot, sr = new_slot()
            WO = slot[:, 0:8192].rearrange("p (k c) -> p k c", k=8)
            load_w(WO, wout_d[l].rearrange("(k p) c -> p k c", p=128), sr)
            for m in range(8):
                for h in range(2):
                    hs = slice(h * 512, (h + 1) * 512)
                    b = nb()
                    for kc in range(8):
                        mm(pb[b], WO[:, kc, m * 128:(m + 1) * 128], MG[:, kc, hs], kc == 0, kc == 7, [sr, MG_R[h]], [pb_R[b]])
                    stt(xT[:, m, hs], pb[b], modT[:, l, 16 + m, g:g + 1], xT[:, m, hs], MUL, ADD, [pb_R[b], MR, xT_R[m][h]], [xT_R[m][h]])
            if DEBUG and l == 0 and g == GROUPS[0]:
                for kc in range(8):
                    kb.dma("sp", dbg_x1[:, kc, :], xT[:, kc, :], reads=[xT_R[kc][0], xT_R[kc][1]])
                    o_, or_ = nstg()
                    cp(o_[:], MG[:, kc, :], MG_R, [or_])
                    kb.dma("sp", dbg_mg[:, kc, :], o_[:], reads=[or_])
            norm_mod(A2, 24, l, g)
            AT = av(8192, 22528).rearrange("p (f n) -> p f n", f=22); AT_R = [Res("AT0"), Res("AT1")]
            U_ = stg[0]; U_R = stg_R[0]; Cc = stg[1]; Cc_R = stg_R[1]
            nsq = 1 if S_GRP else 4
            sl = 1024 // nsq
            for f0 in range(0, 22, 4):
                nf = min(4, 22 - f0)
                slot, sr = new_slot()
                UW = slot[:, 0:4096].rearrange("p (k c) -> p k c", k=8)
                GW2 = slot[:, 4096:8192].rearrange("p (k c) -> p k c", k=8)
                load_w(UW[:, :, 0:nf * 128], wfu_d[l].rearrange("(k p) c -> p k c", p=128)[:, :, f0 * 128:(f0 + nf) * 128], sr)
                load_w(GW2[:, :, 0:nf * 128], wfg_d[l].rearrange("(k p) c -> p k c", p=128)[:, :, f0 * 128:(f0 + nf) * 128], sr)
                for fi in range(nf):
                    f = f0 + fi
                    gb_ = []
                    for h in range(2):
                        hs = slice(h * 512, (h + 1) * 512)
                        bu = nb(); bg = nb()
                        gb_.append(bg)
                        for kc in range(8):
                            mm(pb[bu], UW[:, kc, fi * 128:(fi + 1) * 128], hT[:, kc, hs], kc == 0, kc == 7, [sr, hT_R[h]], [pb_R[bu]])
                        for kc in range(8):
                            mm(pb[bg], GW2[:, kc, fi * 128:(fi + 1) * 128], hT[:, kc, hs], kc == 0, kc == 7, [sr, hT_R[h]], [pb_R[bg]])
                        cp(U_[:, hs], pb[bu], [pb_R[bu]], [U_R], eng="act")
                    act(Cc[:], U_[:], AF.Identity, [U_R, LP], [Cc_R], scale=cwT[:, 1, f:f + 1], bias=cbT[:, f:f + 1])
                    Uv = U_[:].rearrange("p (s n) -> p s n", s=nsq); Cv = Cc[:].rearrange("p (s n) -> p s n", s=nsq)
                    stt(Cv[:, :, 1:sl], Uv[:, :, 0:sl - 1], cwT[:, 0, f:f + 1], Cv[:, :, 1:sl], MUL, ADD, [U_R, Cc_R, LP], [Cc_R])
                    stt(Cv[:, :, 0:sl - 1], Uv[:, :, 1:sl], cwT[:, 2, f:f + 1], Cv[:, :, 0:sl - 1], MUL, ADD, [U_R, Cc_R, LP], [Cc_R])
                    act(tA[:], Cc[:], AF.Gelu_apprx_tanh, [Cc_R], [tA_R])
                    for h in range(2):
                        hs = slice(h * 512, (h + 1) * 512)
                        tt(AT[:, f, hs], tA[:, hs], pb[gb_[h]], MUL, [tA_R, pb_R[gb_[h]]], [AT_R[h]])
            for m0 in range(0, 8, 2):
                slot, sr = new_slot()
                WD = slot[:, 0:5632].rearrange("p (f c) -> p f c", f=22)
                load_w(WD, wfd_d[l].rearrange("(f p) c -> p f c", p=128)[:, :, m0 * 128:(m0 + 2) * 128], sr)
                for mi in range(2):
                    m = m0 + mi
                    for h in range(2):
                        hs = slice(h * 512, (h + 1) * 512)
                        b = nb()
                        for f in range(22):
                            mm(pb[b], WD[:, f, mi * 128:(mi + 1) * 128], AT[:, f, hs], f == 0, f == 21, [sr, AT_R[h]], [pb_R[b]])
                        stt(xT[:, m, hs], pb[b], modT[:, l, 40 + m, g:g + 1], xT[:, m, hs], MUL, ADD, [pb_R[b], MR, xT_R[m][h]], [xT_R[m][h]])

            if DEBUG and l == 0 and g == GROUPS[0]:
                for kc in range(8):
                    kb.dma("sp", dbg_x2[:, kc, :], xT[:, kc, :], reads=[xT_R[kc][0], xT_R[kc][1]])
        barrier(kb)
        kb.dma("sp", fgB, fg_d.partition_broadcast(128), writes=[fgB_R])
        for t in range(8):
            b0 = nb()
            while b0 >= 5:
                b0 = nb()
            for kc in range(8):
                bi = b0 + kc // 4
                tr(PS[:, bi * 512 + (kc % 4) * 128: bi * 512 + (kc % 4 + 1) * 128], xT[:, kc, t * 128:(t + 1) * 128], identf[:], [xT_R[kc][t // 4]], [pb_R[bi]])
            yp = PS[:, b0 * 512:b0 * 512 + 1024]
            yr = [pb_R[b0], pb_R[b0 + 1]]
            act(tA[:], yp, AF.Square, yr, [tA_R], accum_out=sm[:, 24:25])
            act(sm[:, 24:25], sm[:, 24:25], AF.Sqrt, [tA_R, CR], [sm_R], scale=1.0 / 1024, bias=epsT[:])
            recip(sm[:, 24:25], sm[:, 24:25], [sm_R], [sm_R])
            o_, or_ = nstg()
            stt(o_[:], yp, sm[:, 24:25], fgB, MUL, MUL, yr + [sm_R, fgB_R], [or_])
            kb.dma("sp", y_d[g, t * 128:(t + 1) * 128, :], o_[:], reads=[or_])
    return kb


def _consts():
    c = {}
    c["k_ident"] = np.eye(128, dtype=np.float32)
    bd = np.zeros((128, 128), np.float32); bd[:64, :64] = 1; bd[64:, 64:] = 1
    c["k_bd64"] = bd
    s = np.arange(128)[:, None]; t = np.arange(128)[None, :]
    c["k_tri"] = np.stack([(s <= t), (s > t), (s >= t), (s < t)]).astype(np.float32)
    hmk = np.zeros((128, 4), np.float32)
    for h in range(4):
        hmk[h * 32:(h + 1) * 32, h] = 1
    c["k_hm"] = hmk
    pmk = np.zeros((128, 2), np.float32)
    for p in range(128):
        pmk[p, (p // 32) % 2] = 1
    c["k_pm"] = pmk
    tok = np.arange(1024)
    row = (tok // 64).astype(np.float32); col = (tok % 64).astype(np.float32)

    def tab(half):
        inv = (10000.0 ** (-np.arange(half, dtype=np.float32) / half)).astype(np.float32)
        ar = row[:, None] * inv[None, :]; ac = col[:, None] * inv[None, :]
        return (np.concatenate([np.cos(ar), np.cos(ac)], 1).astype(np.float32), np.concatenate([np.sin(ar), np.sin(ac)], 1).astype(np.float32))
    c["k_cosA"], c["k_sinA"] = tab(16)
    cC, sC = tab(8)
    c["k_cosC"], c["k_sinC"] = cC, sC
    cr, cc_ = cC[:, 0:8], cC[:, 8:16]; sr_, sc_ = sC[:, 0:8], sC[:, 8:16]
    c["k_cosD"] = np.ascontiguousarray(np.concatenate([cr, cr, cc_, cc_], 1).T)
    c["k_sinD"] = np.ascontiguousarray(np.concatenate([-sr_, sr_, -sc_, sc_], 1).T)
    return c


WNAMES = ["w_mod", "b_mod", "norm1_g", "norm2_g", "w_in", "a_qnorm_g", "a_knorm_g", "b_gate_w_fwd", "b_gate_b_fwd", "b_gate_w_bwd",
          "b_gate_b_bwd", "b_onorm_g", "c_lq1", "c_lk1", "c_lq2", "c_lk2", "c_onorm_g", "d_qnorm_g", "d_w_uq", "d_kvnorm_g", "d_w_ukv",
          "w_branch", "w_out", "w_ffu", "w_ffg", "conv_w", "conv_b", "w_ffd", "final_g"]


def in_map(inp, c, consts, wts):
    m = dict(wts)
    m.update(consts)
    f = lambda a: np.ascontiguousarray(a, dtype=np.float32)
    m["x"] = f(np.stack([inp["x_prompt"][4 * c:4 * c + 4].reshape(1024, 1024), inp["x_sample"][c]]))
    m["cvec"] = f(np.stack([inp["c_ctx"], inp["c"][c]]))
    m["ca_k"] = f(inp["cache_a_k"][c].reshape(4, 512, 128)); m["ca_v"] = f(inp["cache_a_v"][c].reshape(4, 512, 128))
    m["sb_f"] = f(inp["state_b_fwd"][c].reshape(4, 128, 64)); m["sb_b"] = f(inp["state_b_bwd"][c].reshape(4, 128, 64))
    m["cc_k"] = f(inp["cache_c_k"][c].reshape(4, 512, 256)); m["cc_v"] = f(inp["cache_c_v"][c].reshape(4, 512, 256))
    m["cd_ckv"] = f(inp["cache_d_ckv"][c]); m["cd_kr"] = f(inp["cache_d_krope"][c])
    return m


def assemble(R):
    y_p = np.concatenate([r["y"][0].reshape(4, 256, 1024) for r in R], 0)
    y_s = np.stack([r["y"][1] for r in R], 0)

    def cat(name, shp):
        return np.concatenate([r[name].reshape((4,) + shp) for r in R], 0)
    return (y_p, y_s, cat("nak", (4, 256, 2, 64)), cat("nav", (4, 256, 2, 64)), cat("nbf", (4, 4, 32, 64)), cat("nbb", (4, 4, 32, 64)),
            cat("nck", (4, 256, 4, 2, 32)), cat("ncv", (4, 256, 4, 64)), cat("nckv", (4, 256, 128)), cat("nkr", (4, 256, 32)))


def kernel(**inp):
    inp = {k: np.asarray(v) for k, v in inp.items()}
    kb = build()
    nc = kb.finish()
    consts = _consts()
    wts = {k: np.ascontiguousarray(inp[k], dtype=np.float32) for k in WNAMES}
    in_maps = [in_map(inp, c, consts, wts) for c in range(8)]
    res = run_bass_kernel_spmd(nc, in_maps, core_ids=list(range(8)))
    return assemble(res.results)
```

```python
from contextlib import ExitStack
import numpy as np
import concourse.bass as bass
import concourse.mybir as mybir

F32 = mybir.dt.float32
BF16 = mybir.dt.bfloat16
AF = mybir.ActivationFunctionType
ALU = mybir.AluOpType
AX = mybir.AxisListType

EPOCH = 30000
NDMA_SEM = 20


class Res:
    __slots__ = ("name", "w", "r", "excl")

    def __init__(self, name="", excl=False):
        self.name = name
        self.excl = excl
        self.w = []
        self.r = []


class EngState:
    def __init__(self, name, handle_name, is_compute):
        self.name = name
        self.handle_name = handle_name
        self.is_compute = is_compute
        self.prog = []
        self.sem = None
        self.cnt = 0
        self.known = {}
        self.dma_ring = []
        self.dma_i = 0
        self.nops = 0


class KB:
    def __init__(self):
        self.nc = bass.Bass("TRN2", target_bir_lowering=False)
        self.es = ExitStack()
        self.E = {
            "pe": EngState("pe", "tensor", True),
            "act": EngState("act", "scalar", True),
            "dve": EngState("dve", "vector", True),
            "pool": EngState("pool", "gpsimd", True),
            "sp": EngState("sp", "sync", False),
        }
        self.nsem = 0
        self.all_dma_events = []

    def new_sem(self, name):
        self.nsem += 1
        return self.es.enter_context(self.nc.semaphore(f"{name}_{self.nsem}"))

    def sbuf(self, name, shape, dtype=F32):
        return self.es.enter_context(self.nc.sbuf_tensor(name, list(shape), dtype))

    def psum(self, name, shape, dtype=F32):
        return self.es.enter_context(self.nc.psum_tensor(name, list(shape), dtype))

    def dram(self, name, shape, dtype, kind):
        return self.nc.dram_tensor(name, list(shape), dtype, kind=kind)

    def _wait(self, st, ev):
        sem, val, _ = ev
        k = id(sem)
        if st.known.get(k, 0) >= val:
            return
        st.known[k] = val
        st.prog.append(("wait", sem, val))

    def _deps(self, st, reads, writes, is_dma):
        for r in reads:
            for ev in r.w:
                self._wait(st, ev)
            if r.excl:
                for ev in r.r:
                    if ev[2] != st.name:
                        self._wait(st, ev)
        for w in writes:
            for ev in w.w:
                if is_dma or ev[2] != st.name or not st.is_compute:
                    self._wait(st, ev)
            for ev in w.r:
                if is_dma or ev[2] != st.name or not st.is_compute:
                    self._wait(st, ev)

    def _commit(self, ev, reads, writes):
        for r in reads:
            if r in writes:
                continue
            if ev[2] in ("pe", "act", "dve", "pool"):
                r.r = [e for e in r.r if e[2] != ev[2]]
            r.r.append(ev)
        for w in writes:
            w.w = [ev]
            w.r = []

    @staticmethod
    def _flat(xs):
        out = []
        for x in xs:
            if isinstance(x, (list, tuple)):
                out.extend(KB._flat(x))
            else:
                out.append(x)
        return out

    def op(self, eng, fn, reads=(), writes=()):
        reads = self._flat(reads); writes = self._flat(writes)
        st = self.E[eng]
        assert st.is_compute
        self._deps(st, reads, writes, False)
        if st.sem is None or st.cnt >= EPOCH:
            st.sem = self.new_sem(f"s_{eng}")
            st.cnt = 0
        st.cnt += 1
        ev = (st.sem, st.cnt, eng)
        st.prog.append(("op", fn, st.sem))
        st.nops += 1
        self._commit(ev, reads, writes)
        return ev

    def dma(self, q, out, in_, reads=(), writes=(), **kw):
        st = self.E[q]
        reads = self._flat(reads); writes = self._flat(writes)
        kw.setdefault("allow_slow_non_contiguous", True)
        for r in reads:
            for ev in r.w:
                self._wait(st, ev)
        for w in writes:
            for ev in w.w:
                if ev[2].startswith("dma") and not w.r:
                    continue
                self._wait(st, ev)
            for ev in w.r:
                self._wait(st, ev)
        if not st.dma_ring:
            st.dma_ring = [[self.new_sem(f"d_{q}"), 0] for _ in range(NDMA_SEM)]
        slot = st.dma_ring[st.dma_i % NDMA_SEM]
        st.dma_i += 1
        if slot[1] > 0:
            self._wait(st, (slot[0], slot[1], "dma"))
        slot[1] += 16
        ev = (slot[0], slot[1], "dma_" + q)
        st.prog.append(("dma", out, in_, slot[0], kw))
        keep = {id(w): list(w.w) for w in writes if w.w and not w.r and all(e[2].startswith("dma") for e in w.w)}
        self._commit(ev, reads, writes)
        for w in writes:
            if id(w) in keep:
                w.w = keep[id(w)] + [ev]
        self.all_dma_events.append(ev)
        return ev

    def finish(self):
        nc = self.nc
        sp = self.E["sp"]
        for st in self.E.values():
            for sem, val in st.dma_ring:
                if val > 0:
                    self._wait(sp, (sem, val, "dma"))
        with nc.Block() as block:
            for st in self.E.values():
                if not st.prog:
                    continue

                def body(eng, st=st):
                    for item in st.prog:
                        if item[0] == "wait":
                            eng.wait_ge(item[1], item[2])
                        elif item[0] == "op":
                            ins = item[1](eng)
                            ins.then_inc(item[2], 1)
                        else:
                            _, out, in_, sem, kw = item
                            eng.dma_start(out=out, in_=in_, **kw).then_inc(sem, 16)

                getattr(block, st.handle_name)(body)
        self.es.close()
        return nc

import math
from concourse.bass_utils import run_bass_kernel_spmd

L = 4
EPS = 1e-6
MUL, ADD, SUB = ALU.mult, ALU.add, ALU.subtract
VOFFS = [0, 64, 192, 256]
VDST = [(0, 64, 0, 64), (128, 256, 64, 192), (320, 384, 192, 256)]


def barrier(kb):
    comp = ["pe", "act", "dve", "pool"]
    for e in comp + ["sp"]:
        st = kb.E[e]
        for o in comp:
            so = kb.E[o]
            if o != e and so.sem is not None and so.cnt > 0:
                kb._wait(st, (so.sem, so.cnt, o))
        for q in kb.E.values():
            for sem, val in q.dma_ring:
                if val > 0:
                    kb._wait(st, (sem, val, "dma"))


def build(NLAYERS=L, GROUPS=(0, 1), STOP='', DEBUG=False):
    kb = KB()
    nc = kb.nc
    BCUT = int(STOP.split(':')[1]) if ':' in STOP else 0

    def din(name, shape):
        return kb.dram(name, shape, F32, "ExternalInput").ap()

    def dout(name, shape):
        return kb.dram(name, shape, F32, "ExternalOutput").ap()

    x_d = din("x", [2, 1024, 1024])
    cvec_d = din("cvec", [2, 1024])
    cak_d = din("ca_k", [L, 512, 128]); cav_d = din("ca_v", [L, 512, 128])
    sbf_d = din("sb_f", [L, 128, 64]); sbb_d = din("sb_b", [L, 128, 64])
    cck_d = din("cc_k", [L, 512, 256]); ccv_d = din("cc_v", [L, 512, 256])
    cdc_d = din("cd_ckv", [L, 512, 128]); cdr_d = din("cd_kr", [L, 512, 32])
    wmod_d = din("w_mod", [L, 1024, 6144]); bmod_d = din("b_mod", [L, 6144])
    n1_d = din("norm1_g", [L, 1024]); n2_d = din("norm2_g", [L, 1024])
    win_d = din("w_in", [L, 1024, 6528])
    aqg_d = din("a_qnorm_g", [L, 64]); akg_d = din("a_knorm_g", [L, 64])
    gwf_d = din("b_gate_w_fwd", [L, 16, 128]); gbf_d = din("b_gate_b_fwd", [L, 128])
    gwb_d = din("b_gate_w_bwd", [L, 16, 128]); gbb_d = din("b_gate_b_bwd", [L, 128])
    bon_d = din("b_onorm_g", [L, 64])
    lq1_d = din("c_lq1", [L, 32]); lk1_d = din("c_lk1", [L, 32]); lq2_d = din("c_lq2", [L, 32]); lk2_d = din("c_lk2", [L, 32])
    con_d = din("c_onorm_g", [L, 64])
    dqg_d = din("d_qnorm_g", [L, 192]); wuq_d = din("d_w_uq", [L, 192, 384])
    dkg_d = din("d_kvnorm_g", [L, 128]); wukv_d = din("d_w_ukv", [L, 128, 512])
    wbr_d = din("w_branch", [L, 4, 256, 1024]); wout_d = din("w_out", [L, 1024, 1024])
    wfu_d = din("w_ffu", [L, 1024, 2816]); wfg_d = din("w_ffg", [L, 1024, 2816])
    cw_d = din("conv_w", [L, 3, 2816]); cb_d = din("conv_b", [L, 2816]); wfd_d = din("w_ffd", [L, 2816, 1024])
    fg_d = din("final_g", [1024])
    k_ident = din("k_ident", [128, 128]); k_bd64 = din("k_bd64", [128, 128]); k_tri = din("k_tri", [4, 128, 128])
    k_hm = din("k_hm", [128, 4]); k_pm = din("k_pm", [128, 2])
    k_cosA = din("k_cosA", [1024, 32]); k_sinA = din("k_sinA", [1024, 32])
    k_cosC = din("k_cosC", [1024, 16]); k_sinC = din("k_sinC", [1024, 16])
    k_cosD = din("k_cosD", [32, 1024]); k_sinD = din("k_sinD", [32, 1024])

    y_d = dout("y", [2, 1024, 1024])
    nak_d = dout("nak", [4, L, 256, 128]); nav_d = dout("nav", [4, L, 256, 128])
    nbf_d = dout("nbf", [4, L, 128, 64]); nbb_d = dout("nbb", [4, L, 128, 64])
    nck_d = dout("nck", [4, L, 256, 256]); ncv_d = dout("ncv", [4, L, 256, 256])
    nckv_d = dout("nckv", [4, L, 256, 128]); nkr_d = dout("nkr", [4, L, 256, 32])

    if DEBUG:
        dbg_br = dout('dbg_br', [128, 8, 1024]); dbg_x1 = dout('dbg_x1', [128, 8, 1024]); dbg_x2 = dout('dbg_x2', [128, 8, 1024]); dbg_mg = dout('dbg_mg', [128, 8, 1024])
    xT = kb.sbuf("xT", [128, 8, 1024]); xT_R = [[Res(f"xT{k}_{h}") for h in range(2)] for k in range(8)]
    hT = kb.sbuf("hT", [128, 8, 1024], BF16); hT_R = [Res("hT0"), Res("hT1")]
    NSLOT = 3
    ring = [kb.sbuf(f"ring{i}", [128, 8192], BF16) for i in range(NSLOT)]
    ring_R = [Res(f"ring{i}") for i in range(NSLOT)]
    ring_i = [0]
    arena = kb.sbuf("arena", [128, 30720], BF16)
    PS = kb.psum("ps", [128, 4096])
    pb = [PS[:, i * 512:(i + 1) * 512] for i in range(8)]
    pb_R = [Res(f"pb{i}", excl=True) for i in range(8)]
    PSB = PS[:, 7 * 512:8 * 512].bitcast(BF16)

    def new_slot(avoid=None):
        i = ring_i[0] % NSLOT
        ring_i[0] += 1
        while avoid is not None and ring[i] is avoid:
            i = ring_i[0] % NSLOT
            ring_i[0] += 1
        return ring[i], ring_R[i]

    def av(off, n):
        return arena[:, off:off + n]

    def vap(off, stride):
        return bass.AP(arena, off, [[30720, 128], [stride, 2], [1, 64]])

    identf = kb.sbuf("identf", [128, 128]); identb = kb.sbuf("identb", [128, 128], BF16)
    onesb = kb.sbuf("onesb", [128, 128], BF16); bd64 = kb.sbuf("bd64", [128, 128], BF16)
    tri = kb.sbuf("tri", [128, 4, 128]); hm = kb.sbuf("hm", [128, 4]); pm = kb.sbuf("pm", [128, 2])
    onescol = kb.sbuf("onescol", [128, 1]); onesrow = kb.sbuf("onesrow", [1, 128]); epsT = kb.sbuf("epsT", [128, 1])
    onesf = kb.sbuf("onesf", [128, 128])
    cosA = kb.sbuf("cosA", [128, 8, 32]); sinA = kb.sbuf("sinA", [128, 8, 32])
    cosC = kb.sbuf("cosC", [128, 8, 16]); sinC = kb.sbuf("sinC", [128, 8, 16])
    cosD = kb.sbuf("cosD", [128, 1024]); sinD = kb.sbuf("sinD", [128, 1024])
    CR = Res("consts")
    kb.dma("sp", identf[:], k_ident, writes=[CR])
    kb.dma("pool", identb[:], k_ident, writes=[CR])
    kb.dma("pool", bd64[:], k_bd64, writes=[CR])
    kb.dma("sp", tri[:], k_tri.rearrange("m s t -> s m t"), writes=[CR])
    kb.dma("sp", hm[:], k_hm, writes=[CR]); kb.dma("sp", pm[:], k_pm, writes=[CR])
    kb.dma("sp", cosA[:], k_cosA.rearrange("(t p) d -> p t d", p=128), writes=[CR])
    kb.dma("sp", sinA[:], k_sinA.rearrange("(t p) d -> p t d", p=128), writes=[CR])
    kb.dma("sp", cosC[:], k_cosC.rearrange("(t p) d -> p t d", p=128), writes=[CR])
    kb.dma("sp", sinC[:], k_sinC.rearrange("(t p) d -> p t d", p=128), writes=[CR])
    kb.dma("sp", cosD[64:96, :], k_cosD, writes=[CR]); kb.dma("sp", sinD[64:96, :], k_sinD, writes=[CR])
    kb.op("dve", lambda e: e.memset(onesb[:], 1.0), writes=[CR])
    kb.op("dve", lambda e: e.memset(onescol[:], 1.0), writes=[CR])
    kb.op("dve", lambda e: e.memset(onesrow[:], 1.0), writes=[CR])
    kb.op("dve", lambda e: e.memset(onesf[:], 1.0), writes=[CR])
    kb.op("dve", lambda e: e.memset(epsT[:], EPS), writes=[CR])

    def mm(out, lhsT, rhs, start, stop, reads, writes):
        kb.op("pe", lambda e: e.matmul(out, lhsT=lhsT, rhs=rhs, start=start, stop=stop), reads=reads, writes=writes)

    def tr(out, in_, ident, reads, writes):
        kb.op("pe", lambda e: e.transpose(out=out, in_=in_, identity=ident), reads=reads + [CR], writes=writes)

    def act(out, in_, func, reads, writes, scale=1.0, bias=None, accum_out=None):
        kw = {}
        if bias is not None:
            kw["bias"] = bias
        if accum_out is not None:
            kw["accum_out"] = accum_out
        kb.op("act", lambda e: e.activation(out=out, in_=in_, func=func, scale=scale, **kw), reads=reads, writes=writes)

    def tt(out, in0, in1, op, reads, writes, eng="dve"):
        kb.op(eng, lambda e: e.tensor_tensor(out=out, in0=in0, in1=in1, op=op), reads=reads, writes=writes)

    def stt(out, in0, scalar, in1, op0, op1, reads, writes):
        kb.op("dve", lambda e: e.scalar_tensor_tensor(out=out, in0=in0, scalar=scalar, in1=in1, op0=op0, op1=op1), reads=reads, writes=writes)

    def ts(out, in0, s1, op0, reads, writes, s2=None, op1=None, eng="dve"):
        if op1 is None:
            kb.op(eng, lambda e: e.tensor_scalar(out=out, in0=in0, scalar1=s1, scalar2=None, op0=op0), reads=reads, writes=writes)
        else:
            kb.op(eng, lambda e: e.tensor_scalar(out=out, in0=in0, scalar1=s1, scalar2=s2, op0=op0, op1=op1), reads=reads, writes=writes)

    def cp(out, in_, reads, writes, eng="dve"):
        if eng == "act":
            kb.op("act", lambda e: e.copy(out=out, in_=in_), reads=reads, writes=writes)
        else:
            kb.op(eng, lambda e: e.tensor_copy(out=out, in_=in_), reads=reads, writes=writes)

    def recip(out, in_, reads, writes):
        kb.op("dve", lambda e: e.reciprocal(out=out, in_=in_), reads=reads, writes=writes)

    def bc(ap, shape):
        return ap.broadcast_to(list(shape))

    pbi = [0]

    def nb():
        i = pbi[0] % 7
        pbi[0] += 1
        return i

    scT = kb.sbuf("scT", [128, 8, 2]); modT = kb.sbuf("modT", [128, L, 48, 2]); bmT = kb.sbuf("bmT", [128, L, 48])
    n1T = kb.sbuf("n1T", [128, L, 8]); n2T = kb.sbuf("n2T", [128, L, 8])
    A1 = kb.sbuf("A1", [128, L, 8, 2]); A2 = kb.sbuf("A2", [128, L, 8, 2])
    MR = Res("mod")
    with nc.allow_non_contiguous_dma(reason="tiny transposed vector loads"):
        for g_ in range(2):
            kb.dma("sp", scT[:, :, g_], cvec_d[g_].rearrange("(k p) -> p k", p=128), writes=[MR])
        for l_ in range(L):
            kb.dma("sp", bmT[:, l_, :], bmod_d[l_].rearrange("(j p) -> p j", p=128), writes=[MR])
            kb.dma("sp", n1T[:, l_, :], n1_d[l_].rearrange("(k p) -> p k", p=128), writes=[MR])
            kb.dma("sp", n2T[:, l_, :], n2_d[l_].rearrange("(k p) -> p k", p=128), writes=[MR])
    act(scT[:], scT[:], AF.Silu, [MR], [MR])
    scb = kb.sbuf("scb", [128, 8, 2], BF16)
    cp(scb[:], scT[:], [MR], [MR])
    for l in range(L):
        for cb in range(8):
            slot, wr = new_slot()
            w = slot[:, 0:6144].rearrange("p (k c) -> p k c", k=8)
            kb.dma("pool", w, wmod_d[l].rearrange("(k p) c -> p k c", p=128)[:, :, cb * 768:(cb + 1) * 768], writes=[wr])
            b = nb()
            for jj in range(6):
                for kc in range(8):
                    mm(pb[b][:, jj * 2:(jj + 1) * 2], w[:, kc, jj * 128:(jj + 1) * 128], scb[:, kc, :], kc == 0, kc == 7, [wr, MR], [pb_R[b]])
            tt(modT[:, l, cb * 6:(cb + 1) * 6, :], pb[b][:, 0:12].rearrange("p (j g) -> p j g", g=2),
               bc(bmT[:, l, cb * 6:(cb + 1) * 6].unsqueeze(2), [128, 6, 2]), ADD, [pb_R[b], MR], [MR])
    for l in range(L):
        stt(A1[:, l], modT[:, l, 8:16, :], 1.0, bc(n1T[:, l, :].unsqueeze(2), [128, 8, 2]), ADD, MUL, [MR], [MR])
        stt(A2[:, l], modT[:, l, 32:40, :], 1.0, bc(n2T[:, l, :].unsqueeze(2), [128, 8, 2]), ADD, MUL, [MR], [MR])

    lqk = kb.sbuf("lqk", [32, 4, L]); lamT = kb.sbuf("lamT", [128, L]); neglam = kb.sbuf("neglam", [128, L])
    ocs = kb.sbuf("ocs", [128, L]); lam2 = kb.sbuf("lam2", [128, 2, L])
    with nc.allow_non_contiguous_dma(reason="tiny transposed vector loads"):
        for i, d in enumerate([lq1_d, lk1_d, lq2_d, lk2_d]):
            kb.dma("sp", lqk[:, i, :], d.rearrange("l d -> d l"), writes=[MR])
        kb.dma("sp", ocs[0:64, :], con_d.rearrange("l d -> d l"), writes=[MR])
        kb.dma("sp", ocs[64:128, :], con_d.rearrange("l d -> d l"), writes=[MR])
    tt(lqk[:, 0, :], lqk[:, 0, :], lqk[:, 1, :], MUL, [MR], [MR])
    tt(lqk[:, 2, :], lqk[:, 2, :], lqk[:, 3, :], MUL, [MR], [MR])
    b = nb()
    mm(pb[b][:, 0:L], onesf[0:32, :], lqk[:, 0, :], True, True, [MR, CR], [pb_R[b]])
    mm(pb[b][:, L:2 * L], onesf[0:32, :], lqk[:, 2, :], True, True, [MR, CR], [pb_R[b]])
    act(lam2[:].rearrange("p a l -> p (a l)"), pb[b][:, 0:2 * L], AF.Exp, [pb_R[b]], [MR])
    tt(lamT[:], lam2[:, 0, :], lam2[:, 1, :], SUB, [MR], [MR])
    lam_init = [0.8 - 0.6 * math.exp(-0.3 * l) for l in range(L)]
    for l in range(L):
        ts(lamT[:, l:l + 1], lamT[:, l:l + 1], lam_init[l], ADD, [MR], [MR])
        ts(ocs[:, l:l + 1], ocs[:, l:l + 1], 1.0 - lam_init[l], MUL, [MR], [MR])
    ts(neglam[:], lamT[:], -1.0, MUL, [MR], [MR])

    gA = kb.sbuf("gA", [128, 384]); gDQ = kb.sbuf("gDQ", [128, 192]); gKV = kb.sbuf("gKV", [128, 128]); gBO = kb.sbuf("gBO", [128, 256])
    GW = kb.sbuf("GW", [16, 256]); GBias = kb.sbuf("GBias", [1, 256]); cwT = kb.sbuf("cwT", [128, 3, 22]); cbT = kb.sbuf("cbT", [128, 22])
    fgB = arena[:, 0:2048].bitcast(F32); fgB_R = Res("fgB")
    LP = Res("layer_params")

    def load_layer_params(l):
        def bsrc(d, n, rep):
            return bass.AP(d.tensor, d[l].offset, [[0, 128], [0, rep], [1, n]])
        kb.dma("sp", gA[:, 0:256].rearrange("p (r d) -> p r d", r=4), bsrc(aqg_d, 64, 4), writes=[LP])
        kb.dma("sp", gA[:, 256:384].rearrange("p (r d) -> p r d", r=2), bsrc(akg_d, 64, 2), writes=[LP])
        kb.dma("sp", gDQ[:], dqg_d[l].partition_broadcast(128), writes=[LP])
        kb.dma("sp", gKV[:], dkg_d[l].partition_broadcast(128), writes=[LP])
        kb.dma("sp", gBO[:].rearrange("p (r d) -> p r d", r=4), bsrc(bon_d, 64, 4), writes=[LP])
        kb.dma("sp", GW[0:16, 0:128], gwf_d[l], writes=[LP]); kb.dma("sp", GW[0:16, 128:256], gwb_d[l], writes=[LP])
        kb.dma("sp", GBias[0:1, 0:128], gbf_d[l:l + 1, :], writes=[LP]); kb.dma("sp", GBias[0:1, 128:256], gbb_d[l:l + 1, :], writes=[LP])
        with nc.allow_non_contiguous_dma(reason="tiny transposed vector loads"):
            for w_ in range(3):
                kb.dma("sp", cwT[:, w_, :], cw_d[l, w_].rearrange("(f p) -> p f", p=128), writes=[LP])
            kb.dma("sp", cbT[:], cb_d[l].rearrange("(f p) -> p f", p=128), writes=[LP])

    stg = [kb.sbuf(f"stg{i}", [128, 1024]) for i in range(2)]; stg_R = [Res("stg0"), Res("stg1")]
    stg_i = [0]

    def nstg():
        i = stg_i[0] % 2
        stg_i[0] += 1
        return stg[i], stg_R[i]

    tA = kb.sbuf("tA", [128, 1024]); tA_R = (Res("tA0"), Res("tA1"))
    tB = kb.sbuf("tB", [128, 1024]); tB_R = (Res("tB0"), Res("tB1"))
    tC = kb.sbuf("tC", [128, 512]); tC_R = Res("tC")
    tC1 = kb.sbuf("tC1", [128, 512]); tC1_R = Res("tC1")
    sm = kb.sbuf("sm", [128, 64]); sm_R = Res("sm")
    sm1 = kb.sbuf("sm1", [128, 64]); sm1_R = Res("sm1")
    rstd = tC1; rstd_R = tC1_R
    sqb = arena[:, 0:4096].rearrange("p (k n) -> p k n", k=8); sqb_R = Res("sqb")
    MG_R = [Res("MG0"), Res("MG1")]
    PT = [kb.sbuf(f"PT{i}", [128, 512], BF16) for i in range(3)]; PT_R = [Res(f"PT{i}") for i in range(3)]
    pt_i = [0]
    qkb = kb.sbuf("qkb", [128, 512], BF16); qkb_R = Res("qkb")
    qkb1 = kb.sbuf("qkb1", [128, 512], BF16); qkb1_R = Res("qkb1")
    krpad = kb.sbuf("krpad", [128, 96], BF16); krpad_R = Res("krpad")
    krpad1 = kb.sbuf("krpad1", [128, 96], BF16); krpad1_R = Res("krpad1")
    kb.op("dve", lambda e: e.memset(krpad[:], 0.0), writes=[krpad_R])
    kb.op("dve", lambda e: e.memset(krpad1[:], 0.0), writes=[krpad1_R])
    Sst = [kb.sbuf(f"Sst{i}", [128, 64]) for i in range(2)]; Sst_R = [Res("Sf"), Res("Sb")]
    GL = kb.sbuf("GL", [128, 8, 2]); GL_R = Res("GL")
    zgT = tC[0:16, 256:512]; zgT_R = Res("zgT")
    zgT1 = tC1[0:16, 256:512]; zgT1_R = Res("zgT1")
    TSET = [dict(tA_=tA[:, 0:512], tA_R_=tA_R[0], tB_=tB[:, 0:512], tB_R_=tB_R[0], tC_=tC, tC_R_=tC_R, sm_=sm, sm_R_=sm_R, qkb_=qkb, qkb_R_=qkb_R,
                 krpad_=krpad, krpad_R_=krpad_R, zgT_=zgT, zgT_R_=zgT_R),
            dict(tA_=tA[:, 512:1024], tA_R_=tA_R[1], tB_=tB[:, 512:1024], tB_R_=tB_R[1], tC_=tC1, tC_R_=tC1_R, sm_=sm1, sm_R_=sm1_R, qkb_=qkb1, qkb_R_=qkb1_R,
                 krpad_=krpad1, krpad_R_=krpad1_R, zgT_=zgT1, zgT_R_=zgT1_R)]

    def norm_mod(Asc, shift_c0, l, g):
        for h in range(2):
            hs = slice(h * 512, (h + 1) * 512)
            xr = [xT_R[k][h] for k in range(8)]
            act(sqb, xT[:, :, hs], AF.Square, xr, [sqb_R, MG_R[0], MG_R[1]])
            b = nb()
            for kc in range(8):
                mm(pb[b], onesb[:], sqb[:, kc, :], kc == 0, kc == 7, [sqb_R, CR], [pb_R[b]])
            act(rstd[:], pb[b], AF.Sqrt, [pb_R[b], CR], [rstd_R], scale=1.0 / 1024, bias=epsT[:])
            recip(rstd[:], rstd[:], [rstd_R], [rstd_R])
            for kc in range(8):
                tmp, tr_ = (tA, tA_R) if kc % 2 == 0 else (tB, tB_R)
                tt(tmp[:, 0:512], xT[:, kc, hs], rstd[:], MUL, [xT_R[kc][h], rstd_R], [tr_])
                act(hT[:, kc, hs], tmp[:, 0:512], AF.Identity, [tr_, MR], [hT_R[h]],
                    scale=Asc[:, l, kc, g:g + 1], bias=modT[:, l, shift_c0 + kc, g:g + 1])

    def load_w(dst, src, sr):
        kb.dma("pool", dst, src, writes=[sr])

    def proj_tm(bank0, ncols, W, wr, t):
        nbk = (ncols + 511) // 512
        for bi in range(nbk):
            c0 = bi * 512
            c1 = min(ncols, c0 + 512)
            for kc in range(8):
                mm(pb[bank0 + bi][:, 0:c1 - c0], hT[:, kc, t * 128:(t + 1) * 128], W[:, kc, c0:c1], kc == 0, kc == 7,
                   [hT_R[t // 4], wr], [pb_R[bank0 + bi]])

    def rope_tm(src, src_R, U2, hs, cos_t, sin_t, out, tA, tA_R, tB, tB_R, qkb_R):
        xv = src.rearrange("p (u r h i) -> p u r h i", u=U2, r=2, h=2)
        ov = out.rearrange("p (u r h i) -> p u r h i", u=U2, r=2, h=2)
        cv = bc(cos_t.rearrange("p (r i) -> p r i", r=2).unsqueeze(1), [128, U2, 2, hs])
        sv = bc(sin_t.rearrange("p (r i) -> p r i", r=2).unsqueeze(1), [128, U2, 2, hs])
        n = U2 * 2 * hs
        a = tA[:, 0:n].rearrange("p (u r i) -> p u r i", u=U2, r=2)
        b2 = tB[:, 0:n].rearrange("p (u r i) -> p u r i", u=U2, r=2)
        x0 = xv[:, :, :, 0, :]; x1 = xv[:, :, :, 1, :]
        tt(a, x0, cv, MUL, [src_R, CR], [tA_R]); tt(b2, x1, sv, MUL, [src_R, CR], [tB_R])
        tt(ov[:, :, :, 0, :], a, b2, SUB, [tA_R, tB_R], [qkb_R])
        tt(a, x1, cv, MUL, [src_R, CR], [tA_R]); tt(b2, x0, sv, MUL, [src_R, CR], [tB_R])
        tt(ov[:, :, :, 1, :], a, b2, ADD, [tA_R, tB_R], [qkb_R])

    def attend(kT, qT, vT, kts, q0, N, scale, reads, ob, avoid=()):
        sb = [None] * len(kts)

        def S(i):
            b = nb()
            while b == ob or b in avoid:
                b = nb()
            sb[i] = b
            mm(pb[b][:, 0:N], kT(kts[i]), qT, True, True, reads, [pb_R[b]])
        S(0)
        for i in range(len(kts)):
            if i + 1 < len(kts):
                S(i + 1)
            p = pt_i[0] % 3
            pt_i[0] += 1
            act(PT[p][:, 0:N], pb[sb[i]][:, 0:N], AF.Exp, [pb_R[sb[i]]], [PT_R[p]], scale=scale)
            mm(pb[ob][:, 0:N], vT(kts[i]), PT[p][:, 0:N], i == 0, i == len(kts) - 1, reads + [PT_R[p]], [pb_R[ob]])

    def obank():
        b = nb()
        return b

    for g in GROUPS:
        S_GRP = (g == 1)
        NKT = 12 if S_GRP else 8
        KOFF = 4 if S_GRP else 0
        NK = NKT * 128
        if S_GRP:
            seqs = [(0, 1024, list(range(12)))]
        else:
            seqs = [(s * 256, 256, [2 * s, 2 * s + 1]) for s in range(4)]
        for t in range(8):
            st_, sr = nstg()
            kb.dma("sp", st_[:], x_d[g, t * 128:(t + 1) * 128, :], writes=[sr])
            for hh in range(2):
                b = nb()
                for kk in range(4):
                    kc = hh * 4 + kk
                    tr(pb[b][:, kk * 128:(kk + 1) * 128], st_[:, kc * 128:(kc + 1) * 128], identf[:], [sr], [pb_R[b]])
                cp(xT[:, hh * 4:(hh + 1) * 4, t * 128:(t + 1) * 128], pb[b].rearrange("p (k c) -> p k c", k=4), [pb_R[b]],
                   [xT_R[k][t // 4] for k in range(hh * 4, hh * 4 + 4)], eng=("act" if hh else "dve"))

        for l in range(NLAYERS):
            load_layer_params(l)
            norm_mod(A1, 0, l, g)
            if STOP == 'N':
                continue
            BR_OFF = 22528
            BR = arena[:, BR_OFF:BR_OFF + 8192].rearrange("p (j b n) -> p j b n", j=4, b=2)
            BR_R = [Res(f"BR{j}") for j in range(4)]
            OUTS = not S_GRP

            slot, sr = new_slot()
            W = slot[:, 0:4096].rearrange("p (k c) -> p k c", k=8)
            wsrc = win_d[l].rearrange("(k p) c -> p k c", p=128)
            for g2_ in range(2):
                for kv_ in range(2):
                    load_w(W[:, :, g2_ * 128 + kv_ * 64:g2_ * 128 + kv_ * 64 + 64], wsrc[:, :, (kv_ * 2 + g2_) * 64:(kv_ * 2 + g2_) * 64 + 64], sr)
            load_w(W[:, :, 256:512], wsrc[:, :, 256:512], sr)
            barrier(kb)
            QT = av(0, 2048).rearrange("p (b n) -> p b n", b=2); QT_R = Res("QT_A")
            KT = av(2048, 1536); KT_R = Res("KT_A")
            VA = av(3584, 4608).rearrange("p (k c) -> p k c", c=384); VA_R = Res("VA")
            kb.op("dve", lambda e: e.memset(VA[:, 0:NKT, :], 1.0), writes=[VA_R])
            if S_GRP:
                st_, sr2 = nstg()
                kb.dma("sp", st_[:, 0:512].rearrange("p (k f) -> p k f", k=4), cak_d[l].rearrange("(k p) f -> p k f", p=128), writes=[sr2])
                b = nb()
                for kt in range(4):
                    tr(pb[b][:, kt * 128:(kt + 1) * 128], st_[:, kt * 128:(kt + 1) * 128], identf[:], [sr2], [pb_R[b]])
                cp(KT[:, 0:512], pb[b], [pb_R[b]], [KT_R])
                kb.dma("pool", VA[:, 0:4, 64:128], cav_d[l].rearrange("(k p) f -> p k f", p=128)[:, :, 0:64], writes=[VA_R])
                kb.dma("pool", VA[:, 0:4, 256:320], cav_d[l].rearrange("(k p) f -> p k f", p=128)[:, :, 64:128], writes=[VA_R])
            if STOP == 'A0':
                continue
            for t in range(8):
                TS_ = TSET[t % 2]; tA_ = TS_['tA_']; tA_R_ = TS_['tA_R_']; tB_ = TS_['tB_']; tB_R_ = TS_['tB_R_']; tC_ = TS_['tC_']; tC_R_ = TS_['tC_R_']; sm_ = TS_['sm_']; sm_R_ = TS_['sm_R_']; qkb_ = TS_['qkb_']; qkb_R_ = TS_['qkb_R_']; krpad_ = TS_['krpad_']; krpad_R_ = TS_['krpad_R_']; zgT_ = TS_['zgT_']; zgT_R_ = TS_['zgT_R_']
                zb = nb()
                if zb == 6:
                    zb = nb()
                z = pb[zb]; zr = pb_R[zb]
                for kc in range(8):
                    mm(z, hT[:, kc, t * 128:(t + 1) * 128], W[:, kc, :], kc == 0, kc == 7, [hT_R[t // 4], sr], [zr])
                if STOP == 'A2a':
                    continue
                act(tC_[:, 0:384], z[:, 0:384], AF.Square, [zr], [tC_R_])
                kb.op("dve", lambda e, sm_=sm_, tC_=tC_: e.reduce_sum(out=sm_[:, 0:6], in_=tC_[:, 0:384].rearrange("p (h d) -> p h d", h=6), axis=AX.X), reads=[tC_R_], writes=[sm_R_])
                if STOP == 'A2b':
                    continue
                act(sm_[:, 0:6], sm_[:, 0:6], AF.Sqrt, [sm_R_, CR], [sm_R_], scale=1.0 / 64, bias=epsT[:])
                recip(sm_[:, 0:6], sm_[:, 0:6], [sm_R_], [sm_R_])
                if STOP == 'A2c':
                    continue
                tt(tC_[:, 0:384].rearrange("p (h d) -> p h d", h=6), z[:, 0:384].rearrange("p (h d) -> p h d", h=6),
                   bc(sm_[:, 0:6].unsqueeze(2), [128, 6, 64]), MUL, [zr, sm_R_], [tC_R_])
                tt(tC_[:, 0:384], tC_[:, 0:384], gA[:], MUL, [tC_R_, LP], [tC_R_])
                kt = KOFF + t
                if STOP == 'A2':
                    continue
                cp(VA[:, kt, 64:128], z[:, 384:448], [zr], [VA_R], eng="act")
                cp(VA[:, kt, 256:320], z[:, 448:512], [zr], [VA_R], eng="act")
                if S_GRP:
                    rope_tm(tC_[:, 0:384], tC_R_, 6, 16, cosA[:, t, :], sinA[:, t, :], qkb_[:, 0:384], tA_, tA_R_, tB_, tB_R_, qkb_R_)
                else:
                    cp(qkb_[:, 0:384], tC_[:, 0:384], [tC_R_], [qkb_R_])
                    o_, or_ = nstg()
                    cp(o_[:, 0:128], tC_[:, 256:384], [tC_R_], [or_], eng="act")
                    cp(o_[:, 128:256], z[:, 384:512], [zr], [or_], eng="act")
                    s_, tt_ = t // 2, (t % 2) * 128
                    kb.dma("sp", nak_d[s_, l, tt_:tt_ + 128, :], o_[:, 0:128], reads=[or_])
                    kb.dma("sp", nav_d[s_, l, tt_:tt_ + 128, :], o_[:, 128:256], reads=[or_])
                if STOP == 'A3':
                    continue
                for g2 in range(2):
                    tr(PSB[:, g2 * 128:(g2 + 1) * 128], qkb_[:, g2 * 128:(g2 + 1) * 128], identb[:], [qkb_R_], [pb_R[7]])
                tr(PSB[:, 256:384], qkb_[:, 256:384], identb[:], [qkb_R_], [pb_R[7]])
                if STOP == 'A4':
                    continue
                if STOP != 'A6':
                    cp(QT[:, :, t * 128:(t + 1) * 128], PSB[:, 0:256].rearrange("p (b n) -> p b n", b=2), [pb_R[7]], [QT_R])
                if STOP != 'A5':
                    cp(KT[:, kt * 128:(kt + 1) * 128], PSB[:, 256:384], [pb_R[7], QT_R], [KT_R], eng="act")
            if STOP in ('A1', 'A2', 'A2a', 'A2b', 'A2c', 'A3', 'A4', 'A5', 'A6'):
                continue
            for (q0, qlen, kts) in seqs:
                for qc in range(0, qlen, 512):
                    N = min(512, qlen - qc)
                    for kv in range(2):
                        for g2 in range(2):
                            ob = obank()
                            rows = slice(kv * 64, (kv + 1) * 64)
                            orow = g2 * 64
                            srow = 64 - orow
                            if g2 == 0:
                                vfn = lambda kt, kv=kv: VA[:, kt, kv * 192 + 64:kv * 192 + 192]
                            else:
                                vfn = lambda kt, kv=kv: VA[:, kt, kv * 192:kv * 192 + 128]
                            attend(lambda kt, rows=rows: KT[rows, kt * 128:(kt + 1) * 128], QT[rows, g2, q0 + qc:q0 + qc + N], vfn, kts, q0 + qc, N,
                                   0.125, [KT_R, QT_R, VA_R], ob)
                            recip(tA[srow:srow + 64, 0:N], pb[ob][srow:srow + 64, 0:N], [pb_R[ob]], [tA_R])
                            tt(BR[orow:orow + 64, 0, kv, q0 + qc:q0 + qc + N], pb[ob][orow:orow + 64, 0:N], tA[srow:srow + 64, 0:N], MUL,
                               [pb_R[ob], tA_R], [BR_R[0]])
            if STOP == 'A':
                continue
            slot, sr = new_slot()
            W = slot[:, 0:6144].rearrange("p (k c) -> p k c", k=8)
            load_w(W, win_d[l].rearrange("(k p) c -> p k c", p=128)[:, :, 1312:2080], sr)
            barrier(kb)
            QT = av(0, 2048).rearrange("p (b n) -> p b n", b=2); QT_R = Res("QT_C")
            KC = av(2048, 6144).rearrange("p (v b n) -> p v b n", v=2, b=2); KC_R = Res("KT_C")
            VC_OFF = 8192
            VC = av(VC_OFF, 4608).rearrange("p (k c) -> p k c", c=384); VC_R = Res("VC")
            kb.op("dve", lambda e: e.memset(VC[:, 0:NKT, :], 1.0), writes=[VC_R])

            def kc_store(src, kcols):
                for v in range(2):
                    for bb in range(2):
                        ts(KC[:, v, bb, kcols], src[:, bb, :], pm[:, v:v + 1], MUL, [pb_R[7], pb_R[6], CR], [KC_R])
            if S_GRP:
                for bb in range(2):
                    st_, sr2 = nstg()
                    kb.dma("sp", st_[:, 0:512].rearrange("p (k f) -> p k f", k=4), cck_d[l].rearrange("(k p) f -> p k f", p=128)[:, :, bb * 128:(bb + 1) * 128], writes=[sr2])
                    for kt in range(4):
                        tr(pb[6][:, kt * 128:(kt + 1) * 128], st_[:, kt * 128:(kt + 1) * 128], identf[:], [sr2], [pb_R[6]])
                    for v in range(2):
                        ts(KC[:, v, bb, 0:512], pb[6], pm[:, v:v + 1], MUL, [pb_R[6], CR], [KC_R])
                for (d0, d1, s0, s1) in VDST:
                    kb.dma("pool", VC[:, 0:4, d0:d1], ccv_d[l].rearrange("(k p) f -> p k f", p=128)[:, :, s0:s1], writes=[VC_R])
            for t in range(8):
                TS_ = TSET[t % 2]; tA_ = TS_['tA_']; tA_R_ = TS_['tA_R_']; tB_ = TS_['tB_']; tB_R_ = TS_['tB_R_']; tC_ = TS_['tC_']; tC_R_ = TS_['tC_R_']; sm_ = TS_['sm_']; sm_R_ = TS_['sm_R_']; qkb_ = TS_['qkb_']; qkb_R_ = TS_['qkb_R_']; krpad_ = TS_['krpad_']; krpad_R_ = TS_['krpad_R_']; zgT_ = TS_['zgT_']; zgT_R_ = TS_['zgT_R_']
                zb = nb()
                while zb >= 5:
                    zb = nb()
                zr = [pb_R[zb], pb_R[zb + 1]]
                z = PS[:, zb * 512:zb * 512 + 768]
                for bi in range(2):
                    c0, c1 = bi * 512, min(768, bi * 512 + 512)
                    for kc in range(8):
                        mm(PS[:, (zb + bi) * 512:(zb + bi) * 512 + c1 - c0], hT[:, kc, t * 128:(t + 1) * 128], W[:, kc, c0:c1], kc == 0, kc == 7,
                           [hT_R[t // 4], sr], [pb_R[zb + bi]])
                kt = KOFF + t
                for (d0, d1, s0, s1) in VDST:
                    cp(VC[:, kt, d0:d1], z[:, 512 + s0:512 + s1], zr, [VC_R], eng="act")
                if S_GRP:
                    rope_tm(z[:, 0:512], zr[0], 16, 8, cosC[:, t, :], sinC[:, t, :], qkb_[:, 0:512], tA_, tA_R_, tB_, tB_R_, qkb_R_)
                else:
                    cp(qkb_[:, 0:512], z[:, 0:512], zr, [qkb_R_])
                    o_, or_ = nstg()
                    cp(o_[:, 0:512], z[:, 256:768], zr, [or_], eng="act")
                    s_, tt_ = t // 2, (t % 2) * 128
                    kb.dma("sp", nck_d[s_, l, tt_:tt_ + 128, :], o_[:, 0:256], reads=[or_])
                    kb.dma("sp", ncv_d[s_, l, tt_:tt_ + 128, :], o_[:, 256:512], reads=[or_])
                for i in range(4):
                    tr(PSB[:, i * 128:(i + 1) * 128], qkb_[:, i * 128:(i + 1) * 128], identb[:], [qkb_R_], [pb_R[7]])
                cp(QT[:, :, t * 128:(t + 1) * 128], PSB[:, 0:256].rearrange("p (b n) -> p b n", b=2), [pb_R[7]], [QT_R], eng="act")
                kc_store(PSB[:, 256:512].rearrange("p (b n) -> p b n", b=2), slice(kt * 128, (kt + 1) * 128))
            OCR = tC; OCR_R = tC_R
            for (q0, qlen, kts) in seqs:
                for qc in range(0, qlen, 512):
                    N = min(512, qlen - qc)
                    for bb in range(2):
                        for hf in range(2):
                            h = bb * 2 + hf
                            rows = slice(hf * 64, (hf + 1) * 64)
                            orow = hf * 64
                            srow = 64 - orow
                            if True:
                                vfn = lambda kt, h=h: VC[:, kt, VOFFS[h]:VOFFS[h] + 128]
                            obs = []
                            for j in range(2):
                                ob = obank()
                                while ob in obs:
                                    ob = obank()
                                obs.append(ob)
                                attend(lambda kt, rows=rows, j=j, bb=bb: KC[rows, j, bb, kt * 128:(kt + 1) * 128], QT[rows, bb, q0 + qc:q0 + qc + N], vfn, kts,
                                       q0 + qc, N, 32 ** -0.5, [KC_R, QT_R, VC_R], ob, avoid=tuple(obs))
                            o1, o2 = obs
                            recip(tA[srow:srow + 64, 0:N], pb[o1][srow:srow + 64, 0:N], [pb_R[o1]], [tA_R])
                            recip(tB[srow:srow + 64, 0:N], pb[o2][srow:srow + 64, 0:N], [pb_R[o2]], [tB_R])
                            tt(tA[orow:orow + 64, 512:512 + N], pb[o1][orow:orow + 64, 0:N], tA[srow:srow + 64, 0:N], MUL, [pb_R[o1], tA_R], [tA_R])
                            tt(tB[orow:orow + 64, 512:512 + N], pb[o2][orow:orow + 64, 0:N], tB[srow:srow + 64, 0:N], MUL, [pb_R[o2], tB_R], [tB_R])
                            stt(OCR[orow:orow + 64, 0:N], tB[orow:orow + 64, 512:512 + N], neglam[orow:orow + 64, l:l + 1], tA[orow:orow + 64, 512:512 + N], MUL, ADD,
                                [tA_R, tB_R, MR], [OCR_R])
                        act(PT[0][:, 0:N], OCR[:, 0:N], AF.Square, [OCR_R], [PT_R[0]])
                        b = nb()
                        mm(pb[b][:, 0:N], bd64[:], PT[0][:, 0:N], True, True, [PT_R[0], CR], [pb_R[b]])
                        act(rstd[:, 0:N], pb[b][:, 0:N], AF.Sqrt, [pb_R[b], CR], [rstd_R], scale=1.0 / 64, bias=epsT[:])
                        recip(rstd[:, 0:N], rstd[:, 0:N], [rstd_R], [rstd_R])
                        tt(OCR[:, 0:N], OCR[:, 0:N], rstd[:, 0:N], MUL, [OCR_R, rstd_R], [OCR_R])
                        act(BR[:, 2, bb, q0 + qc:q0 + qc + N], OCR[:, 0:N], AF.Copy, [OCR_R, MR], [BR_R[2]], scale=ocs[:, l:l + 1])

            if STOP == 'C':
                continue
            slot, sr = new_slot()
            W = slot[:, 0:2816].rearrange("p (k c) -> p k c", k=8)
            load_w(W, win_d[l].rearrange("(k p) c -> p k c", p=128)[:, :, 2080:2432], sr)
            UQ = slot[:, 2816:3584].rearrange("p (k c) -> p k c", k=2)
            UQS = slot[:, 3584:4352].rearrange("p (k c) -> p k c", k=2)
            UKV = slot[:, 4352:4864]
            load_w(UQ[:, 0, :], wuq_d[l, 0:128, :], sr); load_w(UQ[0:64, 1, :], wuq_d[l, 128:192, :], sr)
            load_w(UKV, wukv_d[l], sr)
            if S_GRP:
                with nc.allow_non_contiguous_dma(reason="rope column swap"):
                    for (kc_, r0, r1, pr) in ((0, 0, 128, 128), (1, 128, 192, 64)):
                        src = wuq_d[l, r0:r1, :].rearrange("p (h c) -> p h c", h=4)[:, :, 64:96].rearrange("p h (r f i) -> p h r f i", r=2, f=2)
                        dst = UQS[0:pr, kc_, :].rearrange("p (h c) -> p h c", h=4)[:, :, 64:96].rearrange("p h (r f i) -> p h r f i", r=2, f=2)
                        for f in range(2):
                            for r_ in range(2):
                                load_w(dst[:, :, r_, f, :], src[:, :, r_, 1 - f, :], sr)
            barrier(kb)
            DQN = av(0, 2048).rearrange("p (b n) -> p b n", b=2); DQN_R = Res("DQN")
            CKT = av(2048, 1536); CKT_R = Res("CKT")
            KD_ = av(3584, 6144).rearrange("p (h n) -> p h n", h=4); KD_R = Res("KT_D")
            DQT = av(9728, 4096).rearrange("p (h n) -> p h n", h=4); DQT_R = Res("DQT")
            VD_OFF = 13824
            VD = av(VD_OFF, 4608).rearrange("p (k c) -> p k c", c=384); VD_R = Res("VD")
            kb.op("dve", lambda e: e.memset(VD[:, 0:NKT, :], 1.0), writes=[VD_R])
            if S_GRP:
                st_, sr2 = nstg()
                kb.dma("sp", st_[:, 0:512].rearrange("p (k f) -> p k f", k=4), cdc_d[l].rearrange("(k p) f -> p k f", p=128), writes=[sr2])
                for kt in range(4):
                    tr(pb[6][:, kt * 128:(kt + 1) * 128], st_[:, kt * 128:(kt + 1) * 128], identf[:], [sr2], [pb_R[6]])
                cp(CKT[:, 0:512], pb[6], [pb_R[6]], [CKT_R])
                stgkr = st_[:, 512:896].rearrange("p (k f) -> p k f", k=4)
                kb.op("dve", lambda e, stgkr=stgkr: e.memset(stgkr[:, :, 0:64], 0.0), writes=[sr2])
                kb.dma("sp", stgkr[:, :, 64:96], cdr_d[l].rearrange("(k p) f -> p k f", p=128), writes=[sr2])
                b = nb()
                for kt in range(4):
                    mm(pb[b][0:96, kt * 128:(kt + 1) * 128], stgkr[:, kt, :], identf[:], True, True, [sr2, CR], [pb_R[b]])
                cp(KD_[64:96, :, 0:512], bc(pb[b][64:96, :].unsqueeze(1), [32, 4, 512]), [pb_R[b]], [KD_R])
            for t in range(8):
                TS_ = TSET[t % 2]; tA_ = TS_['tA_']; tA_R_ = TS_['tA_R_']; tB_ = TS_['tB_']; tB_R_ = TS_['tB_R_']; tC_ = TS_['tC_']; tC_R_ = TS_['tC_R_']; sm_ = TS_['sm_']; sm_R_ = TS_['sm_R_']; qkb_ = TS_['qkb_']; qkb_R_ = TS_['qkb_R_']; krpad_ = TS_['krpad_']; krpad_R_ = TS_['krpad_R_']; zgT_ = TS_['zgT_']; zgT_R_ = TS_['zgT_R_']
                zb = nb()
                z = pb[zb]; zr = pb_R[zb]
                for kc in range(8):
                    mm(z[:, 0:352], hT[:, kc, t * 128:(t + 1) * 128], W[:, kc, :], kc == 0, kc == 7, [hT_R[t // 4], sr], [zr])
                act(tA_[:, 0:192], z[:, 0:192], AF.Square, [zr], [tA_R_], accum_out=sm_[:, 8:9])
                act(tA_[:, 192:320], z[:, 192:320], AF.Square, [zr], [tA_R_], accum_out=sm_[:, 9:10])
                act(sm_[:, 8:9], sm_[:, 8:9], AF.Sqrt, [tA_R_, CR], [sm_R_], scale=1.0 / 192, bias=epsT[:])
                act(sm_[:, 9:10], sm_[:, 9:10], AF.Sqrt, [tA_R_, CR], [sm_R_], scale=1.0 / 128, bias=epsT[:])
                recip(sm_[:, 8:10], sm_[:, 8:10], [sm_R_], [sm_R_])
                stt(qkb_[:, 0:192], z[:, 0:192], sm_[:, 8:9], gDQ[:], MUL, MUL, [zr, sm_R_, LP], [qkb_R_])
                stt(tB_[:, 0:128], z[:, 192:320], sm_[:, 9:10], gKV[:], MUL, MUL, [zr, sm_R_, LP], [tB_R_])
                cp(qkb_[:, 192:320], tB_[:, 0:128], [tB_R_], [qkb_R_])
                kt = KOFF + t
                if S_GRP:
                    xv = z[:, 320:352].rearrange("p (r h i) -> p r h i", r=2, h=2)
                    ov = krpad_[:, 64:96].rearrange("p (r h i) -> p r h i", r=2, h=2)
                    cv = cosC[:, t, :].rearrange("p (r i) -> p r i", r=2); sv = sinC[:, t, :].rearrange("p (r i) -> p r i", r=2)
                    a = tC_[:, 0:16].rearrange("p (r i) -> p r i", r=2); b2 = tC_[:, 16:32].rearrange("p (r i) -> p r i", r=2)
                    tt(a, xv[:, :, 0, :], cv, MUL, [zr, CR], [tC_R_]); tt(b2, xv[:, :, 1, :], sv, MUL, [zr, CR], [tC_R_])
                    tt(ov[:, :, 0, :], a, b2, SUB, [tC_R_], [krpad_R_])
                    tt(a, xv[:, :, 1, :], cv, MUL, [zr, CR], [tC_R_]); tt(b2, xv[:, :, 0, :], sv, MUL, [zr, CR], [tC_R_])
                    tt(ov[:, :, 1, :], a, b2, ADD, [tC_R_], [krpad_R_])
                else:
                    cp(krpad_[:, 64:96], z[:, 320:352], [zr], [krpad_R_])
                    o_, or_ = nstg()
                    cp(o_[:, 0:128], tB_[:, 0:128], [tB_R_], [or_], eng="act")
                    cp(o_[:, 128:160], z[:, 320:352], [zr], [or_], eng="act")
                    s_, tt_ = t // 2, (t % 2) * 128
                    kb.dma("sp", nckv_d[s_, l, tt_:tt_ + 128, :], o_[:, 0:128], reads=[or_])
                    kb.dma("sp", nkr_d[s_, l, tt_:tt_ + 128, :], o_[:, 128:160], reads=[or_])
                tr(PSB[:, 0:128], qkb_[:, 0:128], identb[:], [qkb_R_], [pb_R[7]])
                tr(PSB[0:64, 128:256], qkb_[:, 128:192], identb[:], [qkb_R_], [pb_R[7]])
                tr(PSB[:, 256:384], qkb_[:, 192:320], identb[:], [qkb_R_], [pb_R[7]])
                cp(DQN[:, 0, t * 128:(t + 1) * 128], PSB[:, 0:128], [pb_R[7]], [DQN_R])
                cp(DQN[0:64, 1, t * 128:(t + 1) * 128], PSB[0:64, 128:256], [pb_R[7]], [DQN_R])
                cp(CKT[:, kt * 128:(kt + 1) * 128], PSB[:, 256:384], [pb_R[7]], [CKT_R], eng="act")
                b = nb()
                mm(pb[b][0:96, 0:128], krpad_[:], identb[:], True, True, [krpad_R_, CR], [pb_R[b]])
                cp(KD_[64:96, :, kt * 128:(kt + 1) * 128], bc(pb[b][64:96, 0:128].unsqueeze(1), [32, 4, 128]), [pb_R[b]], [KD_R])
            for h in range(4):
                for c0 in range(0, NK, 512):
                    b = nb()
                    mm(pb[b][0:64, :], UKV[:, h * 128:h * 128 + 64], CKT[:, c0:c0 + 512], True, True, [sr, CKT_R], [pb_R[b]])
                    cp(KD_[0:64, h, c0:c0 + 512], pb[b][0:64, :], [pb_R[b]], [KD_R], eng=("act" if h % 2 else "dve"))
            for kt in range(NKT):
                b = nb()
                mm(pb[b][:, 0:256].rearrange("p (h e) -> p h e", h=4), CKT[:, kt * 128:(kt + 1) * 128], UKV.rearrange("p (h c) -> p h c", h=4)[:, :, 64:128], True, True,
                   [sr, CKT_R], [pb_R[b]])
                for (d0, d1, s0, s1) in VDST:
                    cp(VD[:, kt, d0:d1], pb[b][:, s0:s1], [pb_R[b]], [VD_R], eng=("act" if kt % 2 else "dve"))
            for h in range(4):
                for qc in range(0, 1024, 512):
                    b = nb()
                    mm(pb[b][0:96, :], UQ[:, 0, h * 96:(h + 1) * 96], DQN[:, 0, qc:qc + 512], True, False, [sr, DQN_R], [pb_R[b]])
                    mm(pb[b][0:96, :], UQ[0:64, 1, h * 96:(h + 1) * 96], DQN[0:64, 1, qc:qc + 512], False, True, [sr, DQN_R], [pb_R[b]])
                    if S_GRP:
                        b2_ = nb()
                        mm(pb[b2_][0:96, :], UQS[:, 0, h * 96:(h + 1) * 96], DQN[:, 0, qc:qc + 512], True, False, [sr, DQN_R], [pb_R[b2_]])
                        mm(pb[b2_][0:96, :], UQS[0:64, 1, h * 96:(h + 1) * 96], DQN[0:64, 1, qc:qc + 512], False, True, [sr, DQN_R], [pb_R[b2_]])
                        cp(DQT[0:64, h, qc:qc + 512], pb[b][0:64, :], [pb_R[b]], [DQT_R], eng="act")
                        tt(tA[64:96, 0:512], pb[b][64:96, :], cosD[64:96, qc:qc + 512], MUL, [pb_R[b], CR], [tA_R])
                        tt(tB[64:96, 0:512], pb[b2_][64:96, :], sinD[64:96, qc:qc + 512], MUL, [pb_R[b2_], CR], [tB_R])
                        tt(DQT[64:96, h, qc:qc + 512], tA[64:96, 0:512], tB[64:96, 0:512], ADD, [tA_R, tB_R], [DQT_R])
                    else:
                        cp(DQT[0:96, h, qc:qc + 512], pb[b][0:96, :], [pb_R[b]], [DQT_R], eng=("act" if h % 2 else "dve"))
            for (q0, qlen, kts) in seqs:
                for qc in range(0, qlen, 512):
                    N = min(512, qlen - qc)
                    for h in range(4):
                        bb, hf = h // 2, h % 2
                        orow = hf * 64
                        srow = 64 - orow
                        if True:
                            vfn = lambda kt, h=h: VD[:, kt, VOFFS[h]:VOFFS[h] + 128]
                        ob = obank()
                        attend(lambda kt, h=h: KD_[0:96, h, kt * 128:(kt + 1) * 128], DQT[0:96, h, q0 + qc:q0 + qc + N], vfn, kts, q0 + qc, N,
                               96 ** -0.5, [KD_R, DQT_R, VD_R], ob)
                        recip(tA[srow:srow + 64, 0:N], pb[ob][srow:srow + 64, 0:N], [pb_R[ob]], [tA_R])
                        tt(BR[orow:orow + 64, 3, bb, q0 + qc:q0 + qc + N], pb[ob][orow:orow + 64, 0:N], tA[srow:srow + 64, 0:N], MUL,
                           [pb_R[ob], tA_R], [BR_R[3]])
            if STOP == 'D':
                continue
            slot, sr = new_slot()
            W = slot[:, 0:6144].rearrange("p (k c) -> p k c", k=8)
            G = slot[:, 6144:6656].rearrange("p (k c) -> p k c", k=8)
            load_w(W, win_d[l].rearrange("(k p) c -> p k c", p=128)[:, :, 512:1280], sr)
            load_w(G[:, :, 0:16], win_d[l].rearrange("(k p) c -> p k c", p=128)[:, :, 1280:1296], sr)
            load_w(G[:, :, 32:48], win_d[l].rearrange("(k p) c -> p k c", p=128)[:, :, 1296:1312], sr)
            barrier(kb)
            BT = av(0, 12288).rearrange("p (t d c) -> p t d c", t=8, d=2); BT_R = Res("BT")
            KDc = av(12288, 2048).rearrange("p (t d c) -> p t d c", t=8, d=2); KDc_R = Res("KDc")
            VB = av(14336, 2048).rearrange("p (t c) -> p t c", t=8); VB_R = Res("VB")
            GR = av(16384, 2048).rearrange("p (t c) -> p t c", t=8); GR_R = Res("GR")
            SIN = av(18432, 4096).rearrange("p (t d c) -> p t d c", t=8, d=2); SIN_R = Res("SIN")
            Lsp = tC; Lsp_R = tC_R
            for t in range(8):
                TS_ = TSET[t % 2]; tA_ = TS_['tA_']; tA_R_ = TS_['tA_R_']; tB_ = TS_['tB_']; tB_R_ = TS_['tB_R_']; tC_ = TS_['tC_']; tC_R_ = TS_['tC_R_']; sm_ = TS_['sm_']; sm_R_ = TS_['sm_R_']; qkb_ = TS_['qkb_']; qkb_R_ = TS_['qkb_R_']; krpad_ = TS_['krpad_']; krpad_R_ = TS_['krpad_R_']; zgT_ = TS_['zgT_']; zgT_R_ = TS_['zgT_R_']
                zb = nb()
                while zb >= 4:
                    zb = nb()
                zr = [pb_R[zb], pb_R[zb + 1]]
                z = PS[:, zb * 512:zb * 512 + 768]
                for bi in range(2):
                    c0, c1 = bi * 512, min(768, bi * 512 + 512)
                    for kc in range(8):
                        mm(PS[:, (zb + bi) * 512:(zb + bi) * 512 + c1 - c0], hT[:, kc, t * 128:(t + 1) * 128], W[:, kc, c0:c1], kc == 0, kc == 7,
                           [hT_R[t // 4], sr], [pb_R[zb + bi]])
                if BCUT == 1:
                    continue
                for d in range(2):
                    for kc in range(8):
                        mm(pb[5][0:16, d * 128:(d + 1) * 128], G[:, kc, d * 32:d * 32 + 16], hT[:, kc, t * 128:(t + 1) * 128], kc == 0, kc == 7, [hT_R[t // 4], sr], [pb_R[5]])
                cp(zgT_, pb[5][0:16, 0:256], [pb_R[5]], [zgT_R_])
                if BCUT == 2:
                    continue
                for d in range(2):
                    mm(pb[5][:, 256 + d * 128:384 + d * 128], zgT_[0:16, d * 128:(d + 1) * 128], GW[0:16, d * 128:(d + 1) * 128], True, False, [zgT_R_, LP], [pb_R[5]])
                    mm(pb[5][:, 256 + d * 128:384 + d * 128], onesrow[0:1, :], GBias[0:1, d * 128:(d + 1) * 128], False, True, [CR, LP], [pb_R[5]])
                if BCUT == 3:
                    continue
                act(tA_[:, 0:256], pb[5][:, 256:512], AF.Exp, [pb_R[5]], [tA_R_], scale=-1.0)
                act(tC_[:, 0:256], tA_[:, 0:256], AF.Ln, [tA_R_, CR], [tC_R_], bias=onescol[:])
                if BCUT == 4:
                    continue
                for i, (m_, d) in enumerate(((0, 0), (1, 0), (2, 1), (3, 1))):
                    mm(pb[6][:, i * 128:(i + 1) * 128], tri[:, m_, :], tC_[:, d * 128:(d + 1) * 128], True, True, [tC_R_, CR], [pb_R[6]])
                if BCUT == 5:
                    continue
                for d in range(2):
                    mm(pb[5][:, 2 * d:2 + 2 * d], tC_[:, d * 128:(d + 1) * 128], onesf[:, 0:2], True, True, [tC_R_, CR], [pb_R[5]])
                act(GL[:, t, :], pb[5][:, 0:4].rearrange("p (d two) -> p d two", two=2)[:, :, 0], AF.Exp, [pb_R[5]], [GL_R], scale=-1.0 / 16)
                if BCUT == 6:
                    continue
                Ea = tA_; Eb = tB_
                act(Ea[:, 0:512], pb[6], AF.Exp, [pb_R[6]], [tA_R_], scale=-1.0 / 16)
                act(Eb[:, 0:256].rearrange("p (a c) -> p a c", a=2), pb[6].rearrange("p (a b c) -> p a b c", a=2, b=2)[:, :, 0, :], AF.Exp, [pb_R[6]], [tB_R_], scale=1.0 / 16)
                if BCUT == 7:
                    continue
                zq, zk = z[:, 0:128], z[:, 128:256]
                stt(qkb_[:, 0:128], zq, 32 ** -0.5, Ea[:, 0:128], MUL, MUL, zr + [tA_R_], [qkb_R_])
                stt(qkb_[:, 128:256], zq, 32 ** -0.5, Ea[:, 256:384], MUL, MUL, zr + [tA_R_], [qkb_R_])
                tt(qkb_[:, 256:384], zk, Eb[:, 0:128], MUL, zr + [tB_R_], [qkb_R_])
                tt(qkb_[:, 384:512], zk, Eb[:, 128:256], MUL, zr + [tB_R_], [qkb_R_])
                tt(KDc[:, t, 0, :], zk, Ea[:, 128:256], MUL, zr + [tA_R_], [KDc_R])
                tt(KDc[:, t, 1, :], zk, Ea[:, 384:512], MUL, zr + [tA_R_], [KDc_R])
                if BCUT == 8:
                    continue
                cp(VB[:, t, :], z[:, 256:512], zr, [VB_R], eng="act")
                act(GR[:, t, :], z[:, 512:768], AF.Silu, zr, [GR_R])
                if BCUT == 9:
                    continue
                for i in range(4):
                    tr(PSB[:, i * 128:(i + 1) * 128], qkb_[:, i * 128:(i + 1) * 128], identb[:], [qkb_R_], [pb_R[7]])
                if BCUT == 10:
                    continue
                for d in range(2):
                    cp(BT[:, t, d, 0:128], PSB[:, 256 + d * 128:384 + d * 128], [pb_R[7]], [BT_R], eng="act")
                    cp(BT[:, t, d, 128:256], PSB[:, d * 128:(d + 1) * 128], [pb_R[7]], [BT_R], eng="act")
                    tt(BT[:, t, d, 256:768].rearrange("p (h n) -> p h n", h=4), bc(PSB[:, d * 128:(d + 1) * 128].unsqueeze(1), [128, 4, 128]),
                       bc(hm[:].unsqueeze(2), [128, 4, 128]), MUL, [pb_R[7], CR], [BT_R])
            if STOP.startswith('B1'):
                continue
            hm3 = bc(hm[:].unsqueeze(2), [128, 4, 64])
            for si, (q0, qlen, kts) in enumerate(seqs):
                t0, nt = q0 // 128, qlen // 128
                for d in range(2):
                    Sf, Sr = Sst[d], Sst_R[d]
                    if S_GRP:
                        kb.dma("sp", Sf[:], (sbf_d if d == 0 else sbb_d)[l], writes=[Sr])
                    else:
                        kb.op("dve", lambda e, Sf=Sf: e.memset(Sf[:], 0.0), writes=[Sr])
                    order = range(t0, t0 + nt) if d == 0 else range(t0 + nt - 1, t0 - 1, -1)
                    for t in order:
                        tt(SIN[:, t, d, :].rearrange("p (h e) -> p h e", h=4), bc(Sf[:].unsqueeze(1), [128, 4, 64]), hm3, MUL, [Sr, CR], [SIN_R])
                        b = nb()
                        mm(pb[b][:, 0:256], KDc[:, t, d, :], VB[:, t, :], True, True, [KDc_R, VB_R], [pb_R[b]])
                        tt(tA[:, 0:256].rearrange("p (h e) -> p h e", h=4), pb[b][:, 0:256].rearrange("p (h e) -> p h e", h=4), hm3, MUL, [pb_R[b], CR], [tA_R])
                        kb.op("dve", lambda e: e.reduce_sum(out=tB[:, 0:64], in_=tA[:, 0:256].rearrange("p (h e) -> p e h", h=4), axis=AX.X), reads=[tA_R], writes=[tB_R])
                        stt(Sf[:], Sf[:], GL[:, t, d:d + 1], tB[:, 0:64], MUL, ADD, [Sr, GL_R, tB_R], [Sr])
                    if not S_GRP:
                        kb.dma("sp", (nbf_d if d == 0 else nbb_d)[si, l], Sf[:], reads=[Sr])
            if STOP == 'B2':
                continue
            for t in range(8):
                ob = nb()
                mm(pb[ob][:, 0:256], BT[:, t, 0, 128:256], SIN[:, t, 0, :], True, False, [BT_R, SIN_R], [pb_R[ob]])
                mm(pb[ob][:, 0:256], BT[:, t, 1, 128:256], SIN[:, t, 1, :], False, False, [BT_R, SIN_R], [pb_R[ob]])
                for d in range(2):
                    ab = nb()
                    while ab == ob:
                        ab = nb()
                    mm(pb[ab], BT[:, t, d, 0:128], BT[:, t, d, 256:768], True, True, [BT_R], [pb_R[ab]])
                    p = pt_i[0] % 3
                    pt_i[0] += 1
                    tt(PT[p][:].rearrange("p (h n) -> p h n", h=4), pb[ab].rearrange("p (h n) -> p h n", h=4),
                       bc(tri[:, (0 if d == 0 else 2), :].unsqueeze(1), [128, 4, 128]), MUL, [pb_R[ab], CR], [PT_R[p]])
                    for h in range(4):
                        mm(pb[ob][:, h * 64:(h + 1) * 64], PT[p][:, h * 128:(h + 1) * 128], VB[:, t, h * 64:(h + 1) * 64], False, (d == 1 and h == 3),
                           [PT_R[p], VB_R], [pb_R[ob]])
                o = pb[ob][:, 0:256]
                act(tA[:, 0:256], o, AF.Square, [pb_R[ob]], [tA_R])
                kb.op("dve", lambda e: e.reduce_sum(out=sm[:, 16:20], in_=tA[:, 0:256].rearrange("p (h d) -> p h d", h=4), axis=AX.X), reads=[tA_R], writes=[sm_R])
                act(sm[:, 16:20], sm[:, 16:20], AF.Sqrt, [sm_R, CR], [sm_R], scale=1.0 / 64, bias=epsT[:])
                recip(sm[:, 16:20], sm[:, 16:20], [sm_R], [sm_R])
                tt(tA[:, 0:256].rearrange("p (h d) -> p h d", h=4), o.rearrange("p (h d) -> p h d", h=4), bc(sm[:, 16:20].unsqueeze(2), [128, 4, 64]), MUL,
                   [pb_R[ob], sm_R], [tA_R])
                tt(tA[:, 0:256], tA[:, 0:256], gBO[:], MUL, [tA_R, LP], [tA_R])
                tt(qkb[:, 0:256], tA[:, 0:256], GR[:, t, :], MUL, [tA_R, GR_R], [qkb_R])
                for i in range(2):
                    tr(PSB[:, i * 128:(i + 1) * 128], qkb[:, i * 128:(i + 1) * 128], identb[:], [qkb_R], [pb_R[7]])
                cp(BR[:, 1, :, t * 128:(t + 1) * 128], PSB[:, 0:256].rearrange("p (b n) -> p b n", b=2), [pb_R[7]], [BR_R[1]])

            if STOP == 'B':
                continue
            bslot, bsr = new_slot()
            BW = bslot[:, 0:8192].rearrange("p (j k c) -> p j k c", j=4, k=2)
            for j in range(4):
                load_w(BW[:, j], wbr_d[l, j].rearrange("(k p) c -> p k c", p=128), bsr)
            barrier(kb)
            if DEBUG and l == 0 and g == GROUPS[0]:
                for jb in range(8):
                    o_, or_ = nstg()
                    cp(o_[:], BR[:, jb // 2, jb % 2, :], BR_R, [or_])
                    kb.dma("sp", dbg_br[:, jb, :], o_[:], reads=[or_])
            MG = av(0, 8192).rearrange("p (k n) -> p k n", k=8)
            for mp in range(4):
                slot, sr = new_slot(avoid=bslot)
                GWT = slot[:, 0:8192].rearrange("p (j k c) -> p j k c", j=4, k=8)
                for j in range(4):
                    c0 = 2432 + j * 1024 + mp * 256
                    load_w(GWT[:, j], win_d[l].rearrange("(k p) c -> p k c", p=128)[:, :, c0:c0 + 256], sr)
                for mi in range(2):
                    m = mp * 2 + mi
                    for h in range(2):
                        hs = slice(h * 512, (h + 1) * 512)
                        for j in range(4):
                            b1 = nb(); b2_ = nb()
                            for kc in range(8):
                                mm(pb[b1], GWT[:, j, kc, mi * 128:(mi + 1) * 128], hT[:, kc, hs], kc == 0, kc == 7, [sr, hT_R[h]], [pb_R[b1]])
                            for kc in range(2):
                                mm(pb[b2_], BW[:, j, kc, m * 128:(m + 1) * 128], BR[:, j, kc, hs], kc == 0, kc == 1, [bsr, BR_R[j]], [pb_R[b2_]])
                            act(tA[:, 0:512], pb[b1], AF.Sigmoid, [pb_R[b1]], [tA_R])
                            if j == 0:
                                tt(tC[:, 0:512], pb[b2_], tA[:, 0:512], MUL, [pb_R[b2_], tA_R], [tC_R])
                            else:
                                tt(tB[:, 0:512], pb[b2_], tA[:, 0:512], MUL, [pb_R[b2_], tA_R], [tB_R])
                                if j < 3:
                                    tt(tC[:, 0:512], tC[:, 0:512], tB[:, 0:512], ADD, [tC_R, tB_R], [tC_R])
                                else:
                                    tt(MG[:, m, hs], tC[:, 0:512], tB[:, 0:512], ADD, [tC_R, tB_R], [MG_R[h]])
            if STOP == 'M':
                continue
            slot, sr = new_slot()
            WO = slot[:, 0:8192].rearrange("p (k c) -> p k c", k=8)
            load_w(WO, wout_d[l].rearrange("(k p) c -> p k c", p=128), sr)
            for m in range(8):
                for h in range(2):
                    hs = slice(h * 512, (h + 1) * 512)
                    b = nb()
                    for kc in range(8):
                        mm(pb[b], WO[:, kc, m * 128:(m + 1) * 128], MG[:, kc, hs], kc == 0, kc == 7, [sr, MG_R[h]], [pb_R[b]])
                    stt(xT[:, m, hs], pb[b], modT[:, l, 16 + m, g:g + 1], xT[:, m, hs], MUL, ADD, [pb_R[b], MR, xT_R[m][h]], [xT_R[m][h]])
            if DEBUG and l == 0 and g == GROUPS[0]:
                for kc in range(8):
                    kb.dma("sp", dbg_x1[:, kc, :], xT[:, kc, :], reads=[xT_R[kc][0], xT_R[kc][1]])
                    o_, or_ = nstg()
                    cp(o_[:], MG[:, kc, :], MG_R, [or_])
                    kb.dma("sp", dbg_mg[:, kc, :], o_[:], reads=[or_])
            norm_mod(A2, 24, l, g)
            AT = av(8192, 22528).rearrange("p (f n) -> p f n", f=22); AT_R = [Res("AT0"), Res("AT1")]
            U_ = stg[0]; U_R = stg_R[0]; Cc = stg[1]; Cc_R = stg_R[1]
            nsq = 1 if S_GRP else 4
            sl = 1024 // nsq
            for f0 in range(0, 22, 4):
                nf = min(4, 22 - f0)
                slot, sr = new_slot()
                UW = slot[:, 0:4096].rearrange("p (k c) -> p k c", k=8)
                GW2 = slot[:, 4096:8192].rearrange("p (k c) -> p k c", k=8)
                load_w(UW[:, :, 0:nf * 128], wfu_d[l].rearrange("(k p) c -> p k c", p=128)[:, :, f0 * 128:(f0 + nf) * 128], sr)
                load_w(GW2[:, :, 0:nf * 128], wfg_d[l].rearrange("(k p) c -> p k c", p=128)[:, :, f0 * 128:(f0 + nf) * 128], sr)
                for fi in range(nf):
                    f = f0 + fi
                    gb_ = []
                    for h in range(2):
                        hs = slice(h * 512, (h + 1) * 512)
                        bu = nb(); bg = nb()
                        gb_.append(bg)
                        for kc in range(8):
                            mm(pb[bu], UW[:, kc, fi * 128:(fi + 1) * 128], hT[:, kc, hs], kc == 0, kc == 7, [sr, hT_R[h]], [pb_R[bu]])
                        for kc in range(8):
                            mm(pb[bg], GW2[:, kc, fi * 128:(fi + 1) * 128], hT[:, kc, hs], kc == 0, kc == 7, [sr, hT_R[h]], [pb_R[bg]])
                        cp(U_[:, hs], pb[bu], [pb_R[bu]], [U_R], eng="act")
                    act(Cc[:], U_[:], AF.Identity, [U_R, LP], [Cc_R], scale=cwT[:, 1, f:f + 1], bias=cbT[:, f:f + 1])
                    Uv = U_[:].rearrange("p (s n) -> p s n", s=nsq); Cv = Cc[:].rearrange("p (s n) -> p s n", s=nsq)
                    stt(Cv[:, :, 1:sl], Uv[:, :, 0:sl - 1], cwT[:, 0, f:f + 1], Cv[:, :, 1:sl], MUL, ADD, [U_R, Cc_R, LP], [Cc_R])
                    stt(Cv[:, :, 0:sl - 1], Uv[:, :, 1:sl], cwT[:, 2, f:f + 1], Cv[:, :, 0:sl - 1], MUL, ADD, [U_R, Cc_R, LP], [Cc_R])
                    act(tA[:], Cc[:], AF.Gelu_apprx_tanh, [Cc_R], [tA_R])
                    for h in range(2):
                        hs = slice(h * 512, (h + 1) * 512)
                        tt(AT[:, f, hs], tA[:, hs], pb[gb_[h]], MUL, [tA_R, pb_R[gb_[h]]], [AT_R[h]])
            for m0 in range(0, 8, 2):
                slot, sr = new_slot()
                WD = slot[:, 0:5632].rearrange("p (f c) -> p f c", f=22)
                load_w(WD, wfd_d[l].rearrange("(f p) c -> p f c", p=128)[:, :, m0 * 128:(m0 + 2) * 128], sr)
                for mi in range(2):
                    m = m0 + mi
                    for h in range(2):
                        hs = slice(h * 512, (h + 1) * 512)
                        b = nb()
                        for f in range(22):
                            mm(pb[b], WD[:, f, mi * 128:(mi + 1) * 128], AT[:, f, hs], f == 0, f == 21, [sr, AT_R[h]], [pb_R[b]])
                        stt(xT[:, m, hs], pb[b], modT[:, l, 40 + m, g:g + 1], xT[:, m, hs], MUL, ADD, [pb_R[b], MR, xT_R[m][h]], [xT_R[m][h]])

            if DEBUG and l == 0 and g == GROUPS[0]:
                for kc in range(8):
                    kb.dma("sp", dbg_x2[:, kc, :], xT[:, kc, :], reads=[xT_R[kc][0], xT_R[kc][1]])
        barrier(kb)
        kb.dma("sp", fgB, fg_d.partition_broadcast(128), writes=[fgB_R])
        for t in range(8):
            b0 = nb()
            while b0 >= 5:
                b0 = nb()
            for kc in range(8):
                bi = b0 + kc // 4
                tr(PS[:, bi * 512 + (kc % 4) * 128: bi * 512 + (kc % 4 + 1) * 128], xT[:, kc, t * 128:(t + 1) * 128], identf[:], [xT_R[kc][t // 4]], [pb_R[bi]])
            yp = PS[:, b0 * 512:b0 * 512 + 1024]
            yr = [pb_R[b0], pb_R[b0 + 1]]
            act(tA[:], yp, AF.Square, yr, [tA_R], accum_out=sm[:, 24:25])
            act(sm[:, 24:25], sm[:, 24:25], AF.Sqrt, [tA_R, CR], [sm_R], scale=1.0 / 1024, bias=epsT[:])
            recip(sm[:, 24:25], sm[:, 24:25], [sm_R], [sm_R])
            o_, or_ = nstg()
            stt(o_[:], yp, sm[:, 24:25], fgB, MUL, MUL, yr + [sm_R, fgB_R], [or_])
            kb.dma("sp", y_d[g, t * 128:(t + 1) * 128, :], o_[:], reads=[or_])
    return kb


def _consts():
    c = {}
    c["k_ident"] = np.eye(128, dtype=np.float32)
    bd = np.zeros((128, 128), np.float32); bd[:64, :64] = 1; bd[64:, 64:] = 1
    c["k_bd64"] = bd
    s = np.arange(128)[:, None]; t = np.arange(128)[None, :]
    c["k_tri"] = np.stack([(s <= t), (s > t), (s >= t), (s < t)]).astype(np.float32)
    hmk = np.zeros((128, 4), np.float32)
    for h in range(4):
        hmk[h * 32:(h + 1) * 32, h] = 1
    c["k_hm"] = hmk
    pmk = np.zeros((128, 2), np.float32)
    for p in range(128):
        pmk[p, (p // 32) % 2] = 1
    c["k_pm"] = pmk
    tok = np.arange(1024)
    row = (tok // 64).astype(np.float32); col = (tok % 64).astype(np.float32)

    def tab(half):
        inv = (10000.0 ** (-np.arange(half, dtype=np.float32) / half)).astype(np.float32)
        ar = row[:, None] * inv[None, :]; ac = col[:, None] * inv[None, :]
        return (np.concatenate([np.cos(ar), np.cos(ac)], 1).astype(np.float32), np.concatenate([np.sin(ar), np.sin(ac)], 1).astype(np.float32))
    c["k_cosA"], c["k_sinA"] = tab(16)
    cC, sC = tab(8)
    c["k_cosC"], c["k_sinC"] = cC, sC
    cr, cc_ = cC[:, 0:8], cC[:, 8:16]; sr_, sc_ = sC[:, 0:8], sC[:, 8:16]
    c["k_cosD"] = np.ascontiguousarray(np.concatenate([cr, cr, cc_, cc_], 1).T)
    c["k_sinD"] = np.ascontiguousarray(np.concatenate([-sr_, sr_, -sc_, sc_], 1).T)
    return c


WNAMES = ["w_mod", "b_mod", "norm1_g", "norm2_g", "w_in", "a_qnorm_g", "a_knorm_g", "b_gate_w_fwd", "b_gate_b_fwd", "b_gate_w_bwd",
          "b_gate_b_bwd", "b_onorm_g", "c_lq1", "c_lk1", "c_lq2", "c_lk2", "c_onorm_g", "d_qnorm_g", "d_w_uq", "d_kvnorm_g", "d_w_ukv",
          "w_branch", "w_out", "w_ffu", "w_ffg", "conv_w", "conv_b", "w_ffd", "final_g"]


def in_map(inp, c, consts, wts):
    m = dict(wts)
    m.update(consts)
    f = lambda a: np.ascontiguousarray(a, dtype=np.float32)
    m["x"] = f(np.stack([inp["x_prompt"][4 * c:4 * c + 4].reshape(1024, 1024), inp["x_sample"][c]]))
    m["cvec"] = f(np.stack([inp["c_ctx"], inp["c"][c]]))
    m["ca_k"] = f(inp["cache_a_k"][c].reshape(4, 512, 128)); m["ca_v"] = f(inp["cache_a_v"][c].reshape(4, 512, 128))
    m["sb_f"] = f(inp["state_b_fwd"][c].reshape(4, 128, 64)); m["sb_b"] = f(inp["state_b_bwd"][c].reshape(4, 128, 64))
    m["cc_k"] = f(inp["cache_c_k"][c].reshape(4, 512, 256)); m["cc_v"] = f(inp["cache_c_v"][c].reshape(4, 512, 256))
    m["cd_ckv"] = f(inp["cache_d_ckv"][c]); m["cd_kr"] = f(inp["cache_d_krope"][c])
    return m


def assemble(R):
    y_p = np.concatenate([r["y"][0].reshape(4, 256, 1024) for r in R], 0)
    y_s = np.stack([r["y"][1] for r in R], 0)

    def cat(name, shp):
        return np.concatenate([r[name].reshape((4,) + shp) for r in R], 0)
    return (y_p, y_s, cat("nak", (4, 256, 2, 64)), cat("nav", (4, 256, 2, 64)), cat("nbf", (4, 4, 32, 64)), cat("nbb", (4, 4, 32, 64)),
            cat("nck", (4, 256, 4, 2, 32)), cat("ncv", (4, 256, 4, 64)), cat("nckv", (4, 256, 128)), cat("nkr", (4, 256, 32)))


def kernel(**inp):
    inp = {k: np.asarray(v) for k, v in inp.items()}
    kb = build()
    nc = kb.finish()
    consts = _consts()
    wts = {k: np.ascontiguousarray(inp[k], dtype=np.float32) for k in WNAMES}
    in_maps = [in_map(inp, c, consts, wts) for c in range(8)]
    res = run_bass_kernel_spmd(nc, in_maps, core_ids=list(range(8)))
    return assemble(res.results)
```

```python
from contextlib import ExitStack
import numpy as np
import concourse.bass as bass
import concourse.mybir as mybir

F32 = mybir.dt.float32
BF16 = mybir.dt.bfloat16
AF = mybir.ActivationFunctionType
ALU = mybir.AluOpType
AX = mybir.AxisListType

EPOCH = 30000
NDMA_SEM = 20


class Res:
    __slots__ = ("name", "w", "r", "excl")

    def __init__(self, name="", excl=False):
        self.name = name
        self.excl = excl
        self.w = []
        self.r = []


class EngState:
    def __init__(self, name, handle_name, is_compute):
        self.name = name
        self.handle_name = handle_name
        self.is_compute = is_compute
        self.prog = []
        self.sem = None
        self.cnt = 0
        self.known = {}
        self.dma_ring = []
        self.dma_i = 0
        self.nops = 0


class KB:
    def __init__(self):
        self.nc = bass.Bass("TRN2", target_bir_lowering=False)
        self.es = ExitStack()
        self.E = {
            "pe": EngState("pe", "tensor", True),
            "act": EngState("act", "scalar", True),
            "dve": EngState("dve", "vector", True),
            "pool": EngState("pool", "gpsimd", True),
            "sp": EngState("sp", "sync", False),
        }
        self.nsem = 0
        self.all_dma_events = []

    def new_sem(self, name):
        self.nsem += 1
        return self.es.enter_context(self.nc.semaphore(f"{name}_{self.nsem}"))

    def sbuf(self, name, shape, dtype=F32):
        return self.es.enter_context(self.nc.sbuf_tensor(name, list(shape), dtype))

    def psum(self, name, shape, dtype=F32):
        return self.es.enter_context(self.nc.psum_tensor(name, list(shape), dtype))

    def dram(self, name, shape, dtype, kind):
        return self.nc.dram_tensor(name, list(shape), dtype, kind=kind)

    def _wait(self, st, ev):
        sem, val, _ = ev
        k = id(sem)
        if st.known.get(k, 0) >= val:
            return
        st.known[k] = val
        st.prog.append(("wait", sem, val))

    def _deps(self, st, reads, writes, is_dma):
        for r in reads:
            for ev in r.w:
                self._wait(st, ev)
            if r.excl:
                for ev in r.r:
                    if ev[2] != st.name:
                        self._wait(st, ev)
        for w in writes:
            for ev in w.w:
                if is_dma or ev[2] != st.name or not st.is_compute:
                    self._wait(st, ev)
            for ev in w.r:
                if is_dma or ev[2] != st.name or not st.is_compute:
                    self._wait(st, ev)

    def _commit(self, ev, reads, writes):
        for r in reads:
            if r in writes:
                continue
            if ev[2] in ("pe", "act", "dve", "pool"):
                r.r = [e for e in r.r if e[2] != ev[2]]
            r.r.append(ev)
        for w in writes:
            w.w = [ev]
            w.r = []

    @staticmethod
    def _flat(xs):
        out = []
        for x in xs:
            if isinstance(x, (list, tuple)):
                out.extend(KB._flat(x))
            else:
                out.append(x)
        return out

    def op(self, eng, fn, reads=(), writes=()):
        reads = self._flat(reads); writes = self._flat(writes)
        st = self.E[eng]
        assert st.is_compute
        self._deps(st, reads, writes, False)
        if st.sem is None or st.cnt >= EPOCH:
            st.sem = self.new_sem(f"s_{eng}")
            st.cnt = 0
        st.cnt += 1
        ev = (st.sem, st.cnt, eng)
        st.prog.append(("op", fn, st.sem))
        st.nops += 1
        self._commit(ev, reads, writes)
        return ev

    def dma(self, q, out, in_, reads=(), writes=(), **kw):
        st = self.E[q]
        reads = self._flat(reads); writes = self._flat(writes)
        kw.setdefault("allow_slow_non_contiguous", True)
        for r in reads:
            for ev in r.w:
                self._wait(st, ev)
        for w in writes:
            for ev in w.w:
                if ev[2].startswith("dma") and not w.r:
                    continue
                self._wait(st, ev)
            for ev in w.r:
                self._wait(st, ev)
        if not st.dma_ring:
            st.dma_ring = [[self.new_sem(f"d_{q}"), 0] for _ in range(NDMA_SEM)]
        slot = st.dma_ring[st.dma_i % NDMA_SEM]
        st.dma_i += 1
        if slot[1] > 0:
            self._wait(st, (slot[0], slot[1], "dma"))
        slot[1] += 16
        ev = (slot[0], slot[1], "dma_" + q)
        st.prog.append(("dma", out, in_, slot[0], kw))
        keep = {id(w): list(w.w) for w in writes if w.w and not w.r and all(e[2].startswith("dma") for e in w.w)}
        self._commit(ev, reads, writes)
        for w in writes:
            if id(w) in keep:
                w.w = keep[id(w)] + [ev]
        self.all_dma_events.append(ev)
        return ev

    def finish(self):
        nc = self.nc
        sp = self.E["sp"]
        for st in self.E.values():
            for sem, val in st.dma_ring:
                if val > 0:
                    self._wait(sp, (sem, val, "dma"))
        with nc.Block() as block:
            for st in self.E.values():
                if not st.prog:
                    continue

                def body(eng, st=st):
                    for item in st.prog:
                        if item[0] == "wait":
                            eng.wait_ge(item[1], item[2])
                        elif item[0] == "op":
                            ins = item[1](eng)
                            ins.then_inc(item[2], 1)
                        else:
                            _, out, in_, sem, kw = item
                            eng.dma_start(out=out, in_=in_, **kw).then_inc(sem, 16)

                getattr(block, st.handle_name)(body)
        self.es.close()
        return nc

import math
from concourse.bass_utils import run_bass_kernel_spmd

L = 4
EPS = 1e-6
MUL, ADD, SUB = ALU.mult, ALU.add, ALU.subtract
VOFFS = [0, 64, 192, 256]
VDST = [(0, 64, 0, 64), (128, 256, 64, 192), (320, 384, 192, 256)]


def barrier(kb):
    comp = ["pe", "act", "dve", "pool"]
    for e in comp + ["sp"]:
        st = kb.E[e]
        for o in comp:
            so = kb.E[o]
            if o != e and so.sem is not None and so.cnt > 0:
                kb._wait(st, (so.sem, so.cnt, o))
        for q in kb.E.values():
            for sem, val in q.dma_ring:
                if val > 0:
                    kb._wait(st, (sem, val, "dma"))


def build(NLAYERS=L, GROUPS=(0, 1), STOP='', DEBUG=False):
    kb = KB()
    nc = kb.nc
    BCUT = int(STOP.split(':')[1]) if ':' in STOP else 0

    def din(name, shape):
        return kb.dram(name, shape, F32, "ExternalInput").ap()

    def dout(name, shape):
        return kb.dram(name, shape, F32, "ExternalOutput").ap()

    x_d = din("x", [2, 1024, 1024])
    cvec_d = din("cvec", [2, 1024])
    cak_d = din("ca_k", [L, 512, 128]); cav_d = din("ca_v", [L, 512, 128])
    sbf_d = din("sb_f", [L, 128, 64]); sbb_d = din("sb_b", [L, 128, 64])
    cck_d = din("cc_k", [L, 512, 256]); ccv_d = din("cc_v", [L, 512, 256])
    cdc_d = din("cd_ckv", [L, 512, 128]); cdr_d = din("cd_kr", [L, 512, 32])
    wmod_d = din("w_mod", [L, 1024, 6144]); bmod_d = din("b_mod", [L, 6144])
    n1_d = din("norm1_g", [L, 1024]); n2_d = din("norm2_g", [L, 1024])
    win_d = din("w_in", [L, 1024, 6528])
    aqg_d = din("a_qnorm_g", [L, 64]); akg_d = din("a_knorm_g", [L, 64])
    gwf_d = din("b_gate_w_fwd", [L, 16, 128]); gbf_d = din("b_gate_b_fwd", [L, 128])
    gwb_d = din("b_gate_w_bwd", [L, 16, 128]); gbb_d = din("b_gate_b_bwd", [L, 128])
    bon_d = din("b_onorm_g", [L, 64])
    lq1_d = din("c_lq1", [L, 32]); lk1_d = din("c_lk1", [L, 32]); lq2_d = din("c_lq2", [L, 32]); lk2_d = din("c_lk2", [L, 32])
    con_d = din("c_onorm_g", [L, 64])
    dqg_d = din("d_qnorm_g", [L, 192]); wuq_d = din("d_w_uq", [L, 192, 384])
    dkg_d = din("d_kvnorm_g", [L, 128]); wukv_d = din("d_w_ukv", [L, 128, 512])
    wbr_d = din("w_branch", [L, 4, 256, 1024]); wout_d = din("w_out", [L, 1024, 1024])
    wfu_d = din("w_ffu", [L, 1024, 2816]); wfg_d = din("w_ffg", [L, 1024, 2816])
    cw_d = din("conv_w", [L, 3, 2816]); cb_d = din("conv_b", [L, 2816]); wfd_d = din("w_ffd", [L, 2816, 1024])
    fg_d = din("final_g", [1024])
    k_ident = din("k_ident", [128, 128]); k_bd64 = din("k_bd64", [128, 128]); k_tri = din("k_tri", [4, 128, 128])
    k_hm = din("k_hm", [128, 4]); k_pm = din("k_pm", [128, 2])
    k_cosA = din("k_cosA", [1024, 32]); k_sinA = din("k_sinA", [1024, 32])
    k_cosC = din("k_cosC", [1024, 16]); k_sinC = din("k_sinC", [1024, 16])
    k_cosD = din("k_cosD", [32, 1024]); k_sinD = din("k_sinD", [32, 1024])

    y_d = dout("y", [2, 1024, 1024])
    nak_d = dout("nak", [4, L, 256, 128]); nav_d = dout("nav", [4, L, 256, 128])
    nbf_d = dout("nbf", [4, L, 128, 64]); nbb_d = dout("nbb", [4, L, 128, 64])
    nck_d = dout("nck", [4, L, 256, 256]); ncv_d = dout("ncv", [4, L, 256, 256])
    nckv_d = dout("nckv", [4, L, 256, 128]); nkr_d = dout("nkr", [4, L, 256, 32])

    if DEBUG:
        dbg_br = dout('dbg_br', [128, 8, 1024]); dbg_x1 = dout('dbg_x1', [128, 8, 1024]); dbg_x2 = dout('dbg_x2', [128, 8, 1024]); dbg_mg = dout('dbg_mg', [128, 8, 1024])
    xT = kb.sbuf("xT", [128, 8, 1024]); xT_R = [[Res(f"xT{k}_{h}") for h in range(2)] for k in range(8)]
    hT = kb.sbuf("hT", [128, 8, 1024], BF16); hT_R = [Res("hT0"), Res("hT1")]
    NSLOT = 3
    ring = [kb.sbuf(f"ring{i}", [128, 8192], BF16) for i in range(NSLOT)]
    ring_R = [Res(f"ring{i}") for i in range(NSLOT)]
    ring_i = [0]
    arena = kb.sbuf("arena", [128, 30720], BF16)
    PS = kb.psum("ps", [128, 4096])
    pb = [PS[:, i * 512:(i + 1) * 512] for i in range(8)]
    pb_R = [Res(f"pb{i}", excl=True) for i in range(8)]
    PSB = PS[:, 7 * 512:8 * 512].bitcast(BF16)

    def new_slot(avoid=None):
        i = ring_i[0] % NSLOT
        ring_i[0] += 1
        while avoid is not None and ring[i] is avoid:
            i = ring_i[0] % NSLOT
            ring_i[0] += 1
        return ring[i], ring_R[i]

    def av(off, n):
        return arena[:, off:off + n]

    def vap(off, stride):
        return bass.AP(arena, off, [[30720, 128], [stride, 2], [1, 64]])

    identf = kb.sbuf("identf", [128, 128]); identb = kb.sbuf("identb", [128, 128], BF16)
    onesb = kb.sbuf("onesb", [128, 128], BF16); bd64 = kb.sbuf("bd64", [128, 128], BF16)
    tri = kb.sbuf("tri", [128, 4, 128]); hm = kb.sbuf("hm", [128, 4]); pm = kb.sbuf("pm", [128, 2])
    onescol = kb.sbuf("onescol", [128, 1]); onesrow = kb.sbuf("onesrow", [1, 128]); epsT = kb.sbuf("epsT", [128, 1])
    onesf = kb.sbuf("onesf", [128, 128])
    cosA = kb.sbuf("cosA", [128, 8, 32]); sinA = kb.sbuf("sinA", [128, 8, 32])
    cosC = kb.sbuf("cosC", [128, 8, 16]); sinC = kb.sbuf("sinC", [128, 8, 16])
    cosD = kb.sbuf("cosD", [128, 1024]); sinD = kb.sbuf("sinD", [128, 1024])
    CR = Res("consts")
    kb.dma("sp", identf[:], k_ident, writes=[CR])
    kb.dma("pool", identb[:], k_ident, writes=[CR])
    kb.dma("pool", bd64[:], k_bd64, writes=[CR])
    kb.dma("sp", tri[:], k_tri.rearrange("m s t -> s m t"), writes=[CR])
    kb.dma("sp", hm[:], k_hm, writes=[CR]); kb.dma("sp", pm[:], k_pm, writes=[CR])
    kb.dma("sp", cosA[:], k_cosA.rearrange("(t p) d -> p t d", p=128), writes=[CR])
    kb.dma("sp", sinA[:], k_sinA.rearrange("(t p) d -> p t d", p=128), writes=[CR])
    kb.dma("sp", cosC[:], k_cosC.rearrange("(t p) d -> p t d", p=128), writes=[CR])
    kb.dma("sp", sinC[:], k_sinC.rearrange("(t p) d -> p t d", p=128), writes=[CR])
    kb.dma("sp", cosD[64:96, :], k_cosD, writes=[CR]); kb.dma("sp", sinD[64:96, :], k_sinD, writes=[CR])
    kb.op("dve", lambda e: e.memset(onesb[:], 1.0), writes=[CR])
    kb.op("dve", lambda e: e.memset(onescol[:], 1.0), writes=[CR])
    kb.op("dve", lambda e: e.memset(onesrow[:], 1.0), writes=[CR])
    kb.op("dve", lambda e: e.memset(onesf[:], 1.0), writes=[CR])
    kb.op("dve", lambda e: e.memset(epsT[:], EPS), writes=[CR])

    def mm(out, lhsT, rhs, start, stop, reads, writes):
        kb.op("pe", lambda e: e.matmul(out, lhsT=lhsT, rhs=rhs, start=start, stop=stop), reads=reads, writes=writes)

    def tr(out, in_, ident, reads, writes):
        kb.op("pe", lambda e: e.transpose(out=out, in_=in_, identity=ident), reads=reads + [CR], writes=writes)

    def act(out, in_, func, reads, writes, scale=1.0, bias=None, accum_out=None):
        kw = {}
        if bias is not None:
            kw["bias"] = bias
        if accum_out is not None:
            kw["accum_out"] = accum_out
        kb.op("act", lambda e: e.activation(out=out, in_=in_, func=func, scale=scale, **kw), reads=reads, writes=writes)

    def tt(out, in0, in1, op, reads, writes, eng="dve"):
        kb.op(eng, lambda e: e.tensor_tensor(out=out, in0=in0, in1=in1, op=op), reads=reads, writes=writes)

    def stt(out, in0, scalar, in1, op0, op1, reads, writes):
        kb.op("dve", lambda e: e.scalar_tensor_tensor(out=out, in0=in0, scalar=scalar, in1=in1, op0=op0, op1=op1), reads=reads, writes=writes)

    def ts(out, in0, s1, op0, reads, writes, s2=None, op1=None, eng="dve"):
        if op1 is None:
            kb.op(eng, lambda e: e.tensor_scalar(out=out, in0=in0, scalar1=s1, scalar2=None, op0=op0), reads=reads, writes=writes)
        else:
            kb.op(eng, lambda e: e.tensor_scalar(out=out, in0=in0, scalar1=s1, scalar2=s2, op0=op0, op1=op1), reads=reads, writes=writes)

    def cp(out, in_, reads, writes, eng="dve"):
        if eng == "act":
            kb.op("act", lambda e: e.copy(out=out, in_=in_), reads=reads, writes=writes)
        else:
            kb.op(eng, lambda e: e.tensor_copy(out=out, in_=in_), reads=reads, writes=writes)

    def recip(out, in_, reads, writes):
        kb.op("dve", lambda e: e.reciprocal(out=out, in_=in_), reads=reads, writes=writes)

    def bc(ap, shape):
        return ap.broadcast_to(list(shape))

    pbi = [0]

    def nb():
        i = pbi[0] % 7
        pbi[0] += 1
        return i

    scT = kb.sbuf("scT", [128, 8, 2]); modT = kb.sbuf("modT", [128, L, 48, 2]); bmT = kb.sbuf("bmT", [128, L, 48])
    n1T = kb.sbuf("n1T", [128, L, 8]); n2T = kb.sbuf("n2T", [128, L, 8])
    A1 = kb.sbuf("A1", [128, L, 8, 2]); A2 = kb.sbuf("A2", [128, L, 8, 2])
    MR = Res("mod")
    with nc.allow_non_contiguous_dma(reason="tiny transposed vector loads"):
        for g_ in range(2):
            kb.dma("sp", scT[:, :, g_], cvec_d[g_].rearrange("(k p) -> p k", p=128), writes=[MR])
        for l_ in range(L):
            kb.dma("sp", bmT[:, l_, :], bmod_d[l_].rearrange("(j p) -> p j", p=128), writes=[MR])
            kb.dma("sp", n1T[:, l_, :], n1_d[l_].rearrange("(k p) -> p k", p=128), writes=[MR])
            kb.dma("sp", n2T[:, l_, :], n2_d[l_].rearrange("(k p) -> p k", p=128), writes=[MR])
    act(scT[:], scT[:], AF.Silu, [MR], [MR])
    scb = kb.sbuf("scb", [128, 8, 2], BF16)
    cp(scb[:], scT[:], [MR], [MR])
    for l in range(L):
        for cb in range(8):
            slot, wr = new_slot()
            w = slot[:, 0:6144].rearrange("p (k c) -> p k c", k=8)
            kb.dma("pool", w, wmod_d[l].rearrange("(k p) c -> p k c", p=128)[:, :, cb * 768:(cb + 1) * 768], writes=[wr])
            b = nb()
            for jj in range(6):
                for kc in range(8):
                    mm(pb[b][:, jj * 2:(jj + 1) * 2], w[:, kc, jj * 128:(jj + 1) * 128], scb[:, kc, :], kc == 0, kc == 7, [wr, MR], [pb_R[b]])
            tt(modT[:, l, cb * 6:(cb + 1) * 6, :], pb[b][:, 0:12].rearrange("p (j g) -> p j g", g=2),
               bc(bmT[:, l, cb * 6:(cb + 1) * 6].unsqueeze(2), [128, 6, 2]), ADD, [pb_R[b], MR], [MR])
    for l in range(L):
        stt(A1[:, l], modT[:, l, 8:16, :], 1.0, bc(n1T[:, l, :].unsqueeze(2), [128, 8, 2]), ADD, MUL, [MR], [MR])
        stt(A2[:, l], modT[:, l, 32:40, :], 1.0, bc(n2T[:, l, :].unsqueeze(2), [128, 8, 2]), ADD, MUL, [MR], [MR])

    lqk = kb.sbuf("lqk", [32, 4, L]); lamT = kb.sbuf("lamT", [128, L]); neglam = kb.sbuf("neglam", [128, L])
    ocs = kb.sbuf("ocs", [128, L]); lam2 = kb.sbuf("lam2", [128, 2, L])
    with nc.allow_non_contiguous_dma(reason="tiny transposed vector loads"):
        for i, d in enumerate([lq1_d, lk1_d, lq2_d, lk2_d]):
            kb.dma("sp", lqk[:, i, :], d.rearrange("l d -> d l"), writes=[MR])
        kb.dma("sp", ocs[0:64, :], con_d.rearrange("l d -> d l"), writes=[MR])
        kb.dma("sp", ocs[64:128, :], con_d.rearrange("l d -> d l"), writes=[MR])
    tt(lqk[:, 0, :], lqk[:, 0, :], lqk[:, 1, :], MUL, [MR], [MR])
    tt(lqk[:, 2, :], lqk[:, 2, :], lqk[:, 3, :], MUL, [MR], [MR])
    b = nb()
    mm(pb[b][:, 0:L], onesf[0:32, :], lqk[:, 0, :], True, True, [MR, CR], [pb_R[b]])
    mm(pb[b][:, L:2 * L], onesf[0:32, :], lqk[:, 2, :], True, True, [MR, CR], [pb_R[b]])
    act(lam2[:].rearrange("p a l -> p (a l)"), pb[b][:, 0:2 * L], AF.Exp, [pb_R[b]], [MR])
    tt(lamT[:], lam2[:, 0, :], lam2[:, 1, :], SUB, [MR], [MR])
    lam_init = [0.8 - 0.6 * math.exp(-0.3 * l) for l in range(L)]
    for l in range(L):
        ts(lamT[:, l:l + 1], lamT[:, l:l + 1], lam_init[l], ADD, [MR], [MR])
        ts(ocs[:, l:l + 1], ocs[:, l:l + 1], 1.0 - lam_init[l], MUL, [MR], [MR])
    ts(neglam[:], lamT[:], -1.0, MUL, [MR], [MR])

    gA = kb.sbuf("gA", [128, 384]); gDQ = kb.sbuf("gDQ", [128, 192]); gKV = kb.sbuf("gKV", [128, 128]); gBO = kb.sbuf("gBO", [128, 256])
    GW = kb.sbuf("GW", [16, 256]); GBias = kb.sbuf("GBias", [1, 256]); cwT = kb.sbuf("cwT", [128, 3, 22]); cbT = kb.sbuf("cbT", [128, 22])
    fgB = arena[:, 0:2048].bitcast(F32); fgB_R = Res("fgB")
    LP = Res("layer_params")

    def load_layer_params(l):
        def bsrc(d, n, rep):
            return bass.AP(d.tensor, d[l].offset, [[0, 128], [0, rep], [1, n]])
        kb.dma("sp", gA[:, 0:256].rearrange("p (r d) -> p r d", r=4), bsrc(aqg_d, 64, 4), writes=[LP])
        kb.dma("sp", gA[:, 256:384].rearrange("p (r d) -> p r d", r=2), bsrc(akg_d, 64, 2), writes=[LP])
        kb.dma("sp", gDQ[:], dqg_d[l].partition_broadcast(128), writes=[LP])
        kb.dma("sp", gKV[:], dkg_d[l].partition_broadcast(128), writes=[LP])
        kb.dma("sp", gBO[:].rearrange("p (r d) -> p r d", r=4), bsrc(bon_d, 64, 4), writes=[LP])
        kb.dma("sp", GW[0:16, 0:128], gwf_d[l], writes=[LP]); kb.dma("sp", GW[0:16, 128:256], gwb_d[l], writes=[LP])
        kb.dma("sp", GBias[0:1, 0:128], gbf_d[l:l + 1, :], writes=[LP]); kb.dma("sp", GBias[0:1, 128:256], gbb_d[l:l + 1, :], writes=[LP])
        with nc.allow_non_contiguous_dma(reason="tiny transposed vector loads"):
            for w_ in range(3):
                kb.dma("sp", cwT[:, w_, :], cw_d[l, w_].rearrange("(f p) -> p f", p=128), writes=[LP])
            kb.dma("sp", cbT[:], cb_d[l].rearrange("(f p) -> p f", p=128), writes=[LP])

    stg = [kb.sbuf(f"stg{i}", [128, 1024]) for i in range(2)]; stg_R = [Res("stg0"), Res("stg1")]
    stg_i = [0]

    def nstg():
        i = stg_i[0] % 2
        stg_i[0] += 1
        return stg[i], stg_R[i]

    tA = kb.sbuf("tA", [128, 1024]); tA_R = (Res("tA0"), Res("tA1"))
    tB = kb.sbuf("tB", [128, 1024]); tB_R = (Res("tB0"), Res("tB1"))
    tC = kb.sbuf("tC", [128, 512]); tC_R = Res("tC")
    tC1 = kb.sbuf("tC1", [128, 512]); tC1_R = Res("tC1")
    sm = kb.sbuf("sm", [128, 64]); sm_R = Res("sm")
    sm1 = kb.sbuf("sm1", [128, 64]); sm1_R = Res("sm1")
    rstd = tC1; rstd_R = tC1_R
    sqb = arena[:, 0:4096].rearrange("p (k n) -> p k n", k=8); sqb_R = Res("sqb")
    MG_R = [Res("MG0"), Res("MG1")]
    PT = [kb.sbuf(f"PT{i}", [128, 512], BF16) for i in range(3)]; PT_R = [Res(f"PT{i}") for i in range(3)]
    pt_i = [0]
    qkb = kb.sbuf("qkb", [128, 512], BF16); qkb_R = Res("qkb")
    qkb1 = kb.sbuf("qkb1", [128, 512], BF16); qkb1_R = Res("qkb1")
    krpad = kb.sbuf("krpad", [128, 96], BF16); krpad_R = Res("krpad")
    krpad1 = kb.sbuf("krpad1", [128, 96], BF16); krpad1_R = Res("krpad1")
    kb.op("dve", lambda e: e.memset(krpad[:], 0.0), writes=[krpad_R])
    kb.op("dve", lambda e: e.memset(krpad1[:], 0.0), writes=[krpad1_R])
    Sst = [kb.sbuf(f"Sst{i}", [128, 64]) for i in range(2)]; Sst_R = [Res("Sf"), Res("Sb")]
    GL = kb.sbuf("GL", [128, 8, 2]); GL_R = Res("GL")
    zgT = tC[0:16, 256:512]; zgT_R = Res("zgT")
    zgT1 = tC1[0:16, 256:512]; zgT1_R = Res("zgT1")
    TSET = [dict(tA_=tA[:, 0:512], tA_R_=tA_R[0], tB_=tB[:, 0:512], tB_R_=tB_R[0], tC_=tC, tC_R_=tC_R, sm_=sm, sm_R_=sm_R, qkb_=qkb, qkb_R_=qkb_R,
                 krpad_=krpad, krpad_R_=krpad_R, zgT_=zgT, zgT_R_=zgT_R),
            dict(tA_=tA[:, 512:1024], tA_R_=tA_R[1], tB_=tB[:, 512:1024], tB_R_=tB_R[1], tC_=tC1, tC_R_=tC1_R, sm_=sm1, sm_R_=sm1_R, qkb_=qkb1, qkb_R_=qkb1_R,
                 krpad_=krpad1, krpad_R_=krpad1_R, zgT_=zgT1, zgT_R_=zgT1_R)]

    def norm_mod(Asc, shift_c0, l, g):
        for h in range(2):
            hs = slice(h * 512, (h + 1) * 512)
            xr = [xT_R[k][h] for k in range(8)]
            act(sqb, xT[:, :, hs], AF.Square, xr, [sqb_R, MG_R[0], MG_R[1]])
            b = nb()
            for kc in range(8):
                mm(pb[b], onesb[:], sqb[:, kc, :], kc == 0, kc == 7, [sqb_R, CR], [pb_R[b]])
            act(rstd[:], pb[b], AF.Sqrt, [pb_R[b], CR], [rstd_R], scale=1.0 / 1024, bias=epsT[:])
            recip(rstd[:], rstd[:], [rstd_R], [rstd_R])
            for kc in range(8):
                tmp, tr_ = (tA, tA_R) if kc % 2 == 0 else (tB, tB_R)
                tt(tmp[:, 0:512], xT[:, kc, hs], rstd[:], MUL, [xT_R[kc][h], rstd_R], [tr_])
                act(hT[:, kc, hs], tmp[:, 0:512], AF.Identity, [tr_, MR], [hT_R[h]],
                    scale=Asc[:, l, kc, g:g + 1], bias=modT[:, l, shift_c0 + kc, g:g + 1])

    def load_w(dst, src, sr):
        kb.dma("pool", dst, src, writes=[sr])

    def proj_tm(bank0, ncols, W, wr, t):
        nbk = (ncols + 511) // 512
        for bi in range(nbk):
            c0 = bi * 512
            c1 = min(ncols, c0 + 512)
            for kc in range(8):
                mm(pb[bank0 + bi][:, 0:c1 - c0], hT[:, kc, t * 128:(t + 1) * 128], W[:, kc, c0:c1], kc == 0, kc == 7,
                   [hT_R[t // 4], wr], [pb_R[bank0 + bi]])

    def rope_tm(src, src_R, U2, hs, cos_t, sin_t, out, tA, tA_R, tB, tB_R, qkb_R):
        xv = src.rearrange("p (u r h i) -> p u r h i", u=U2, r=2, h=2)
        ov = out.rearrange("p (u r h i) -> p u r h i", u=U2, r=2, h=2)
        cv = bc(cos_t.rearrange("p (r i) -> p r i", r=2).unsqueeze(1), [128, U2, 2, hs])
        sv = bc(sin_t.rearrange("p (r i) -> p r i", r=2).unsqueeze(1), [128, U2, 2, hs])
        n = U2 * 2 * hs
        a = tA[:, 0:n].rearrange("p (u r i) -> p u r i", u=U2, r=2)
        b2 = tB[:, 0:n].rearrange("p (u r i) -> p u r i", u=U2, r=2)
        x0 = xv[:, :, :, 0, :]; x1 = xv[:, :, :, 1, :]
        tt(a, x0, cv, MUL, [src_R, CR], [tA_R]); tt(b2, x1, sv, MUL, [src_R, CR], [tB_R])
        tt(ov[:, :, :, 0, :], a, b2, SUB, [tA_R, tB_R], [qkb_R])
        tt(a, x1, cv, MUL, [src_R, CR], [tA_R]); tt(b2, x0, sv, MUL, [src_R, CR], [tB_R])
        tt(ov[:, :, :, 1, :], a, b2, ADD, [tA_R, tB_R], [qkb_R])

    def attend(kT, qT, vT, kts, q0, N, scale, reads, ob, avoid=()):
        sb = [None] * len(kts)

        def S(i):
            b = nb()
            while b == ob or b in avoid:
                b = nb()
            sb[i] = b
            mm(pb[b][:, 0:N], kT(kts[i]), qT, True, True, reads, [pb_R[b]])
        S(0)
        for i in range(len(kts)):
            if i + 1 < len(kts):
                S(i + 1)
            p = pt_i[0] % 3
            pt_i[0] += 1
            act(PT[p][:, 0:N], pb[sb[i]][:, 0:N], AF.Exp, [pb_R[sb[i]]], [PT_R[p]], scale=scale)
            mm(pb[ob][:, 0:N], vT(kts[i]), PT[p][:, 0:N], i == 0, i == len(kts) - 1, reads + [PT_R[p]], [pb_R[ob]])

    def obank():
        b = nb()
        return b

    def attend_multi(jobs, finishers, LA=2):
        flat = [(ji, i) for ji, jb in enumerate(jobs) for i in range(len(jb["kts"]))]
        sbank = {}
        ob_of = {}
        live = []
        gobs = {}

        def alloc():
            b = nb()
            while b in live or b in sbank.values():
                b = nb()
            return b

        def S(idx):
            ji, i = flat[idx]
            jb = jobs[ji]
            b = alloc()
            sbank[idx] = b
            mm(pb[b][:, 0:jb["N"]], jb["kT"](jb["kts"][i]), jb["qT"], True, True, jb["reads"], [pb_R[b]])
        for idx in range(min(LA, len(flat))):
            S(idx)
        for idx, (ji, i) in enumerate(flat):
            if idx + LA < len(flat):
                S(idx + LA)
            jb = jobs[ji]
            N = jb["N"]
            if i == 0:
                ob_of[ji] = alloc()
                live.append(ob_of[ji])
            ob = ob_of[ji]
            p = pt_i[0] % 3
            pt_i[0] += 1
            sb_ = sbank.pop(idx)
            act(PT[p][:, 0:N], pb[sb_][:, 0:N], AF.Exp, [pb_R[sb_]], [PT_R[p]], scale=jb["scale"])
            last = (i == len(jb["kts"]) - 1)
            mm(pb[ob][:, 0:N], jb["vT"](jb["kts"][i]), PT[p][:, 0:N], i == 0, last, jb["reads"] + [PT_R[p]], [pb_R[ob]])
            if last:
                gid = jb["gid"]
                gobs.setdefault(gid, []).append(ob)
                if ji + 1 >= len(jobs) or jobs[ji + 1]["gid"] != gid:
                    finishers[gid](gobs[gid])
                    for o in gobs[gid]:
                        live.remove(o)

    for g in GROUPS:
        S_GRP = (g == 1)
        NKT = 12 if S_GRP else 8
        KOFF = 4 if S_GRP else 0
        NK = NKT * 128
        if S_GRP:
            seqs = [(0, 1024, list(range(12)))]
        else:
            seqs = [(s * 256, 256, [2 * s, 2 * s + 1]) for s in range(4)]
        for t in range(8):
            st_, sr = nstg()
            kb.dma("sp", st_[:], x_d[g, t * 128:(t + 1) * 128, :], writes=[sr])
            for hh in range(2):
                b = nb()
                for kk in range(4):
                    kc = hh * 4 + kk
                    tr(pb[b][:, kk * 128:(kk + 1) * 128], st_[:, kc * 128:(kc + 1) * 128], identf[:], [sr], [pb_R[b]])
                cp(xT[:, hh * 4:(hh + 1) * 4, t * 128:(t + 1) * 128], pb[b].rearrange("p (k c) -> p k c", k=4), [pb_R[b]],
                   [xT_R[k][t // 4] for k in range(hh * 4, hh * 4 + 4)], eng=("act" if hh else "dve"))

        for l in range(NLAYERS):
            load_layer_params(l)
            norm_mod(A1, 0, l, g)
            if STOP == 'N':
                continue
            BR_OFF = 22528
            BR = arena[:, BR_OFF:BR_OFF + 8192].rearrange("p (j b n) -> p j b n", j=4, b=2)
            BR_R = [Res(f"BR{j}") for j in range(4)]
            OUTS = not S_GRP

            slot, sr = new_slot()
            W = slot[:, 0:4096].rearrange("p (k c) -> p k c", k=8)
            wsrc = win_d[l].rearrange("(k p) c -> p k c", p=128)
            for g2_ in range(2):
                for kv_ in range(2):
                    load_w(W[:, :, g2_ * 128 + kv_ * 64:g2_ * 128 + kv_ * 64 + 64], wsrc[:, :, (kv_ * 2 + g2_) * 64:(kv_ * 2 + g2_) * 64 + 64], sr)
            load_w(W[:, :, 256:512], wsrc[:, :, 256:512], sr)
            barrier(kb)
            QT = av(0, 2048).rearrange("p (b n) -> p b n", b=2); QT_R = Res("QT_A")
            KT = av(2048, 1536); KT_R = Res("KT_A")
            VA = av(3584, 4608).rearrange("p (k c) -> p k c", c=384); VA_R = Res("VA")
            kb.op("dve", lambda e: e.memset(VA[:, 0:NKT, :], 1.0), writes=[VA_R])
            if S_GRP:
                st_, sr2 = nstg()
                kb.dma("sp", st_[:, 0:512].rearrange("p (k f) -> p k f", k=4), cak_d[l].rearrange("(k p) f -> p k f", p=128), writes=[sr2])
                b = nb()
                for kt in range(4):
                    tr(pb[b][:, kt * 128:(kt + 1) * 128], st_[:, kt * 128:(kt + 1) * 128], identf[:], [sr2], [pb_R[b]])
                cp(KT[:, 0:512], pb[b], [pb_R[b]], [KT_R])
                kb.dma("pool", VA[:, 0:4, 64:128], cav_d[l].rearrange("(k p) f -> p k f", p=128)[:, :, 0:64], writes=[VA_R])
                kb.dma("pool", VA[:, 0:4, 256:320], cav_d[l].rearrange("(k p) f -> p k f", p=128)[:, :, 64:128], writes=[VA_R])
            if STOP == 'A0':
                continue
            for t in range(8):
                TS_ = TSET[t % 2]; tA_ = TS_['tA_']; tA_R_ = TS_['tA_R_']; tB_ = TS_['tB_']; tB_R_ = TS_['tB_R_']; tC_ = TS_['tC_']; tC_R_ = TS_['tC_R_']; sm_ = TS_['sm_']; sm_R_ = TS_['sm_R_']; qkb_ = TS_['qkb_']; qkb_R_ = TS_['qkb_R_']; krpad_ = TS_['krpad_']; krpad_R_ = TS_['krpad_R_']; zgT_ = TS_['zgT_']; zgT_R_ = TS_['zgT_R_']
                zb = nb()
                if zb == 6:
                    zb = nb()
                z = pb[zb]; zr = pb_R[zb]
                for kc in range(8):
                    mm(z, hT[:, kc, t * 128:(t + 1) * 128], W[:, kc, :], kc == 0, kc == 7, [hT_R[t // 4], sr], [zr])
                if STOP == 'A2a':
                    continue
                act(tC_[:, 0:384], z[:, 0:384], AF.Square, [zr], [tC_R_])
                kb.op("dve", lambda e, sm_=sm_, tC_=tC_: e.reduce_sum(out=sm_[:, 0:6], in_=tC_[:, 0:384].rearrange("p (h d) -> p h d", h=6), axis=AX.X), reads=[tC_R_], writes=[sm_R_])
                if STOP == 'A2b':
                    continue
                act(sm_[:, 0:6], sm_[:, 0:6], AF.Sqrt, [sm_R_, CR], [sm_R_], scale=1.0 / 64, bias=epsT[:])
                recip(sm_[:, 0:6], sm_[:, 0:6], [sm_R_], [sm_R_])
                if STOP == 'A2c':
                    continue
                tt(tC_[:, 0:384].rearrange("p (h d) -> p h d", h=6), z[:, 0:384].rearrange("p (h d) -> p h d", h=6),
                   bc(sm_[:, 0:6].unsqueeze(2), [128, 6, 64]), MUL, [zr, sm_R_], [tC_R_])
                tt(tC_[:, 0:384], tC_[:, 0:384], gA[:], MUL, [tC_R_, LP], [tC_R_])
                kt = KOFF + t
                if STOP == 'A2':
                    continue
                cp(VA[:, kt, 64:128], z[:, 384:448], [zr], [VA_R], eng="act")
                cp(VA[:, kt, 256:320], z[:, 448:512], [zr], [VA_R], eng="act")
                if S_GRP:
                    rope_tm(tC_[:, 0:384], tC_R_, 6, 16, cosA[:, t, :], sinA[:, t, :], qkb_[:, 0:384], tA_, tA_R_, tB_, tB_R_, qkb_R_)
                else:
                    cp(qkb_[:, 0:384], tC_[:, 0:384], [tC_R_], [qkb_R_])
                    o_, or_ = nstg()
                    cp(o_[:, 0:128], tC_[:, 256:384], [tC_R_], [or_], eng="act")
                    cp(o_[:, 128:256], z[:, 384:512], [zr], [or_], eng="act")
                    s_, tt_ = t // 2, (t % 2) * 128
                    kb.dma("sp", nak_d[s_, l, tt_:tt_ + 128, :], o_[:, 0:128], reads=[or_])
                    kb.dma("sp", nav_d[s_, l, tt_:tt_ + 128, :], o_[:, 128:256], reads=[or_])
                if STOP == 'A3':
                    continue
                for g2 in range(2):
                    tr(PSB[:, g2 * 128:(g2 + 1) * 128], qkb_[:, g2 * 128:(g2 + 1) * 128], identb[:], [qkb_R_], [pb_R[7]])
                tr(PSB[:, 256:384], qkb_[:, 256:384], identb[:], [qkb_R_], [pb_R[7]])
                if STOP == 'A4':
                    continue
                if STOP != 'A6':
                    cp(QT[:, :, t * 128:(t + 1) * 128], PSB[:, 0:256].rearrange("p (b n) -> p b n", b=2), [pb_R[7]], [QT_R])
                if STOP != 'A5':
                    cp(KT[:, kt * 128:(kt + 1) * 128], PSB[:, 256:384], [pb_R[7], QT_R], [KT_R], eng="act")
            if STOP in ('A1', 'A2', 'A2a', 'A2b', 'A2c', 'A3', 'A4', 'A5', 'A6'):
                continue
            jobs = []; fins = {}
            for (q0, qlen, kts) in seqs:
                for qc in range(0, qlen, 512):
                    N = min(512, qlen - qc)
                    for kv in range(2):
                        for g2 in range(2):
                            rows = slice(kv * 64, (kv + 1) * 64)
                            orow = g2 * 64
                            srow = 64 - orow
                            if g2 == 0:
                                vfn = lambda kt, kv=kv: VA[:, kt, kv * 192 + 64:kv * 192 + 192]
                            else:
                                vfn = lambda kt, kv=kv: VA[:, kt, kv * 192:kv * 192 + 128]
                            gid = len(jobs)

                            def fin(obs, orow=orow, srow=srow, N=N, kv=kv, c0=q0 + qc):
                                ob = obs[0]
                                recip(tA[srow:srow + 64, 0:N], pb[ob][srow:srow + 64, 0:N], [pb_R[ob]], [tA_R])
                                tt(BR[orow:orow + 64, 0, kv, c0:c0 + N], pb[ob][orow:orow + 64, 0:N], tA[srow:srow + 64, 0:N], MUL,
                                   [pb_R[ob], tA_R], [BR_R[0]])
                            fins[gid] = fin
                            jobs.append(dict(kT=(lambda kt, rows=rows: KT[rows, kt * 128:(kt + 1) * 128]), qT=QT[rows, g2, q0 + qc:q0 + qc + N], vT=vfn,
                                             kts=kts, N=N, scale=0.125, reads=[KT_R, QT_R, VA_R], gid=gid))
            attend_multi(jobs, fins)
            if STOP == 'A':
                continue
            slot, sr = new_slot()
            W = slot[:, 0:6144].rearrange("p (k c) -> p k c", k=8)
            load_w(W, win_d[l].rearrange("(k p) c -> p k c", p=128)[:, :, 1312:2080], sr)
            barrier(kb)
            QT = av(0, 2048).rearrange("p (b n) -> p b n", b=2); QT_R = Res("QT_C")
            KC = av(2048, 6144).rearrange("p (v b n) -> p v b n", v=2, b=2); KC_R = Res("KT_C")
            VC_OFF = 8192
            VC = av(VC_OFF, 4608).rearrange("p (k c) -> p k c", c=384); VC_R = Res("VC")
            kb.op("dve", lambda e: e.memset(VC[:, 0:NKT, :], 1.0), writes=[VC_R])

            def kc_store(src, kcols):
                for v in range(2):
                    for bb in range(2):
                        ts(KC[:, v, bb, kcols], src[:, bb, :], pm[:, v:v + 1], MUL, [pb_R[7], pb_R[6], CR], [KC_R])
            if S_GRP:
                for bb in range(2):
                    st_, sr2 = nstg()
                    kb.dma("sp", st_[:, 0:512].rearrange("p (k f) -> p k f", k=4), cck_d[l].rearrange("(k p) f -> p k f", p=128)[:, :, bb * 128:(bb + 1) * 128], writes=[sr2])
                    for kt in range(4):
                        tr(pb[6][:, kt * 128:(kt + 1) * 128], st_[:, kt * 128:(kt + 1) * 128], identf[:], [sr2], [pb_R[6]])
                    for v in range(2):
                        ts(KC[:, v, bb, 0:512], pb[6], pm[:, v:v + 1], MUL, [pb_R[6], CR], [KC_R])
                for (d0, d1, s0, s1) in VDST:
                    kb.dma("pool", VC[:, 0:4, d0:d1], ccv_d[l].rearrange("(k p) f -> p k f", p=128)[:, :, s0:s1], writes=[VC_R])
            for t in range(8):
                TS_ = TSET[t % 2]; tA_ = TS_['tA_']; tA_R_ = TS_['tA_R_']; tB_ = TS_['tB_']; tB_R_ = TS_['tB_R_']; tC_ = TS_['tC_']; tC_R_ = TS_['tC_R_']; sm_ = TS_['sm_']; sm_R_ = TS_['sm_R_']; qkb_ = TS_['qkb_']; qkb_R_ = TS_['qkb_R_']; krpad_ = TS_['krpad_']; krpad_R_ = TS_['krpad_R_']; zgT_ = TS_['zgT_']; zgT_R_ = TS_['zgT_R_']
                zb = nb()
                while zb >= 5:
                    zb = nb()
                zr = [pb_R[zb], pb_R[zb + 1]]
                z = PS[:, zb * 512:zb * 512 + 768]
                for bi in range(2):
                    c0, c1 = bi * 512, min(768, bi * 512 + 512)
                    for kc in range(8):
                        mm(PS[:, (zb + bi) * 512:(zb + bi) * 512 + c1 - c0], hT[:, kc, t * 128:(t + 1) * 128], W[:, kc, c0:c1], kc == 0, kc == 7,
                           [hT_R[t // 4], sr], [pb_R[zb + bi]])
                kt = KOFF + t
                for (d0, d1, s0, s1) in VDST:
                    cp(VC[:, kt, d0:d1], z[:, 512 + s0:512 + s1], zr, [VC_R], eng="act")
                if S_GRP:
                    rope_tm(z[:, 0:512], zr[0], 16, 8, cosC[:, t, :], sinC[:, t, :], qkb_[:, 0:512], tA_, tA_R_, tB_, tB_R_, qkb_R_)
                else:
                    cp(qkb_[:, 0:512], z[:, 0:512], zr, [qkb_R_])
                    o_, or_ = nstg()
                    cp(o_[:, 0:512], z[:, 256:768], zr, [or_], eng="act")
                    s_, tt_ = t // 2, (t % 2) * 128
                    kb.dma("sp", nck_d[s_, l, tt_:tt_ + 128, :], o_[:, 0:256], reads=[or_])
                    kb.dma("sp", ncv_d[s_, l, tt_:tt_ + 128, :], o_[:, 256:512], reads=[or_])
                for i in range(4):
                    tr(PSB[:, i * 128:(i + 1) * 128], qkb_[:, i * 128:(i + 1) * 128], identb[:], [qkb_R_], [pb_R[7]])
                cp(QT[:, :, t * 128:(t + 1) * 128], PSB[:, 0:256].rearrange("p (b n) -> p b n", b=2), [pb_R[7]], [QT_R], eng="act")
                kc_store(PSB[:, 256:512].rearrange("p (b n) -> p b n", b=2), slice(kt * 128, (kt + 1) * 128))
            OCR = tC; OCR_R = tC_R
            jobs = []; fins = {}
            for (q0, qlen, kts) in seqs:
                for qc in range(0, qlen, 512):
                    N = min(512, qlen - qc)
                    for bb in range(2):
                        for hf in range(2):
                            h = bb * 2 + hf
                            rows = slice(hf * 64, (hf + 1) * 64)
                            orow = hf * 64
                            srow = 64 - orow
                            vfn = lambda kt, h=h: VC[:, kt, VOFFS[h]:VOFFS[h] + 128]
                            gid = len(jobs)

                            def fin(obs, orow=orow, srow=srow, N=N, bb=bb, hf=hf, c0=q0 + qc):
                                o1, o2 = obs
                                recip(tA[srow:srow + 64, 0:N], pb[o1][srow:srow + 64, 0:N], [pb_R[o1]], [tA_R])
                                recip(tB[srow:srow + 64, 0:N], pb[o2][srow:srow + 64, 0:N], [pb_R[o2]], [tB_R])
                                tt(tA[orow:orow + 64, 512:512 + N], pb[o1][orow:orow + 64, 0:N], tA[srow:srow + 64, 0:N], MUL, [pb_R[o1], tA_R], [tA_R])
                                tt(tB[orow:orow + 64, 512:512 + N], pb[o2][orow:orow + 64, 0:N], tB[srow:srow + 64, 0:N], MUL, [pb_R[o2], tB_R], [tB_R])
                                stt(OCR[orow:orow + 64, 0:N], tB[orow:orow + 64, 512:512 + N], neglam[orow:orow + 64, l:l + 1], tA[orow:orow + 64, 512:512 + N], MUL, ADD,
                                    [tA_R, tB_R, MR], [OCR_R])
                                if hf == 1:
                                    act(PT[0][:, 0:N], OCR[:, 0:N], AF.Square, [OCR_R], [PT_R[0]])
                                    b = nb()
                                    while b in obs:
                                        b = nb()
                                    mm(pb[b][:, 0:N], bd64[:], PT[0][:, 0:N], True, True, [PT_R[0], CR], [pb_R[b]])
                                    act(rstd[:, 0:N], pb[b][:, 0:N], AF.Sqrt, [pb_R[b], CR], [rstd_R], scale=1.0 / 64, bias=epsT[:])
                                    recip(rstd[:, 0:N], rstd[:, 0:N], [rstd_R], [rstd_R])
                                    tt(OCR[:, 0:N], OCR[:, 0:N], rstd[:, 0:N], MUL, [OCR_R, rstd_R], [OCR_R])
                                    act(BR[:, 2, bb, c0:c0 + N], OCR[:, 0:N], AF.Copy, [OCR_R, MR], [BR_R[2]], scale=ocs[:, l:l + 1])
                            fins[gid] = fin
                            for j in range(2):
                                jobs.append(dict(kT=(lambda kt, rows=rows, j=j, bb=bb: KC[rows, j, bb, kt * 128:(kt + 1) * 128]), qT=QT[rows, bb, q0 + qc:q0 + qc + N],
                                                 vT=vfn, kts=kts, N=N, scale=32 ** -0.5, reads=[KC_R, QT_R, VC_R], gid=gid))
            attend_multi(jobs, fins)

            if STOP == 'C':
                continue
            slot, sr = new_slot()
            W = slot[:, 0:2816].rearrange("p (k c) -> p k c", k=8)
            load_w(W, win_d[l].rearrange("(k p) c -> p k c", p=128)[:, :, 2080:2432], sr)
            UQ = slot[:, 2816:3584].rearrange("p (k c) -> p k c", k=2)
            UQS = slot[:, 3584:4352].rearrange("p (k c) -> p k c", k=2)
            UKV = slot[:, 4352:4864]
            load_w(UQ[:, 0, :], wuq_d[l, 0:128, :], sr); load_w(UQ[0:64, 1, :], wuq_d[l, 128:192, :], sr)
            load_w(UKV, wukv_d[l], sr)
            if S_GRP:
                with nc.allow_non_contiguous_dma(reason="rope column swap"):
                    for (kc_, r0, r1, pr) in ((0, 0, 128, 128), (1, 128, 192, 64)):
                        src = wuq_d[l, r0:r1, :].rearrange("p (h c) -> p h c", h=4)[:, :, 64:96].rearrange("p h (r f i) -> p h r f i", r=2, f=2)
                        dst = UQS[0:pr, kc_, :].rearrange("p (h c) -> p h c", h=4)[:, :, 64:96].rearrange("p h (r f i) -> p h r f i", r=2, f=2)
                        for f in range(2):
                            for r_ in range(2):
                                load_w(dst[:, :, r_, f, :], src[:, :, r_, 1 - f, :], sr)
            barrier(kb)
            DQN = av(0, 2048).rearrange("p (b n) -> p b n", b=2); DQN_R = Res("DQN")
            CKT = av(2048, 1536); CKT_R = Res("CKT")
            KD_ = av(3584, 6144).rearrange("p (h n) -> p h n", h=4); KD_R = Res("KT_D")
            DQT = av(9728, 4096).rearrange("p (h n) -> p h n", h=4); DQT_R = Res("DQT")
            VD_OFF = 13824
            VD = av(VD_OFF, 4608).rearrange("p (k c) -> p k c", c=384); VD_R = Res("VD")
            kb.op("dve", lambda e: e.memset(VD[:, 0:NKT, :], 1.0), writes=[VD_R])
            if S_GRP:
                st_, sr2 = nstg()
                kb.dma("sp", st_[:, 0:512].rearrange("p (k f) -> p k f", k=4), cdc_d[l].rearrange("(k p) f -> p k f", p=128), writes=[sr2])
                for kt in range(4):
                    tr(pb[6][:, kt * 128:(kt + 1) * 128], st_[:, kt * 128:(kt + 1) * 128], identf[:], [sr2], [pb_R[6]])
                cp(CKT[:, 0:512], pb[6], [pb_R[6]], [CKT_R])
                stgkr = st_[:, 512:896].rearrange("p (k f) -> p k f", k=4)
                kb.op("dve", lambda e, stgkr=stgkr: e.memset(stgkr[:, :, 0:64], 0.0), writes=[sr2])
                kb.dma("sp", stgkr[:, :, 64:96], cdr_d[l].rearrange("(k p) f -> p k f", p=128), writes=[sr2])
                b = nb()
                for kt in range(4):
                    mm(pb[b][0:96, kt * 128:(kt + 1) * 128], stgkr[:, kt, :], identf[:], True, True, [sr2, CR], [pb_R[b]])
                cp(KD_[64:96, :, 0:512], bc(pb[b][64:96, :].unsqueeze(1), [32, 4, 512]), [pb_R[b]], [KD_R])
            for t in range(8):
                TS_ = TSET[t % 2]; tA_ = TS_['tA_']; tA_R_ = TS_['tA_R_']; tB_ = TS_['tB_']; tB_R_ = TS_['tB_R_']; tC_ = TS_['tC_']; tC_R_ = TS_['tC_R_']; sm_ = TS_['sm_']; sm_R_ = TS_['sm_R_']; qkb_ = TS_['qkb_']; qkb_R_ = TS_['qkb_R_']; krpad_ = TS_['krpad_']; krpad_R_ = TS_['krpad_R_']; zgT_ = TS_['zgT_']; zgT_R_ = TS_['zgT_R_']
                zb = nb()
                z = pb[zb]; zr = pb_R[zb]
                for kc in range(8):
                    mm(z[:, 0:352], hT[:, kc, t * 128:(t + 1) * 128], W[:, kc, :], kc == 0, kc == 7, [hT_R[t // 4], sr], [zr])
                act(tA_[:, 0:192], z[:, 0:192], AF.Square, [zr], [tA_R_], accum_out=sm_[:, 8:9])
                act(tA_[:, 192:320], z[:, 192:320], AF.Square, [zr], [tA_R_], accum_out=sm_[:, 9:10])
                act(sm_[:, 8:9], sm_[:, 8:9], AF.Sqrt, [tA_R_, CR], [sm_R_], scale=1.0 / 192, bias=epsT[:])
                act(sm_[:, 9:10], sm_[:, 9:10], AF.Sqrt, [tA_R_, CR], [sm_R_], scale=1.0 / 128, bias=epsT[:])
                recip(sm_[:, 8:10], sm_[:, 8:10], [sm_R_], [sm_R_])
                stt(qkb_[:, 0:192], z[:, 0:192], sm_[:, 8:9], gDQ[:], MUL, MUL, [zr, sm_R_, LP], [qkb_R_])
                stt(tB_[:, 0:128], z[:, 192:320], sm_[:, 9:10], gKV[:], MUL, MUL, [zr, sm_R_, LP], [tB_R_])
                cp(qkb_[:, 192:320], tB_[:, 0:128], [tB_R_], [qkb_R_])
                kt = KOFF + t
                if S_GRP:
                    xv = z[:, 320:352].rearrange("p (r h i) -> p r h i", r=2, h=2)
                    ov = krpad_[:, 64:96].rearrange("p (r h i) -> p r h i", r=2, h=2)
                    cv = cosC[:, t, :].rearrange("p (r i) -> p r i", r=2); sv = sinC[:, t, :].rearrange("p (r i) -> p r i", r=2)
                    a = tC_[:, 0:16].rearrange("p (r i) -> p r i", r=2); b2 = tC_[:, 16:32].rearrange("p (r i) -> p r i", r=2)
                    tt(a, xv[:, :, 0, :], cv, MUL, [zr, CR], [tC_R_]); tt(b2, xv[:, :, 1, :], sv, MUL, [zr, CR], [tC_R_])
                    tt(ov[:, :, 0, :], a, b2, SUB, [tC_R_], [krpad_R_])
                    tt(a, xv[:, :, 1, :], cv, MUL, [zr, CR], [tC_R_]); tt(b2, xv[:, :, 0, :], sv, MUL, [zr, CR], [tC_R_])
                    tt(ov[:, :, 1, :], a, b2, ADD, [tC_R_], [krpad_R_])
                else:
                    cp(krpad_[:, 64:96], z[:, 320:352], [zr], [krpad_R_])
                    o_, or_ = nstg()
                    cp(o_[:, 0:128], tB_[:, 0:128], [tB_R_], [or_], eng="act")
                    cp(o_[:, 128:160], z[:, 320:352], [zr], [or_], eng="act")
                    s_, tt_ = t // 2, (t % 2) * 128
                    kb.dma("sp", nckv_d[s_, l, tt_:tt_ + 128, :], o_[:, 0:128], reads=[or_])
                    kb.dma("sp", nkr_d[s_, l, tt_:tt_ + 128, :], o_[:, 128:160], reads=[or_])
                tr(PSB[:, 0:128], qkb_[:, 0:128], identb[:], [qkb_R_], [pb_R[7]])
                tr(PSB[0:64, 128:256], qkb_[:, 128:192], identb[:], [qkb_R_], [pb_R[7]])
                tr(PSB[:, 256:384], qkb_[:, 192:320], identb[:], [qkb_R_], [pb_R[7]])
                cp(DQN[:, 0, t * 128:(t + 1) * 128], PSB[:, 0:128], [pb_R[7]], [DQN_R])
                cp(DQN[0:64, 1, t * 128:(t + 1) * 128], PSB[0:64, 128:256], [pb_R[7]], [DQN_R])
                cp(CKT[:, kt * 128:(kt + 1) * 128], PSB[:, 256:384], [pb_R[7]], [CKT_R], eng="act")
                b = nb()
                mm(pb[b][0:96, 0:128], krpad_[:], identb[:], True, True, [krpad_R_, CR], [pb_R[b]])
                cp(KD_[64:96, :, kt * 128:(kt + 1) * 128], bc(pb[b][64:96, 0:128].unsqueeze(1), [32, 4, 128]), [pb_R[b]], [KD_R])
            for h in range(4):
                for c0 in range(0, NK, 512):
                    b = nb()
                    mm(pb[b][0:64, :], UKV[:, h * 128:h * 128 + 64], CKT[:, c0:c0 + 512], True, True, [sr, CKT_R], [pb_R[b]])
                    cp(KD_[0:64, h, c0:c0 + 512], pb[b][0:64, :], [pb_R[b]], [KD_R], eng=("act" if h % 2 else "dve"))
            for kt in range(NKT):
                b = nb()
                mm(pb[b][:, 0:256].rearrange("p (h e) -> p h e", h=4), CKT[:, kt * 128:(kt + 1) * 128], UKV.rearrange("p (h c) -> p h c", h=4)[:, :, 64:128], True, True,
                   [sr, CKT_R], [pb_R[b]])
                for (d0, d1, s0, s1) in VDST:
                    cp(VD[:, kt, d0:d1], pb[b][:, s0:s1], [pb_R[b]], [VD_R], eng=("act" if kt % 2 else "dve"))
            for h in range(4):
                for qc in range(0, 1024, 512):
                    b = nb()
                    mm(pb[b][0:96, :], UQ[:, 0, h * 96:(h + 1) * 96], DQN[:, 0, qc:qc + 512], True, False, [sr, DQN_R], [pb_R[b]])
                    mm(pb[b][0:96, :], UQ[0:64, 1, h * 96:(h + 1) * 96], DQN[0:64, 1, qc:qc + 512], False, True, [sr, DQN_R], [pb_R[b]])
                    if S_GRP:
                        b2_ = nb()
                        mm(pb[b2_][0:96, :], UQS[:, 0, h * 96:(h + 1) * 96], DQN[:, 0, qc:qc + 512], True, False, [sr, DQN_R], [pb_R[b2_]])
                        mm(pb[b2_][0:96, :], UQS[0:64, 1, h * 96:(h + 1) * 96], DQN[0:64, 1, qc:qc + 512], False, True, [sr, DQN_R], [pb_R[b2_]])
                        cp(DQT[0:64, h, qc:qc + 512], pb[b][0:64, :], [pb_R[b]], [DQT_R], eng="act")
                        tt(tA[64:96, 0:512], pb[b][64:96, :], cosD[64:96, qc:qc + 512], MUL, [pb_R[b], CR], [tA_R])
                        tt(tB[64:96, 0:512], pb[b2_][64:96, :], sinD[64:96, qc:qc + 512], MUL, [pb_R[b2_], CR], [tB_R])
                        tt(DQT[64:96, h, qc:qc + 512], tA[64:96, 0:512], tB[64:96, 0:512], ADD, [tA_R, tB_R], [DQT_R])
                    else:
                        cp(DQT[0:96, h, qc:qc + 512], pb[b][0:96, :], [pb_R[b]], [DQT_R], eng=("act" if h % 2 else "dve"))
            jobs = []; fins = {}
            for (q0, qlen, kts) in seqs:
                for qc in range(0, qlen, 512):
                    N = min(512, qlen - qc)
                    for h in range(4):
                        bb, hf = h // 2, h % 2
                        orow = hf * 64
                        srow = 64 - orow
                        vfn = lambda kt, h=h: VD[:, kt, VOFFS[h]:VOFFS[h] + 128]
                        gid = len(jobs)

                        def fin(obs, orow=orow, srow=srow, N=N, bb=bb, c0=q0 + qc):
                            ob = obs[0]
                            recip(tA[srow:srow + 64, 0:N], pb[ob][srow:srow + 64, 0:N], [pb_R[ob]], [tA_R])
                            tt(BR[orow:orow + 64, 3, bb, c0:c0 + N], pb[ob][orow:orow + 64, 0:N], tA[srow:srow + 64, 0:N], MUL,
                               [pb_R[ob], tA_R], [BR_R[3]])
                        fins[gid] = fin
                        jobs.append(dict(kT=(lambda kt, h=h: KD_[0:96, h, kt * 128:(kt + 1) * 128]), qT=DQT[0:96, h, q0 + qc:q0 + qc + N], vT=vfn,
                                         kts=kts, N=N, scale=96 ** -0.5, reads=[KD_R, DQT_R, VD_R], gid=gid))
            attend_multi(jobs, fins)
            if STOP == 'D':
                continue
            slot, sr = new_slot()
            W = slot[:, 0:6144].rearrange("p (k c) -> p k c", k=8)
            G = slot[:, 6144:6656].rearrange("p (k c) -> p k c", k=8)
            load_w(W, win_d[l].rearrange("(k p) c -> p k c", p=128)[:, :, 512:1280], sr)
            load_w(G[:, :, 0:16], win_d[l].rearrange("(k p) c -> p k c", p=128)[:, :, 1280:1296], sr)
            load_w(G[:, :, 32:48], win_d[l].rearrange("(k p) c -> p k c", p=128)[:, :, 1296:1312], sr)
            barrier(kb)
            BT = av(0, 12288).rearrange("p (t d c) -> p t d c", t=8, d=2); BT_R = Res("BT")
            KDc = av(12288, 2048).rearrange("p (t d c) -> p t d c", t=8, d=2); KDc_R = Res("KDc")
            VB = av(14336, 2048).rearrange("p (t c) -> p t c", t=8); VB_R = Res("VB")
            GR = av(16384, 2048).rearrange("p (t c) -> p t c", t=8); GR_R = Res("GR")
            SIN = av(18432, 4096).rearrange("p (t d c) -> p t d c", t=8, d=2); SIN_R = Res("SIN")
            Lsp = tC; Lsp_R = tC_R
            for t in range(8):
                TS_ = TSET[t % 2]; tA_ = TS_['tA_']; tA_R_ = TS_['tA_R_']; tB_ = TS_['tB_']; tB_R_ = TS_['tB_R_']; tC_ = TS_['tC_']; tC_R_ = TS_['tC_R_']; sm_ = TS_['sm_']; sm_R_ = TS_['sm_R_']; qkb_ = TS_['qkb_']; qkb_R_ = TS_['qkb_R_']; krpad_ = TS_['krpad_']; krpad_R_ = TS_['krpad_R_']; zgT_ = TS_['zgT_']; zgT_R_ = TS_['zgT_R_']
                zb = nb()
                while zb >= 4:
                    zb = nb()
                zr = [pb_R[zb], pb_R[zb + 1]]
                z = PS[:, zb * 512:zb * 512 + 768]
                for bi in range(2):
                    c0, c1 = bi * 512, min(768, bi * 512 + 512)
                    for kc in range(8):
                        mm(PS[:, (zb + bi) * 512:(zb + bi) * 512 + c1 - c0], hT[:, kc, t * 128:(t + 1) * 128], W[:, kc, c0:c1], kc == 0, kc == 7,
                           [hT_R[t // 4], sr], [pb_R[zb + bi]])
                if BCUT == 1:
                    continue
                for d in range(2):
                    for kc in range(8):
                        mm(pb[5][0:16, d * 128:(d + 1) * 128], G[:, kc, d * 32:d * 32 + 16], hT[:, kc, t * 128:(t + 1) * 128], kc == 0, kc == 7, [hT_R[t // 4], sr], [pb_R[5]])
                cp(zgT_, pb[5][0:16, 0:256], [pb_R[5]], [zgT_R_])
                if BCUT == 2:
                    continue
                for d in range(2):
                    mm(pb[5][:, 256 + d * 128:384 + d * 128], zgT_[0:16, d * 128:(d + 1) * 128], GW[0:16, d * 128:(d + 1) * 128], True, False, [zgT_R_, LP], [pb_R[5]])
                    mm(pb[5][:, 256 + d * 128:384 + d * 128], onesrow[0:1, :], GBias[0:1, d * 128:(d + 1) * 128], False, True, [CR, LP], [pb_R[5]])
                if BCUT == 3:
                    continue
                act(tA_[:, 0:256], pb[5][:, 256:512], AF.Exp, [pb_R[5]], [tA_R_], scale=-1.0)
                act(tC_[:, 0:256], tA_[:, 0:256], AF.Ln, [tA_R_, CR], [tC_R_], bias=onescol[:])
                if BCUT == 4:
                    continue
                for i, (m_, d) in enumerate(((0, 0), (1, 0), (2, 1), (3, 1))):
                    mm(pb[6][:, i * 128:(i + 1) * 128], tri[:, m_, :], tC_[:, d * 128:(d + 1) * 128], True, True, [tC_R_, CR], [pb_R[6]])
                if BCUT == 5:
                    continue
                for d in range(2):
                    mm(pb[5][:, 2 * d:2 + 2 * d], tC_[:, d * 128:(d + 1) * 128], onesf[:, 0:2], True, True, [tC_R_, CR], [pb_R[5]])
                act(GL[:, t, :], pb[5][:, 0:4].rearrange("p (d two) -> p d two", two=2)[:, :, 0], AF.Exp, [pb_R[5]], [GL_R], scale=-1.0 / 16)
                if BCUT == 6:
                    continue
                Ea = tA_; Eb = tB_
                act(Ea[:, 0:512], pb[6], AF.Exp, [pb_R[6]], [tA_R_], scale=-1.0 / 16)
                act(Eb[:, 0:256].rearrange("p (a c) -> p a c", a=2), pb[6].rearrange("p (a b c) -> p a b c", a=2, b=2)[:, :, 0, :], AF.Exp, [pb_R[6]], [tB_R_], scale=1.0 / 16)
                if BCUT == 7:
                    continue
                zq, zk = z[:, 0:128], z[:, 128:256]
                stt(qkb_[:, 0:128], zq, 32 ** -0.5, Ea[:, 0:128], MUL, MUL, zr + [tA_R_], [qkb_R_])
                stt(qkb_[:, 128:256], zq, 32 ** -0.5, Ea[:, 256:384], MUL, MUL, zr + [tA_R_], [qkb_R_])
                tt(qkb_[:, 256:384], zk, Eb[:, 0:128], MUL, zr + [tB_R_], [qkb_R_])
                tt(qkb_[:, 384:512], zk, Eb[:, 128:256], MUL, zr + [tB_R_], [qkb_R_])
                tt(KDc[:, t, 0, :], zk, Ea[:, 128:256], MUL, zr + [tA_R_], [KDc_R])
                tt(KDc[:, t, 1, :], zk, Ea[:, 384:512], MUL, zr + [tA_R_], [KDc_R])
                if BCUT == 8:
                    continue
                cp(VB[:, t, :], z[:, 256:512], zr, [VB_R], eng="act")
                act(GR[:, t, :], z[:, 512:768], AF.Silu, zr, [GR_R])
                if BCUT == 9:
                    continue
                for i in range(4):
                    tr(PSB[:, i * 128:(i + 1) * 128], qkb_[:, i * 128:(i + 1) * 128], identb[:], [qkb_R_], [pb_R[7]])
                if BCUT == 10:
                    continue
                for d in range(2):
                    cp(BT[:, t, d, 0:128], PSB[:, 256 + d * 128:384 + d * 128], [pb_R[7]], [BT_R], eng="act")
                    cp(BT[:, t, d, 128:256], PSB[:, d * 128:(d + 1) * 128], [pb_R[7]], [BT_R], eng="act")
                    tt(BT[:, t, d, 256:768].rearrange("p (h n) -> p h n", h=4), bc(PSB[:, d * 128:(d + 1) * 128].unsqueeze(1), [128, 4, 128]),
                       bc(hm[:].unsqueeze(2), [128, 4, 128]), MUL, [pb_R[7], CR], [BT_R])
            if STOP.startswith('B1'):
                continue
            hm3 = bc(hm[:].unsqueeze(2), [128, 4, 64])
            for si, (q0, qlen, kts) in enumerate(seqs):
                t0, nt = q0 // 128, qlen // 128
                for d in range(2):
                    Sf, Sr = Sst[d], Sst_R[d]
                    if S_GRP:
                        kb.dma("sp", Sf[:], (sbf_d if d == 0 else sbb_d)[l], writes=[Sr])
                    else:
                        kb.op("dve", lambda e, Sf=Sf: e.memset(Sf[:], 0.0), writes=[Sr])
                    order = range(t0, t0 + nt) if d == 0 else range(t0 + nt - 1, t0 - 1, -1)
                    for t in order:
                        tt(SIN[:, t, d, :].rearrange("p (h e) -> p h e", h=4), bc(Sf[:].unsqueeze(1), [128, 4, 64]), hm3, MUL, [Sr, CR], [SIN_R])
                        b = nb()
                        mm(pb[b][:, 0:256], KDc[:, t, d, :], VB[:, t, :], True, True, [KDc_R, VB_R], [pb_R[b]])
                        tt(tA[:, 0:256].rearrange("p (h e) -> p h e", h=4), pb[b][:, 0:256].rearrange("p (h e) -> p h e", h=4), hm3, MUL, [pb_R[b], CR], [tA_R])
                        kb.op("dve", lambda e: e.reduce_sum(out=tB[:, 0:64], in_=tA[:, 0:256].rearrange("p (h e) -> p e h", h=4), axis=AX.X), reads=[tA_R], writes=[tB_R])
                        stt(Sf[:], Sf[:], GL[:, t, d:d + 1], tB[:, 0:64], MUL, ADD, [Sr, GL_R, tB_R], [Sr])
                    if not S_GRP:
                        kb.dma("sp", (nbf_d if d == 0 else nbb_d)[si, l], Sf[:], reads=[Sr])
            if STOP == 'B2':
                continue
            for t in range(8):
                ob = nb()
                mm(pb[ob][:, 0:256], BT[:, t, 0, 128:256], SIN[:, t, 0, :], True, False, [BT_R, SIN_R], [pb_R[ob]])
                mm(pb[ob][:, 0:256], BT[:, t, 1, 128:256], SIN[:, t, 1, :], False, False, [BT_R, SIN_R], [pb_R[ob]])
                for d in range(2):
                    ab = nb()
                    while ab == ob:
                        ab = nb()
                    mm(pb[ab], BT[:, t, d, 0:128], BT[:, t, d, 256:768], True, True, [BT_R], [pb_R[ab]])
                    p = pt_i[0] % 3
                    pt_i[0] += 1
                    tt(PT[p][:].rearrange("p (h n) -> p h n", h=4), pb[ab].rearrange("p (h n) -> p h n", h=4),
                       bc(tri[:, (0 if d == 0 else 2), :].unsqueeze(1), [128, 4, 128]), MUL, [pb_R[ab], CR], [PT_R[p]])
                    for h in range(4):
                        mm(pb[ob][:, h * 64:(h + 1) * 64], PT[p][:, h * 128:(h + 1) * 128], VB[:, t, h * 64:(h + 1) * 64], False, (d == 1 and h == 3),
                           [PT_R[p], VB_R], [pb_R[ob]])
                o = pb[ob][:, 0:256]
                act(tA[:, 0:256], o, AF.Square, [pb_R[ob]], [tA_R])
                kb.op("dve", lambda e: e.reduce_sum(out=sm[:, 16:20], in_=tA[:, 0:256].rearrange("p (h d) -> p h d", h=4), axis=AX.X), reads=[tA_R], writes=[sm_R])
                act(sm[:, 16:20], sm[:, 16:20], AF.Sqrt, [sm_R, CR], [sm_R], scale=1.0 / 64, bias=epsT[:])
                recip(sm[:, 16:20], sm[:, 16:20], [sm_R], [sm_R])
                tt(tA[:, 0:256].rearrange("p (h d) -> p h d", h=4), o.rearrange("p (h d) -> p h d", h=4), bc(sm[:, 16:20].unsqueeze(2), [128, 4, 64]), MUL,
                   [pb_R[ob], sm_R], [tA_R])
                tt(tA[:, 0:256], tA[:, 0:256], gBO[:], MUL, [tA_R, LP], [tA_R])
                tt(qkb[:, 0:256], tA[:, 0:256], GR[:, t, :], MUL, [tA_R, GR_R], [qkb_R])
                for i in range(2):
                    tr(PSB[:, i * 128:(i + 1) * 128], qkb[:, i * 128:(i + 1) * 128], identb[:], [qkb_R], [pb_R[7]])
                cp(BR[:, 1, :, t * 128:(t + 1) * 128], PSB[:, 0:256].rearrange("p (b n) -> p b n", b=2), [pb_R[7]], [BR_R[1]])

            if STOP == 'B':
                continue
            bslot, bsr = new_slot()
            BW = bslot[:, 0:8192].rearrange("p (j k c) -> p j k c", j=4, k=2)
            for j in range(4):
                load_w(BW[:, j], wbr_d[l, j].rearrange("(k p) c -> p k c", p=128), bsr)
            barrier(kb)
            if DEBUG and l == 0 and g == GROUPS[0]:
                for jb in range(8):
                    o_, or_ = nstg()
                    cp(o_[:], BR[:, jb // 2, jb % 2, :], BR_R, [or_])
                    kb.dma("sp", dbg_br[:, jb, :], o_[:], reads=[or_])
            MG = av(0, 8192).rearrange("p (k n) -> p k n", k=8)
            for mp in range(4):
                slot, sr = new_slot(avoid=bslot)
                GWT = slot[:, 0:8192].rearrange("p (j k c) -> p j k c", j=4, k=8)
                for j in range(4):
                    c0 = 2432 + j * 1024 + mp * 256
                    load_w(GWT[:, j], win_d[l].rearrange("(k p) c -> p k c", p=128)[:, :, c0:c0 + 256], sr)
                for mi in range(2):
                    m = mp * 2 + mi
                    for h in range(2):
                        hs = slice(h * 512, (h + 1) * 512)
                        for j in range(4):
                            b1 = nb(); b2_ = nb()
                            for kc in range(8):
                                mm(pb[b1], GWT[:, j, kc, mi * 128:(mi + 1) * 128], hT[:, kc, hs], kc == 0, kc == 7, [sr, hT_R[h]], [pb_R[b1]])
                            for kc in range(2):
                                mm(pb[b2_], BW[:, j, kc, m * 128:(m + 1) * 128], BR[:, j, kc, hs], kc == 0, kc == 1, [bsr, BR_R[j]], [pb_R[b2_]])
                            act(tA[:, 0:512], pb[b1], AF.Sigmoid, [pb_R[b1]], [tA_R])
                            if j == 0:
                                tt(tC[:, 0:512], pb[b2_], tA[:, 0:512], MUL, [pb_R[b2_], tA_R], [tC_R])
                            else:
                                tt(tB[:, 0:512], pb[b2_], tA[:, 0:512], MUL, [pb_R[b2_], tA_R], [tB_R])
                                if j < 3:
                                    tt(tC[:, 0:512], tC[:, 0:512], tB[:, 0:512], ADD, [tC_R, tB_R], [tC_R])
                                else:
                                    tt(MG[:, m, hs], tC[:, 0:512], tB[:, 0:512], ADD, [tC_R, tB_R], [MG_R[h]])
            if STOP == 'M':
                continue
            slot, sr = new_slot()
            WO = slot[:, 0:8192].rearrange("p (k c) -> p k c", k=8)
            load_w(WO, wout_d[l].rearrange("(k p) c -> p k c", p=128), sr)
            for m in range(8):
                for h in range(2):
                    hs = slice(h * 512, (h + 1) * 512)
                    b = nb()
                    for kc in range(8):
                        mm(pb[b], WO[:, kc, m * 128:(m + 1) * 128], MG[:, kc, hs], kc == 0, kc == 7, [sr, MG_R[h]], [pb_R[b]])
                    stt(xT[:, m, hs], pb[b], modT[:, l, 16 + m, g:g + 1], xT[:, m, hs], MUL, ADD, [pb_R[b], MR, xT_R[m][h]], [xT_R[m][h]])
            if DEBUG and l == 0 and g == GROUPS[0]:
                for kc in range(8):
                    kb.dma("sp", dbg_x1[:, kc, :], xT[:, kc, :], reads=[xT_R[kc][0], xT_R[kc][1]])
                    o_, or_ = nstg()
                    cp(o_[:], MG[:, kc, :], MG_R, [or_])
                    kb.dma("sp", dbg_mg[:, kc, :], o_[:], reads=[or_])
            norm_mod(A2, 24, l, g)
            AT = av(8192, 22528).rearrange("p (f n) -> p f n", f=22); AT_R = [Res("AT0"), Res("AT1")]
            U_ = stg[0]; U_R = stg_R[0]; Cc = stg[1]; Cc_R = stg_R[1]
            nsq = 1 if S_GRP else 4
            sl = 1024 // nsq
            for f0 in range(0, 22, 4):
                nf = min(4, 22 - f0)
                slot, sr = new_slot()
                UW = slot[:, 0:4096].rearrange("p (k c) -> p k c", k=8)
                GW2 = slot[:, 4096:8192].rearrange("p (k c) -> p k c", k=8)
                load_w(UW[:, :, 0:nf * 128], wfu_d[l].rearrange("(k p) c -> p k c", p=128)[:, :, f0 * 128:(f0 + nf) * 128], sr)
                load_w(GW2[:, :, 0:nf * 128], wfg_d[l].rearrange("(k p) c -> p k c", p=128)[:, :, f0 * 128:(f0 + nf) * 128], sr)
                for fi in range(nf):
                    f = f0 + fi
                    gb_ = []
                    for h in range(2):
                        hs = slice(h * 512, (h + 1) * 512)
                        bu = nb(); bg = nb()
                        gb_.append(bg)
                        for kc in range(8):
                            mm(pb[bu], UW[:, kc, fi * 128:(fi + 1) * 128], hT[:, kc, hs], kc == 0, kc == 7, [sr, hT_R[h]], [pb_R[bu]])
                        for kc in range(8):
                            mm(pb[bg], GW2[:, kc, fi * 128:(fi + 1) * 128], hT[:, kc, hs], kc == 0, kc == 7, [sr, hT_R[h]], [pb_R[bg]])
                        cp(U_[:, hs], pb[bu], [pb_R[bu]], [U_R], eng="act")
                    act(Cc[:], U_[:], AF.Identity, [U_R, LP], [Cc_R], scale=cwT[:, 1, f:f + 1], bias=cbT[:, f:f + 1])
                    Uv = U_[:].rearrange("p (s n) -> p s n", s=nsq); Cv = Cc[:].rearrange("p (s n) -> p s n", s=nsq)
                    stt(Cv[:, :, 1:sl], Uv[:, :, 0:sl - 1], cwT[:, 0, f:f + 1], Cv[:, :, 1:sl], MUL, ADD, [U_R, Cc_R, LP], [Cc_R])
                    stt(Cv[:, :, 0:sl - 1], Uv[:, :, 1:sl], cwT[:, 2, f:f + 1], Cv[:, :, 0:sl - 1], MUL, ADD, [U_R, Cc_R, LP], [Cc_R])
                    act(tA[:], Cc[:], AF.Gelu_apprx_tanh, [Cc_R], [tA_R])
                    for h in range(2):
                        hs = slice(h * 512, (h + 1) * 512)
                        tt(AT[:, f, hs], tA[:, hs], pb[gb_[h]], MUL, [tA_R, pb_R[gb_[h]]], [AT_R[h]])
            for m0 in range(0, 8, 2):
                slot, sr = new_slot()
                WD = slot[:, 0:5632].rearrange("p (f c) -> p f c", f=22)
                load_w(WD, wfd_d[l].rearrange("(f p) c -> p f c", p=128)[:, :, m0 * 128:(m0 + 2) * 128], sr)
                for mi in range(2):
                    m = m0 + mi
                    for h in range(2):
                        hs = slice(h * 512, (h + 1) * 512)
                        b = nb()
                        for f in range(22):
                            mm(pb[b], WD[:, f, mi * 128:(mi + 1) * 128], AT[:, f, hs], f == 0, f == 21, [sr, AT_R[h]], [pb_R[b]])
                        stt(xT[:, m, hs], pb[b], modT[:, l, 40 + m, g:g + 1], xT[:, m, hs], MUL, ADD, [pb_R[b], MR, xT_R[m][h]], [xT_R[m][h]])

            if DEBUG and l == 0 and g == GROUPS[0]:
                for kc in range(8):
                    kb.dma("sp", dbg_x2[:, kc, :], xT[:, kc, :], reads=[xT_R[kc][0], xT_R[kc][1]])
        barrier(kb)
        kb.dma("sp", fgB, fg_d.partition_broadcast(128), writes=[fgB_R])
        for t in range(8):
            b0 = nb()
            while b0 >= 5:
                b0 = nb()
            for kc in range(8):
                bi = b0 + kc // 4
                tr(PS[:, bi * 512 + (kc % 4) * 128: bi * 512 + (kc % 4 + 1) * 128], xT[:, kc, t * 128:(t + 1) * 128], identf[:], [xT_R[kc][t // 4]], [pb_R[bi]])
            yp = PS[:, b0 * 512:b0 * 512 + 1024]
            yr = [pb_R[b0], pb_R[b0 + 1]]
            act(tA[:], yp, AF.Square, yr, [tA_R], accum_out=sm[:, 24:25])
            act(sm[:, 24:25], sm[:, 24:25], AF.Sqrt, [tA_R, CR], [sm_R], scale=1.0 / 1024, bias=epsT[:])
            recip(sm[:, 24:25], sm[:, 24:25], [sm_R], [sm_R])
            o_, or_ = nstg()
            stt(o_[:], yp, sm[:, 24:25], fgB, MUL, MUL, yr + [sm_R, fgB_R], [or_])
            kb.dma("sp", y_d[g, t * 128:(t + 1) * 128, :], o_[:], reads=[or_])
    return kb


def _consts():
    c = {}
    c["k_ident"] = np.eye(128, dtype=np.float32)
    bd = np.zeros((128, 128), np.float32); bd[:64, :64] = 1; bd[64:, 64:] = 1
    c["k_bd64"] = bd
    s = np.arange(128)[:, None]; t = np.arange(128)[None, :]
    c["k_tri"] = np.stack([(s <= t), (s > t), (s >= t), (s < t)]).astype(np.float32)
    hmk = np.zeros((128, 4), np.float32)
    for h in range(4):
        hmk[h * 32:(h + 1) * 32, h] = 1
    c["k_hm"] = hmk
    pmk = np.zeros((128, 2), np.float32)
    for p in range(128):
        pmk[p, (p // 32) % 2] = 1
    c["k_pm"] = pmk
    tok = np.arange(1024)
    row = (tok // 64).astype(np.float32); col = (tok % 64).astype(np.float32)

    def tab(half):
        inv = (10000.0 ** (-np.arange(half, dtype=np.float32) / half)).astype(np.float32)
        ar = row[:, None] * inv[None, :]; ac = col[:, None] * inv[None, :]
        return (np.concatenate([np.cos(ar), np.cos(ac)], 1).astype(np.float32), np.concatenate([np.sin(ar), np.sin(ac)], 1).astype(np.float32))
    c["k_cosA"], c["k_sinA"] = tab(16)
    cC, sC = tab(8)
    c["k_cosC"], c["k_sinC"] = cC, sC
    cr, cc_ = cC[:, 0:8], cC[:, 8:16]; sr_, sc_ = sC[:, 0:8], sC[:, 8:16]
    c["k_cosD"] = np.ascontiguousarray(np.concatenate([cr, cr, cc_, cc_], 1).T)
    c["k_sinD"] = np.ascontiguousarray(np.concatenate([-sr_, sr_, -sc_, sc_], 1).T)
    return c


WNAMES = ["w_mod", "b_mod", "norm1_g", "norm2_g", "w_in", "a_qnorm_g", "a_knorm_g", "b_gate_w_fwd", "b_gate_b_fwd", "b_gate_w_bwd",
          "b_gate_b_bwd", "b_onorm_g", "c_lq1", "c_lk1", "c_lq2", "c_lk2", "c_onorm_g", "d_qnorm_g", "d_w_uq", "d_kvnorm_g", "d_w_ukv",
          "w_branch", "w_out", "w_ffu", "w_ffg", "conv_w", "conv_b", "w_ffd", "final_g"]


def in_map(inp, c, consts, wts):
    m = dict(wts)
    m.update(consts)
    f = lambda a: np.ascontiguousarray(a, dtype=np.float32)
    m["x"] = f(np.stack([inp["x_prompt"][4 * c:4 * c + 4].reshape(1024, 1024), inp["x_sample"][c]]))
    m["cvec"] = f(np.stack([inp["c_ctx"], inp["c"][c]]))
    m["ca_k"] = f(inp["cache_a_k"][c].reshape(4, 512, 128)); m["ca_v"] = f(inp["cache_a_v"][c].reshape(4, 512, 128))
    m["sb_f"] = f(inp["state_b_fwd"][c].reshape(4, 128, 64)); m["sb_b"] = f(inp["state_b_bwd"][c].reshape(4, 128, 64))
    m["cc_k"] = f(inp["cache_c_k"][c].reshape(4, 512, 256)); m["cc_v"] = f(inp["cache_c_v"][c].reshape(4, 512, 256))
    m["cd_ckv"] = f(inp["cache_d_ckv"][c]); m["cd_kr"] = f(inp["cache_d_krope"][c])
    return m


def assemble(R):
    y_p = np.concatenate([r["y"][0].reshape(4, 256, 1024) for r in R], 0)
    y_s = np.stack([r["y"][1] for r in R], 0)

    def cat(name, shp):
        return np.concatenate([r[name].reshape((4,) + shp) for r in R], 0)
    return (y_p, y_s, cat("nak", (4, 256, 2, 64)), cat("nav", (4, 256, 2, 64)), cat("nbf", (4, 4, 32, 64)), cat("nbb", (4, 4, 32, 64)),
            cat("nck", (4, 256, 4, 2, 32)), cat("ncv", (4, 256, 4, 64)), cat("nckv", (4, 256, 128)), cat("nkr", (4, 256, 32)))


def kernel(**inp):
    inp = {k: np.asarray(v) for k, v in inp.items()}
    kb = build()
    nc = kb.finish()
    consts = _consts()
    wts = {k: np.ascontiguousarray(inp[k], dtype=np.float32) for k in WNAMES}
    in_maps = [in_map(inp, c, consts, wts) for c in range(8)]
    res = run_bass_kernel_spmd(nc, in_maps, core_ids=list(range(8)))
    return assemble(res.results)
```

```python
from contextlib import ExitStack
import numpy as np
import concourse.bass as bass
import concourse.mybir as mybir

F32 = mybir.dt.float32
BF16 = mybir.dt.bfloat16
AF = mybir.ActivationFunctionType
ALU = mybir.AluOpType
AX = mybir.AxisListType

EPOCH = 30000
NDMA_SEM = 20


class Res:
    __slots__ = ("name", "w", "r", "excl")

    def __init__(self, name="", excl=False):
        self.name = name
        self.excl = excl
        self.w = []
        self.r = []


class EngState:
    def __init__(self, name, handle_name, is_compute):
        self.name = name
        self.handle_name = handle_name
        self.is_compute = is_compute
        self.prog = []
        self.sem = None
        self.cnt = 0
        self.known = {}
        self.dma_ring = []
        self.dma_i = 0
        self.nops = 0


class KB:
    def __init__(self):
        self.nc = bass.Bass("TRN2", target_bir_lowering=False)
        self.es = ExitStack()
        self.E = {
            "pe": EngState("pe", "tensor", True),
            "act": EngState("act", "scalar", True),
            "dve": EngState("dve", "vector", True),
            "pool": EngState("pool", "gpsimd", True),
            "sp": EngState("sp", "sync", False),
        }
        self.nsem = 0
        self.all_dma_events = []

    def new_sem(self, name):
        self.nsem += 1
        return self.es.enter_context(self.nc.semaphore(f"{name}_{self.nsem}"))

    def sbuf(self, name, shape, dtype=F32):
        return self.es.enter_context(self.nc.sbuf_tensor(name, list(shape), dtype))

    def psum(self, name, shape, dtype=F32):
        return self.es.enter_context(self.nc.psum_tensor(name, list(shape), dtype))

    def dram(self, name, shape, dtype, kind):
        return self.nc.dram_tensor(name, list(shape), dtype, kind=kind)

    def _wait(self, st, ev):
        sem, val, _ = ev
        k = id(sem)
        if st.known.get(k, 0) >= val:
            return
        st.known[k] = val
        st.prog.append(("wait", sem, val))

    def _deps(self, st, reads, writes, is_dma):
        for r in reads:
            for ev in r.w:
                self._wait(st, ev)
            if r.excl:
                for ev in r.r:
                    if ev[2] != st.name:
                        self._wait(st, ev)
        for w in writes:
            for ev in w.w:
                if is_dma or ev[2] != st.name or not st.is_compute:
                    self._wait(st, ev)
            for ev in w.r:
                if is_dma or ev[2] != st.name or not st.is_compute:
                    self._wait(st, ev)

    def _commit(self, ev, reads, writes):
        for r in reads:
            if r in writes:
                continue
            if ev[2] in ("pe", "act", "dve", "pool"):
                r.r = [e for e in r.r if e[2] != ev[2]]
            r.r.append(ev)
        for w in writes:
            w.w = [ev]
            w.r = []

    @staticmethod
    def _flat(xs):
        out = []
        for x in xs:
            if isinstance(x, (list, tuple)):
                out.extend(KB._flat(x))
            else:
                out.append(x)
        return out

    def op(self, eng, fn, reads=(), writes=()):
        reads = self._flat(reads); writes = self._flat(writes)
        st = self.E[eng]
        assert st.is_compute
        self._deps(st, reads, writes, False)
        if st.sem is None or st.cnt >= EPOCH:
            st.sem = self.new_sem(f"s_{eng}")
            st.cnt = 0
        st.cnt += 1
        ev = (st.sem, st.cnt, eng)
        st.prog.append(("op", fn, st.sem))
        st.nops += 1
        self._commit(ev, reads, writes)
        return ev

    def dma(self, q, out, in_, reads=(), writes=(), **kw):
        st = self.E[q]
        reads = self._flat(reads); writes = self._flat(writes)
        kw.setdefault("allow_slow_non_contiguous", True)
        for r in reads:
            for ev in r.w:
                self._wait(st, ev)
        for w in writes:
            for ev in w.w:
                if ev[2].startswith("dma") and not w.r:
                    continue
                self._wait(st, ev)
            for ev in w.r:
                self._wait(st, ev)
        if not st.dma_ring:
            st.dma_ring = [[self.new_sem(f"d_{q}"), 0] for _ in range(NDMA_SEM)]
        slot = st.dma_ring[st.dma_i % NDMA_SEM]
        st.dma_i += 1
        if slot[1] > 0:
            self._wait(st, (slot[0], slot[1], "dma"))
        slot[1] += 16
        ev = (slot[0], slot[1], "dma_" + q)
        st.prog.append(("dma", out, in_, slot[0], kw))
        keep = {id(w): list(w.w) for w in writes if w.w and not w.r and all(e[2].startswith("dma") for e in w.w)}
        self._commit(ev, reads, writes)
        for w in writes:
            if id(w) in keep:
                w.w = keep[id(w)] + [ev]
        self.all_dma_events.append(ev)
        return ev

    def finish(self):
        nc = self.nc
        sp = self.E["sp"]
        for st in self.E.values():
            for sem, val in st.dma_ring:
                if val > 0:
                    self._wait(sp, (sem, val, "dma"))
        with nc.Block() as block:
            for st in self.E.values():
                if not st.prog:
                    continue

                def body(eng, st=st):
                    for item in st.prog:
                        if item[0] == "wait":
                            eng.wait_ge(item[1], item[2])
                        elif item[0] == "op":
                            ins = item[1](eng)
                            ins.then_inc(item[2], 1)
                        else:
                            _, out, in_, sem, kw = item
                            eng.dma_start(out=out, in_=in_, **kw).then_inc(sem, 16)

                getattr(block, st.handle_name)(body)
        self.es.close()
        return nc

import math
from concourse.bass_utils import run_bass_kernel_spmd

L = 4
EPS = 1e-6
MUL, ADD, SUB = ALU.mult, ALU.add, ALU.subtract
VOFFS = [0, 64, 192, 256]
VDST = [(0, 64, 0, 64), (128, 256, 64, 192), (320, 384, 192, 256)]


def barrier(kb):
    comp = ["pe", "act", "dve", "pool"]
    for e in comp + ["sp"]:
        st = kb.E[e]
        for o in comp:
            so = kb.E[o]
            if o != e and so.sem is not None and so.cnt > 0:
                kb._wait(st, (so.sem, so.cnt, o))
        for q in kb.E.values():
            for sem, val in q.dma_ring:
                if val > 0:
                    kb._wait(st, (sem, val, "dma"))


def build(NLAYERS=L, GROUPS=(0, 1), STOP='', DEBUG=False):
    kb = KB()
    nc = kb.nc
    BCUT = int(STOP.split(':')[1]) if ':' in STOP else 0

    def din(name, shape):
        return kb.dram(name, shape, F32, "ExternalInput").ap()

    def dout(name, shape):
        return kb.dram(name, shape, F32, "ExternalOutput").ap()

    x_d = din("x", [2, 1024, 1024])
    cvec_d = din("cvec", [2, 1024])
    cak_d = din("ca_k", [L, 512, 128]); cav_d = din("ca_v", [L, 512, 128])
    sbf_d = din("sb_f", [L, 128, 64]); sbb_d = din("sb_b", [L, 128, 64])
    cck_d = din("cc_k", [L, 512, 256]); ccv_d = din("cc_v", [L, 512, 256])
    cdc_d = din("cd_ckv", [L, 512, 128]); cdr_d = din("cd_kr", [L, 512, 32])
    wmod_d = din("w_mod", [L, 1024, 6144]); bmod_d = din("b_mod", [L, 6144])
    n1_d = din("norm1_g", [L, 1024]); n2_d = din("norm2_g", [L, 1024])
    win_d = din("w_in", [L, 1024, 6528])
    aqg_d = din("a_qnorm_g", [L, 64]); akg_d = din("a_knorm_g", [L, 64])
    gwf_d = din("b_gate_w_fwd", [L, 16, 128]); gbf_d = din("b_gate_b_fwd", [L, 128])
    gwb_d = din("b_gate_w_bwd", [L, 16, 128]); gbb_d = din("b_gate_b_bwd", [L, 128])
    bon_d = din("b_onorm_g", [L, 64])
    lq1_d = din("c_lq1", [L, 32]); lk1_d = din("c_lk1", [L, 32]); lq2_d = din("c_lq2", [L, 32]); lk2_d = din("c_lk2", [L, 32])
    con_d = din("c_onorm_g", [L, 64])
    dqg_d = din("d_qnorm_g", [L, 192]); wuq_d = din("d_w_uq", [L, 192, 384])
    dkg_d = din("d_kvnorm_g", [L, 128]); wukv_d = din("d_w_ukv", [L, 128, 512])
    wbr_d = din("w_branch", [L, 4, 256, 1024]); wout_d = din("w_out", [L, 1024, 1024])
    wfu_d = din("w_ffu", [L, 1024, 2816]); wfg_d = din("w_ffg", [L, 1024, 2816])
    cw_d = din("conv_w", [L, 3, 2816]); cb_d = din("conv_b", [L, 2816]); wfd_d = din("w_ffd", [L, 2816, 1024])
    fg_d = din("final_g", [1024])
    k_ident = din("k_ident", [128, 128]); k_bd64 = din("k_bd64", [128, 128]); k_tri = din("k_tri", [4, 128, 128])
    k_hm = din("k_hm", [128, 4]); k_pm = din("k_pm", [128, 2])
    k_cosA = din("k_cosA", [1024, 32]); k_sinA = din("k_sinA", [1024, 32])
    k_cosC = din("k_cosC", [1024, 16]); k_sinC = din("k_sinC", [1024, 16])
    k_cosD = din("k_cosD", [32, 1024]); k_sinD = din("k_sinD", [32, 1024])

    y_d = dout("y", [2, 1024, 1024])
    nak_d = dout("nak", [4, L, 256, 128]); nav_d = dout("nav", [4, L, 256, 128])
    nbf_d = dout("nbf", [4, L, 128, 64]); nbb_d = dout("nbb", [4, L, 128, 64])
    nck_d = dout("nck", [4, L, 256, 256]); ncv_d = dout("ncv", [4, L, 256, 256])
    nckv_d = dout("nckv", [4, L, 256, 128]); nkr_d = dout("nkr", [4, L, 256, 32])

    if DEBUG:
        dbg_br = dout('dbg_br', [128, 8, 1024]); dbg_x1 = dout('dbg_x1', [128, 8, 1024]); dbg_x2 = dout('dbg_x2', [128, 8, 1024]); dbg_mg = dout('dbg_mg', [128, 8, 1024])
    xT = kb.sbuf("xT", [128, 8, 1024]); xT_R = [[Res(f"xT{k}_{h}") for h in range(2)] for k in range(8)]
    hT = kb.sbuf("hT", [128, 8, 1024], BF16); hT_R = [Res("hT0"), Res("hT1")]
    NSLOT = 3
    ring = [kb.sbuf(f"ring{i}", [128, 8192], BF16) for i in range(NSLOT)]
    ring_R = [Res(f"ring{i}") for i in range(NSLOT)]
    ring_i = [0]
    arena = kb.sbuf("arena", [128, 30720], BF16)
    PS = kb.psum("ps", [128, 4096])
    pb = [PS[:, i * 512:(i + 1) * 512] for i in range(8)]
    pb_R = [Res(f"pb{i}", excl=True) for i in range(8)]
    PSB = PS[:, 7 * 512:8 * 512].bitcast(BF16)

    def new_slot(avoid=None):
        i = ring_i[0] % NSLOT
        ring_i[0] += 1
        while avoid is not None and ring[i] is avoid:
            i = ring_i[0] % NSLOT
            ring_i[0] += 1
        return ring[i], ring_R[i]

    def av(off, n):
        return arena[:, off:off + n]

    def vap(off, stride):
        return bass.AP(arena, off, [[30720, 128], [stride, 2], [1, 64]])

    identf = kb.sbuf("identf", [128, 128]); identb = kb.sbuf("identb", [128, 128], BF16)
    onesb = kb.sbuf("onesb", [128, 128], BF16); bd64 = kb.sbuf("bd64", [128, 128], BF16)
    tri = kb.sbuf("tri", [128, 4, 128]); hm = kb.sbuf("hm", [128, 4]); pm = kb.sbuf("pm", [128, 2])
    onescol = kb.sbuf("onescol", [128, 1]); onesrow = kb.sbuf("onesrow", [1, 128]); epsT = kb.sbuf("epsT", [128, 1])
    onesf = kb.sbuf("onesf", [128, 128])
    cosA = kb.sbuf("cosA", [128, 8, 32]); sinA = kb.sbuf("sinA", [128, 8, 32])
    cosC = kb.sbuf("cosC", [128, 8, 16]); sinC = kb.sbuf("sinC", [128, 8, 16])
    cosD = kb.sbuf("cosD", [128, 1024]); sinD = kb.sbuf("sinD", [128, 1024])
    CR = Res("consts")
    kb.dma("sp", identf[:], k_ident, writes=[CR])
    kb.dma("pool", identb[:], k_ident, writes=[CR])
    kb.dma("pool", bd64[:], k_bd64, writes=[CR])
    kb.dma("sp", tri[:], k_tri.rearrange("m s t -> s m t"), writes=[CR])
    kb.dma("sp", hm[:], k_hm, writes=[CR]); kb.dma("sp", pm[:], k_pm, writes=[CR])
    kb.dma("sp", cosA[:], k_cosA.rearrange("(t p) d -> p t d", p=128), writes=[CR])
    kb.dma("sp", sinA[:], k_sinA.rearrange("(t p) d -> p t d", p=128), writes=[CR])
    kb.dma("sp", cosC[:], k_cosC.rearrange("(t p) d -> p t d", p=128), writes=[CR])
    kb.dma("sp", sinC[:], k_sinC.rearrange("(t p) d -> p t d", p=128), writes=[CR])
    kb.dma("sp", cosD[64:96, :], k_cosD, writes=[CR]); kb.dma("sp", sinD[64:96, :], k_sinD, writes=[CR])
    kb.op("dve", lambda e: e.memset(onesb[:], 1.0), writes=[CR])
    kb.op("dve", lambda e: e.memset(onescol[:], 1.0), writes=[CR])
    kb.op("dve", lambda e: e.memset(onesrow[:], 1.0), writes=[CR])
    kb.op("dve", lambda e: e.memset(onesf[:], 1.0), writes=[CR])
    kb.op("dve", lambda e: e.memset(epsT[:], EPS), writes=[CR])

    def mm(out, lhsT, rhs, start, stop, reads, writes):
        kb.op("pe", lambda e: e.matmul(out, lhsT=lhsT, rhs=rhs, start=start, stop=stop), reads=reads, writes=writes)

    def tr(out, in_, ident, reads, writes):
        kb.op("pe", lambda e: e.transpose(out=out, in_=in_, identity=ident), reads=reads + [CR], writes=writes)

    def act(out, in_, func, reads, writes, scale=1.0, bias=None, accum_out=None):
        kw = {}
        if bias is not None:
            kw["bias"] = bias
        if accum_out is not None:
            kw["accum_out"] = accum_out
        kb.op("act", lambda e: e.activation(out=out, in_=in_, func=func, scale=scale, **kw), reads=reads, writes=writes)

    def tt(out, in0, in1, op, reads, writes, eng="dve"):
        kb.op(eng, lambda e: e.tensor_tensor(out=out, in0=in0, in1=in1, op=op), reads=reads, writes=writes)

    def stt(out, in0, scalar, in1, op0, op1, reads, writes):
        kb.op("dve", lambda e: e.scalar_tensor_tensor(out=out, in0=in0, scalar=scalar, in1=in1, op0=op0, op1=op1), reads=reads, writes=writes)

    def ts(out, in0, s1, op0, reads, writes, s2=None, op1=None, eng="dve"):
        if op1 is None:
            kb.op(eng, lambda e: e.tensor_scalar(out=out, in0=in0, scalar1=s1, scalar2=None, op0=op0), reads=reads, writes=writes)
        else:
            kb.op(eng, lambda e: e.tensor_scalar(out=out, in0=in0, scalar1=s1, scalar2=s2, op0=op0, op1=op1), reads=reads, writes=writes)

    def cp(out, in_, reads, writes, eng="dve"):
        if eng == "act":
            kb.op("act", lambda e: e.copy(out=out, in_=in_), reads=reads, writes=writes)
        else:
            kb.op(eng, lambda e: e.tensor_copy(out=out, in_=in_), reads=reads, writes=writes)

    def recipf(out, in_, reads, writes):
        act(out, in_, AF.Ln, reads, writes)
        act(out, out, AF.Exp, writes, writes, scale=-1.0)

    def recip(out, in_, reads, writes):
        kb.op("dve", lambda e: e.reciprocal(out=out, in_=in_), reads=reads, writes=writes)

    def bc(ap, shape):
        return ap.broadcast_to(list(shape))

    pbi = [0]

    def nb():
        i = pbi[0] % 7
        pbi[0] += 1
        return i

    scT = kb.sbuf("scT", [128, 8, 2]); modT = kb.sbuf("modT", [128, L, 48, 2]); bmT = kb.sbuf("bmT", [128, L, 48])
    n1T = kb.sbuf("n1T", [128, L, 8]); n2T = kb.sbuf("n2T", [128, L, 8])
    A1 = kb.sbuf("A1", [128, L, 8, 2]); A2 = kb.sbuf("A2", [128, L, 8, 2])
    MR = Res("mod")
    with nc.allow_non_contiguous_dma(reason="tiny transposed vector loads"):
        for g_ in range(2):
            kb.dma("sp", scT[:, :, g_], cvec_d[g_].rearrange("(k p) -> p k", p=128), writes=[MR])
        for l_ in range(L):
            kb.dma("sp", bmT[:, l_, :], bmod_d[l_].rearrange("(j p) -> p j", p=128), writes=[MR])
            kb.dma("sp", n1T[:, l_, :], n1_d[l_].rearrange("(k p) -> p k", p=128), writes=[MR])
            kb.dma("sp", n2T[:, l_, :], n2_d[l_].rearrange("(k p) -> p k", p=128), writes=[MR])
    act(scT[:], scT[:], AF.Silu, [MR], [MR])
    scb = kb.sbuf("scb", [128, 8, 2], BF16)
    cp(scb[:], scT[:], [MR], [MR])
    for l in range(L):
        for cb in range(8):
            slot, wr = new_slot()
            w = slot[:, 0:6144].rearrange("p (k c) -> p k c", k=8)
            kb.dma("pool", w, wmod_d[l].rearrange("(k p) c -> p k c", p=128)[:, :, cb * 768:(cb + 1) * 768], writes=[wr])
            b = nb()
            for jj in range(6):
                for kc in range(8):
                    mm(pb[b][:, jj * 2:(jj + 1) * 2], w[:, kc, jj * 128:(jj + 1) * 128], scb[:, kc, :], kc == 0, kc == 7, [wr, MR], [pb_R[b]])
            tt(modT[:, l, cb * 6:(cb + 1) * 6, :], pb[b][:, 0:12].rearrange("p (j g) -> p j g", g=2),
               bc(bmT[:, l, cb * 6:(cb + 1) * 6].unsqueeze(2), [128, 6, 2]), ADD, [pb_R[b], MR], [MR])
    for l in range(L):
        stt(A1[:, l], modT[:, l, 8:16, :], 1.0, bc(n1T[:, l, :].unsqueeze(2), [128, 8, 2]), ADD, MUL, [MR], [MR])
        stt(A2[:, l], modT[:, l, 32:40, :], 1.0, bc(n2T[:, l, :].unsqueeze(2), [128, 8, 2]), ADD, MUL, [MR], [MR])

    lqk = kb.sbuf("lqk", [32, 4, L]); lamT = kb.sbuf("lamT", [128, L]); neglam = kb.sbuf("neglam", [128, L])
    ocs = kb.sbuf("ocs", [128, L]); lam2 = kb.sbuf("lam2", [128, 2, L])
    with nc.allow_non_contiguous_dma(reason="tiny transposed vector loads"):
        for i, d in enumerate([lq1_d, lk1_d, lq2_d, lk2_d]):
            kb.dma("sp", lqk[:, i, :], d.rearrange("l d -> d l"), writes=[MR])
        kb.dma("sp", ocs[0:64, :], con_d.rearrange("l d -> d l"), writes=[MR])
        kb.dma("sp", ocs[64:128, :], con_d.rearrange("l d -> d l"), writes=[MR])
    tt(lqk[:, 0, :], lqk[:, 0, :], lqk[:, 1, :], MUL, [MR], [MR])
    tt(lqk[:, 2, :], lqk[:, 2, :], lqk[:, 3, :], MUL, [MR], [MR])
    b = nb()
    mm(pb[b][:, 0:L], onesf[0:32, :], lqk[:, 0, :], True, True, [MR, CR], [pb_R[b]])
    mm(pb[b][:, L:2 * L], onesf[0:32, :], lqk[:, 2, :], True, True, [MR, CR], [pb_R[b]])
    act(lam2[:].rearrange("p a l -> p (a l)"), pb[b][:, 0:2 * L], AF.Exp, [pb_R[b]], [MR])
    tt(lamT[:], lam2[:, 0, :], lam2[:, 1, :], SUB, [MR], [MR])
    lam_init = [0.8 - 0.6 * math.exp(-0.3 * l) for l in range(L)]
    for l in range(L):
        ts(lamT[:, l:l + 1], lamT[:, l:l + 1], lam_init[l], ADD, [MR], [MR])
        ts(ocs[:, l:l + 1], ocs[:, l:l + 1], 1.0 - lam_init[l], MUL, [MR], [MR])
    ts(neglam[:], lamT[:], -1.0, MUL, [MR], [MR])

    gA = kb.sbuf("gA", [128, 384]); gDQ = kb.sbuf("gDQ", [128, 192]); gKV = kb.sbuf("gKV", [128, 128]); gBO = kb.sbuf("gBO", [128, 256])
    GW = kb.sbuf("GW", [16, 256]); GBias = kb.sbuf("GBias", [1, 256]); cwT = kb.sbuf("cwT", [128, 3, 22]); cbT = kb.sbuf("cbT", [128, 22])
    fgB = arena[:, 0:2048].bitcast(F32); fgB_R = Res("fgB")
    LP = Res("layer_params")

    def load_layer_params(l):
        def bsrc(d, n, rep):
            return bass.AP(d.tensor, d[l].offset, [[0, 128], [0, rep], [1, n]])
        kb.dma("sp", gA[:, 0:256].rearrange("p (r d) -> p r d", r=4), bsrc(aqg_d, 64, 4), writes=[LP])
        kb.dma("sp", gA[:, 256:384].rearrange("p (r d) -> p r d", r=2), bsrc(akg_d, 64, 2), writes=[LP])
        kb.dma("sp", gDQ[:], dqg_d[l].partition_broadcast(128), writes=[LP])
        kb.dma("sp", gKV[:], dkg_d[l].partition_broadcast(128), writes=[LP])
        kb.dma("sp", gBO[:].rearrange("p (r d) -> p r d", r=4), bsrc(bon_d, 64, 4), writes=[LP])
        kb.dma("sp", GW[0:16, 0:128], gwf_d[l], writes=[LP]); kb.dma("sp", GW[0:16, 128:256], gwb_d[l], writes=[LP])
        kb.dma("sp", GBias[0:1, 0:128], gbf_d[l:l + 1, :], writes=[LP]); kb.dma("sp", GBias[0:1, 128:256], gbb_d[l:l + 1, :], writes=[LP])
        with nc.allow_non_contiguous_dma(reason="tiny transposed vector loads"):
            for w_ in range(3):
                kb.dma("sp", cwT[:, w_, :], cw_d[l, w_].rearrange("(f p) -> p f", p=128), writes=[LP])
            kb.dma("sp", cbT[:], cb_d[l].rearrange("(f p) -> p f", p=128), writes=[LP])

    stg = [kb.sbuf(f"stg{i}", [128, 1024]) for i in range(2)]; stg_R = [Res("stg0"), Res("stg1")]
    stg_i = [0]

    def nstg():
        i = stg_i[0] % 2
        stg_i[0] += 1
        return stg[i], stg_R[i]

    tA = kb.sbuf("tA", [128, 1024]); tA_R = (Res("tA0"), Res("tA1"))
    tB = kb.sbuf("tB", [128, 1024]); tB_R = (Res("tB0"), Res("tB1"))
    tC = kb.sbuf("tC", [128, 512]); tC_R = Res("tC")
    tC1 = kb.sbuf("tC1", [128, 512]); tC1_R = Res("tC1")
    sm = kb.sbuf("sm", [128, 64]); sm_R = Res("sm")
    sm1 = kb.sbuf("sm1", [128, 64]); sm1_R = Res("sm1")
    rstd = tC1; rstd_R = tC1_R
    sqb = arena[:, 0:4096].rearrange("p (k n) -> p k n", k=8); sqb_R = Res("sqb")
    MG_R = [Res("MG0"), Res("MG1")]
    PT = [kb.sbuf(f"PT{i}", [128, 512], BF16) for i in range(3)]; PT_R = [Res(f"PT{i}") for i in range(3)]
    pt_i = [0]
    qkb = kb.sbuf("qkb", [128, 512], BF16); qkb_R = Res("qkb")
    qkb1 = kb.sbuf("qkb1", [128, 512], BF16); qkb1_R = Res("qkb1")
    krpad = kb.sbuf("krpad", [128, 96], BF16); krpad_R = Res("krpad")
    krpad1 = kb.sbuf("krpad1", [128, 96], BF16); krpad1_R = Res("krpad1")
    kb.op("dve", lambda e: e.memset(krpad[:], 0.0), writes=[krpad_R])
    kb.op("dve", lambda e: e.memset(krpad1[:], 0.0), writes=[krpad1_R])
    Sst = [kb.sbuf(f"Sst{i}", [128, 64]) for i in range(2)]; Sst_R = [Res("Sf"), Res("Sb")]
    GL = kb.sbuf("GL", [128, 8, 2]); GL_R = Res("GL")
    zgT = tC[0:16, 256:512]; zgT_R = Res("zgT")
    zgT1 = tC1[0:16, 256:512]; zgT1_R = Res("zgT1")
    TSET = [dict(tA_=tA[:, 0:512], tA_R_=tA_R[0], tB_=tB[:, 0:512], tB_R_=tB_R[0], tC_=tC, tC_R_=tC_R, sm_=sm, sm_R_=sm_R, qkb_=qkb, qkb_R_=qkb_R,
                 krpad_=krpad, krpad_R_=krpad_R, zgT_=zgT, zgT_R_=zgT_R),
            dict(tA_=tA[:, 512:1024], tA_R_=tA_R[1], tB_=tB[:, 512:1024], tB_R_=tB_R[1], tC_=tC1, tC_R_=tC1_R, sm_=sm1, sm_R_=sm1_R, qkb_=qkb1, qkb_R_=qkb1_R,
                 krpad_=krpad1, krpad_R_=krpad1_R, zgT_=zgT1, zgT_R_=zgT1_R)]

    def norm_mod(Asc, shift_c0, l, g):
        for h in range(2):
            hs = slice(h * 512, (h + 1) * 512)
            xr = [xT_R[k][h] for k in range(8)]
            act(sqb, xT[:, :, hs], AF.Square, xr, [sqb_R, MG_R[0], MG_R[1]])
            b = nb()
            for kc in range(8):
                mm(pb[b], onesb[:], sqb[:, kc, :], kc == 0, kc == 7, [sqb_R, CR], [pb_R[b]])
            act(rstd[:], pb[b], AF.Ln, [pb_R[b], CR], [rstd_R], scale=1.0 / 1024, bias=epsT[:])
            act(rstd[:], rstd[:], AF.Exp, [rstd_R], [rstd_R], scale=-0.5)
            for kc in range(8):
                tmp, tr_ = (tA, tA_R) if kc % 2 == 0 else (tB, tB_R)
                tt(tmp[:, 0:512], xT[:, kc, hs], rstd[:], MUL, [xT_R[kc][h], rstd_R], [tr_])
                act(hT[:, kc, hs], tmp[:, 0:512], AF.Identity, [tr_, MR], [hT_R[h]],
                    scale=Asc[:, l, kc, g:g + 1], bias=modT[:, l, shift_c0 + kc, g:g + 1])

    def load_w(dst, src, sr):
        kb.dma("pool", dst, src, writes=[sr])

    def proj_tm(bank0, ncols, W, wr, t):
        nbk = (ncols + 511) // 512
        for bi in range(nbk):
            c0 = bi * 512
            c1 = min(ncols, c0 + 512)
            for kc in range(8):
                mm(pb[bank0 + bi][:, 0:c1 - c0], hT[:, kc, t * 128:(t + 1) * 128], W[:, kc, c0:c1], kc == 0, kc == 7,
                   [hT_R[t // 4], wr], [pb_R[bank0 + bi]])

    def rope_tm(src, src_R, U2, hs, cos_t, sin_t, out, tA, tA_R, tB, tB_R, qkb_R):
        xv = src.rearrange("p (u r h i) -> p u r h i", u=U2, r=2, h=2)
        ov = out.rearrange("p (u r h i) -> p u r h i", u=U2, r=2, h=2)
        cv = bc(cos_t.rearrange("p (r i) -> p r i", r=2).unsqueeze(1), [128, U2, 2, hs])
        sv = bc(sin_t.rearrange("p (r i) -> p r i", r=2).unsqueeze(1), [128, U2, 2, hs])
        n = U2 * 2 * hs
        a = tA[:, 0:n].rearrange("p (u r i) -> p u r i", u=U2, r=2)
        b2 = tB[:, 0:n].rearrange("p (u r i) -> p u r i", u=U2, r=2)
        x0 = xv[:, :, :, 0, :]; x1 = xv[:, :, :, 1, :]
        tt(a, x0, cv, MUL, [src_R, CR], [tA_R]); tt(b2, x1, sv, MUL, [src_R, CR], [tB_R])
        tt(ov[:, :, :, 0, :], a, b2, SUB, [tA_R, tB_R], [qkb_R])
        tt(a, x1, cv, MUL, [src_R, CR], [tA_R]); tt(b2, x0, sv, MUL, [src_R, CR], [tB_R])
        tt(ov[:, :, :, 1, :], a, b2, ADD, [tA_R, tB_R], [qkb_R])

    def attend(kT, qT, vT, kts, q0, N, scale, reads, ob, avoid=()):
        sb = [None] * len(kts)

        def S(i):
            b = nb()
            while b == ob or b in avoid:
                b = nb()
            sb[i] = b
            mm(pb[b][:, 0:N], kT(kts[i]), qT, True, True, reads, [pb_R[b]])
        S(0)
        for i in range(len(kts)):
            if i + 1 < len(kts):
                S(i + 1)
            p = pt_i[0] % 3
            pt_i[0] += 1
            act(PT[p][:, 0:N], pb[sb[i]][:, 0:N], AF.Exp, [pb_R[sb[i]]], [PT_R[p]], scale=scale)
            mm(pb[ob][:, 0:N], vT(kts[i]), PT[p][:, 0:N], i == 0, i == len(kts) - 1, reads + [PT_R[p]], [pb_R[ob]])

    def obank():
        b = nb()
        return b

    def attend_multi(jobs, finishers, LA=2):
        flat = [(ji, i) for ji, jb in enumerate(jobs) for i in range(len(jb["kts"]))]
        sbank = {}
        ob_of = {}
        live = []
        gobs = {}

        def alloc():
            b = nb()
            while b in live or b in sbank.values():
                b = nb()
            return b

        def S(idx):
            ji, i = flat[idx]
            jb = jobs[ji]
            b = alloc()
            sbank[idx] = b
            mm(pb[b][:, 0:jb["N"]], jb["kT"](jb["kts"][i]), jb["qT"], True, True, jb["reads"], [pb_R[b]])
        for idx in range(min(LA, len(flat))):
            S(idx)
        for idx, (ji, i) in enumerate(flat):
            if idx + LA < len(flat):
                S(idx + LA)
            jb = jobs[ji]
            N = jb["N"]
            if i == 0:
                ob_of[ji] = alloc()
                live.append(ob_of[ji])
            ob = ob_of[ji]
            p = pt_i[0] % 3
            pt_i[0] += 1
            sb_ = sbank.pop(idx)
            act(PT[p][:, 0:N], pb[sb_][:, 0:N], AF.Exp, [pb_R[sb_]], [PT_R[p]], scale=jb["scale"])
            last = (i == len(jb["kts"]) - 1)
            mm(pb[ob][:, 0:N], jb["vT"](jb["kts"][i]), PT[p][:, 0:N], i == 0, last, jb["reads"] + [PT_R[p]], [pb_R[ob]])
            if last:
                gid = jb["gid"]
                gobs.setdefault(gid, []).append(ob)
                if ji + 1 >= len(jobs) or jobs[ji + 1]["gid"] != gid:
                    finishers[gid](gobs[gid])
                    for o in gobs[gid]:
                        live.remove(o)

    for g in GROUPS:
        S_GRP = (g == 1)
        NKT = 12 if S_GRP else 8
        KOFF = 4 if S_GRP else 0
        NK = NKT * 128
        if S_GRP:
            seqs = [(0, 1024, list(range(12)))]
        else:
            seqs = [(s * 256, 256, [2 * s, 2 * s + 1]) for s in range(4)]
        for t in range(8):
            st_, sr = nstg()
            kb.dma("sp", st_[:], x_d[g, t * 128:(t + 1) * 128, :], writes=[sr])
            for hh in range(2):
                b = nb()
                for kk in range(4):
                    kc = hh * 4 + kk
                    tr(pb[b][:, kk * 128:(kk + 1) * 128], st_[:, kc * 128:(kc + 1) * 128], identf[:], [sr], [pb_R[b]])
                cp(xT[:, hh * 4:(hh + 1) * 4, t * 128:(t + 1) * 128], pb[b].rearrange("p (k c) -> p k c", k=4), [pb_R[b]],
                   [xT_R[k][t // 4] for k in range(hh * 4, hh * 4 + 4)], eng=("act" if hh else "dve"))

        for l in range(NLAYERS):
            load_layer_params(l)
            norm_mod(A1, 0, l, g)
            if STOP == 'N':
                continue
            BR_OFF = 22528
            BR = arena[:, BR_OFF:BR_OFF + 8192].rearrange("p (j b n) -> p j b n", j=4, b=2)
            BR_R = [Res(f"BR{j}") for j in range(4)]
            OUTS = not S_GRP

            slot, sr = new_slot()
            W = slot[:, 0:4096].rearrange("p (k c) -> p k c", k=8)
            wsrc = win_d[l].rearrange("(k p) c -> p k c", p=128)
            for g2_ in range(2):
                for kv_ in range(2):
                    load_w(W[:, :, g2_ * 128 + kv_ * 64:g2_ * 128 + kv_ * 64 + 64], wsrc[:, :, (kv_ * 2 + g2_) * 64:(kv_ * 2 + g2_) * 64 + 64], sr)
            load_w(W[:, :, 256:512], wsrc[:, :, 256:512], sr)
            barrier(kb)
            QT = av(0, 2048).rearrange("p (b n) -> p b n", b=2); QT_R = Res("QT_A")
            KT = av(2048, 1536); KT_R = Res("KT_A")
            VA = av(3584, 4608).rearrange("p (k c) -> p k c", c=384); VA_R = Res("VA")
            kb.op("dve", lambda e: e.memset(VA[:, 0:NKT, :], 1.0), writes=[VA_R])
            if S_GRP:
                st_, sr2 = nstg()
                kb.dma("sp", st_[:, 0:512].rearrange("p (k f) -> p k f", k=4), cak_d[l].rearrange("(k p) f -> p k f", p=128), writes=[sr2])
                b = nb()
                for kt in range(4):
                    tr(pb[b][:, kt * 128:(kt + 1) * 128], st_[:, kt * 128:(kt + 1) * 128], identf[:], [sr2], [pb_R[b]])
                cp(KT[:, 0:512], pb[b], [pb_R[b]], [KT_R])
                kb.dma("pool", VA[:, 0:4, 64:128], cav_d[l].rearrange("(k p) f -> p k f", p=128)[:, :, 0:64], writes=[VA_R])
                kb.dma("pool", VA[:, 0:4, 256:320], cav_d[l].rearrange("(k p) f -> p k f", p=128)[:, :, 64:128], writes=[VA_R])
            if STOP == 'A0':
                continue
            for t in range(8):
                TS_ = TSET[t % 2]; tA_ = TS_['tA_']; tA_R_ = TS_['tA_R_']; tB_ = TS_['tB_']; tB_R_ = TS_['tB_R_']; tC_ = TS_['tC_']; tC_R_ = TS_['tC_R_']; sm_ = TS_['sm_']; sm_R_ = TS_['sm_R_']; qkb_ = TS_['qkb_']; qkb_R_ = TS_['qkb_R_']; krpad_ = TS_['krpad_']; krpad_R_ = TS_['krpad_R_']; zgT_ = TS_['zgT_']; zgT_R_ = TS_['zgT_R_']
                zb = nb()
                if zb == 6:
                    zb = nb()
                z = pb[zb]; zr = pb_R[zb]
                for kc in range(8):
                    mm(z, hT[:, kc, t * 128:(t + 1) * 128], W[:, kc, :], kc == 0, kc == 7, [hT_R[t // 4], sr], [zr])
                if STOP == 'A2a':
                    continue
                act(tC_[:, 0:384], z[:, 0:384], AF.Square, [zr], [tC_R_])
                kb.op("dve", lambda e, sm_=sm_, tC_=tC_: e.reduce_sum(out=sm_[:, 0:6], in_=tC_[:, 0:384].rearrange("p (h d) -> p h d", h=6), axis=AX.X), reads=[tC_R_], writes=[sm_R_])
                if STOP == 'A2b':
                    continue
                act(sm_[:, 0:6], sm_[:, 0:6], AF.Ln, [sm_R_, CR], [sm_R_], scale=1.0 / 64, bias=epsT[:])
                act(sm_[:, 0:6], sm_[:, 0:6], AF.Exp, [sm_R_], [sm_R_], scale=-0.5)
                if STOP == 'A2c':
                    continue
                tt(tC_[:, 0:384].rearrange("p (h d) -> p h d", h=6), z[:, 0:384].rearrange("p (h d) -> p h d", h=6),
                   bc(sm_[:, 0:6].unsqueeze(2), [128, 6, 64]), MUL, [zr, sm_R_], [tC_R_])
                tt(tC_[:, 0:384], tC_[:, 0:384], gA[:], MUL, [tC_R_, LP], [tC_R_])
                kt = KOFF + t
                if STOP == 'A2':
                    continue
                cp(VA[:, kt, 64:128], z[:, 384:448], [zr], [VA_R], eng="act")
                cp(VA[:, kt, 256:320], z[:, 448:512], [zr], [VA_R], eng="act")
                if S_GRP:
                    rope_tm(tC_[:, 0:384], tC_R_, 6, 16, cosA[:, t, :], sinA[:, t, :], qkb_[:, 0:384], tA_, tA_R_, tB_, tB_R_, qkb_R_)
                else:
                    cp(qkb_[:, 0:384], tC_[:, 0:384], [tC_R_], [qkb_R_])
                    o_, or_ = nstg()
                    cp(o_[:, 0:128], tC_[:, 256:384], [tC_R_], [or_], eng="act")
                    cp(o_[:, 128:256], z[:, 384:512], [zr], [or_], eng="act")
                    s_, tt_ = t // 2, (t % 2) * 128
                    kb.dma("sp", nak_d[s_, l, tt_:tt_ + 128, :], o_[:, 0:128], reads=[or_])
                    kb.dma("sp", nav_d[s_, l, tt_:tt_ + 128, :], o_[:, 128:256], reads=[or_])
                if STOP == 'A3':
                    continue
                for g2 in range(2):
                    tr(PSB[:, g2 * 128:(g2 + 1) * 128], qkb_[:, g2 * 128:(g2 + 1) * 128], identb[:], [qkb_R_], [pb_R[7]])
                tr(PSB[:, 256:384], qkb_[:, 256:384], identb[:], [qkb_R_], [pb_R[7]])
                if STOP == 'A4':
                    continue
                if STOP != 'A6':
                    cp(QT[:, :, t * 128:(t + 1) * 128], PSB[:, 0:256].rearrange("p (b n) -> p b n", b=2), [pb_R[7]], [QT_R])
                if STOP != 'A5':
                    cp(KT[:, kt * 128:(kt + 1) * 128], PSB[:, 256:384], [pb_R[7], QT_R], [KT_R], eng="act")
            if STOP in ('A1', 'A2', 'A2a', 'A2b', 'A2c', 'A3', 'A4', 'A5', 'A6'):
                continue
            jobs = []; fins = {}
            for (q0, qlen, kts) in seqs:
                for qc in range(0, qlen, 512):
                    N = min(512, qlen - qc)
                    for kv in range(2):
                        for g2 in range(2):
                            rows = slice(kv * 64, (kv + 1) * 64)
                            orow = g2 * 64
                            srow = 64 - orow
                            if g2 == 0:
                                vfn = lambda kt, kv=kv: VA[:, kt, kv * 192 + 64:kv * 192 + 192]
                            else:
                                vfn = lambda kt, kv=kv: VA[:, kt, kv * 192:kv * 192 + 128]
                            gid = len(jobs)

                            def fin(obs, orow=orow, srow=srow, N=N, kv=kv, c0=q0 + qc):
                                ob = obs[0]
                                recipf(tA[srow:srow + 64, 0:N], pb[ob][srow:srow + 64, 0:N], [pb_R[ob]], [tA_R])
                                tt(BR[orow:orow + 64, 0, kv, c0:c0 + N], pb[ob][orow:orow + 64, 0:N], tA[srow:srow + 64, 0:N], MUL,
                                   [pb_R[ob], tA_R], [BR_R[0]])
                            fins[gid] = fin
                            jobs.append(dict(kT=(lambda kt, rows=rows: KT[rows, kt * 128:(kt + 1) * 128]), qT=QT[rows, g2, q0 + qc:q0 + qc + N], vT=vfn,
                                             kts=kts, N=N, scale=0.125, reads=[KT_R, QT_R, VA_R], gid=gid))
            attend_multi(jobs, fins)
            if STOP == 'A':
                continue
            slot, sr = new_slot()
            W = slot[:, 0:6144].rearrange("p (k c) -> p k c", k=8)
            load_w(W, win_d[l].rearrange("(k p) c -> p k c", p=128)[:, :, 1312:2080], sr)
            barrier(kb)
            QT = av(0, 2048).rearrange("p (b n) -> p b n", b=2); QT_R = Res("QT_C")
            KC = av(2048, 6144).rearrange("p (v b n) -> p v b n", v=2, b=2); KC_R = Res("KT_C")
            VC_OFF = 8192
            VC = av(VC_OFF, 4608).rearrange("p (k c) -> p k c", c=384); VC_R = Res("VC")
            kb.op("dve", lambda e: e.memset(VC[:, 0:NKT, :], 1.0), writes=[VC_R])

            def kc_store(src, kcols):
                for v in range(2):
                    for bb in range(2):
                        ts(KC[:, v, bb, kcols], src[:, bb, :], pm[:, v:v + 1], MUL, [pb_R[7], pb_R[6], CR], [KC_R])
            if S_GRP:
                for bb in range(2):
                    st_, sr2 = nstg()
                    kb.dma("sp", st_[:, 0:512].rearrange("p (k f) -> p k f", k=4), cck_d[l].rearrange("(k p) f -> p k f", p=128)[:, :, bb * 128:(bb + 1) * 128], writes=[sr2])
                    for kt in range(4):
                        tr(pb[6][:, kt * 128:(kt + 1) * 128], st_[:, kt * 128:(kt + 1) * 128], identf[:], [sr2], [pb_R[6]])
                    for v in range(2):
                        ts(KC[:, v, bb, 0:512], pb[6], pm[:, v:v + 1], MUL, [pb_R[6], CR], [KC_R])
                for (d0, d1, s0, s1) in VDST:
                    kb.dma("pool", VC[:, 0:4, d0:d1], ccv_d[l].rearrange("(k p) f -> p k f", p=128)[:, :, s0:s1], writes=[VC_R])
            for t in range(8):
                TS_ = TSET[t % 2]; tA_ = TS_['tA_']; tA_R_ = TS_['tA_R_']; tB_ = TS_['tB_']; tB_R_ = TS_['tB_R_']; tC_ = TS_['tC_']; tC_R_ = TS_['tC_R_']; sm_ = TS_['sm_']; sm_R_ = TS_['sm_R_']; qkb_ = TS_['qkb_']; qkb_R_ = TS_['qkb_R_']; krpad_ = TS_['krpad_']; krpad_R_ = TS_['krpad_R_']; zgT_ = TS_['zgT_']; zgT_R_ = TS_['zgT_R_']
                zb = nb()
                while zb >= 5:
                    zb = nb()
                zr = [pb_R[zb], pb_R[zb + 1]]
                z = PS[:, zb * 512:zb * 512 + 768]
                for bi in range(2):
                    c0, c1 = bi * 512, min(768, bi * 512 + 512)
                    for kc in range(8):
                        mm(PS[:, (zb + bi) * 512:(zb + bi) * 512 + c1 - c0], hT[:, kc, t * 128:(t + 1) * 128], W[:, kc, c0:c1], kc == 0, kc == 7,
                           [hT_R[t // 4], sr], [pb_R[zb + bi]])
                kt = KOFF + t
                for (d0, d1, s0, s1) in VDST:
                    cp(VC[:, kt, d0:d1], z[:, 512 + s0:512 + s1], zr, [VC_R], eng="act")
                if S_GRP:
                    rope_tm(z[:, 0:512], zr[0], 16, 8, cosC[:, t, :], sinC[:, t, :], qkb_[:, 0:512], tA_, tA_R_, tB_, tB_R_, qkb_R_)
                else:
                    cp(qkb_[:, 0:512], z[:, 0:512], zr, [qkb_R_])
                    o_, or_ = nstg()
                    cp(o_[:, 0:512], z[:, 256:768], zr, [or_], eng="act")
                    s_, tt_ = t // 2, (t % 2) * 128
                    kb.dma("sp", nck_d[s_, l, tt_:tt_ + 128, :], o_[:, 0:256], reads=[or_])
                    kb.dma("sp", ncv_d[s_, l, tt_:tt_ + 128, :], o_[:, 256:512], reads=[or_])
                for i in range(4):
                    tr(PSB[:, i * 128:(i + 1) * 128], qkb_[:, i * 128:(i + 1) * 128], identb[:], [qkb_R_], [pb_R[7]])
                cp(QT[:, :, t * 128:(t + 1) * 128], PSB[:, 0:256].rearrange("p (b n) -> p b n", b=2), [pb_R[7]], [QT_R], eng="act")
                kc_store(PSB[:, 256:512].rearrange("p (b n) -> p b n", b=2), slice(kt * 128, (kt + 1) * 128))
            OCR = tC; OCR_R = tC_R
            jobs = []; fins = {}
            for (q0, qlen, kts) in seqs:
                for qc in range(0, qlen, 512):
                    N = min(512, qlen - qc)
                    for bb in range(2):
                        for hf in range(2):
                            h = bb * 2 + hf
                            rows = slice(hf * 64, (hf + 1) * 64)
                            orow = hf * 64
                            srow = 64 - orow
                            vfn = lambda kt, h=h: VC[:, kt, VOFFS[h]:VOFFS[h] + 128]
                            gid = len(jobs)

                            def fin(obs, orow=orow, srow=srow, N=N, bb=bb, hf=hf, c0=q0 + qc):
                                o1, o2 = obs
                                recipf(tA[srow:srow + 64, 0:N], pb[o1][srow:srow + 64, 0:N], [pb_R[o1]], [tA_R])
                                recipf(tB[srow:srow + 64, 0:N], pb[o2][srow:srow + 64, 0:N], [pb_R[o2]], [tB_R])
                                tt(tA[orow:orow + 64, 512:512 + N], pb[o1][orow:orow + 64, 0:N], tA[srow:srow + 64, 0:N], MUL, [pb_R[o1], tA_R], [tA_R])
                                tt(tB[orow:orow + 64, 512:512 + N], pb[o2][orow:orow + 64, 0:N], tB[srow:srow + 64, 0:N], MUL, [pb_R[o2], tB_R], [tB_R])
                                stt(OCR[orow:orow + 64, 0:N], tB[orow:orow + 64, 512:512 + N], neglam[orow:orow + 64, l:l + 1], tA[orow:orow + 64, 512:512 + N], MUL, ADD,
                                    [tA_R, tB_R, MR], [OCR_R])
                                if hf == 1:
                                    act(PT[0][:, 0:N], OCR[:, 0:N], AF.Square, [OCR_R], [PT_R[0]])
                                    b = nb()
                                    while b in obs:
                                        b = nb()
                                    mm(pb[b][:, 0:N], bd64[:], PT[0][:, 0:N], True, True, [PT_R[0], CR], [pb_R[b]])
                                    act(rstd[:, 0:N], pb[b][:, 0:N], AF.Ln, [pb_R[b], CR], [rstd_R], scale=1.0 / 64, bias=epsT[:])
                                    act(rstd[:, 0:N], rstd[:, 0:N], AF.Exp, [rstd_R], [rstd_R], scale=-0.5)
                                    tt(OCR[:, 0:N], OCR[:, 0:N], rstd[:, 0:N], MUL, [OCR_R, rstd_R], [OCR_R])
                                    act(BR[:, 2, bb, c0:c0 + N], OCR[:, 0:N], AF.Copy, [OCR_R, MR], [BR_R[2]], scale=ocs[:, l:l + 1])
                            fins[gid] = fin
                            for j in range(2):
                                jobs.append(dict(kT=(lambda kt, rows=rows, j=j, bb=bb: KC[rows, j, bb, kt * 128:(kt + 1) * 128]), qT=QT[rows, bb, q0 + qc:q0 + qc + N],
                                                 vT=vfn, kts=kts, N=N, scale=32 ** -0.5, reads=[KC_R, QT_R, VC_R], gid=gid))
            attend_multi(jobs, fins)

            if STOP == 'C':
                continue
            slot, sr = new_slot()
            W = slot[:, 0:2816].rearrange("p (k c) -> p k c", k=8)
            load_w(W, win_d[l].rearrange("(k p) c -> p k c", p=128)[:, :, 2080:2432], sr)
            UQ = slot[:, 2816:3584].rearrange("p (k c) -> p k c", k=2)
            UQS = slot[:, 3584:4352].rearrange("p (k c) -> p k c", k=2)
            UKV = slot[:, 4352:4864]
            load_w(UQ[:, 0, :], wuq_d[l, 0:128, :], sr); load_w(UQ[0:64, 1, :], wuq_d[l, 128:192, :], sr)
            load_w(UKV, wukv_d[l], sr)
            if S_GRP:
                with nc.allow_non_contiguous_dma(reason="rope column swap"):
                    for (kc_, r0, r1, pr) in ((0, 0, 128, 128), (1, 128, 192, 64)):
                        src = wuq_d[l, r0:r1, :].rearrange("p (h c) -> p h c", h=4)[:, :, 64:96].rearrange("p h (r f i) -> p h r f i", r=2, f=2)
                        dst = UQS[0:pr, kc_, :].rearrange("p (h c) -> p h c", h=4)[:, :, 64:96].rearrange("p h (r f i) -> p h r f i", r=2, f=2)
                        for f in range(2):
                            for r_ in range(2):
                                load_w(dst[:, :, r_, f, :], src[:, :, r_, 1 - f, :], sr)
            barrier(kb)
            DQN = av(0, 2048).rearrange("p (b n) -> p b n", b=2); DQN_R = Res("DQN")
            CKT = av(2048, 1536); CKT_R = Res("CKT")
            KD_ = av(3584, 6144).rearrange("p (h n) -> p h n", h=4); KD_R = Res("KT_D")
            DQT = av(9728, 4096).rearrange("p (h n) -> p h n", h=4); DQT_R = Res("DQT")
            VD_OFF = 13824
            VD = av(VD_OFF, 4608).rearrange("p (k c) -> p k c", c=384); VD_R = Res("VD")
            kb.op("dve", lambda e: e.memset(VD[:, 0:NKT, :], 1.0), writes=[VD_R])
            if S_GRP:
                st_, sr2 = nstg()
                kb.dma("sp", st_[:, 0:512].rearrange("p (k f) -> p k f", k=4), cdc_d[l].rearrange("(k p) f -> p k f", p=128), writes=[sr2])
                for kt in range(4):
                    tr(pb[6][:, kt * 128:(kt + 1) * 128], st_[:, kt * 128:(kt + 1) * 128], identf[:], [sr2], [pb_R[6]])
                cp(CKT[:, 0:512], pb[6], [pb_R[6]], [CKT_R])
                stgkr = st_[:, 512:896].rearrange("p (k f) -> p k f", k=4)
                kb.op("dve", lambda e, stgkr=stgkr: e.memset(stgkr[:, :, 0:64], 0.0), writes=[sr2])
                kb.dma("sp", stgkr[:, :, 64:96], cdr_d[l].rearrange("(k p) f -> p k f", p=128), writes=[sr2])
                b = nb()
                for kt in range(4):
                    mm(pb[b][0:96, kt * 128:(kt + 1) * 128], stgkr[:, kt, :], identf[:], True, True, [sr2, CR], [pb_R[b]])
                cp(KD_[64:96, :, 0:512], bc(pb[b][64:96, :].unsqueeze(1), [32, 4, 512]), [pb_R[b]], [KD_R])
            for t in range(8):
                TS_ = TSET[t % 2]; tA_ = TS_['tA_']; tA_R_ = TS_['tA_R_']; tB_ = TS_['tB_']; tB_R_ = TS_['tB_R_']; tC_ = TS_['tC_']; tC_R_ = TS_['tC_R_']; sm_ = TS_['sm_']; sm_R_ = TS_['sm_R_']; qkb_ = TS_['qkb_']; qkb_R_ = TS_['qkb_R_']; krpad_ = TS_['krpad_']; krpad_R_ = TS_['krpad_R_']; zgT_ = TS_['zgT_']; zgT_R_ = TS_['zgT_R_']
                zb = nb()
                z = pb[zb]; zr = pb_R[zb]
                for kc in range(8):
                    mm(z[:, 0:352], hT[:, kc, t * 128:(t + 1) * 128], W[:, kc, :], kc == 0, kc == 7, [hT_R[t // 4], sr], [zr])
                act(tA_[:, 0:192], z[:, 0:192], AF.Square, [zr], [tA_R_], accum_out=sm_[:, 8:9])
                act(tA_[:, 192:320], z[:, 192:320], AF.Square, [zr], [tA_R_], accum_out=sm_[:, 9:10])
                act(sm_[:, 8:9], sm_[:, 8:9], AF.Ln, [tA_R_, CR], [sm_R_], scale=1.0 / 192, bias=epsT[:])
                act(sm_[:, 9:10], sm_[:, 9:10], AF.Ln, [tA_R_, CR], [sm_R_], scale=1.0 / 128, bias=epsT[:])
                act(sm_[:, 8:10], sm_[:, 8:10], AF.Exp, [sm_R_], [sm_R_], scale=-0.5)
                stt(qkb_[:, 0:192], z[:, 0:192], sm_[:, 8:9], gDQ[:], MUL, MUL, [zr, sm_R_, LP], [qkb_R_])
                stt(tB_[:, 0:128], z[:, 192:320], sm_[:, 9:10], gKV[:], MUL, MUL, [zr, sm_R_, LP], [tB_R_])
                cp(qkb_[:, 192:320], tB_[:, 0:128], [tB_R_], [qkb_R_])
                kt = KOFF + t
                if S_GRP:
                    xv = z[:, 320:352].rearrange("p (r h i) -> p r h i", r=2, h=2)
                    ov = krpad_[:, 64:96].rearrange("p (r h i) -> p r h i", r=2, h=2)
                    cv = cosC[:, t, :].rearrange("p (r i) -> p r i", r=2); sv = sinC[:, t, :].rearrange("p (r i) -> p r i", r=2)
                    a = tC_[:, 0:16].rearrange("p (r i) -> p r i", r=2); b2 = tC_[:, 16:32].rearrange("p (r i) -> p r i", r=2)
                    tt(a, xv[:, :, 0, :], cv, MUL, [zr, CR], [tC_R_]); tt(b2, xv[:, :, 1, :], sv, MUL, [zr, CR], [tC_R_])
                    tt(ov[:, :, 0, :], a, b2, SUB, [tC_R_], [krpad_R_])
                    tt(a, xv[:, :, 1, :], cv, MUL, [zr, CR], [tC_R_]); tt(b2, xv[:, :, 0, :], sv, MUL, [zr, CR], [tC_R_])
                    tt(ov[:, :, 1, :], a, b2, ADD, [tC_R_], [krpad_R_])
                else:
                    cp(krpad_[:, 64:96], z[:, 320:352], [zr], [krpad_R_])
                    o_, or_ = nstg()
                    cp(o_[:, 0:128], tB_[:, 0:128], [tB_R_], [or_], eng="act")
                    cp(o_[:, 128:160], z[:, 320:352], [zr], [or_], eng="act")
                    s_, tt_ = t // 2, (t % 2) * 128
                    kb.dma("sp", nckv_d[s_, l, tt_:tt_ + 128, :], o_[:, 0:128], reads=[or_])
                    kb.dma("sp", nkr_d[s_, l, tt_:tt_ + 128, :], o_[:, 128:160], reads=[or_])
                tr(PSB[:, 0:128], qkb_[:, 0:128], identb[:], [qkb_R_], [pb_R[7]])
                tr(PSB[0:64, 128:256], qkb_[:, 128:192], identb[:], [qkb_R_], [pb_R[7]])
                tr(PSB[:, 256:384], qkb_[:, 192:320], identb[:], [qkb_R_], [pb_R[7]])
                cp(DQN[:, 0, t * 128:(t + 1) * 128], PSB[:, 0:128], [pb_R[7]], [DQN_R])
                cp(DQN[0:64, 1, t * 128:(t + 1) * 128], PSB[0:64, 128:256], [pb_R[7]], [DQN_R])
                cp(CKT[:, kt * 128:(kt + 1) * 128], PSB[:, 256:384], [pb_R[7]], [CKT_R], eng="act")
                b = nb()
                mm(pb[b][0:96, 0:128], krpad_[:], identb[:], True, True, [krpad_R_, CR], [pb_R[b]])
                cp(KD_[64:96, :, kt * 128:(kt + 1) * 128], bc(pb[b][64:96, 0:128].unsqueeze(1), [32, 4, 128]), [pb_R[b]], [KD_R])
            for h in range(4):
                for c0 in range(0, NK, 512):
                    b = nb()
                    mm(pb[b][0:64, :], UKV[:, h * 128:h * 128 + 64], CKT[:, c0:c0 + 512], True, True, [sr, CKT_R], [pb_R[b]])
                    cp(KD_[0:64, h, c0:c0 + 512], pb[b][0:64, :], [pb_R[b]], [KD_R], eng=("act" if h % 2 else "dve"))
            for kt in range(NKT):
                b = nb()
                mm(pb[b][:, 0:256].rearrange("p (h e) -> p h e", h=4), CKT[:, kt * 128:(kt + 1) * 128], UKV.rearrange("p (h c) -> p h c", h=4)[:, :, 64:128], True, True,
                   [sr, CKT_R], [pb_R[b]])
                for (d0, d1, s0, s1) in VDST:
                    cp(VD[:, kt, d0:d1], pb[b][:, s0:s1], [pb_R[b]], [VD_R], eng=("act" if kt % 2 else "dve"))
            for h in range(4):
                for qc in range(0, 1024, 512):
                    b = nb()
                    mm(pb[b][0:96, :], UQ[:, 0, h * 96:(h + 1) * 96], DQN[:, 0, qc:qc + 512], True, False, [sr, DQN_R], [pb_R[b]])
                    mm(pb[b][0:96, :], UQ[0:64, 1, h * 96:(h + 1) * 96], DQN[0:64, 1, qc:qc + 512], False, True, [sr, DQN_R], [pb_R[b]])
                    if S_GRP:
                        b2_ = nb()
                        mm(pb[b2_][0:96, :], UQS[:, 0, h * 96:(h + 1) * 96], DQN[:, 0, qc:qc + 512], True, False, [sr, DQN_R], [pb_R[b2_]])
                        mm(pb[b2_][0:96, :], UQS[0:64, 1, h * 96:(h + 1) * 96], DQN[0:64, 1, qc:qc + 512], False, True, [sr, DQN_R], [pb_R[b2_]])
                        cp(DQT[0:64, h, qc:qc + 512], pb[b][0:64, :], [pb_R[b]], [DQT_R], eng="act")
                        tt(tA[64:96, 0:512], pb[b][64:96, :], cosD[64:96, qc:qc + 512], MUL, [pb_R[b], CR], [tA_R])
                        tt(tB[64:96, 0:512], pb[b2_][64:96, :], sinD[64:96, qc:qc + 512], MUL, [pb_R[b2_], CR], [tB_R])
                        tt(DQT[64:96, h, qc:qc + 512], tA[64:96, 0:512], tB[64:96, 0:512], ADD, [tA_R, tB_R], [DQT_R])
                    else:
                        cp(DQT[0:96, h, qc:qc + 512], pb[b][0:96, :], [pb_R[b]], [DQT_R], eng=("act" if h % 2 else "dve"))
            jobs = []; fins = {}
            for (q0, qlen, kts) in seqs:
                for qc in range(0, qlen, 512):
                    N = min(512, qlen - qc)
                    for h in range(4):
                        bb, hf = h // 2, h % 2
                        orow = hf * 64
                        srow = 64 - orow
                        vfn = lambda kt, h=h: VD[:, kt, VOFFS[h]:VOFFS[h] + 128]
                        gid = len(jobs)

                        def fin(obs, orow=orow, srow=srow, N=N, bb=bb, c0=q0 + qc):
                            ob = obs[0]
                            recipf(tA[srow:srow + 64, 0:N], pb[ob][srow:srow + 64, 0:N], [pb_R[ob]], [tA_R])
                            tt(BR[orow:orow + 64, 3, bb, c0:c0 + N], pb[ob][orow:orow + 64, 0:N], tA[srow:srow + 64, 0:N], MUL,
                               [pb_R[ob], tA_R], [BR_R[3]])
                        fins[gid] = fin
                        jobs.append(dict(kT=(lambda kt, h=h: KD_[0:96, h, kt * 128:(kt + 1) * 128]), qT=DQT[0:96, h, q0 + qc:q0 + qc + N], vT=vfn,
                                         kts=kts, N=N, scale=96 ** -0.5, reads=[KD_R, DQT_R, VD_R], gid=gid))
            attend_multi(jobs, fins)
            if STOP == 'D':
                continue
            slot, sr = new_slot()
            W = slot[:, 0:6144].rearrange("p (k c) -> p k c", k=8)
            G = slot[:, 6144:6656].rearrange("p (k c) -> p k c", k=8)
            load_w(W, win_d[l].rearrange("(k p) c -> p k c", p=128)[:, :, 512:1280], sr)
            load_w(G[:, :, 0:16], win_d[l].rearrange("(k p) c -> p k c", p=128)[:, :, 1280:1296], sr)
            load_w(G[:, :, 32:48], win_d[l].rearrange("(k p) c -> p k c", p=128)[:, :, 1296:1312], sr)
            barrier(kb)
            BT = av(0, 12288).rearrange("p (t d c) -> p t d c", t=8, d=2); BT_R = Res("BT")
            KDc = av(12288, 2048).rearrange("p (t d c) -> p t d c", t=8, d=2); KDc_R = Res("KDc")
            VB = av(14336, 2048).rearrange("p (t c) -> p t c", t=8); VB_R = Res("VB")
            GR = av(16384, 2048).rearrange("p (t c) -> p t c", t=8); GR_R = Res("GR")
            SIN = av(18432, 4096).rearrange("p (t d c) -> p t d c", t=8, d=2); SIN_R = Res("SIN")
            Lsp = tC; Lsp_R = tC_R
            for t in range(8):
                TS_ = TSET[t % 2]; tA_ = TS_['tA_']; tA_R_ = TS_['tA_R_']; tB_ = TS_['tB_']; tB_R_ = TS_['tB_R_']; tC_ = TS_['tC_']; tC_R_ = TS_['tC_R_']; sm_ = TS_['sm_']; sm_R_ = TS_['sm_R_']; qkb_ = TS_['qkb_']; qkb_R_ = TS_['qkb_R_']; krpad_ = TS_['krpad_']; krpad_R_ = TS_['krpad_R_']; zgT_ = TS_['zgT_']; zgT_R_ = TS_['zgT_R_']
                zb = nb()
                while zb >= 4:
                    zb = nb()
                zr = [pb_R[zb], pb_R[zb + 1]]
                z = PS[:, zb * 512:zb * 512 + 768]
                for bi in range(2):
                    c0, c1 = bi * 512, min(768, bi * 512 + 512)
                    for kc in range(8):
                        mm(PS[:, (zb + bi) * 512:(zb + bi) * 512 + c1 - c0], hT[:, kc, t * 128:(t + 1) * 128], W[:, kc, c0:c1], kc == 0, kc == 7,
                           [hT_R[t // 4], sr], [pb_R[zb + bi]])
                if BCUT == 1:
                    continue
                for d in range(2):
                    for kc in range(8):
                        mm(pb[5][0:16, d * 128:(d + 1) * 128], G[:, kc, d * 32:d * 32 + 16], hT[:, kc, t * 128:(t + 1) * 128], kc == 0, kc == 7, [hT_R[t // 4], sr], [pb_R[5]])
                cp(zgT_, pb[5][0:16, 0:256], [pb_R[5]], [zgT_R_])
                if BCUT == 2:
                    continue
                for d in range(2):
                    mm(pb[5][:, 256 + d * 128:384 + d * 128], zgT_[0:16, d * 128:(d + 1) * 128], GW[0:16, d * 128:(d + 1) * 128], True, False, [zgT_R_, LP], [pb_R[5]])
                    mm(pb[5][:, 256 + d * 128:384 + d * 128], onesrow[0:1, :], GBias[0:1, d * 128:(d + 1) * 128], False, True, [CR, LP], [pb_R[5]])
                if BCUT == 3:
                    continue
                act(tA_[:, 0:256], pb[5][:, 256:512], AF.Exp, [pb_R[5]], [tA_R_], scale=-1.0)
                act(tC_[:, 0:256], tA_[:, 0:256], AF.Ln, [tA_R_, CR], [tC_R_], bias=onescol[:])
                if BCUT == 4:
                    continue
                for i, (m_, d) in enumerate(((0, 0), (1, 0), (2, 1), (3, 1))):
                    mm(pb[6][:, i * 128:(i + 1) * 128], tri[:, m_, :], tC_[:, d * 128:(d + 1) * 128], True, True, [tC_R_, CR], [pb_R[6]])
                if BCUT == 5:
                    continue
                for d in range(2):
                    mm(pb[5][:, 2 * d:2 + 2 * d], tC_[:, d * 128:(d + 1) * 128], onesf[:, 0:2], True, True, [tC_R_, CR], [pb_R[5]])
                act(GL[:, t, :], pb[5][:, 0:4].rearrange("p (d two) -> p d two", two=2)[:, :, 0], AF.Exp, [pb_R[5]], [GL_R], scale=-1.0 / 16)
                if BCUT == 6:
                    continue
                Ea = tA_; Eb = tB_
                act(Ea[:, 0:512], pb[6], AF.Exp, [pb_R[6]], [tA_R_], scale=-1.0 / 16)
                act(Eb[:, 0:256].rearrange("p (a c) -> p a c", a=2), pb[6].rearrange("p (a b c) -> p a b c", a=2, b=2)[:, :, 0, :], AF.Exp, [pb_R[6]], [tB_R_], scale=1.0 / 16)
                if BCUT == 7:
                    continue
                zq, zk = z[:, 0:128], z[:, 128:256]
                stt(qkb_[:, 0:128], zq, 32 ** -0.5, Ea[:, 0:128], MUL, MUL, zr + [tA_R_], [qkb_R_])
                stt(qkb_[:, 128:256], zq, 32 ** -0.5, Ea[:, 256:384], MUL, MUL, zr + [tA_R_], [qkb_R_])
                tt(qkb_[:, 256:384], zk, Eb[:, 0:128], MUL, zr + [tB_R_], [qkb_R_])
                tt(qkb_[:, 384:512], zk, Eb[:, 128:256], MUL, zr + [tB_R_], [qkb_R_])
                tt(KDc[:, t, 0, :], zk, Ea[:, 128:256], MUL, zr + [tA_R_], [KDc_R])
                tt(KDc[:, t, 1, :], zk, Ea[:, 384:512], MUL, zr + [tA_R_], [KDc_R])
                if BCUT == 8:
                    continue
                cp(VB[:, t, :], z[:, 256:512], zr, [VB_R], eng="act")
                act(GR[:, t, :], z[:, 512:768], AF.Silu, zr, [GR_R])
                if BCUT == 9:
                    continue
                for i in range(4):
                    tr(PSB[:, i * 128:(i + 1) * 128], qkb_[:, i * 128:(i + 1) * 128], identb[:], [qkb_R_], [pb_R[7]])
                if BCUT == 10:
                    continue
                for d in range(2):
                    cp(BT[:, t, d, 0:128], PSB[:, 256 + d * 128:384 + d * 128], [pb_R[7]], [BT_R], eng="act")
                    cp(BT[:, t, d, 128:256], PSB[:, d * 128:(d + 1) * 128], [pb_R[7]], [BT_R], eng="act")
                    tt(BT[:, t, d, 256:768].rearrange("p (h n) -> p h n", h=4), bc(PSB[:, d * 128:(d + 1) * 128].unsqueeze(1), [128, 4, 128]),
                       bc(hm[:].unsqueeze(2), [128, 4, 128]), MUL, [pb_R[7], CR], [BT_R])
            if STOP.startswith('B1'):
                continue
            hm3 = bc(hm[:].unsqueeze(2), [128, 4, 64])
            for si, (q0, qlen, kts) in enumerate(seqs):
                t0, nt = q0 // 128, qlen // 128
                for d in range(2):
                    Sf, Sr = Sst[d], Sst_R[d]
                    if S_GRP:
                        kb.dma("sp", Sf[:], (sbf_d if d == 0 else sbb_d)[l], writes=[Sr])
                    else:
                        kb.op("dve", lambda e, Sf=Sf: e.memset(Sf[:], 0.0), writes=[Sr])
                    order = range(t0, t0 + nt) if d == 0 else range(t0 + nt - 1, t0 - 1, -1)
                    for t in order:
                        tt(SIN[:, t, d, :].rearrange("p (h e) -> p h e", h=4), bc(Sf[:].unsqueeze(1), [128, 4, 64]), hm3, MUL, [Sr, CR], [SIN_R])
                        b = nb()
                        mm(pb[b][:, 0:256], KDc[:, t, d, :], VB[:, t, :], True, True, [KDc_R, VB_R], [pb_R[b]])
                        tt(tA[:, 0:256].rearrange("p (h e) -> p h e", h=4), pb[b][:, 0:256].rearrange("p (h e) -> p h e", h=4), hm3, MUL, [pb_R[b], CR], [tA_R])
                        kb.op("dve", lambda e: e.reduce_sum(out=tB[:, 0:64], in_=tA[:, 0:256].rearrange("p (h e) -> p e h", h=4), axis=AX.X), reads=[tA_R], writes=[tB_R])
                        stt(Sf[:], Sf[:], GL[:, t, d:d + 1], tB[:, 0:64], MUL, ADD, [Sr, GL_R, tB_R], [Sr])
                    if not S_GRP:
                        kb.dma("sp", (nbf_d if d == 0 else nbb_d)[si, l], Sf[:], reads=[Sr])
            if STOP == 'B2':
                continue
            for t in range(8):
                ob = nb()
                mm(pb[ob][:, 0:256], BT[:, t, 0, 128:256], SIN[:, t, 0, :], True, False, [BT_R, SIN_R], [pb_R[ob]])
                mm(pb[ob][:, 0:256], BT[:, t, 1, 128:256], SIN[:, t, 1, :], False, False, [BT_R, SIN_R], [pb_R[ob]])
                for d in range(2):
                    ab = nb()
                    while ab == ob:
                        ab = nb()
                    mm(pb[ab], BT[:, t, d, 0:128], BT[:, t, d, 256:768], True, True, [BT_R], [pb_R[ab]])
                    p = pt_i[0] % 3
                    pt_i[0] += 1
                    tt(PT[p][:].rearrange("p (h n) -> p h n", h=4), pb[ab].rearrange("p (h n) -> p h n", h=4),
                       bc(tri[:, (0 if d == 0 else 2), :].unsqueeze(1), [128, 4, 128]), MUL, [pb_R[ab], CR], [PT_R[p]])
                    for h in range(4):
                        mm(pb[ob][:, h * 64:(h + 1) * 64], PT[p][:, h * 128:(h + 1) * 128], VB[:, t, h * 64:(h + 1) * 64], False, (d == 1 and h == 3),
                           [PT_R[p], VB_R], [pb_R[ob]])
                o = pb[ob][:, 0:256]
                act(tA[:, 0:256], o, AF.Square, [pb_R[ob]], [tA_R])
                kb.op("dve", lambda e: e.reduce_sum(out=sm[:, 16:20], in_=tA[:, 0:256].rearrange("p (h d) -> p h d", h=4), axis=AX.X), reads=[tA_R], writes=[sm_R])
                act(sm[:, 16:20], sm[:, 16:20], AF.Ln, [sm_R, CR], [sm_R], scale=1.0 / 64, bias=epsT[:])
                act(sm[:, 16:20], sm[:, 16:20], AF.Exp, [sm_R], [sm_R], scale=-0.5)
                tt(tA[:, 0:256].rearrange("p (h d) -> p h d", h=4), o.rearrange("p (h d) -> p h d", h=4), bc(sm[:, 16:20].unsqueeze(2), [128, 4, 64]), MUL,
                   [pb_R[ob], sm_R], [tA_R])
                tt(tA[:, 0:256], tA[:, 0:256], gBO[:], MUL, [tA_R, LP], [tA_R])
                tt(qkb[:, 0:256], tA[:, 0:256], GR[:, t, :], MUL, [tA_R, GR_R], [qkb_R])
                for i in range(2):
                    tr(PSB[:, i * 128:(i + 1) * 128], qkb[:, i * 128:(i + 1) * 128], identb[:], [qkb_R], [pb_R[7]])
                cp(BR[:, 1, :, t * 128:(t + 1) * 128], PSB[:, 0:256].rearrange("p (b n) -> p b n", b=2), [pb_R[7]], [BR_R[1]])

            if STOP == 'B':
                continue
            bslot, bsr = new_slot()
            BW = bslot[:, 0:8192].rearrange("p (j k c) -> p j k c", j=4, k=2)
            for j in range(4):
                load_w(BW[:, j], wbr_d[l, j].rearrange("(k p) c -> p k c", p=128), bsr)
            barrier(kb)
            if DEBUG and l == 0 and g == GROUPS[0]:
                for jb in range(8):
                    o_, or_ = nstg()
                    cp(o_[:], BR[:, jb // 2, jb % 2, :], BR_R, [or_])
                    kb.dma("sp", dbg_br[:, jb, :], o_[:], reads=[or_])
            MG = av(0, 8192).rearrange("p (k n) -> p k n", k=8)
            for mp in range(4):
                slot, sr = new_slot(avoid=bslot)
                GWT = slot[:, 0:8192].rearrange("p (j k c) -> p j k c", j=4, k=8)
                for j in range(4):
                    c0 = 2432 + j * 1024 + mp * 256
                    load_w(GWT[:, j], win_d[l].rearrange("(k p) c -> p k c", p=128)[:, :, c0:c0 + 256], sr)
                for mi in range(2):
                    m = mp * 2 + mi
                    for h in range(2):
                        hs = slice(h * 512, (h + 1) * 512)
                        for j in range(4):
                            b1 = nb(); b2_ = nb()
                            for kc in range(8):
                                mm(pb[b1], GWT[:, j, kc, mi * 128:(mi + 1) * 128], hT[:, kc, hs], kc == 0, kc == 7, [sr, hT_R[h]], [pb_R[b1]])
                            for kc in range(2):
                                mm(pb[b2_], BW[:, j, kc, m * 128:(m + 1) * 128], BR[:, j, kc, hs], kc == 0, kc == 1, [bsr, BR_R[j]], [pb_R[b2_]])
                            act(tA[:, 0:512], pb[b1], AF.Sigmoid, [pb_R[b1]], [tA_R])
                            if j == 0:
                                tt(tC[:, 0:512], pb[b2_], tA[:, 0:512], MUL, [pb_R[b2_], tA_R], [tC_R])
                            else:
                                tt(tB[:, 0:512], pb[b2_], tA[:, 0:512], MUL, [pb_R[b2_], tA_R], [tB_R])
                                if j < 3:
                                    tt(tC[:, 0:512], tC[:, 0:512], tB[:, 0:512], ADD, [tC_R, tB_R], [tC_R])
                                else:
                                    tt(MG[:, m, hs], tC[:, 0:512], tB[:, 0:512], ADD, [tC_R, tB_R], [MG_R[h]])
            if STOP == 'M':
                continue
            slot, sr = new_slot()
            WO = slot[:, 0:8192].rearrange("p (k c) -> p k c", k=8)
            load_w(WO, wout_d[l].rearrange("(k p) c -> p k c", p=128), sr)
            for m in range(8):
                for h in range(2):
                    hs = slice(h * 512, (h + 1) * 512)
                    b = nb()
                    for kc in range(8):
                        mm(pb[b], WO[:, kc, m * 128:(m + 1) * 128], MG[:, kc, hs], kc == 0, kc == 7, [sr, MG_R[h]], [pb_R[b]])
                    stt(xT[:, m, hs], pb[b], modT[:, l, 16 + m, g:g + 1], xT[:, m, hs], MUL, ADD, [pb_R[b], MR, xT_R[m][h]], [xT_R[m][h]])
            if DEBUG and l == 0 and g == GROUPS[0]:
                for kc in range(8):
                    kb.dma("sp", dbg_x1[:, kc, :], xT[:, kc, :], reads=[xT_R[kc][0], xT_R[kc][1]])
                    o_, or_ = nstg()
                    cp(o_[:], MG[:, kc, :], MG_R, [or_])
                    kb.dma("sp", dbg_mg[:, kc, :], o_[:], reads=[or_])
            norm_mod(A2, 24, l, g)
            AT = av(8192, 22528).rearrange("p (f n) -> p f n", f=22); AT_R = [Res("AT0"), Res("AT1")]
            U_ = stg[0]; U_R = stg_R[0]; Cc = stg[1]; Cc_R = stg_R[1]
            nsq = 1 if S_GRP else 4
            sl = 1024 // nsq
            for f0 in range(0, 22, 4):
                nf = min(4, 22 - f0)
                slot, sr = new_slot()
                UW = slot[:, 0:4096].rearrange("p (k c) -> p k c", k=8)
                GW2 = slot[:, 4096:8192].rearrange("p (k c) -> p k c", k=8)
                load_w(UW[:, :, 0:nf * 128], wfu_d[l].rearrange("(k p) c -> p k c", p=128)[:, :, f0 * 128:(f0 + nf) * 128], sr)
                load_w(GW2[:, :, 0:nf * 128], wfg_d[l].rearrange("(k p) c -> p k c", p=128)[:, :, f0 * 128:(f0 + nf) * 128], sr)
                for fi in range(nf):
                    f = f0 + fi
                    gb_ = []
                    for h in range(2):
                        hs = slice(h * 512, (h + 1) * 512)
                        bu = nb(); bg = nb()
                        gb_.append(bg)
                        for kc in range(8):
                            mm(pb[bu], UW[:, kc, fi * 128:(fi + 1) * 128], hT[:, kc, hs], kc == 0, kc == 7, [sr, hT_R[h]], [pb_R[bu]])
                        for kc in range(8):
                            mm(pb[bg], GW2[:, kc, fi * 128:(fi + 1) * 128], hT[:, kc, hs], kc == 0, kc == 7, [sr, hT_R[h]], [pb_R[bg]])
                        cp(U_[:, hs], pb[bu], [pb_R[bu]], [U_R], eng="act")
                    act(Cc[:], U_[:], AF.Identity, [U_R, LP], [Cc_R], scale=cwT[:, 1, f:f + 1], bias=cbT[:, f:f + 1])
                    Uv = U_[:].rearrange("p (s n) -> p s n", s=nsq); Cv = Cc[:].rearrange("p (s n) -> p s n", s=nsq)
                    stt(Cv[:, :, 1:sl], Uv[:, :, 0:sl - 1], cwT[:, 0, f:f + 1], Cv[:, :, 1:sl], MUL, ADD, [U_R, Cc_R, LP], [Cc_R])
                    stt(Cv[:, :, 0:sl - 1], Uv[:, :, 1:sl], cwT[:, 2, f:f + 1], Cv[:, :, 0:sl - 1], MUL, ADD, [U_R, Cc_R, LP], [Cc_R])
                    act(tA[:], Cc[:], AF.Gelu_apprx_tanh, [Cc_R], [tA_R])
                    for h in range(2):
                        hs = slice(h * 512, (h + 1) * 512)
                        tt(AT[:, f, hs], tA[:, hs], pb[gb_[h]], MUL, [tA_R, pb_R[gb_[h]]], [AT_R[h]])
            for m0 in range(0, 8, 2):
                slot, sr = new_slot()
                WD = slot[:, 0:5632].rearrange("p (f c) -> p f c", f=22)
                load_w(WD, wfd_d[l].rearrange("(f p) c -> p f c", p=128)[:, :, m0 * 128:(m0 + 2) * 128], sr)
                for mi in range(2):
                    m = m0 + mi
                    for h in range(2):
                        hs = slice(h * 512, (h + 1) * 512)
                        b = nb()
                        for f in range(22):
                            mm(pb[b], WD[:, f, mi * 128:(mi + 1) * 128], AT[:, f, hs], f == 0, f == 21, [sr, AT_R[h]], [pb_R[b]])
                        stt(xT[:, m, hs], pb[b], modT[:, l, 40 + m, g:g + 1], xT[:, m, hs], MUL, ADD, [pb_R[b], MR, xT_R[m][h]], [xT_R[m][h]])

            if DEBUG and l == 0 and g == GROUPS[0]:
                for kc in range(8):
                    kb.dma("sp", dbg_x2[:, kc, :], xT[:, kc, :], reads=[xT_R[kc][0], xT_R[kc][1]])
        barrier(kb)
        kb.dma("sp", fgB, fg_d.partition_broadcast(128), writes=[fgB_R])
        for t in range(8):
            b0 = nb()
            while b0 >= 5:
                b0 = nb()
            for kc in range(8):
                bi = b0 + kc // 4
                tr(PS[:, bi * 512 + (kc % 4) * 128: bi * 512 + (kc % 4 + 1) * 128], xT[:, kc, t * 128:(t + 1) * 128], identf[:], [xT_R[kc][t // 4]], [pb_R[bi]])
            yp = PS[:, b0 * 512:b0 * 512 + 1024]
            yr = [pb_R[b0], pb_R[b0 + 1]]
            act(tA[:], yp, AF.Square, yr, [tA_R], accum_out=sm[:, 24:25])
            act(sm[:, 24:25], sm[:, 24:25], AF.Ln, [tA_R, CR], [sm_R], scale=1.0 / 1024, bias=epsT[:])
            act(sm[:, 24:25], sm[:, 24:25], AF.Exp, [sm_R], [sm_R], scale=-0.5)
            o_, or_ = nstg()
            stt(o_[:], yp, sm[:, 24:25], fgB, MUL, MUL, yr + [sm_R, fgB_R], [or_])
            kb.dma("sp", y_d[g, t * 128:(t + 1) * 128, :], o_[:], reads=[or_])
    return kb


def _consts():
    c = {}
    c["k_ident"] = np.eye(128, dtype=np.float32)
    bd = np.zeros((128, 128), np.float32); bd[:64, :64] = 1; bd[64:, 64:] = 1
    c["k_bd64"] = bd
    s = np.arange(128)[:, None]; t = np.arange(128)[None, :]
    c["k_tri"] = np.stack([(s <= t), (s > t), (s >= t), (s < t)]).astype(np.float32)
    hmk = np.zeros((128, 4), np.float32)
    for h in range(4):
        hmk[h * 32:(h + 1) * 32, h] = 1
    c["k_hm"] = hmk
    pmk = np.zeros((128, 2), np.float32)
    for p in range(128):
        pmk[p, (p // 32) % 2] = 1
    c["k_pm"] = pmk
    tok = np.arange(1024)
    row = (tok // 64).astype(np.float32); col = (tok % 64).astype(np.float32)

    def tab(half):
        inv = (10000.0 ** (-np.arange(half, dtype=np.float32) / half)).astype(np.float32)
        ar = row[:, None] * inv[None, :]; ac = col[:, None] * inv[None, :]
        return (np.concatenate([np.cos(ar), np.cos(ac)], 1).astype(np.float32), np.concatenate([np.sin(ar), np.sin(ac)], 1).astype(np.float32))
    c["k_cosA"], c["k_sinA"] = tab(16)
    cC, sC = tab(8)
    c["k_cosC"], c["k_sinC"] = cC, sC
    cr, cc_ = cC[:, 0:8], cC[:, 8:16]; sr_, sc_ = sC[:, 0:8], sC[:, 8:16]
    c["k_cosD"] = np.ascontiguousarray(np.concatenate([cr, cr, cc_, cc_], 1).T)
    c["k_sinD"] = np.ascontiguousarray(np.concatenate([-sr_, sr_, -sc_, sc_], 1).T)
    return c


WNAMES = ["w_mod", "b_mod", "norm1_g", "norm2_g", "w_in", "a_qnorm_g", "a_knorm_g", "b_gate_w_fwd", "b_gate_b_fwd", "b_gate_w_bwd",
          "b_gate_b_bwd", "b_onorm_g", "c_lq1", "c_lk1", "c_lq2", "c_lk2", "c_onorm_g", "d_qnorm_g", "d_w_uq", "d_kvnorm_g", "d_w_ukv",
          "w_branch", "w_out", "w_ffu", "w_ffg", "conv_w", "conv_b", "w_ffd", "final_g"]


def in_map(inp, c, consts, wts):
    m = dict(wts)
    m.update(consts)
    f = lambda a: np.ascontiguousarray(a, dtype=np.float32)
    m["x"] = f(np.stack([inp["x_prompt"][4 * c:4 * c + 4].reshape(1024, 1024), inp["x_sample"][c]]))
    m["cvec"] = f(np.stack([inp["c_ctx"], inp["c"][c]]))
    m["ca_k"] = f(inp["cache_a_k"][c].reshape(4, 512, 128)); m["ca_v"] = f(inp["cache_a_v"][c].reshape(4, 512, 128))
    m["sb_f"] = f(inp["state_b_fwd"][c].reshape(4, 128, 64)); m["sb_b"] = f(inp["state_b_bwd"][c].reshape(4, 128, 64))
    m["cc_k"] = f(inp["cache_c_k"][c].reshape(4, 512, 256)); m["cc_v"] = f(inp["cache_c_v"][c].reshape(4, 512, 256))
    m["cd_ckv"] = f(inp["cache_d_ckv"][c]); m["cd_kr"] = f(inp["cache_d_krope"][c])
    return m


def assemble(R):
    y_p = np.concatenate([r["y"][0].reshape(4, 256, 1024) for r in R], 0)
    y_s = np.stack([r["y"][1] for r in R], 0)

    def cat(name, shp):
        return np.concatenate([r[name].reshape((4,) + shp) for r in R], 0)
    return (y_p, y_s, cat("nak", (4, 256, 2, 64)), cat("nav", (4, 256, 2, 64)), cat("nbf", (4, 4, 32, 64)), cat("nbb", (4, 4, 32, 64)),
            cat("nck", (4, 256, 4, 2, 32)), cat("ncv", (4, 256, 4, 64)), cat("nckv", (4, 256, 128)), cat("nkr", (4, 256, 32)))


def kernel(**inp):
    inp = {k: np.asarray(v) for k, v in inp.items()}
    kb = build()
    nc = kb.finish()
    consts = _consts()
    wts = {k: np.ascontiguousarray(inp[k], dtype=np.float32) for k in WNAMES}
    in_maps = [in_map(inp, c, consts, wts) for c in range(8)]
    res = run_bass_kernel_spmd(nc, in_maps, core_ids=list(range(8)))
    return assemble(res.results)
```

```python
from contextlib import ExitStack
import numpy as np
import concourse.bass as bass
import concourse.mybir as mybir

F32 = mybir.dt.float32
BF16 = mybir.dt.bfloat16
AF = mybir.ActivationFunctionType
ALU = mybir.AluOpType
AX = mybir.AxisListType

EPOCH = 30000
NDMA_SEM = 20


class Res:
    __slots__ = ("name", "w", "r", "excl")

    def __init__(self, name="", excl=False):
        self.name = name
        self.excl = excl
        self.w = []
        self.r = []


class EngState:
    def __init__(self, name, handle_name, is_compute):
        self.name = name
        self.handle_name = handle_name
        self.is_compute = is_compute
        self.prog = []
        self.sem = None
        self.cnt = 0
        self.known = {}
        self.dma_ring = []
        self.dma_i = 0
        self.nops = 0


class KB:
    def __init__(self):
        self.nc = bass.Bass("TRN2", target_bir_lowering=False)
        self.es = ExitStack()
        self.E = {
            "pe": EngState("pe", "tensor", True),
            "act": EngState("act", "scalar", True),
            "dve": EngState("dve", "vector", True),
            "pool": EngState("pool", "gpsimd", True),
            "sp": EngState("sp", "sync", False),
        }
        self.nsem = 0
        self.all_dma_events = []

    def new_sem(self, name):
        self.nsem += 1
        return self.es.enter_context(self.nc.semaphore(f"{name}_{self.nsem}"))

    def sbuf(self, name, shape, dtype=F32):
        return self.es.enter_context(self.nc.sbuf_tensor(name, list(shape), dtype))

    def psum(self, name, shape, dtype=F32):
        return self.es.enter_context(self.nc.psum_tensor(name, list(shape), dtype))

    def dram(self, name, shape, dtype, kind):
        return self.nc.dram_tensor(name, list(shape), dtype, kind=kind)

    def _wait(self, st, ev):
        sem, val, _ = ev
        k = id(sem)
        if st.known.get(k, 0) >= val:
            return
        st.known[k] = val
        st.prog.append(("wait", sem, val))

    def _deps(self, st, reads, writes, is_dma):
        for r in reads:
            for ev in r.w:
                self._wait(st, ev)
            if r.excl:
                for ev in r.r:
                    if ev[2] != st.name:
                        self._wait(st, ev)
        for w in writes:
            for ev in w.w:
                if is_dma or ev[2] != st.name or not st.is_compute:
                    self._wait(st, ev)
            for ev in w.r:
                if is_dma or ev[2] != st.name or not st.is_compute:
                    self._wait(st, ev)

    def _commit(self, ev, reads, writes):
        for r in reads:
            if r in writes:
                continue
            if ev[2] in ("pe", "act", "dve", "pool"):
                r.r = [e for e in r.r if e[2] != ev[2]]
            r.r.append(ev)
        for w in writes:
            w.w = [ev]
            w.r = []

    @staticmethod
    def _flat(xs):
        out = []
        for x in xs:
            if isinstance(x, (list, tuple)):
                out.extend(KB._flat(x))
            else:
                out.append(x)
        return out

    def op(self, eng, fn, reads=(), writes=()):
        reads = self._flat(reads); writes = self._flat(writes)
        st = self.E[eng]
        assert st.is_compute
        self._deps(st, reads, writes, False)
        if st.sem is None or st.cnt >= EPOCH:
            st.sem = self.new_sem(f"s_{eng}")
            st.cnt = 0
        st.cnt += 1
        ev = (st.sem, st.cnt, eng)
        st.prog.append(("op", fn, st.sem))
        st.nops += 1
        self._commit(ev, reads, writes)
        return ev

    def dma(self, q, out, in_, reads=(), writes=(), **kw):
        st = self.E[q]
        reads = self._flat(reads); writes = self._flat(writes)
        kw.setdefault("allow_slow_non_contiguous", True)
        for r in reads:
            for ev in r.w:
                self._wait(st, ev)
        for w in writes:
            for ev in w.w:
                if ev[2].startswith("dma") and not w.r:
                    continue
                self._wait(st, ev)
            for ev in w.r:
                self._wait(st, ev)
        if not st.dma_ring:
            st.dma_ring = [[self.new_sem(f"d_{q}"), 0] for _ in range(NDMA_SEM)]
        slot = st.dma_ring[st.dma_i % NDMA_SEM]
        st.dma_i += 1
        if slot[1] > 0:
            self._wait(st, (slot[0], slot[1], "dma"))
        slot[1] += 16
        ev = (slot[0], slot[1], "dma_" + q)
        st.prog.append(("dma", out, in_, slot[0], kw))
        keep = {id(w): list(w.w) for w in writes if w.w and not w.r and all(e[2].startswith("dma") for e in w.w)}
        self._commit(ev, reads, writes)
        for w in writes:
            if id(w) in keep:
                w.w = keep[id(w)] + [ev]
        self.all_dma_events.append(ev)
        return ev

    def finish(self):
        nc = self.nc
        sp = self.E["sp"]
        for st in self.E.values():
            for sem, val in st.dma_ring:
                if val > 0:
                    self._wait(sp, (sem, val, "dma"))
        with nc.Block() as block:
            for st in self.E.values():
                if not st.prog:
                    continue

                def body(eng, st=st):
                    for item in st.prog:
                        if item[0] == "wait":
                            eng.wait_ge(item[1], item[2])
                        elif item[0] == "op":
                            ins = item[1](eng)
                            ins.then_inc(item[2], 1)
                        else:
                            _, out, in_, sem, kw = item
                            eng.dma_start(out=out, in_=in_, **kw).then_inc(sem, 16)

                getattr(block, st.handle_name)(body)
        self.es.close()
        return nc

import math
from concourse.bass_utils import run_bass_kernel_spmd

L = 4
EPS = 1e-6
MUL, ADD, SUB = ALU.mult, ALU.add, ALU.subtract
VOFFS = [0, 64, 192, 256]
VDST = [(0, 64, 0, 64), (128, 256, 64, 192), (320, 384, 192, 256)]


def barrier(kb):
    comp = ["pe", "act", "dve", "pool"]
    for e in comp + ["sp"]:
        st = kb.E[e]
        for o in comp:
            so = kb.E[o]
            if o != e and so.sem is not None and so.cnt > 0:
                kb._wait(st, (so.sem, so.cnt, o))
        for q in kb.E.values():
            for sem, val in q.dma_ring:
                if val > 0:
                    kb._wait(st, (sem, val, "dma"))


def build(NLAYERS=L, GROUPS=(0, 1), STOP='', DEBUG=False):
    kb = KB()
    nc = kb.nc
    BCUT = int(STOP.split(':')[1]) if ':' in STOP else 0

    def din(name, shape):
        return kb.dram(name, shape, F32, "ExternalInput").ap()

    def dout(name, shape):
        return kb.dram(name, shape, F32, "ExternalOutput").ap()

    x_d = din("x", [2, 1024, 1024])
    cvec_d = din("cvec", [2, 1024])
    cak_d = din("ca_k", [L, 512, 128]); cav_d = din("ca_v", [L, 512, 128])
    sbf_d = din("sb_f", [L, 128, 64]); sbb_d = din("sb_b", [L, 128, 64])
    cck_d = din("cc_k", [L, 512, 256]); ccv_d = din("cc_v", [L, 512, 256])
    cdc_d = din("cd_ckv", [L, 512, 128]); cdr_d = din("cd_kr", [L, 512, 32])
    wmod_d = din("w_mod", [L, 1024, 6144]); bmod_d = din("b_mod", [L, 6144])
    n1_d = din("norm1_g", [L, 1024]); n2_d = din("norm2_g", [L, 1024])
    win_d = din("w_in", [L, 1024, 6528])
    aqg_d = din("a_qnorm_g", [L, 64]); akg_d = din("a_knorm_g", [L, 64])
    gwf_d = din("b_gate_w_fwd", [L, 16, 128]); gbf_d = din("b_gate_b_fwd", [L, 128])
    gwb_d = din("b_gate_w_bwd", [L, 16, 128]); gbb_d = din("b_gate_b_bwd", [L, 128])
    bon_d = din("b_onorm_g", [L, 64])
    lq1_d = din("c_lq1", [L, 32]); lk1_d = din("c_lk1", [L, 32]); lq2_d = din("c_lq2", [L, 32]); lk2_d = din("c_lk2", [L, 32])
    con_d = din("c_onorm_g", [L, 64])
    dqg_d = din("d_qnorm_g", [L, 192]); wuq_d = din("d_w_uq", [L, 192, 384])
    dkg_d = din("d_kvnorm_g", [L, 128]); wukv_d = din("d_w_ukv", [L, 128, 512])
    wbr_d = din("w_branch", [L, 4, 256, 1024]); wout_d = din("w_out", [L, 1024, 1024])
    wfu_d = din("w_ffu", [L, 1024, 2816]); wfg_d = din("w_ffg", [L, 1024, 2816])
    cw_d = din("conv_w", [L, 3, 2816]); cb_d = din("conv_b", [L, 2816]); wfd_d = din("w_ffd", [L, 2816, 1024])
    fg_d = din("final_g", [1024])
    k_ident = din("k_ident", [128, 128]); k_bd64 = din("k_bd64", [128, 128]); k_tri = din("k_tri", [4, 128, 128])
    k_hm = din("k_hm", [128, 4]); k_pm = din("k_pm", [128, 2])
    k_cosA = din("k_cosA", [1024, 32]); k_sinA = din("k_sinA", [1024, 32])
    k_cosC = din("k_cosC", [1024, 16]); k_sinC = din("k_sinC", [1024, 16])
    k_cosD = din("k_cosD", [32, 1024]); k_sinD = din("k_sinD", [32, 1024])

    y_d = dout("y", [2, 1024, 1024])
    nak_d = dout("nak", [4, L, 256, 128]); nav_d = dout("nav", [4, L, 256, 128])
    nbf_d = dout("nbf", [4, L, 128, 64]); nbb_d = dout("nbb", [4, L, 128, 64])
    nck_d = dout("nck", [4, L, 256, 256]); ncv_d = dout("ncv", [4, L, 256, 256])
    nckv_d = dout("nckv", [4, L, 256, 128]); nkr_d = dout("nkr", [4, L, 256, 32])

    if DEBUG:
        dbg_br = dout('dbg_br', [128, 8, 1024]); dbg_x1 = dout('dbg_x1', [128, 8, 1024]); dbg_x2 = dout('dbg_x2', [128, 8, 1024]); dbg_mg = dout('dbg_mg', [128, 8, 1024])
    xT = kb.sbuf("xT", [128, 8, 1024]); xT_R = [[Res(f"xT{k}_{h}") for h in range(2)] for k in range(8)]
    hT = kb.sbuf("hT", [128, 8, 1024], BF16); hT_R = [Res("hT0"), Res("hT1")]
    NSLOT = 3
    ring = [kb.sbuf(f"ring{i}", [128, 8192], BF16) for i in range(NSLOT)]
    ring_R = [Res(f"ring{i}") for i in range(NSLOT)]
    ring_i = [0]
    arena = kb.sbuf("arena", [128, 30720], BF16)
    PS = kb.psum("ps", [128, 4096])
    pb = [PS[:, i * 512:(i + 1) * 512] for i in range(8)]
    pb_R = [Res(f"pb{i}", excl=True) for i in range(8)]
    PSB = PS[:, 7 * 512:8 * 512].bitcast(BF16)

    def new_slot(avoid=None):
        i = ring_i[0] % NSLOT
        ring_i[0] += 1
        while avoid is not None and ring[i] is avoid:
            i = ring_i[0] % NSLOT
            ring_i[0] += 1
        return ring[i], ring_R[i]

    def av(off, n):
        return arena[:, off:off + n]

    def vap(off, stride):
        return bass.AP(arena, off, [[30720, 128], [stride, 2], [1, 64]])

    identf = kb.sbuf("identf", [128, 128]); identb = kb.sbuf("identb", [128, 128], BF16)
    onesb = kb.sbuf("onesb", [128, 128], BF16); bd64 = kb.sbuf("bd64", [128, 128], BF16)
    tri = kb.sbuf("tri", [128, 4, 128]); hm = kb.sbuf("hm", [128, 4]); pm = kb.sbuf("pm", [128, 2])
    onescol = kb.sbuf("onescol", [128, 1]); onesrow = kb.sbuf("onesrow", [1, 128]); epsT = kb.sbuf("epsT", [128, 1])
    onesf = kb.sbuf("onesf", [128, 128])
    cosA = kb.sbuf("cosA", [128, 8, 32]); sinA = kb.sbuf("sinA", [128, 8, 32])
    cosC = kb.sbuf("cosC", [128, 8, 16]); sinC = kb.sbuf("sinC", [128, 8, 16])
    cosD = kb.sbuf("cosD", [128, 1024]); sinD = kb.sbuf("sinD", [128, 1024])
    CR = Res("consts")
    kb.dma("sp", identf[:], k_ident, writes=[CR])
    kb.dma("pool", identb[:], k_ident, writes=[CR])
    kb.dma("pool", bd64[:], k_bd64, writes=[CR])
    kb.dma("sp", tri[:], k_tri.rearrange("m s t -> s m t"), writes=[CR])
    kb.dma("sp", hm[:], k_hm, writes=[CR]); kb.dma("sp", pm[:], k_pm, writes=[CR])
    kb.dma("sp", cosA[:], k_cosA.rearrange("(t p) d -> p t d", p=128), writes=[CR])
    kb.dma("sp", sinA[:], k_sinA.rearrange("(t p) d -> p t d", p=128), writes=[CR])
    kb.dma("sp", cosC[:], k_cosC.rearrange("(t p) d -> p t d", p=128), writes=[CR])
    kb.dma("sp", sinC[:], k_sinC.rearrange("(t p) d -> p t d", p=128), writes=[CR])
    kb.dma("sp", cosD[64:96, :], k_cosD, writes=[CR]); kb.dma("sp", sinD[64:96, :], k_sinD, writes=[CR])
    kb.op("dve", lambda e: e.memset(onesb[:], 1.0), writes=[CR])
    kb.op("dve", lambda e: e.memset(onescol[:], 1.0), writes=[CR])
    kb.op("dve", lambda e: e.memset(onesrow[:], 1.0), writes=[CR])
    kb.op("dve", lambda e: e.memset(onesf[:], 1.0), writes=[CR])
    kb.op("dve", lambda e: e.memset(epsT[:], EPS), writes=[CR])

    def mm(out, lhsT, rhs, start, stop, reads, writes):
        kb.op("pe", lambda e: e.matmul(out, lhsT=lhsT, rhs=rhs, start=start, stop=stop), reads=reads, writes=writes)

    def tr(out, in_, ident, reads, writes):
        kb.op("pe", lambda e: e.transpose(out=out, in_=in_, identity=ident), reads=reads + [CR], writes=writes)

    def act(out, in_, func, reads, writes, scale=1.0, bias=None, accum_out=None):
        kw = {}
        if bias is not None:
            kw["bias"] = bias
        if accum_out is not None:
            kw["accum_out"] = accum_out
        kb.op("act", lambda e: e.activation(out=out, in_=in_, func=func, scale=scale, **kw), reads=reads, writes=writes)

    def tt(out, in0, in1, op, reads, writes, eng="dve"):
        kb.op(eng, lambda e: e.tensor_tensor(out=out, in0=in0, in1=in1, op=op), reads=reads, writes=writes)

    def stt(out, in0, scalar, in1, op0, op1, reads, writes):
        kb.op("dve", lambda e: e.scalar_tensor_tensor(out=out, in0=in0, scalar=scalar, in1=in1, op0=op0, op1=op1), reads=reads, writes=writes)

    def ts(out, in0, s1, op0, reads, writes, s2=None, op1=None, eng="dve"):
        if op1 is None:
            kb.op(eng, lambda e: e.tensor_scalar(out=out, in0=in0, scalar1=s1, scalar2=None, op0=op0), reads=reads, writes=writes)
        else:
            kb.op(eng, lambda e: e.tensor_scalar(out=out, in0=in0, scalar1=s1, scalar2=s2, op0=op0, op1=op1), reads=reads, writes=writes)

    def cp(out, in_, reads, writes, eng="dve"):
        if eng == "act":
            kb.op("act", lambda e: e.copy(out=out, in_=in_), reads=reads, writes=writes)
        else:
            kb.op(eng, lambda e: e.tensor_copy(out=out, in_=in_), reads=reads, writes=writes)

    def recipf(out, in_, reads, writes):
        act(out, in_, AF.Ln, reads, writes)
        act(out, out, AF.Exp, writes, writes, scale=-1.0)

    def recip(out, in_, reads, writes):
        kb.op("dve", lambda e: e.reciprocal(out=out, in_=in_), reads=reads, writes=writes)

    def bc(ap, shape):
        return ap.broadcast_to(list(shape))

    pbi = [0]

    def nb():
        i = pbi[0] % 7
        pbi[0] += 1
        return i

    scT = kb.sbuf("scT", [128, 8, 2]); modT = kb.sbuf("modT", [128, L, 48, 2]); bmT = kb.sbuf("bmT", [128, L, 48])
    n1T = kb.sbuf("n1T", [128, L, 8]); n2T = kb.sbuf("n2T", [128, L, 8])
    A1 = kb.sbuf("A1", [128, L, 8, 2]); A2 = kb.sbuf("A2", [128, L, 8, 2])
    MR = Res("mod")
    with nc.allow_non_contiguous_dma(reason="tiny transposed vector loads"):
        for g_ in range(2):
            kb.dma("sp", scT[:, :, g_], cvec_d[g_].rearrange("(k p) -> p k", p=128), writes=[MR])
        for l_ in range(L):
            kb.dma("sp", bmT[:, l_, :], bmod_d[l_].rearrange("(j p) -> p j", p=128), writes=[MR])
            kb.dma("sp", n1T[:, l_, :], n1_d[l_].rearrange("(k p) -> p k", p=128), writes=[MR])
            kb.dma("sp", n2T[:, l_, :], n2_d[l_].rearrange("(k p) -> p k", p=128), writes=[MR])
    act(scT[:], scT[:], AF.Silu, [MR], [MR])
    scb = kb.sbuf("scb", [128, 8, 2], BF16)
    cp(scb[:], scT[:], [MR], [MR])
    for l in range(L):
        for cb in range(8):
            slot, wr = new_slot()
            w = slot[:, 0:6144].rearrange("p (k c) -> p k c", k=8)
            kb.dma("pool", w, wmod_d[l].rearrange("(k p) c -> p k c", p=128)[:, :, cb * 768:(cb + 1) * 768], writes=[wr])
            b = nb()
            for jj in range(6):
                for kc in range(8):
                    mm(pb[b][:, jj * 2:(jj + 1) * 2], w[:, kc, jj * 128:(jj + 1) * 128], scb[:, kc, :], kc == 0, kc == 7, [wr, MR], [pb_R[b]])
            tt(modT[:, l, cb * 6:(cb + 1) * 6, :], pb[b][:, 0:12].rearrange("p (j g) -> p j g", g=2),
               bc(bmT[:, l, cb * 6:(cb + 1) * 6].unsqueeze(2), [128, 6, 2]), ADD, [pb_R[b], MR], [MR])
    for l in range(L):
        stt(A1[:, l], modT[:, l, 8:16, :], 1.0, bc(n1T[:, l, :].unsqueeze(2), [128, 8, 2]), ADD, MUL, [MR], [MR])
        stt(A2[:, l], modT[:, l, 32:40, :], 1.0, bc(n2T[:, l, :].unsqueeze(2), [128, 8, 2]), ADD, MUL, [MR], [MR])

    lqk = kb.sbuf("lqk", [32, 4, L]); lamT = kb.sbuf("lamT", [128, L]); neglam = kb.sbuf("neglam", [128, L])
    ocs = kb.sbuf("ocs", [128, L]); lam2 = kb.sbuf("lam2", [128, 2, L])
    with nc.allow_non_contiguous_dma(reason="tiny transposed vector loads"):
        for i, d in enumerate([lq1_d, lk1_d, lq2_d, lk2_d]):
            kb.dma("sp", lqk[:, i, :], d.rearrange("l d -> d l"), writes=[MR])
        kb.dma("sp", ocs[0:64, :], con_d.rearrange("l d -> d l"), writes=[MR])
        kb.dma("sp", ocs[64:128, :], con_d.rearrange("l d -> d l"), writes=[MR])
    tt(lqk[:, 0, :], lqk[:, 0, :], lqk[:, 1, :], MUL, [MR], [MR])
    tt(lqk[:, 2, :], lqk[:, 2, :], lqk[:, 3, :], MUL, [MR], [MR])
    b = nb()
    mm(pb[b][:, 0:L], onesf[0:32, :], lqk[:, 0, :], True, True, [MR, CR], [pb_R[b]])
    mm(pb[b][:, L:2 * L], onesf[0:32, :], lqk[:, 2, :], True, True, [MR, CR], [pb_R[b]])
    act(lam2[:].rearrange("p a l -> p (a l)"), pb[b][:, 0:2 * L], AF.Exp, [pb_R[b]], [MR])
    tt(lamT[:], lam2[:, 0, :], lam2[:, 1, :], SUB, [MR], [MR])
    lam_init = [0.8 - 0.6 * math.exp(-0.3 * l) for l in range(L)]
    for l in range(L):
        ts(lamT[:, l:l + 1], lamT[:, l:l + 1], lam_init[l], ADD, [MR], [MR])
        ts(ocs[:, l:l + 1], ocs[:, l:l + 1], 1.0 - lam_init[l], MUL, [MR], [MR])
    ts(neglam[:], lamT[:], -1.0, MUL, [MR], [MR])

    gA = kb.sbuf("gA", [128, 384]); gDQ = kb.sbuf("gDQ", [128, 192]); gKV = kb.sbuf("gKV", [128, 128]); gBO = kb.sbuf("gBO", [128, 256])
    GW = kb.sbuf("GW", [16, 256]); GBias = kb.sbuf("GBias", [1, 256]); cwT = kb.sbuf("cwT", [128, 3, 22]); cbT = kb.sbuf("cbT", [128, 22])
    fgB = arena[:, 0:2048].bitcast(F32); fgB_R = Res("fgB")
    LP = Res("layer_params")

    def load_layer_params(l):
        def bsrc(d, n, rep):
            return bass.AP(d.tensor, d[l].offset, [[0, 128], [0, rep], [1, n]])
        kb.dma("sp", gA[:, 0:256].rearrange("p (r d) -> p r d", r=4), bsrc(aqg_d, 64, 4), writes=[LP])
        kb.dma("sp", gA[:, 256:384].rearrange("p (r d) -> p r d", r=2), bsrc(akg_d, 64, 2), writes=[LP])
        kb.dma("sp", gDQ[:], dqg_d[l].partition_broadcast(128), writes=[LP])
        kb.dma("sp", gKV[:], dkg_d[l].partition_broadcast(128), writes=[LP])
        kb.dma("sp", gBO[:].rearrange("p (r d) -> p r d", r=4), bsrc(bon_d, 64, 4), writes=[LP])
        kb.dma("sp", GW[0:16, 0:128], gwf_d[l], writes=[LP]); kb.dma("sp", GW[0:16, 128:256], gwb_d[l], writes=[LP])
        kb.dma("sp", GBias[0:1, 0:128], gbf_d[l:l + 1, :], writes=[LP]); kb.dma("sp", GBias[0:1, 128:256], gbb_d[l:l + 1, :], writes=[LP])
        with nc.allow_non_contiguous_dma(reason="tiny transposed vector loads"):
            for w_ in range(3):
                kb.dma("sp", cwT[:, w_, :], cw_d[l, w_].rearrange("(f p) -> p f", p=128), writes=[LP])
            kb.dma("sp", cbT[:], cb_d[l].rearrange("(f p) -> p f", p=128), writes=[LP])

    stg = [kb.sbuf(f"stg{i}", [128, 1024]) for i in range(2)]; stg_R = [Res("stg0"), Res("stg1")]
    stg_i = [0]

    def nstg():
        i = stg_i[0] % 2
        stg_i[0] += 1
        return stg[i], stg_R[i]

    tA = kb.sbuf("tA", [128, 1024]); tA_R = (Res("tA0"), Res("tA1"))
    tB = kb.sbuf("tB", [128, 1024]); tB_R = (Res("tB0"), Res("tB1"))
    tC = kb.sbuf("tC", [128, 512]); tC_R = Res("tC")
    tC1 = kb.sbuf("tC1", [128, 512]); tC1_R = Res("tC1")
    sm = kb.sbuf("sm", [128, 64]); sm_R = Res("sm")
    sm1 = kb.sbuf("sm1", [128, 64]); sm1_R = Res("sm1")
    rstd = tC1; rstd_R = tC1_R
    sqb = arena[:, 0:4096].rearrange("p (k n) -> p k n", k=8); sqb_R = Res("sqb")
    MG_R = [Res("MG0"), Res("MG1")]
    PT = [kb.sbuf(f"PT{i}", [128, 512], BF16) for i in range(3)]; PT_R = [Res(f"PT{i}") for i in range(3)]
    pt_i = [0]
    qkb = kb.sbuf("qkb", [128, 512], BF16); qkb_R = Res("qkb")
    qkb1 = kb.sbuf("qkb1", [128, 512], BF16); qkb1_R = Res("qkb1")
    krpad = kb.sbuf("krpad", [128, 96], BF16); krpad_R = Res("krpad")
    krpad1 = kb.sbuf("krpad1", [128, 96], BF16); krpad1_R = Res("krpad1")
    kb.op("dve", lambda e: e.memset(krpad[:], 0.0), writes=[krpad_R])
    kb.op("dve", lambda e: e.memset(krpad1[:], 0.0), writes=[krpad1_R])
    Sst = [kb.sbuf(f"Sst{i}", [128, 64]) for i in range(2)]; Sst_R = [Res("Sf"), Res("Sb")]
    GL = kb.sbuf("GL", [128, 8, 2]); GL_R = Res("GL")
    zgT = tC[0:16, 256:512]; zgT_R = Res("zgT")
    zgT1 = tC1[0:16, 256:512]; zgT1_R = Res("zgT1")
    TSET = [dict(tA_=tA[:, 0:512], tA_R_=tA_R[0], tB_=tB[:, 0:512], tB_R_=tB_R[0], tC_=tC, tC_R_=tC_R, sm_=sm, sm_R_=sm_R, qkb_=qkb, qkb_R_=qkb_R,
                 krpad_=krpad, krpad_R_=krpad_R, zgT_=zgT, zgT_R_=zgT_R),
            dict(tA_=tA[:, 512:1024], tA_R_=tA_R[1], tB_=tB[:, 512:1024], tB_R_=tB_R[1], tC_=tC1, tC_R_=tC1_R, sm_=sm1, sm_R_=sm1_R, qkb_=qkb1, qkb_R_=qkb1_R,
                 krpad_=krpad1, krpad_R_=krpad1_R, zgT_=zgT1, zgT_R_=zgT1_R)]

    def norm_mod(Asc, shift_c0, l, g):
        for h in range(2):
            hs = slice(h * 512, (h + 1) * 512)
            xr = [xT_R[k][h] for k in range(8)]
            act(sqb, xT[:, :, hs], AF.Square, xr, [sqb_R, MG_R[0], MG_R[1]])
            b = nb()
            for kc in range(8):
                mm(pb[b], onesb[:], sqb[:, kc, :], kc == 0, kc == 7, [sqb_R, CR], [pb_R[b]])
            act(rstd[:], pb[b], AF.Ln, [pb_R[b], CR], [rstd_R], scale=1.0 / 1024, bias=epsT[:])
            act(rstd[:], rstd[:], AF.Exp, [rstd_R], [rstd_R], scale=-0.5)
            for kc in range(8):
                tmp, tr_ = (tA, tA_R) if kc % 2 == 0 else (tB, tB_R)
                tt(tmp[:, 0:512], xT[:, kc, hs], rstd[:], MUL, [xT_R[kc][h], rstd_R], [tr_])
                act(hT[:, kc, hs], tmp[:, 0:512], AF.Identity, [tr_, MR], [hT_R[h]],
                    scale=Asc[:, l, kc, g:g + 1], bias=modT[:, l, shift_c0 + kc, g:g + 1])

    def load_w(dst, src, sr):
        kb.dma("pool", dst, src, writes=[sr])

    def proj_tm(bank0, ncols, W, wr, t):
        nbk = (ncols + 511) // 512
        for bi in range(nbk):
            c0 = bi * 512
            c1 = min(ncols, c0 + 512)
            for kc in range(8):
                mm(pb[bank0 + bi][:, 0:c1 - c0], hT[:, kc, t * 128:(t + 1) * 128], W[:, kc, c0:c1], kc == 0, kc == 7,
                   [hT_R[t // 4], wr], [pb_R[bank0 + bi]])

    def rope_tm(src, src_R, U2, hs, cos_t, sin_t, out, tA, tA_R, tB, tB_R, qkb_R):
        xv = src.rearrange("p (u r h i) -> p u r h i", u=U2, r=2, h=2)
        ov = out.rearrange("p (u r h i) -> p u r h i", u=U2, r=2, h=2)
        cv = bc(cos_t.rearrange("p (r i) -> p r i", r=2).unsqueeze(1), [128, U2, 2, hs])
        sv = bc(sin_t.rearrange("p (r i) -> p r i", r=2).unsqueeze(1), [128, U2, 2, hs])
        n = U2 * 2 * hs
        a = tA[:, 0:n].rearrange("p (u r i) -> p u r i", u=U2, r=2)
        b2 = tB[:, 0:n].rearrange("p (u r i) -> p u r i", u=U2, r=2)
        x0 = xv[:, :, :, 0, :]; x1 = xv[:, :, :, 1, :]
        tt(a, x0, cv, MUL, [src_R, CR], [tA_R]); tt(b2, x1, sv, MUL, [src_R, CR], [tB_R])
        tt(ov[:, :, :, 0, :], a, b2, SUB, [tA_R, tB_R], [qkb_R])
        tt(a, x1, cv, MUL, [src_R, CR], [tA_R]); tt(b2, x0, sv, MUL, [src_R, CR], [tB_R])
        tt(ov[:, :, :, 1, :], a, b2, ADD, [tA_R, tB_R], [qkb_R])

    def attend(kT, qT, vT, kts, q0, N, scale, reads, ob, avoid=()):
        sb = [None] * len(kts)

        def S(i):
            b = nb()
            while b == ob or b in avoid:
                b = nb()
            sb[i] = b
            mm(pb[b][:, 0:N], kT(kts[i]), qT, True, True, reads, [pb_R[b]])
        S(0)
        for i in range(len(kts)):
            if i + 1 < len(kts):
                S(i + 1)
            p = pt_i[0] % 3
            pt_i[0] += 1
            act(PT[p][:, 0:N], pb[sb[i]][:, 0:N], AF.Exp, [pb_R[sb[i]]], [PT_R[p]], scale=scale)
            mm(pb[ob][:, 0:N], vT(kts[i]), PT[p][:, 0:N], i == 0, i == len(kts) - 1, reads + [PT_R[p]], [pb_R[ob]])

    def obank():
        b = nb()
        return b

    def attend_multi(jobs, finishers, LA=2):
        flat = [(ji, i) for ji, jb in enumerate(jobs) for i in range(len(jb["kts"]))]
        sbank = {}
        ob_of = {}
        live = []
        gobs = {}

        def alloc():
            b = nb()
            while b in live or b in sbank.values():
                b = nb()
            return b

        def S(idx):
            ji, i = flat[idx]
            jb = jobs[ji]
            b = alloc()
            sbank[idx] = b
            mm(pb[b][:, 0:jb["N"]], jb["kT"](jb["kts"][i]), jb["qT"], True, True, jb["reads"], [pb_R[b]])
        for idx in range(min(LA, len(flat))):
            S(idx)
        for idx, (ji, i) in enumerate(flat):
            if idx + LA < len(flat):
                S(idx + LA)
            jb = jobs[ji]
            N = jb["N"]
            if i == 0:
                ob_of[ji] = alloc()
                live.append(ob_of[ji])
            ob = ob_of[ji]
            p = pt_i[0] % 3
            pt_i[0] += 1
            sb_ = sbank.pop(idx)
            act(PT[p][:, 0:N], pb[sb_][:, 0:N], AF.Exp, [pb_R[sb_]], [PT_R[p]], scale=jb["scale"])
            last = (i == len(jb["kts"]) - 1)
            mm(pb[ob][:, 0:N], jb["vT"](jb["kts"][i]), PT[p][:, 0:N], i == 0, last, jb["reads"] + [PT_R[p]], [pb_R[ob]])
            if last:
                gid = jb["gid"]
                gobs.setdefault(gid, []).append(ob)
                if ji + 1 >= len(jobs) or jobs[ji + 1]["gid"] != gid:
                    finishers[gid](gobs[gid])
                    for o in gobs[gid]:
                        live.remove(o)

    for g in GROUPS:
        S_GRP = (g == 1)
        NKT = 12 if S_GRP else 8
        KOFF = 4 if S_GRP else 0
        NK = NKT * 128
        if S_GRP:
            seqs = [(0, 1024, list(range(12)))]
        else:
            seqs = [(s * 256, 256, [2 * s, 2 * s + 1]) for s in range(4)]
        for t in range(8):
            st_, sr = nstg()
            kb.dma("sp", st_[:], x_d[g, t * 128:(t + 1) * 128, :], writes=[sr])
            for hh in range(2):
                b = nb()
                for kk in range(4):
                    kc = hh * 4 + kk
                    tr(pb[b][:, kk * 128:(kk + 1) * 128], st_[:, kc * 128:(kc + 1) * 128], identf[:], [sr], [pb_R[b]])
                cp(xT[:, hh * 4:(hh + 1) * 4, t * 128:(t + 1) * 128], pb[b].rearrange("p (k c) -> p k c", k=4), [pb_R[b]],
                   [xT_R[k][t // 4] for k in range(hh * 4, hh * 4 + 4)], eng=("act" if hh else "dve"))

        for l in range(NLAYERS):
            load_layer_params(l)
            norm_mod(A1, 0, l, g)
            if STOP == 'N':
                continue
            BR_OFF = 22528
            BR = arena[:, BR_OFF:BR_OFF + 8192].rearrange("p (j b n) -> p j b n", j=4, b=2)
            BR_R = [Res(f"BR{j}") for j in range(4)]
            OUTS = not S_GRP

            slot, sr = new_slot()
            W = slot[:, 0:4096].rearrange("p (k c) -> p k c", k=8)
            wsrc = win_d[l].rearrange("(k p) c -> p k c", p=128)
            for g2_ in range(2):
                for kv_ in range(2):
                    load_w(W[:, :, g2_ * 128 + kv_ * 64:g2_ * 128 + kv_ * 64 + 64], wsrc[:, :, (kv_ * 2 + g2_) * 64:(kv_ * 2 + g2_) * 64 + 64], sr)
            load_w(W[:, :, 256:512], wsrc[:, :, 256:512], sr)
            barrier(kb)
            QT = av(0, 2048).rearrange("p (b n) -> p b n", b=2); QT_R = Res("QT_A")
            KT = av(2048, 1536); KT_R = Res("KT_A")
            VA = av(3584, 4608).rearrange("p (k c) -> p k c", c=384); VA_R = Res("VA")
            kb.op("dve", lambda e: e.memset(VA[:, 0:NKT, :], 1.0), writes=[VA_R])
            if S_GRP:
                st_, sr2 = nstg()
                kb.dma("sp", st_[:, 0:512].rearrange("p (k f) -> p k f", k=4), cak_d[l].rearrange("(k p) f -> p k f", p=128), writes=[sr2])
                b = nb()
                for kt in range(4):
                    tr(pb[b][:, kt * 128:(kt + 1) * 128], st_[:, kt * 128:(kt + 1) * 128], identf[:], [sr2], [pb_R[b]])
                cp(KT[:, 0:512], pb[b], [pb_R[b]], [KT_R])
                kb.dma("pool", VA[:, 0:4, 64:128], cav_d[l].rearrange("(k p) f -> p k f", p=128)[:, :, 0:64], writes=[VA_R])
                kb.dma("pool", VA[:, 0:4, 256:320], cav_d[l].rearrange("(k p) f -> p k f", p=128)[:, :, 64:128], writes=[VA_R])
            if STOP == 'A0':
                continue
            def tile_gen(t):
                TS_ = TSET[t % 2]; tA_ = TS_['tA_']; tA_R_ = TS_['tA_R_']; tB_ = TS_['tB_']; tB_R_ = TS_['tB_R_']; tC_ = TS_['tC_']; tC_R_ = TS_['tC_R_']; sm_ = TS_['sm_']; sm_R_ = TS_['sm_R_']; qkb_ = TS_['qkb_']; qkb_R_ = TS_['qkb_R_']; krpad_ = TS_['krpad_']; krpad_R_ = TS_['krpad_R_']; zgT_ = TS_['zgT_']; zgT_R_ = TS_['zgT_R_']
                zb = t % 2
                PSBp = PSB[:, (t % 2) * 512:(t % 2) * 512 + 512]
                z = pb[zb]; zr = pb_R[zb]
                for kc in range(8):
                    mm(z, hT[:, kc, t * 128:(t + 1) * 128], W[:, kc, :], kc == 0, kc == 7, [hT_R[t // 4], sr], [zr])
                if STOP == 'A2a':
                    return
                yield
                act(tC_[:, 0:384], z[:, 0:384], AF.Square, [zr], [tC_R_])
                kb.op("dve", lambda e, sm_=sm_, tC_=tC_: e.reduce_sum(out=sm_[:, 0:6], in_=tC_[:, 0:384].rearrange("p (h d) -> p h d", h=6), axis=AX.X), reads=[tC_R_], writes=[sm_R_])
                if STOP == 'A2b':
                    return
                act(sm_[:, 0:6], sm_[:, 0:6], AF.Ln, [sm_R_, CR], [sm_R_], scale=1.0 / 64, bias=epsT[:])
                act(sm_[:, 0:6], sm_[:, 0:6], AF.Exp, [sm_R_], [sm_R_], scale=-0.5)
                if STOP == 'A2c':
                    return
                yield
                tt(tC_[:, 0:384].rearrange("p (h d) -> p h d", h=6), z[:, 0:384].rearrange("p (h d) -> p h d", h=6),
                   bc(sm_[:, 0:6].unsqueeze(2), [128, 6, 64]), MUL, [zr, sm_R_], [tC_R_])
                tt(tC_[:, 0:384], tC_[:, 0:384], gA[:], MUL, [tC_R_, LP], [tC_R_])
                kt = KOFF + t
                if STOP == 'A2':
                    return
                yield
                cp(VA[:, kt, 64:128], z[:, 384:448], [zr], [VA_R], eng="act")
                cp(VA[:, kt, 256:320], z[:, 448:512], [zr], [VA_R], eng="act")
                if S_GRP:
                    rope_tm(tC_[:, 0:384], tC_R_, 6, 16, cosA[:, t, :], sinA[:, t, :], qkb_[:, 0:384], tA_, tA_R_, tB_, tB_R_, qkb_R_)
                else:
                    cp(qkb_[:, 0:384], tC_[:, 0:384], [tC_R_], [qkb_R_])
                    o_, or_ = nstg()
                    cp(o_[:, 0:128], tC_[:, 256:384], [tC_R_], [or_], eng="act")
                    cp(o_[:, 128:256], z[:, 384:512], [zr], [or_], eng="act")
                    s_, tt_ = t // 2, (t % 2) * 128
                    kb.dma("sp", nak_d[s_, l, tt_:tt_ + 128, :], o_[:, 0:128], reads=[or_])
                    kb.dma("sp", nav_d[s_, l, tt_:tt_ + 128, :], o_[:, 128:256], reads=[or_])
                if STOP == 'A3':
                    return
                yield
                for g2 in range(2):
                    tr(PSBp[:, g2 * 128:(g2 + 1) * 128], qkb_[:, g2 * 128:(g2 + 1) * 128], identb[:], [qkb_R_], [pb_R[7]])
                tr(PSBp[:, 256:384], qkb_[:, 256:384], identb[:], [qkb_R_], [pb_R[7]])
                if STOP == 'A4':
                    return
                yield
                if STOP != 'A6':
                    cp(QT[:, :, t * 128:(t + 1) * 128], PSBp[:, 0:256].rearrange("p (b n) -> p b n", b=2), [pb_R[7]], [QT_R])
                if STOP != 'A5':
                    cp(KT[:, kt * 128:(kt + 1) * 128], PSBp[:, 256:384], [pb_R[7], QT_R], [KT_R], eng="act")
                yield
            gens_ = [tile_gen(t_) for t_ in range(8)]
            for t0_ in range(0, 8, 2):
                act_ = [gens_[t0_], gens_[t0_ + 1]]
                while act_:
                    for g_ in list(act_):
                        try:
                            next(g_)
                        except StopIteration:
                            act_.remove(g_)
            if STOP in ('A1', 'A2', 'A2a', 'A2b', 'A2c', 'A3', 'A4', 'A5', 'A6'):
                continue
            jobs = []; fins = {}
            for (q0, qlen, kts) in seqs:
                for qc in range(0, qlen, 512):
                    N = min(512, qlen - qc)
                    for kv in range(2):
                        for g2 in range(2):
                            rows = slice(kv * 64, (kv + 1) * 64)
                            orow = g2 * 64
                            srow = 64 - orow
                            if g2 == 0:
                                vfn = lambda kt, kv=kv: VA[:, kt, kv * 192 + 64:kv * 192 + 192]
                            else:
                                vfn = lambda kt, kv=kv: VA[:, kt, kv * 192:kv * 192 + 128]
                            gid = len(jobs)

                            def fin(obs, orow=orow, srow=srow, N=N, kv=kv, c0=q0 + qc):
                                ob = obs[0]
                                recipf(tA[srow:srow + 64, 0:N], pb[ob][srow:srow + 64, 0:N], [pb_R[ob]], [tA_R])
                                tt(BR[orow:orow + 64, 0, kv, c0:c0 + N], pb[ob][orow:orow + 64, 0:N], tA[srow:srow + 64, 0:N], MUL,
                                   [pb_R[ob], tA_R], [BR_R[0]])
                            fins[gid] = fin
                            jobs.append(dict(kT=(lambda kt, rows=rows: KT[rows, kt * 128:(kt + 1) * 128]), qT=QT[rows, g2, q0 + qc:q0 + qc + N], vT=vfn,
                                             kts=kts, N=N, scale=0.125, reads=[KT_R, QT_R, VA_R], gid=gid))
            attend_multi(jobs, fins)
            if STOP == 'A':
                continue
            slot, sr = new_slot()
            W = slot[:, 0:6144].rearrange("p (k c) -> p k c", k=8)
            load_w(W, win_d[l].rearrange("(k p) c -> p k c", p=128)[:, :, 1312:2080], sr)
            barrier(kb)
            QT = av(0, 2048).rearrange("p (b n) -> p b n", b=2); QT_R = Res("QT_C")
            KC = av(2048, 6144).rearrange("p (v b n) -> p v b n", v=2, b=2); KC_R = Res("KT_C")
            VC_OFF = 8192
            VC = av(VC_OFF, 4608).rearrange("p (k c) -> p k c", c=384); VC_R = Res("VC")
            kb.op("dve", lambda e: e.memset(VC[:, 0:NKT, :], 1.0), writes=[VC_R])

            def kc_store(src, kcols):
                for v in range(2):
                    for bb in range(2):
                        ts(KC[:, v, bb, kcols], src[:, bb, :], pm[:, v:v + 1], MUL, [pb_R[7], pb_R[6], CR], [KC_R])
            if S_GRP:
                for bb in range(2):
                    st_, sr2 = nstg()
                    kb.dma("sp", st_[:, 0:512].rearrange("p (k f) -> p k f", k=4), cck_d[l].rearrange("(k p) f -> p k f", p=128)[:, :, bb * 128:(bb + 1) * 128], writes=[sr2])
                    for kt in range(4):
                        tr(pb[6][:, kt * 128:(kt + 1) * 128], st_[:, kt * 128:(kt + 1) * 128], identf[:], [sr2], [pb_R[6]])
                    for v in range(2):
                        ts(KC[:, v, bb, 0:512], pb[6], pm[:, v:v + 1], MUL, [pb_R[6], CR], [KC_R])
                for (d0, d1, s0, s1) in VDST:
                    kb.dma("pool", VC[:, 0:4, d0:d1], ccv_d[l].rearrange("(k p) f -> p k f", p=128)[:, :, s0:s1], writes=[VC_R])
            def tile_gen(t):
                TS_ = TSET[t % 2]; tA_ = TS_['tA_']; tA_R_ = TS_['tA_R_']; tB_ = TS_['tB_']; tB_R_ = TS_['tB_R_']; tC_ = TS_['tC_']; tC_R_ = TS_['tC_R_']; sm_ = TS_['sm_']; sm_R_ = TS_['sm_R_']; qkb_ = TS_['qkb_']; qkb_R_ = TS_['qkb_R_']; krpad_ = TS_['krpad_']; krpad_R_ = TS_['krpad_R_']; zgT_ = TS_['zgT_']; zgT_R_ = TS_['zgT_R_']
                zb = 2 * (t % 2)
                PSBp = PSB[:, (t % 2) * 512:(t % 2) * 512 + 512]
                zr = [pb_R[zb], pb_R[zb + 1]]
                z = PS[:, zb * 512:zb * 512 + 768]
                for bi in range(2):
                    c0, c1 = bi * 512, min(768, bi * 512 + 512)
                    for kc in range(8):
                        mm(PS[:, (zb + bi) * 512:(zb + bi) * 512 + c1 - c0], hT[:, kc, t * 128:(t + 1) * 128], W[:, kc, c0:c1], kc == 0, kc == 7,
                           [hT_R[t // 4], sr], [pb_R[zb + bi]])
                yield
                kt = KOFF + t
                for (d0, d1, s0, s1) in VDST:
                    cp(VC[:, kt, d0:d1], z[:, 512 + s0:512 + s1], zr, [VC_R], eng="act")
                if S_GRP:
                    rope_tm(z[:, 0:512], zr[0], 16, 8, cosC[:, t, :], sinC[:, t, :], qkb_[:, 0:512], tA_, tA_R_, tB_, tB_R_, qkb_R_)
                else:
                    cp(qkb_[:, 0:512], z[:, 0:512], zr, [qkb_R_])
                    o_, or_ = nstg()
                    cp(o_[:, 0:512], z[:, 256:768], zr, [or_], eng="act")
                    s_, tt_ = t // 2, (t % 2) * 128
                    kb.dma("sp", nck_d[s_, l, tt_:tt_ + 128, :], o_[:, 0:256], reads=[or_])
                    kb.dma("sp", ncv_d[s_, l, tt_:tt_ + 128, :], o_[:, 256:512], reads=[or_])
                yield
                for i in range(4):
                    tr(PSBp[:, i * 128:(i + 1) * 128], qkb_[:, i * 128:(i + 1) * 128], identb[:], [qkb_R_], [pb_R[7]])
                yield
                cp(QT[:, :, t * 128:(t + 1) * 128], PSBp[:, 0:256].rearrange("p (b n) -> p b n", b=2), [pb_R[7]], [QT_R], eng="act")
                kc_store(PSBp[:, 256:512].rearrange("p (b n) -> p b n", b=2), slice(kt * 128, (kt + 1) * 128))
                yield
            gens_ = [tile_gen(t_) for t_ in range(8)]
            for t0_ in range(0, 8, 2):
                act_ = [gens_[t0_], gens_[t0_ + 1]]
                while act_:
                    for g_ in list(act_):
                        try:
                            next(g_)
                        except StopIteration:
                            act_.remove(g_)
            OCR = tC; OCR_R = tC_R
            jobs = []; fins = {}
            for (q0, qlen, kts) in seqs:
                for qc in range(0, qlen, 512):
                    N = min(512, qlen - qc)
                    for bb in range(2):
                        for hf in range(2):
                            h = bb * 2 + hf
                            rows = slice(hf * 64, (hf + 1) * 64)
                            orow = hf * 64
                            srow = 64 - orow
                            vfn = lambda kt, h=h: VC[:, kt, VOFFS[h]:VOFFS[h] + 128]
                            gid = len(jobs)

                            def fin(obs, orow=orow, srow=srow, N=N, bb=bb, hf=hf, c0=q0 + qc):
                                o1, o2 = obs
                                recipf(tA[srow:srow + 64, 0:N], pb[o1][srow:srow + 64, 0:N], [pb_R[o1]], [tA_R])
                                recipf(tB[srow:srow + 64, 0:N], pb[o2][srow:srow + 64, 0:N], [pb_R[o2]], [tB_R])
                                tt(tA[orow:orow + 64, 512:512 + N], pb[o1][orow:orow + 64, 0:N], tA[srow:srow + 64, 0:N], MUL, [pb_R[o1], tA_R], [tA_R])
                                tt(tB[orow:orow + 64, 512:512 + N], pb[o2][orow:orow + 64, 0:N], tB[srow:srow + 64, 0:N], MUL, [pb_R[o2], tB_R], [tB_R])
                                stt(OCR[orow:orow + 64, 0:N], tB[orow:orow + 64, 512:512 + N], neglam[orow:orow + 64, l:l + 1], tA[orow:orow + 64, 512:512 + N], MUL, ADD,
                                    [tA_R, tB_R, MR], [OCR_R])
                                if hf == 1:
                                    act(PT[0][:, 0:N], OCR[:, 0:N], AF.Square, [OCR_R], [PT_R[0]])
                                    b = nb()
                                    while b in obs:
                                        b = nb()
                                    mm(pb[b][:, 0:N], bd64[:], PT[0][:, 0:N], True, True, [PT_R[0], CR], [pb_R[b]])
                                    act(rstd[:, 0:N], pb[b][:, 0:N], AF.Ln, [pb_R[b], CR], [rstd_R], scale=1.0 / 64, bias=epsT[:])
                                    act(rstd[:, 0:N], rstd[:, 0:N], AF.Exp, [rstd_R], [rstd_R], scale=-0.5)
                                    tt(OCR[:, 0:N], OCR[:, 0:N], rstd[:, 0:N], MUL, [OCR_R, rstd_R], [OCR_R])
                                    act(BR[:, 2, bb, c0:c0 + N], OCR[:, 0:N], AF.Copy, [OCR_R, MR], [BR_R[2]], scale=ocs[:, l:l + 1])
                            fins[gid] = fin
                            for j in range(2):
                                jobs.append(dict(kT=(lambda kt, rows=rows, j=j, bb=bb: KC[rows, j, bb, kt * 128:(kt + 1) * 128]), qT=QT[rows, bb, q0 + qc:q0 + qc + N],
                                                 vT=vfn, kts=kts, N=N, scale=32 ** -0.5, reads=[KC_R, QT_R, VC_R], gid=gid))
            attend_multi(jobs, fins)

            if STOP == 'C':
                continue
            slot, sr = new_slot()
            W = slot[:, 0:2816].rearrange("p (k c) -> p k c", k=8)
            load_w(W, win_d[l].rearrange("(k p) c -> p k c", p=128)[:, :, 2080:2432], sr)
            UQ = slot[:, 2816:3584].rearrange("p (k c) -> p k c", k=2)
            UQS = slot[:, 3584:4352].rearrange("p (k c) -> p k c", k=2)
            UKV = slot[:, 4352:4864]
            load_w(UQ[:, 0, :], wuq_d[l, 0:128, :], sr); load_w(UQ[0:64, 1, :], wuq_d[l, 128:192, :], sr)
            load_w(UKV, wukv_d[l], sr)
            if S_GRP:
                with nc.allow_non_contiguous_dma(reason="rope column swap"):
                    for (kc_, r0, r1, pr) in ((0, 0, 128, 128), (1, 128, 192, 64)):
                        src = wuq_d[l, r0:r1, :].rearrange("p (h c) -> p h c", h=4)[:, :, 64:96].rearrange("p h (r f i) -> p h r f i", r=2, f=2)
                        dst = UQS[0:pr, kc_, :].rearrange("p (h c) -> p h c", h=4)[:, :, 64:96].rearrange("p h (r f i) -> p h r f i", r=2, f=2)
                        for f in range(2):
                            for r_ in range(2):
                                load_w(dst[:, :, r_, f, :], src[:, :, r_, 1 - f, :], sr)
            barrier(kb)
            DQN = av(0, 2048).rearrange("p (b n) -> p b n", b=2); DQN_R = Res("DQN")
            CKT = av(2048, 1536); CKT_R = Res("CKT")
            KD_ = av(3584, 6144).rearrange("p (h n) -> p h n", h=4); KD_R = Res("KT_D")
            DQT = av(9728, 4096).rearrange("p (h n) -> p h n", h=4); DQT_R = Res("DQT")
            VD_OFF = 13824
            VD = av(VD_OFF, 4608).rearrange("p (k c) -> p k c", c=384); VD_R = Res("VD")
            kb.op("dve", lambda e: e.memset(VD[:, 0:NKT, :], 1.0), writes=[VD_R])
            if S_GRP:
                st_, sr2 = nstg()
                kb.dma("sp", st_[:, 0:512].rearrange("p (k f) -> p k f", k=4), cdc_d[l].rearrange("(k p) f -> p k f", p=128), writes=[sr2])
                for kt in range(4):
                    tr(pb[6][:, kt * 128:(kt + 1) * 128], st_[:, kt * 128:(kt + 1) * 128], identf[:], [sr2], [pb_R[6]])
                cp(CKT[:, 0:512], pb[6], [pb_R[6]], [CKT_R])
                stgkr = st_[:, 512:896].rearrange("p (k f) -> p k f", k=4)
                kb.op("dve", lambda e, stgkr=stgkr: e.memset(stgkr[:, :, 0:64], 0.0), writes=[sr2])
                kb.dma("sp", stgkr[:, :, 64:96], cdr_d[l].rearrange("(k p) f -> p k f", p=128), writes=[sr2])
                b = nb()
                for kt in range(4):
                    mm(pb[b][0:96, kt * 128:(kt + 1) * 128], stgkr[:, kt, :], identf[:], True, True, [sr2, CR], [pb_R[b]])
                cp(KD_[64:96, :, 0:512], bc(pb[b][64:96, :].unsqueeze(1), [32, 4, 512]), [pb_R[b]], [KD_R])
            def tile_gen(t):
                TS_ = TSET[t % 2]; tA_ = TS_['tA_']; tA_R_ = TS_['tA_R_']; tB_ = TS_['tB_']; tB_R_ = TS_['tB_R_']; tC_ = TS_['tC_']; tC_R_ = TS_['tC_R_']; sm_ = TS_['sm_']; sm_R_ = TS_['sm_R_']; qkb_ = TS_['qkb_']; qkb_R_ = TS_['qkb_R_']; krpad_ = TS_['krpad_']; krpad_R_ = TS_['krpad_R_']; zgT_ = TS_['zgT_']; zgT_R_ = TS_['zgT_R_']
                zb = t % 2
                PSBp = PSB[:, (t % 2) * 512:(t % 2) * 512 + 512]
                z = pb[zb]; zr = pb_R[zb]
                for kc in range(8):
                    mm(z[:, 0:352], hT[:, kc, t * 128:(t + 1) * 128], W[:, kc, :], kc == 0, kc == 7, [hT_R[t // 4], sr], [zr])
                yield
                act(tA_[:, 0:320], z[:, 0:320], AF.Square, [zr], [tA_R_])
                kb.op("dve", lambda e, sm_=sm_, tA_=tA_: e.reduce_sum(out=sm_[:, 8:9], in_=tA_[:, 0:192], axis=AX.X), reads=[tA_R_], writes=[sm_R_])
                kb.op("dve", lambda e, sm_=sm_, tA_=tA_: e.reduce_sum(out=sm_[:, 9:10], in_=tA_[:, 192:320], axis=AX.X), reads=[tA_R_], writes=[sm_R_])
                act(sm_[:, 8:9], sm_[:, 8:9], AF.Ln, [sm_R_, CR], [sm_R_], scale=1.0 / 192, bias=epsT[:])
                act(sm_[:, 9:10], sm_[:, 9:10], AF.Ln, [sm_R_, CR], [sm_R_], scale=1.0 / 128, bias=epsT[:])
                act(sm_[:, 8:10], sm_[:, 8:10], AF.Exp, [sm_R_], [sm_R_], scale=-0.5)
                yield
                stt(qkb_[:, 0:192], z[:, 0:192], sm_[:, 8:9], gDQ[:], MUL, MUL, [zr, sm_R_, LP], [qkb_R_])
                stt(tB_[:, 0:128], z[:, 192:320], sm_[:, 9:10], gKV[:], MUL, MUL, [zr, sm_R_, LP], [tB_R_])
                cp(qkb_[:, 192:320], tB_[:, 0:128], [tB_R_], [qkb_R_])
                yield
                kt = KOFF + t
                if S_GRP:
                    xv = z[:, 320:352].rearrange("p (r h i) -> p r h i", r=2, h=2)
                    ov = krpad_[:, 64:96].rearrange("p (r h i) -> p r h i", r=2, h=2)
                    cv = cosC[:, t, :].rearrange("p (r i) -> p r i", r=2); sv = sinC[:, t, :].rearrange("p (r i) -> p r i", r=2)
                    a = tC_[:, 0:16].rearrange("p (r i) -> p r i", r=2); b2 = tC_[:, 16:32].rearrange("p (r i) -> p r i", r=2)
                    tt(a, xv[:, :, 0, :], cv, MUL, [zr, CR], [tC_R_]); tt(b2, xv[:, :, 1, :], sv, MUL, [zr, CR], [tC_R_])
                    tt(ov[:, :, 0, :], a, b2, SUB, [tC_R_], [krpad_R_])
                    tt(a, xv[:, :, 1, :], cv, MUL, [zr, CR], [tC_R_]); tt(b2, xv[:, :, 0, :], sv, MUL, [zr, CR], [tC_R_])
                    tt(ov[:, :, 1, :], a, b2, ADD, [tC_R_], [krpad_R_])
                else:
                    cp(krpad_[:, 64:96], z[:, 320:352], [zr], [krpad_R_])
                    o_, or_ = nstg()
                    cp(o_[:, 0:128], tB_[:, 0:128], [tB_R_], [or_], eng="act")
                    cp(o_[:, 128:160], z[:, 320:352], [zr], [or_], eng="act")
                    s_, tt_ = t // 2, (t % 2) * 128
                    kb.dma("sp", nckv_d[s_, l, tt_:tt_ + 128, :], o_[:, 0:128], reads=[or_])
                    kb.dma("sp", nkr_d[s_, l, tt_:tt_ + 128, :], o_[:, 128:160], reads=[or_])
                yield
                tr(PSBp[:, 0:128], qkb_[:, 0:128], identb[:], [qkb_R_], [pb_R[7]])
                tr(PSBp[0:64, 128:256], qkb_[:, 128:192], identb[:], [qkb_R_], [pb_R[7]])
                tr(PSBp[:, 256:384], qkb_[:, 192:320], identb[:], [qkb_R_], [pb_R[7]])
                yield
                cp(DQN[:, 0, t * 128:(t + 1) * 128], PSBp[:, 0:128], [pb_R[7]], [DQN_R])
                cp(DQN[0:64, 1, t * 128:(t + 1) * 128], PSBp[0:64, 128:256], [pb_R[7]], [DQN_R])
                cp(CKT[:, kt * 128:(kt + 1) * 128], PSBp[:, 256:384], [pb_R[7]], [CKT_R], eng="act")
                b = 2 + (t % 2)
                mm(pb[b][0:96, 0:128], krpad_[:], identb[:], True, True, [krpad_R_, CR], [pb_R[b]])
                cp(KD_[64:96, :, kt * 128:(kt + 1) * 128], bc(pb[b][64:96, 0:128].unsqueeze(1), [32, 4, 128]), [pb_R[b]], [KD_R])
                yield
            gens_ = [tile_gen(t_) for t_ in range(8)]
            for t0_ in range(0, 8, 2):
                act_ = [gens_[t0_], gens_[t0_ + 1]]
                while act_:
                    for g_ in list(act_):
                        try:
                            next(g_)
                        except StopIteration:
                            act_.remove(g_)
            for h in range(4):
                for c0 in range(0, NK, 512):
                    b = nb()
                    mm(pb[b][0:64, :], UKV[:, h * 128:h * 128 + 64], CKT[:, c0:c0 + 512], True, True, [sr, CKT_R], [pb_R[b]])
                    cp(KD_[0:64, h, c0:c0 + 512], pb[b][0:64, :], [pb_R[b]], [KD_R], eng=("act" if h % 2 else "dve"))
            for kt in range(NKT):
                b = nb()
                mm(pb[b][:, 0:256].rearrange("p (h e) -> p h e", h=4), CKT[:, kt * 128:(kt + 1) * 128], UKV.rearrange("p (h c) -> p h c", h=4)[:, :, 64:128], True, True,
                   [sr, CKT_R], [pb_R[b]])
                for (d0, d1, s0, s1) in VDST:
                    cp(VD[:, kt, d0:d1], pb[b][:, s0:s1], [pb_R[b]], [VD_R], eng=("act" if kt % 2 else "dve"))
            for h in range(4):
                for qc in range(0, 1024, 512):
                    b = nb()
                    mm(pb[b][0:96, :], UQ[:, 0, h * 96:(h + 1) * 96], DQN[:, 0, qc:qc + 512], True, False, [sr, DQN_R], [pb_R[b]])
                    mm(pb[b][0:96, :], UQ[0:64, 1, h * 96:(h + 1) * 96], DQN[0:64, 1, qc:qc + 512], False, True, [sr, DQN_R], [pb_R[b]])
                    if S_GRP:
                        b2_ = nb()
                        mm(pb[b2_][0:96, :], UQS[:, 0, h * 96:(h + 1) * 96], DQN[:, 0, qc:qc + 512], True, False, [sr, DQN_R], [pb_R[b2_]])
                        mm(pb[b2_][0:96, :], UQS[0:64, 1, h * 96:(h + 1) * 96], DQN[0:64, 1, qc:qc + 512], False, True, [sr, DQN_R], [pb_R[b2_]])
                        cp(DQT[0:64, h, qc:qc + 512], pb[b][0:64, :], [pb_R[b]], [DQT_R], eng="act")
                        tt(tA[64:96, 0:512], pb[b][64:96, :], cosD[64:96, qc:qc + 512], MUL, [pb_R[b], CR], [tA_R])
                        tt(tB[64:96, 0:512], pb[b2_][64:96, :], sinD[64:96, qc:qc + 512], MUL, [pb_R[b2_], CR], [tB_R])
                        tt(DQT[64:96, h, qc:qc + 512], tA[64:96, 0:512], tB[64:96, 0:512], ADD, [tA_R, tB_R], [DQT_R])
                    else:
                        cp(DQT[0:96, h, qc:qc + 512], pb[b][0:96, :], [pb_R[b]], [DQT_R], eng=("act" if h % 2 else "dve"))
            jobs = []; fins = {}
            for (q0, qlen, kts) in seqs:
                for qc in range(0, qlen, 512):
                    N = min(512, qlen - qc)
                    for h in range(4):
                        bb, hf = h // 2, h % 2
                        orow = hf * 64
                        srow = 64 - orow
                        vfn = lambda kt, h=h: VD[:, kt, VOFFS[h]:VOFFS[h] + 128]
                        gid = len(jobs)

                        def fin(obs, orow=orow, srow=srow, N=N, bb=bb, c0=q0 + qc):
                            ob = obs[0]
                            recipf(tA[srow:srow + 64, 0:N], pb[ob][srow:srow + 64, 0:N], [pb_R[ob]], [tA_R])
                            tt(BR[orow:orow + 64, 3, bb, c0:c0 + N], pb[ob][orow:orow + 64, 0:N], tA[srow:srow + 64, 0:N], MUL,
                               [pb_R[ob], tA_R], [BR_R[3]])
                        fins[gid] = fin
                        jobs.append(dict(kT=(lambda kt, h=h: KD_[0:96, h, kt * 128:(kt + 1) * 128]), qT=DQT[0:96, h, q0 + qc:q0 + qc + N], vT=vfn,
                                         kts=kts, N=N, scale=96 ** -0.5, reads=[KD_R, DQT_R, VD_R], gid=gid))
            attend_multi(jobs, fins)
            if STOP == 'D':
                continue
            slot, sr = new_slot()
            W = slot[:, 0:6144].rearrange("p (k c) -> p k c", k=8)
            G = slot[:, 6144:6656].rearrange("p (k c) -> p k c", k=8)
            load_w(W, win_d[l].rearrange("(k p) c -> p k c", p=128)[:, :, 512:1280], sr)
            load_w(G[:, :, 0:16], win_d[l].rearrange("(k p) c -> p k c", p=128)[:, :, 1280:1296], sr)
            load_w(G[:, :, 32:48], win_d[l].rearrange("(k p) c -> p k c", p=128)[:, :, 1296:1312], sr)
            barrier(kb)
            BT = av(0, 12288).rearrange("p (t d c) -> p t d c", t=8, d=2); BT_R = Res("BT")
            KDc = av(12288, 2048).rearrange("p (t d c) -> p t d c", t=8, d=2); KDc_R = Res("KDc")
            VB = av(14336, 2048).rearrange("p (t c) -> p t c", t=8); VB_R = Res("VB")
            GR = av(16384, 2048).rearrange("p (t c) -> p t c", t=8); GR_R = Res("GR")
            SIN = av(18432, 4096).rearrange("p (t d c) -> p t d c", t=8, d=2); SIN_R = Res("SIN")
            Lsp = tC; Lsp_R = tC_R
            for t in range(8):
                TS_ = TSET[t % 2]; tA_ = TS_['tA_']; tA_R_ = TS_['tA_R_']; tB_ = TS_['tB_']; tB_R_ = TS_['tB_R_']; tC_ = TS_['tC_']; tC_R_ = TS_['tC_R_']; sm_ = TS_['sm_']; sm_R_ = TS_['sm_R_']; qkb_ = TS_['qkb_']; qkb_R_ = TS_['qkb_R_']; krpad_ = TS_['krpad_']; krpad_R_ = TS_['krpad_R_']; zgT_ = TS_['zgT_']; zgT_R_ = TS_['zgT_R_']
                zb = nb()
                while zb >= 4:
                    zb = nb()
                zr = [pb_R[zb], pb_R[zb + 1]]
                z = PS[:, zb * 512:zb * 512 + 768]
                for bi in range(2):
                    c0, c1 = bi * 512, min(768, bi * 512 + 512)
                    for kc in range(8):
                        mm(PS[:, (zb + bi) * 512:(zb + bi) * 512 + c1 - c0], hT[:, kc, t * 128:(t + 1) * 128], W[:, kc, c0:c1], kc == 0, kc == 7,
                           [hT_R[t // 4], sr], [pb_R[zb + bi]])
                if BCUT == 1:
                    continue
                for d in range(2):
                    for kc in range(8):
                        mm(pb[5][0:16, d * 128:(d + 1) * 128], G[:, kc, d * 32:d * 32 + 16], hT[:, kc, t * 128:(t + 1) * 128], kc == 0, kc == 7, [hT_R[t // 4], sr], [pb_R[5]])
                cp(zgT_, pb[5][0:16, 0:256], [pb_R[5]], [zgT_R_])
                if BCUT == 2:
                    continue
                for d in range(2):
                    mm(pb[5][:, 256 + d * 128:384 + d * 128], zgT_[0:16, d * 128:(d + 1) * 128], GW[0:16, d * 128:(d + 1) * 128], True, False, [zgT_R_, LP], [pb_R[5]])
                    mm(pb[5][:, 256 + d * 128:384 + d * 128], onesrow[0:1, :], GBias[0:1, d * 128:(d + 1) * 128], False, True, [CR, LP], [pb_R[5]])
                if BCUT == 3:
                    continue
                act(tA_[:, 0:256], pb[5][:, 256:512], AF.Exp, [pb_R[5]], [tA_R_], scale=-1.0)
                act(tC_[:, 0:256], tA_[:, 0:256], AF.Ln, [tA_R_, CR], [tC_R_], bias=onescol[:])
                if BCUT == 4:
                    continue
                for i, (m_, d) in enumerate(((0, 0), (1, 0), (2, 1), (3, 1))):
                    mm(pb[6][:, i * 128:(i + 1) * 128], tri[:, m_, :], tC_[:, d * 128:(d + 1) * 128], True, True, [tC_R_, CR], [pb_R[6]])
                if BCUT == 5:
                    continue
                for d in range(2):
                    mm(pb[5][:, 2 * d:2 + 2 * d], tC_[:, d * 128:(d + 1) * 128], onesf[:, 0:2], True, True, [tC_R_, CR], [pb_R[5]])
                act(GL[:, t, :], pb[5][:, 0:4].rearrange("p (d two) -> p d two", two=2)[:, :, 0], AF.Exp, [pb_R[5]], [GL_R], scale=-1.0 / 16)
                if BCUT == 6:
                    continue
                Ea = tA_; Eb = tB_
                act(Ea[:, 0:512], pb[6], AF.Exp, [pb_R[6]], [tA_R_], scale=-1.0 / 16)
                act(Eb[:, 0:256].rearrange("p (a c) -> p a c", a=2), pb[6].rearrange("p (a b c) -> p a b c", a=2, b=2)[:, :, 0, :], AF.Exp, [pb_R[6]], [tB_R_], scale=1.0 / 16)
                if BCUT == 7:
                    continue
                zq, zk = z[:, 0:128], z[:, 128:256]
                stt(qkb_[:, 0:128], zq, 32 ** -0.5, Ea[:, 0:128], MUL, MUL, zr + [tA_R_], [qkb_R_])
                stt(qkb_[:, 128:256], zq, 32 ** -0.5, Ea[:, 256:384], MUL, MUL, zr + [tA_R_], [qkb_R_])
                tt(qkb_[:, 256:384], zk, Eb[:, 0:128], MUL, zr + [tB_R_], [qkb_R_])
                tt(qkb_[:, 384:512], zk, Eb[:, 128:256], MUL, zr + [tB_R_], [qkb_R_])
                tt(KDc[:, t, 0, :], zk, Ea[:, 128:256], MUL, zr + [tA_R_], [KDc_R])
                tt(KDc[:, t, 1, :], zk, Ea[:, 384:512], MUL, zr + [tA_R_], [KDc_R])
                if BCUT == 8:
                    continue
                cp(VB[:, t, :], z[:, 256:512], zr, [VB_R], eng="act")
                act(GR[:, t, :], z[:, 512:768], AF.Silu, zr, [GR_R])
                if BCUT == 9:
                    continue
                for i in range(4):
                    tr(PSB[:, i * 128:(i + 1) * 128], qkb_[:, i * 128:(i + 1) * 128], identb[:], [qkb_R_], [pb_R[7]])
                if BCUT == 10:
                    continue
                for d in range(2):
                    cp(BT[:, t, d, 0:128], PSB[:, 256 + d * 128:384 + d * 128], [pb_R[7]], [BT_R], eng="act")
                    cp(BT[:, t, d, 128:256], PSB[:, d * 128:(d + 1) * 128], [pb_R[7]], [BT_R], eng="act")
                    tt(BT[:, t, d, 256:768].rearrange("p (h n) -> p h n", h=4), bc(PSB[:, d * 128:(d + 1) * 128].unsqueeze(1), [128, 4, 128]),
                       bc(hm[:].unsqueeze(2), [128, 4, 128]), MUL, [pb_R[7], CR], [BT_R])
            if STOP.startswith('B1'):
                continue
            hm3 = bc(hm[:].unsqueeze(2), [128, 4, 64])
            for si, (q0, qlen, kts) in enumerate(seqs):
                t0, nt = q0 // 128, qlen // 128
                for d in range(2):
                    Sf, Sr = Sst[d], Sst_R[d]
                    if S_GRP:
                        kb.dma("sp", Sf[:], (sbf_d if d == 0 else sbb_d)[l], writes=[Sr])
                    else:
                        kb.op("dve", lambda e, Sf=Sf: e.memset(Sf[:], 0.0), writes=[Sr])
                    order = range(t0, t0 + nt) if d == 0 else range(t0 + nt - 1, t0 - 1, -1)
                    for t in order:
                        tt(SIN[:, t, d, :].rearrange("p (h e) -> p h e", h=4), bc(Sf[:].unsqueeze(1), [128, 4, 64]), hm3, MUL, [Sr, CR], [SIN_R])
                        b = nb()
                        mm(pb[b][:, 0:256], KDc[:, t, d, :], VB[:, t, :], True, True, [KDc_R, VB_R], [pb_R[b]])
                        tt(tA[:, 0:256].rearrange("p (h e) -> p h e", h=4), pb[b][:, 0:256].rearrange("p (h e) -> p h e", h=4), hm3, MUL, [pb_R[b], CR], [tA_R])
                        kb.op("dve", lambda e: e.reduce_sum(out=tB[:, 0:64], in_=tA[:, 0:256].rearrange("p (h e) -> p e h", h=4), axis=AX.X), reads=[tA_R], writes=[tB_R])
                        stt(Sf[:], Sf[:], GL[:, t, d:d + 1], tB[:, 0:64], MUL, ADD, [Sr, GL_R, tB_R], [Sr])
                    if not S_GRP:
                        kb.dma("sp", (nbf_d if d == 0 else nbb_d)[si, l], Sf[:], reads=[Sr])
            if STOP == 'B2':
                continue
            for t in range(8):
                ob = nb()
                mm(pb[ob][:, 0:256], BT[:, t, 0, 128:256], SIN[:, t, 0, :], True, False, [BT_R, SIN_R], [pb_R[ob]])
                mm(pb[ob][:, 0:256], BT[:, t, 1, 128:256], SIN[:, t, 1, :], False, False, [BT_R, SIN_R], [pb_R[ob]])
                for d in range(2):
                    ab = nb()
                    while ab == ob:
                        ab = nb()
                    mm(pb[ab], BT[:, t, d, 0:128], BT[:, t, d, 256:768], True, True, [BT_R], [pb_R[ab]])
                    p = pt_i[0] % 3
                    pt_i[0] += 1
                    tt(PT[p][:].rearrange("p (h n) -> p h n", h=4), pb[ab].rearrange("p (h n) -> p h n", h=4),
                       bc(tri[:, (0 if d == 0 else 2), :].unsqueeze(1), [128, 4, 128]), MUL, [pb_R[ab], CR], [PT_R[p]])
                    for h in range(4):
                        mm(pb[ob][:, h * 64:(h + 1) * 64], PT[p][:, h * 128:(h + 1) * 128], VB[:, t, h * 64:(h + 1) * 64], False, (d == 1 and h == 3),
                           [PT_R[p], VB_R], [pb_R[ob]])
                o = pb[ob][:, 0:256]
                act(tA[:, 0:256], o, AF.Square, [pb_R[ob]], [tA_R])
                kb.op("dve", lambda e: e.reduce_sum(out=sm[:, 16:20], in_=tA[:, 0:256].rearrange("p (h d) -> p h d", h=4), axis=AX.X), reads=[tA_R], writes=[sm_R])
                act(sm[:, 16:20], sm[:, 16:20], AF.Ln, [sm_R, CR], [sm_R], scale=1.0 / 64, bias=epsT[:])
                act(sm[:, 16:20], sm[:, 16:20], AF.Exp, [sm_R], [sm_R], scale=-0.5)
                tt(tA[:, 0:256].rearrange("p (h d) -> p h d", h=4), o.rearrange("p (h d) -> p h d", h=4), bc(sm[:, 16:20].unsqueeze(2), [128, 4, 64]), MUL,
                   [pb_R[ob], sm_R], [tA_R])
                tt(tA[:, 0:256], tA[:, 0:256], gBO[:], MUL, [tA_R, LP], [tA_R])
                tt(qkb[:, 0:256], tA[:, 0:256], GR[:, t, :], MUL, [tA_R, GR_R], [qkb_R])
                for i in range(2):
                    tr(PSB[:, i * 128:(i + 1) * 128], qkb[:, i * 128:(i + 1) * 128], identb[:], [qkb_R], [pb_R[7]])
                cp(BR[:, 1, :, t * 128:(t + 1) * 128], PSB[:, 0:256].rearrange("p (b n) -> p b n", b=2), [pb_R[7]], [BR_R[1]])

            if STOP == 'B':
                continue
            bslot, bsr = new_slot()
            BW = bslot[:, 0:8192].rearrange("p (j k c) -> p j k c", j=4, k=2)
            for j in range(4):
                load_w(BW[:, j], wbr_d[l, j].rearrange("(k p) c -> p k c", p=128), bsr)
            barrier(kb)
            if DEBUG and l == 0 and g == GROUPS[0]:
                for jb in range(8):
                    o_, or_ = nstg()
                    cp(o_[:], BR[:, jb // 2, jb % 2, :], BR_R, [or_])
                    kb.dma("sp", dbg_br[:, jb, :], o_[:], reads=[or_])
            MG = av(0, 8192).rearrange("p (k n) -> p k n", k=8)
            for mp in range(4):
                slot, sr = new_slot(avoid=bslot)
                GWT = slot[:, 0:8192].rearrange("p (j k c) -> p j k c", j=4, k=8)
                for j in range(4):
                    c0 = 2432 + j * 1024 + mp * 256
                    load_w(GWT[:, j], win_d[l].rearrange("(k p) c -> p k c", p=128)[:, :, c0:c0 + 256], sr)
                for mi in range(2):
                    m = mp * 2 + mi
                    for h in range(2):
                        hs = slice(h * 512, (h + 1) * 512)
                        for j in range(4):
                            b1 = nb(); b2_ = nb()
                            for kc in range(8):
                                mm(pb[b1], GWT[:, j, kc, mi * 128:(mi + 1) * 128], hT[:, kc, hs], kc == 0, kc == 7, [sr, hT_R[h]], [pb_R[b1]])
                            for kc in range(2):
                                mm(pb[b2_], BW[:, j, kc, m * 128:(m + 1) * 128], BR[:, j, kc, hs], kc == 0, kc == 1, [bsr, BR_R[j]], [pb_R[b2_]])
                            act(tA[:, 0:512], pb[b1], AF.Sigmoid, [pb_R[b1]], [tA_R])
                            if j == 0:
                                tt(tC[:, 0:512], pb[b2_], tA[:, 0:512], MUL, [pb_R[b2_], tA_R], [tC_R])
                            else:
                                tt(tB[:, 0:512], pb[b2_], tA[:, 0:512], MUL, [pb_R[b2_], tA_R], [tB_R])
                                if j < 3:
                                    tt(tC[:, 0:512], tC[:, 0:512], tB[:, 0:512], ADD, [tC_R, tB_R], [tC_R])
                                else:
                                    tt(MG[:, m, hs], tC[:, 0:512], tB[:, 0:512], ADD, [tC_R, tB_R], [MG_R[h]])
            if STOP == 'M':
                continue
            slot, sr = new_slot()
            WO = slot[:, 0:8192].rearrange("p (k c) -> p k c", k=8)
            load_w(WO, wout_d[l].rearrange("(k p) c -> p k c", p=128), sr)
            for m in range(8):
                for h in range(2):
                    hs = slice(h * 512, (h + 1) * 512)
                    b = nb()
                    for kc in range(8):
                        mm(pb[b], WO[:, kc, m * 128:(m + 1) * 128], MG[:, kc, hs], kc == 0, kc == 7, [sr, MG_R[h]], [pb_R[b]])
                    stt(xT[:, m, hs], pb[b], modT[:, l, 16 + m, g:g + 1], xT[:, m, hs], MUL, ADD, [pb_R[b], MR, xT_R[m][h]], [xT_R[m][h]])
            if DEBUG and l == 0 and g == GROUPS[0]:
                for kc in range(8):
                    kb.dma("sp", dbg_x1[:, kc, :], xT[:, kc, :], reads=[xT_R[kc][0], xT_R[kc][1]])
                    o_, or_ = nstg()
                    cp(o_[:], MG[:, kc, :], MG_R, [or_])
                    kb.dma("sp", dbg_mg[:, kc, :], o_[:], reads=[or_])
            norm_mod(A2, 24, l, g)
            AT = av(8192, 22528).rearrange("p (f n) -> p f n", f=22); AT_R = [Res("AT0"), Res("AT1")]
            U_ = stg[0]; U_R = stg_R[0]; Cc = stg[1]; Cc_R = stg_R[1]
            nsq = 1 if S_GRP else 4
            sl = 1024 // nsq
            for f0 in range(0, 22, 4):
                nf = min(4, 22 - f0)
                slot, sr = new_slot()
                UW = slot[:, 0:4096].rearrange("p (k c) -> p k c", k=8)
                GW2 = slot[:, 4096:8192].rearrange("p (k c) -> p k c", k=8)
                load_w(UW[:, :, 0:nf * 128], wfu_d[l].rearrange("(k p) c -> p k c", p=128)[:, :, f0 * 128:(f0 + nf) * 128], sr)
                load_w(GW2[:, :, 0:nf * 128], wfg_d[l].rearrange("(k p) c -> p k c", p=128)[:, :, f0 * 128:(f0 + nf) * 128], sr)
                for fi in range(nf):
                    f = f0 + fi
                    gb_ = []
                    for h in range(2):
                        hs = slice(h * 512, (h + 1) * 512)
                        bu = nb(); bg = nb()
                        gb_.append(bg)
                        for kc in range(8):
                            mm(pb[bu], UW[:, kc, fi * 128:(fi + 1) * 128], hT[:, kc, hs], kc == 0, kc == 7, [sr, hT_R[h]], [pb_R[bu]])
                        for kc in range(8):
                            mm(pb[bg], GW2[:, kc, fi * 128:(fi + 1) * 128], hT[:, kc, hs], kc == 0, kc == 7, [sr, hT_R[h]], [pb_R[bg]])
                        cp(U_[:, hs], pb[bu], [pb_R[bu]], [U_R], eng="act")
                    act(Cc[:], U_[:], AF.Identity, [U_R, LP], [Cc_R], scale=cwT[:, 1, f:f + 1], bias=cbT[:, f:f + 1])
                    Uv = U_[:].rearrange("p (s n) -> p s n", s=nsq); Cv = Cc[:].rearrange("p (s n) -> p s n", s=nsq)
                    stt(Cv[:, :, 1:sl], Uv[:, :, 0:sl - 1], cwT[:, 0, f:f + 1], Cv[:, :, 1:sl], MUL, ADD, [U_R, Cc_R, LP], [Cc_R])
                    stt(Cv[:, :, 0:sl - 1], Uv[:, :, 1:sl], cwT[:, 2, f:f + 1], Cv[:, :, 0:sl - 1], MUL, ADD, [U_R, Cc_R, LP], [Cc_R])
                    act(tA[:], Cc[:], AF.Gelu_apprx_tanh, [Cc_R], [tA_R])
                    for h in range(2):
                        hs = slice(h * 512, (h + 1) * 512)
                        tt(AT[:, f, hs], tA[:, hs], pb[gb_[h]], MUL, [tA_R, pb_R[gb_[h]]], [AT_R[h]])
            for m0 in range(0, 8, 2):
                slot, sr = new_slot()
                WD = slot[:, 0:5632].rearrange("p (f c) -> p f c", f=22)
                load_w(WD, wfd_d[l].rearrange("(f p) c -> p f c", p=128)[:, :, m0 * 128:(m0 + 2) * 128], sr)
                for mi in range(2):
                    m = m0 + mi
                    for h in range(2):
                        hs = slice(h * 512, (h + 1) * 512)
                        b = nb()
                        for f in range(22):
                            mm(pb[b], WD[:, f, mi * 128:(mi + 1) * 128], AT[:, f, hs], f == 0, f == 21, [sr, AT_R[h]], [pb_R[b]])
                        stt(xT[:, m, hs], pb[b], modT[:, l, 40 + m, g:g + 1], xT[:, m, hs], MUL, ADD, [pb_R[b], MR, xT_R[m][h]], [xT_R[m][h]])

            if DEBUG and l == 0 and g == GROUPS[0]:
                for kc in range(8):
                    kb.dma("sp", dbg_x2[:, kc, :], xT[:, kc, :], reads=[xT_R[kc][0], xT_R[kc][1]])
        barrier(kb)
        kb.dma("sp", fgB, fg_d.partition_broadcast(128), writes=[fgB_R])
        for t in range(8):
            b0 = nb()
            while b0 >= 5:
                b0 = nb()
            for kc in range(8):
                bi = b0 + kc // 4
                tr(PS[:, bi * 512 + (kc % 4) * 128: bi * 512 + (kc % 4 + 1) * 128], xT[:, kc, t * 128:(t + 1) * 128], identf[:], [xT_R[kc][t // 4]], [pb_R[bi]])
            yp = PS[:, b0 * 512:b0 * 512 + 1024]
            yr = [pb_R[b0], pb_R[b0 + 1]]
            act(tA[:], yp, AF.Square, yr, [tA_R])
            kb.op("dve", lambda e: e.reduce_sum(out=sm[:, 24:25], in_=tA[:], axis=AX.X), reads=[tA_R], writes=[sm_R])
            act(sm[:, 24:25], sm[:, 24:25], AF.Ln, [sm_R, CR], [sm_R], scale=1.0 / 1024, bias=epsT[:])
            act(sm[:, 24:25], sm[:, 24:25], AF.Exp, [sm_R], [sm_R], scale=-0.5)
            o_, or_ = nstg()
            stt(o_[:], yp, sm[:, 24:25], fgB, MUL, MUL, yr + [sm_R, fgB_R], [or_])
            kb.dma("sp", y_d[g, t * 128:(t + 1) * 128, :], o_[:], reads=[or_])
    return kb


def _consts():
    c = {}
    c["k_ident"] = np.eye(128, dtype=np.float32)
    bd = np.zeros((128, 128), np.float32); bd[:64, :64] = 1; bd[64:, 64:] = 1
    c["k_bd64"] = bd
    s = np.arange(128)[:, None]; t = np.arange(128)[None, :]
    c["k_tri"] = np.stack([(s <= t), (s > t), (s >= t), (s < t)]).astype(np.float32)
    hmk = np.zeros((128, 4), np.float32)
    for h in range(4):
        hmk[h * 32:(h + 1) * 32, h] = 1
    c["k_hm"] = hmk
    pmk = np.zeros((128, 2), np.float32)
    for p in range(128):
        pmk[p, (p // 32) % 2] = 1
    c["k_pm"] = pmk
    tok = np.arange(1024)
    row = (tok // 64).astype(np.float32); col = (tok % 64).astype(np.float32)

    def tab(half):
        inv = (10000.0 ** (-np.arange(half, dtype=np.float32) / half)).astype(np.float32)
        ar = row[:, None] * inv[None, :]; ac = col[:, None] * inv[None, :]
        return (np.concatenate([np.cos(ar), np.cos(ac)], 1).astype(np.float32), np.concatenate([np.sin(ar), np.sin(ac)], 1).astype(np.float32))
    c["k_cosA"], c["k_sinA"] = tab(16)
    cC, sC = tab(8)
    c["k_cosC"], c["k_sinC"] = cC, sC
    cr, cc_ = cC[:, 0:8], cC[:, 8:16]; sr_, sc_ = sC[:, 0:8], sC[:, 8:16]
    c["k_cosD"] = np.ascontiguousarray(np.concatenate([cr, cr, cc_, cc_], 1).T)
    c["k_sinD"] = np.ascontiguousarray(np.concatenate([-sr_, sr_, -sc_, sc_], 1).T)
    return c


WNAMES = ["w_mod", "b_mod", "norm1_g", "norm2_g", "w_in", "a_qnorm_g", "a_knorm_g", "b_gate_w_fwd", "b_gate_b_fwd", "b_gate_w_bwd",
          "b_gate_b_bwd", "b_onorm_g", "c_lq1", "c_lk1", "c_lq2", "c_lk2", "c_onorm_g", "d_qnorm_g", "d_w_uq", "d_kvnorm_g", "d_w_ukv",
          "w_branch", "w_out", "w_ffu", "w_ffg", "conv_w", "conv_b", "w_ffd", "final_g"]


def in_map(inp, c, consts, wts):
    m = dict(wts)
    m.update(consts)
    f = lambda a: np.ascontiguousarray(a, dtype=np.float32)
    m["x"] = f(np.stack([inp["x_prompt"][4 * c:4 * c + 4].reshape(1024, 1024), inp["x_sample"][c]]))
    m["cvec"] = f(np.stack([inp["c_ctx"], inp["c"][c]]))
    m["ca_k"] = f(inp["cache_a_k"][c].reshape(4, 512, 128)); m["ca_v"] = f(inp["cache_a_v"][c].reshape(4, 512, 128))
    m["sb_f"] = f(inp["state_b_fwd"][c].reshape(4, 128, 64)); m["sb_b"] = f(inp["state_b_bwd"][c].reshape(4, 128, 64))
    m["cc_k"] = f(inp["cache_c_k"][c].reshape(4, 512, 256)); m["cc_v"] = f(inp["cache_c_v"][c].reshape(4, 512, 256))
    m["cd_ckv"] = f(inp["cache_d_ckv"][c]); m["cd_kr"] = f(inp["cache_d_krope"][c])
    return m


def assemble(R):
    y_p = np.concatenate([r["y"][0].reshape(4, 256, 1024) for r in R], 0)
    y_s = np.stack([r["y"][1] for r in R], 0)

    def cat(name, shp):
        return np.concatenate([r[name].reshape((4,) + shp) for r in R], 0)
    return (y_p, y_s, cat("nak", (4, 256, 2, 64)), cat("nav", (4, 256, 2, 64)), cat("nbf", (4, 4, 32, 64)), cat("nbb", (4, 4, 32, 64)),
            cat("nck", (4, 256, 4, 2, 32)), cat("ncv", (4, 256, 4, 64)), cat("nckv", (4, 256, 128)), cat("nkr", (4, 256, 32)))


def kernel(**inp):
    inp = {k: np.asarray(v) for k, v in inp.items()}
    kb = build()
    nc = kb.finish()
    consts = _consts()
    wts = {k: np.ascontiguousarray(inp[k], dtype=np.float32) for k in WNAMES}
    in_maps = [in_map(inp, c, consts, wts) for c in range(8)]
    res = run_bass_kernel_spmd(nc, in_maps, core_ids=list(range(8)))
    return assemble(res.results)
```

```python
from contextlib import ExitStack
import numpy as np
import concourse.bass as bass
import concourse.mybir as mybir

F32 = mybir.dt.float32
BF16 = mybir.dt.bfloat16
AF = mybir.ActivationFunctionType
ALU = mybir.AluOpType
AX = mybir.AxisListType

EPOCH = 30000
NDMA_SEM = 20


class Res:
    __slots__ = ("name", "w", "r", "excl")

    def __init__(self, name="", excl=False):
        self.name = name
        self.excl = excl
        self.w = []
        self.r = []


class EngState:
    def __init__(self, name, handle_name, is_compute):
        self.name = name
        self.handle_name = handle_name
        self.is_compute = is_compute
        self.prog = []
        self.sem = None
        self.cnt = 0
        self.known = {}
        self.dma_ring = []
        self.dma_i = 0
        self.nops = 0


class KB:
    def __init__(self):
        self.nc = bass.Bass("TRN2", target_bir_lowering=False)
        self.es = ExitStack()
        self.E = {
            "pe": EngState("pe", "tensor", True),
            "act": EngState("act", "scalar", True),
            "dve": EngState("dve", "vector", True),
            "pool": EngState("pool", "gpsimd", True),
            "sp": EngState("sp", "sync", False),
        }
        self.nsem = 0
        self.all_dma_events = []

    def new_sem(self, name):
        self.nsem += 1
        return self.es.enter_context(self.nc.semaphore(f"{name}_{self.nsem}"))

    def sbuf(self, name, shape, dtype=F32):
        return self.es.enter_context(self.nc.sbuf_tensor(name, list(shape), dtype))

    def psum(self, name, shape, dtype=F32):
        return self.es.enter_context(self.nc.psum_tensor(name, list(shape), dtype))

    def dram(self, name, shape, dtype, kind):
        return self.nc.dram_tensor(name, list(shape), dtype, kind=kind)

    def _wait(self, st, ev):
        sem, val, _ = ev
        k = id(sem)
        if st.known.get(k, 0) >= val:
            return
        st.known[k] = val
        st.prog.append(("wait", sem, val))

    def _deps(self, st, reads, writes, is_dma):
        for r in reads:
            for ev in r.w:
                self._wait(st, ev)
            if r.excl:
                for ev in r.r:
                    if ev[2] != st.name:
                        self._wait(st, ev)
        for w in writes:
            for ev in w.w:
                if is_dma or ev[2] != st.name or not st.is_compute:
                    self._wait(st, ev)
            for ev in w.r:
                self._wait(st, ev)

    def _commit(self, ev, reads, writes):
        for r in reads:
            if r in writes:
                continue
            if ev[2] in ("pe", "act", "dve", "pool"):
                r.r = [e for e in r.r if e[2] != ev[2]]
            r.r.append(ev)
        for w in writes:
            w.w = [ev]
            w.r = []

    @staticmethod
    def _flat(xs):
        out = []
        for x in xs:
            if isinstance(x, (list, tuple)):
                out.extend(KB._flat(x))
            else:
                out.append(x)
        return out

    def op(self, eng, fn, reads=(), writes=()):
        reads = self._flat(reads); writes = self._flat(writes)
        st = self.E[eng]
        assert st.is_compute
        self._deps(st, reads, writes, False)
        if st.sem is None or st.cnt >= EPOCH:
            st.sem = self.new_sem(f"s_{eng}")
            st.cnt = 0
        st.cnt += 1
        ev = (st.sem, st.cnt, eng)
        st.prog.append(("op", fn, st.sem))
        st.nops += 1
        self._commit(ev, reads, writes)
        return ev

    def dma(self, q, out, in_, reads=(), writes=(), **kw):
        st = self.E[q]
        reads = self._flat(reads); writes = self._flat(writes)
        kw.setdefault("allow_slow_non_contiguous", True)
        for r in reads:
            for ev in r.w:
                self._wait(st, ev)
        for w in writes:
            for ev in w.w:
                if ev[2].startswith("dma") and not w.r:
                    continue
                self._wait(st, ev)
            for ev in w.r:
                self._wait(st, ev)
        if not st.dma_ring:
            st.dma_ring = [[self.new_sem(f"d_{q}"), 0] for _ in range(NDMA_SEM)]
        slot = st.dma_ring[st.dma_i % NDMA_SEM]
        st.dma_i += 1
        if slot[1] > 0:
            self._wait(st, (slot[0], slot[1], "dma"))
        slot[1] += 16
        ev = (slot[0], slot[1], "dma_" + q)
        st.prog.append(("dma", out, in_, slot[0], kw))
        keep = {id(w): list(w.w) for w in writes if w.w and not w.r and all(e[2].startswith("dma") for e in w.w)}
        self._commit(ev, reads, writes)
        for w in writes:
            if id(w) in keep:
                w.w = keep[id(w)] + [ev]
        self.all_dma_events.append(ev)
        return ev

    def finish(self):
        nc = self.nc
        sp = self.E["sp"]
        for st in self.E.values():
            for sem, val in st.dma_ring:
                if val > 0:
                    self._wait(sp, (sem, val, "dma"))
        with nc.Block() as block:
            for st in self.E.values():
                if not st.prog:
                    continue

                def body(eng, st=st):
                    for item in st.prog:
                        if item[0] == "wait":
                            eng.wait_ge(item[1], item[2])
                        elif item[0] == "op":
                            ins = item[1](eng)
                            ins.then_inc(item[2], 1)
                        else:
                            _, out, in_, sem, kw = item
                            eng.dma_start(out=out, in_=in_, **kw).then_inc(sem, 16)

                getattr(block, st.handle_name)(body)
        self.es.close()
        return nc

import math
from concourse.bass_utils import run_bass_kernel_spmd

L = 4
EPS = 1e-6
MUL, ADD, SUB = ALU.mult, ALU.add, ALU.subtract
VOFFS = [0, 64, 192, 256]
VDST = [(0, 64, 0, 64), (128, 256, 64, 192), (320, 384, 192, 256)]


def barrier(kb):
    comp = ["pe", "act", "dve", "pool"]
    for e in comp + ["sp"]:
        st = kb.E[e]
        for o in comp:
            so = kb.E[o]
            if o != e and so.sem is not None and so.cnt > 0:
                kb._wait(st, (so.sem, so.cnt, o))
        for q in kb.E.values():
            for sem, val in q.dma_ring:
                if val > 0:
                    kb._wait(st, (sem, val, "dma"))


def build(NLAYERS=L, GROUPS=(0, 1), STOP='', DEBUG=False):
    kb = KB()
    nc = kb.nc
    BCUT = int(STOP.split(':')[1]) if ':' in STOP else 0

    def din(name, shape):
        return kb.dram(name, shape, F32, "ExternalInput").ap()

    def dout(name, shape):
        return kb.dram(name, shape, F32, "ExternalOutput").ap()

    x_d = din("x", [2, 1024, 1024])
    cvec_d = din("cvec", [2, 1024])
    cak_d = din("ca_k", [L, 512, 128]); cav_d = din("ca_v", [L, 512, 128])
    sbf_d = din("sb_f", [L, 128, 64]); sbb_d = din("sb_b", [L, 128, 64])
    cck_d = din("cc_k", [L, 512, 256]); ccv_d = din("cc_v", [L, 512, 256])
    cdc_d = din("cd_ckv", [L, 512, 128]); cdr_d = din("cd_kr", [L, 512, 32])
    wmod_d = din("w_mod", [L, 1024, 6144]); bmod_d = din("b_mod", [L, 6144])
    n1_d = din("norm1_g", [L, 1024]); n2_d = din("norm2_g", [L, 1024])
    win_d = din("w_in", [L, 1024, 6528])
    aqg_d = din("a_qnorm_g", [L, 64]); akg_d = din("a_knorm_g", [L, 64])
    gwf_d = din("b_gate_w_fwd", [L, 16, 128]); gbf_d = din("b_gate_b_fwd", [L, 128])
    gwb_d = din("b_gate_w_bwd", [L, 16, 128]); gbb_d = din("b_gate_b_bwd", [L, 128])
    bon_d = din("b_onorm_g", [L, 64])
    lq1_d = din("c_lq1", [L, 32]); lk1_d = din("c_lk1", [L, 32]); lq2_d = din("c_lq2", [L, 32]); lk2_d = din("c_lk2", [L, 32])
    con_d = din("c_onorm_g", [L, 64])
    dqg_d = din("d_qnorm_g", [L, 192]); wuq_d = din("d_w_uq", [L, 192, 384])
    dkg_d = din("d_kvnorm_g", [L, 128]); wukv_d = din("d_w_ukv", [L, 128, 512])
    wbr_d = din("w_branch", [L, 4, 256, 1024]); wout_d = din("w_out", [L, 1024, 1024])
    wfu_d = din("w_ffu", [L, 1024, 2816]); wfg_d = din("w_ffg", [L, 1024, 2816])
    cw_d = din("conv_w", [L, 3, 2816]); cb_d = din("conv_b", [L, 2816]); wfd_d = din("w_ffd", [L, 2816, 1024])
    fg_d = din("final_g", [1024])
    k_ident = din("k_ident", [128, 128]); k_bd64 = din("k_bd64", [128, 128]); k_tri = din("k_tri", [4, 128, 128])
    k_hm = din("k_hm", [128, 4]); k_pm = din("k_pm", [128, 2])
    k_cosA = din("k_cosA", [1024, 32]); k_sinA = din("k_sinA", [1024, 32])
    k_cosC = din("k_cosC", [1024, 16]); k_sinC = din("k_sinC", [1024, 16])
    k_cosD = din("k_cosD", [32, 1024]); k_sinD = din("k_sinD", [32, 1024])

    y_d = dout("y", [2, 1024, 1024])
    nak_d = dout("nak", [4, L, 256, 128]); nav_d = dout("nav", [4, L, 256, 128])
    nbf_d = dout("nbf", [4, L, 128, 64]); nbb_d = dout("nbb", [4, L, 128, 64])
    nck_d = dout("nck", [4, L, 256, 256]); ncv_d = dout("ncv", [4, L, 256, 256])
    nckv_d = dout("nckv", [4, L, 256, 128]); nkr_d = dout("nkr", [4, L, 256, 32])

    if DEBUG:
        dbg_br = dout('dbg_br', [128, 8, 1024]); dbg_x1 = dout('dbg_x1', [128, 8, 1024]); dbg_x2 = dout('dbg_x2', [128, 8, 1024]); dbg_mg = dout('dbg_mg', [128, 8, 1024])
    xT = kb.sbuf("xT", [128, 8, 1024]); xT_R = [[Res(f"xT{k}_{h}") for h in range(2)] for k in range(8)]
    hT = kb.sbuf("hT", [128, 8, 1024], BF16); hT_R = [Res("hT0"), Res("hT1")]
    NSLOT = 3
    ring = [kb.sbuf(f"ring{i}", [128, 8192], BF16) for i in range(NSLOT)]
    ring_R = [Res(f"ring{i}") for i in range(NSLOT)]
    ring_i = [0]
    arena = kb.sbuf("arena", [128, 30720], BF16)
    PS = kb.psum("ps", [128, 4096])
    pb = [PS[:, i * 512:(i + 1) * 512] for i in range(8)]
    pb_R = [Res(f"pb{i}", excl=True) for i in range(8)]
    PSB = PS[:, 7 * 512:8 * 512].bitcast(BF16)

    def new_slot(avoid=None):
        i = ring_i[0] % NSLOT
        ring_i[0] += 1
        while avoid is not None and ring[i] is avoid:
            i = ring_i[0] % NSLOT
            ring_i[0] += 1
        return ring[i], ring_R[i]

    def av(off, n):
        return arena[:, off:off + n]

    def vap(off, stride):
        return bass.AP(arena, off, [[30720, 128], [stride, 2], [1, 64]])

    identf = kb.sbuf("identf", [128, 128]); identb = kb.sbuf("identb", [128, 128], BF16)
    onesb = kb.sbuf("onesb", [128, 128], BF16); bd64 = kb.sbuf("bd64", [128, 128], BF16)
    tri = kb.sbuf("tri", [128, 4, 128]); hm = kb.sbuf("hm", [128, 4]); pm = kb.sbuf("pm", [128, 2])
    onescol = kb.sbuf("onescol", [128, 1]); onesrow = kb.sbuf("onesrow", [1, 128]); epsT = kb.sbuf("epsT", [128, 1])
    onesf = kb.sbuf("onesf", [128, 128])
    cosA = kb.sbuf("cosA", [128, 8, 32]); sinA = kb.sbuf("sinA", [128, 8, 32])
    cosC = kb.sbuf("cosC", [128, 8, 16]); sinC = kb.sbuf("sinC", [128, 8, 16])
    cosD = kb.sbuf("cosD", [128, 1024]); sinD = kb.sbuf("sinD", [128, 1024])
    CR = Res("consts")
    kb.dma("sp", identf[:], k_ident, writes=[CR])
    kb.dma("pool", identb[:], k_ident, writes=[CR])
    kb.dma("pool", bd64[:], k_bd64, writes=[CR])
    kb.dma("sp", tri[:], k_tri.rearrange("m s t -> s m t"), writes=[CR])
    kb.dma("sp", hm[:], k_hm, writes=[CR]); kb.dma("sp", pm[:], k_pm, writes=[CR])
    kb.dma("sp", cosA[:], k_cosA.rearrange("(t p) d -> p t d", p=128), writes=[CR])
    kb.dma("sp", sinA[:], k_sinA.rearrange("(t p) d -> p t d", p=128), writes=[CR])
    kb.dma("sp", cosC[:], k_cosC.rearrange("(t p) d -> p t d", p=128), writes=[CR])
    kb.dma("sp", sinC[:], k_sinC.rearrange("(t p) d -> p t d", p=128), writes=[CR])
    kb.dma("sp", cosD[64:96, :], k_cosD, writes=[CR]); kb.dma("sp", sinD[64:96, :], k_sinD, writes=[CR])
    kb.op("dve", lambda e: e.memset(onesb[:], 1.0), writes=[CR])
    kb.op("dve", lambda e: e.memset(onescol[:], 1.0), writes=[CR])
    kb.op("dve", lambda e: e.memset(onesrow[:], 1.0), writes=[CR])
    kb.op("dve", lambda e: e.memset(onesf[:], 1.0), writes=[CR])
    kb.op("dve", lambda e: e.memset(epsT[:], EPS), writes=[CR])

    def mm(out, lhsT, rhs, start, stop, reads, writes):
        kb.op("pe", lambda e: e.matmul(out, lhsT=lhsT, rhs=rhs, start=start, stop=stop), reads=reads, writes=writes)

    def tr(out, in_, ident, reads, writes):
        kb.op("pe", lambda e: e.transpose(out=out, in_=in_, identity=ident), reads=reads + [CR], writes=writes)

    def act(out, in_, func, reads, writes, scale=1.0, bias=None, accum_out=None):
        kw = {}
        if bias is not None:
            kw["bias"] = bias
        if accum_out is not None:
            kw["accum_out"] = accum_out
        kb.op("act", lambda e: e.activation(out=out, in_=in_, func=func, scale=scale, **kw), reads=reads, writes=writes)

    def tt(out, in0, in1, op, reads, writes, eng="dve"):
        kb.op(eng, lambda e: e.tensor_tensor(out=out, in0=in0, in1=in1, op=op), reads=reads, writes=writes)

    def stt(out, in0, scalar, in1, op0, op1, reads, writes):
        kb.op("dve", lambda e: e.scalar_tensor_tensor(out=out, in0=in0, scalar=scalar, in1=in1, op0=op0, op1=op1), reads=reads, writes=writes)

    def ts(out, in0, s1, op0, reads, writes, s2=None, op1=None, eng="dve"):
        if op1 is None:
            kb.op(eng, lambda e: e.tensor_scalar(out=out, in0=in0, scalar1=s1, scalar2=None, op0=op0), reads=reads, writes=writes)
        else:
            kb.op(eng, lambda e: e.tensor_scalar(out=out, in0=in0, scalar1=s1, scalar2=s2, op0=op0, op1=op1), reads=reads, writes=writes)

    def cp(out, in_, reads, writes, eng="dve"):
        if eng == "act":
            kb.op("act", lambda e: e.copy(out=out, in_=in_), reads=reads, writes=writes)
        else:
            kb.op(eng, lambda e: e.tensor_copy(out=out, in_=in_), reads=reads, writes=writes)

    def recipf(out, in_, reads, writes):
        act(out, in_, AF.Ln, reads, writes)
        act(out, out, AF.Exp, writes, writes, scale=-1.0)

    def recip(out, in_, reads, writes):
        kb.op("dve", lambda e: e.reciprocal(out=out, in_=in_), reads=reads, writes=writes)

    def bc(ap, shape):
        return ap.broadcast_to(list(shape))

    pbi = [0]

    def nb():
        i = pbi[0] % 7
        pbi[0] += 1
        return i

    scT = kb.sbuf("scT", [128, 8, 2]); modT = kb.sbuf("modT", [128, L, 48, 2]); bmT = kb.sbuf("bmT", [128, L, 48])
    n1T = kb.sbuf("n1T", [128, L, 8]); n2T = kb.sbuf("n2T", [128, L, 8])
    A1 = kb.sbuf("A1", [128, L, 8, 2]); A2 = kb.sbuf("A2", [128, L, 8, 2])
    MR = Res("mod")
    with nc.allow_non_contiguous_dma(reason="tiny transposed vector loads"):
        for g_ in range(2):
            kb.dma("sp", scT[:, :, g_], cvec_d[g_].rearrange("(k p) -> p k", p=128), writes=[MR])
        for l_ in range(L):
            kb.dma("sp", bmT[:, l_, :], bmod_d[l_].rearrange("(j p) -> p j", p=128), writes=[MR])
            kb.dma("sp", n1T[:, l_, :], n1_d[l_].rearrange("(k p) -> p k", p=128), writes=[MR])
            kb.dma("sp", n2T[:, l_, :], n2_d[l_].rearrange("(k p) -> p k", p=128), writes=[MR])
    act(scT[:], scT[:], AF.Silu, [MR], [MR])
    scb = kb.sbuf("scb", [128, 8, 2], BF16)
    cp(scb[:], scT[:], [MR], [MR])
    for l in range(L):
        for cb in range(8):
            slot, wr = new_slot()
            w = slot[:, 0:6144].rearrange("p (k c) -> p k c", k=8)
            kb.dma("pool", w, wmod_d[l].rearrange("(k p) c -> p k c", p=128)[:, :, cb * 768:(cb + 1) * 768], writes=[wr])
            b = nb()
            for jj in range(6):
                for kc in range(8):
                    mm(pb[b][:, jj * 2:(jj + 1) * 2], w[:, kc, jj * 128:(jj + 1) * 128], scb[:, kc, :], kc == 0, kc == 7, [wr, MR], [pb_R[b]])
            tt(modT[:, l, cb * 6:(cb + 1) * 6, :], pb[b][:, 0:12].rearrange("p (j g) -> p j g", g=2),
               bc(bmT[:, l, cb * 6:(cb + 1) * 6].unsqueeze(2), [128, 6, 2]), ADD, [pb_R[b], MR], [MR])
    for l in range(L):
        stt(A1[:, l], modT[:, l, 8:16, :], 1.0, bc(n1T[:, l, :].unsqueeze(2), [128, 8, 2]), ADD, MUL, [MR], [MR])
        stt(A2[:, l], modT[:, l, 32:40, :], 1.0, bc(n2T[:, l, :].unsqueeze(2), [128, 8, 2]), ADD, MUL, [MR], [MR])

    lqk = kb.sbuf("lqk", [32, 4, L]); lamT = kb.sbuf("lamT", [128, L]); neglam = kb.sbuf("neglam", [128, L])
    ocs = kb.sbuf("ocs", [128, L]); lam2 = kb.sbuf("lam2", [128, 2, L])
    with nc.allow_non_contiguous_dma(reason="tiny transposed vector loads"):
        for i, d in enumerate([lq1_d, lk1_d, lq2_d, lk2_d]):
            kb.dma("sp", lqk[:, i, :], d.rearrange("l d -> d l"), writes=[MR])
        kb.dma("sp", ocs[0:64, :], con_d.rearrange("l d -> d l"), writes=[MR])
        kb.dma("sp", ocs[64:128, :], con_d.rearrange("l d -> d l"), writes=[MR])
    tt(lqk[:, 0, :], lqk[:, 0, :], lqk[:, 1, :], MUL, [MR], [MR])
    tt(lqk[:, 2, :], lqk[:, 2, :], lqk[:, 3, :], MUL, [MR], [MR])
    b = nb()
    mm(pb[b][:, 0:L], onesf[0:32, :], lqk[:, 0, :], True, True, [MR, CR], [pb_R[b]])
    mm(pb[b][:, L:2 * L], onesf[0:32, :], lqk[:, 2, :], True, True, [MR, CR], [pb_R[b]])
    act(lam2[:].rearrange("p a l -> p (a l)"), pb[b][:, 0:2 * L], AF.Exp, [pb_R[b]], [MR])
    tt(lamT[:], lam2[:, 0, :], lam2[:, 1, :], SUB, [MR], [MR])
    lam_init = [0.8 - 0.6 * math.exp(-0.3 * l) for l in range(L)]
    for l in range(L):
        ts(lamT[:, l:l + 1], lamT[:, l:l + 1], lam_init[l], ADD, [MR], [MR])
        ts(ocs[:, l:l + 1], ocs[:, l:l + 1], 1.0 - lam_init[l], MUL, [MR], [MR])
    ts(neglam[:], lamT[:], -1.0, MUL, [MR], [MR])

    gA = kb.sbuf("gA", [128, 384]); gDQ = kb.sbuf("gDQ", [128, 192]); gKV = kb.sbuf("gKV", [128, 128]); gBO = kb.sbuf("gBO", [128, 256])
    GW = kb.sbuf("GW", [16, 256]); GBias = kb.sbuf("GBias", [1, 256]); cwT = kb.sbuf("cwT", [128, 3, 22]); cbT = kb.sbuf("cbT", [128, 22])
    fgB = arena[:, 0:2048].bitcast(F32); fgB_R = Res("fgB")
    LP = Res("layer_params")

    def load_layer_params(l):
        def bsrc(d, n, rep):
            return bass.AP(d.tensor, d[l].offset, [[0, 128], [0, rep], [1, n]])
        kb.dma("sp", gA[:, 0:256].rearrange("p (r d) -> p r d", r=4), bsrc(aqg_d, 64, 4), writes=[LP])
        kb.dma("sp", gA[:, 256:384].rearrange("p (r d) -> p r d", r=2), bsrc(akg_d, 64, 2), writes=[LP])
        kb.dma("sp", gDQ[:], dqg_d[l].partition_broadcast(128), writes=[LP])
        kb.dma("sp", gKV[:], dkg_d[l].partition_broadcast(128), writes=[LP])
        kb.dma("sp", gBO[:].rearrange("p (r d) -> p r d", r=4), bsrc(bon_d, 64, 4), writes=[LP])
        kb.dma("sp", GW[0:16, 0:128], gwf_d[l], writes=[LP]); kb.dma("sp", GW[0:16, 128:256], gwb_d[l], writes=[LP])
        kb.dma("sp", GBias[0:1, 0:128], gbf_d[l:l + 1, :], writes=[LP]); kb.dma("sp", GBias[0:1, 128:256], gbb_d[l:l + 1, :], writes=[LP])
        with nc.allow_non_contiguous_dma(reason="tiny transposed vector loads"):
            for w_ in range(3):
                kb.dma("sp", cwT[:, w_, :], cw_d[l, w_].rearrange("(f p) -> p f", p=128), writes=[LP])
            kb.dma("sp", cbT[:], cb_d[l].rearrange("(f p) -> p f", p=128), writes=[LP])

    stg = [kb.sbuf(f"stg{i}", [128, 1024]) for i in range(2)]; stg_R = [Res("stg0"), Res("stg1")]
    stg_i = [0]

    def nstg():
        i = stg_i[0] % 2
        stg_i[0] += 1
        return stg[i], stg_R[i]

    tA = kb.sbuf("tA", [128, 1024]); tA_R = (Res("tA0"), Res("tA1"))
    tB = kb.sbuf("tB", [128, 1024]); tB_R = (Res("tB0"), Res("tB1"))
    tC = kb.sbuf("tC", [128, 512]); tC_R = Res("tC")
    tC1 = kb.sbuf("tC1", [128, 512]); tC1_R = Res("tC1")
    sm = kb.sbuf("sm", [128, 64]); sm_R = Res("sm")
    sm1 = kb.sbuf("sm1", [128, 64]); sm1_R = Res("sm1")
    rstd = tC1; rstd_R = tC1_R
    sqb = arena[:, 0:4096].rearrange("p (k n) -> p k n", k=8); sqb_R = Res("sqb")
    MG_R = [Res("MG0"), Res("MG1")]
    PT = [kb.sbuf(f"PT{i}", [128, 512], BF16) for i in range(3)]; PT_R = [Res(f"PT{i}") for i in range(3)]
    pt_i = [0]
    qkb = kb.sbuf("qkb", [128, 512], BF16); qkb_R = Res("qkb")
    qkb1 = kb.sbuf("qkb1", [128, 512], BF16); qkb1_R = Res("qkb1")
    krpad = kb.sbuf("krpad", [128, 96], BF16); krpad_R = Res("krpad")
    krpad1 = kb.sbuf("krpad1", [128, 96], BF16); krpad1_R = Res("krpad1")
    kb.op("dve", lambda e: e.memset(krpad[:], 0.0), writes=[krpad_R])
    kb.op("dve", lambda e: e.memset(krpad1[:], 0.0), writes=[krpad1_R])
    Sst = [kb.sbuf(f"Sst{i}", [128, 64]) for i in range(2)]; Sst_R = [Res("Sf"), Res("Sb")]
    GL = kb.sbuf("GL", [128, 8, 2]); GL_R = Res("GL")
    zgT = tC[0:16, 256:512]; zgT_R = Res("zgT")
    zgT1 = tC1[0:16, 256:512]; zgT1_R = Res("zgT1")
    TSET = [dict(tA_=tA[:, 0:512], tA_R_=tA_R[0], tB_=tB[:, 0:512], tB_R_=tB_R[0], tC_=tC, tC_R_=tC_R, sm_=sm, sm_R_=sm_R, qkb_=qkb, qkb_R_=qkb_R,
                 krpad_=krpad, krpad_R_=krpad_R, zgT_=zgT, zgT_R_=zgT_R),
            dict(tA_=tA[:, 512:1024], tA_R_=tA_R[1], tB_=tB[:, 512:1024], tB_R_=tB_R[1], tC_=tC1, tC_R_=tC1_R, sm_=sm1, sm_R_=sm1_R, qkb_=qkb1, qkb_R_=qkb1_R,
                 krpad_=krpad1, krpad_R_=krpad1_R, zgT_=zgT1, zgT_R_=zgT1_R)]

    def norm_mod(Asc, shift_c0, l, g):
        for h in range(2):
            hs = slice(h * 512, (h + 1) * 512)
            xr = [xT_R[k][h] for k in range(8)]
            act(sqb, xT[:, :, hs], AF.Square, xr, [sqb_R, MG_R[0], MG_R[1]])
            b = nb()
            for kc in range(8):
                mm(pb[b], onesb[:], sqb[:, kc, :], kc == 0, kc == 7, [sqb_R, CR], [pb_R[b]])
            act(rstd[:], pb[b], AF.Ln, [pb_R[b], CR], [rstd_R], scale=1.0 / 1024, bias=epsT[:])
            act(rstd[:], rstd[:], AF.Exp, [rstd_R], [rstd_R], scale=-0.5)
            for kc in range(8):
                tmp, tr_ = (tA, tA_R) if kc % 2 == 0 else (tB, tB_R)
                tt(tmp[:, 0:512], xT[:, kc, hs], rstd[:], MUL, [xT_R[kc][h], rstd_R], [tr_])
                act(hT[:, kc, hs], tmp[:, 0:512], AF.Identity, [tr_, MR], [hT_R[h]],
                    scale=Asc[:, l, kc, g:g + 1], bias=modT[:, l, shift_c0 + kc, g:g + 1])

    def load_w(dst, src, sr):
        kb.dma("pool", dst, src, writes=[sr])

    def proj_tm(bank0, ncols, W, wr, t):
        nbk = (ncols + 511) // 512
        for bi in range(nbk):
            c0 = bi * 512
            c1 = min(ncols, c0 + 512)
            for kc in range(8):
                mm(pb[bank0 + bi][:, 0:c1 - c0], hT[:, kc, t * 128:(t + 1) * 128], W[:, kc, c0:c1], kc == 0, kc == 7,
                   [hT_R[t // 4], wr], [pb_R[bank0 + bi]])

    def rope_tm(src, src_R, U2, hs, cos_t, sin_t, out, tA, tA_R, tB, tB_R, qkb_R):
        xv = src.rearrange("p (u r h i) -> p u r h i", u=U2, r=2, h=2)
        ov = out.rearrange("p (u r h i) -> p u r h i", u=U2, r=2, h=2)
        cv = bc(cos_t.rearrange("p (r i) -> p r i", r=2).unsqueeze(1), [128, U2, 2, hs])
        sv = bc(sin_t.rearrange("p (r i) -> p r i", r=2).unsqueeze(1), [128, U2, 2, hs])
        n = U2 * 2 * hs
        a = tA[:, 0:n].rearrange("p (u r i) -> p u r i", u=U2, r=2)
        b2 = tB[:, 0:n].rearrange("p (u r i) -> p u r i", u=U2, r=2)
        x0 = xv[:, :, :, 0, :]; x1 = xv[:, :, :, 1, :]
        tt(a, x0, cv, MUL, [src_R, CR], [tA_R]); tt(b2, x1, sv, MUL, [src_R, CR], [tB_R])
        tt(ov[:, :, :, 0, :], a, b2, SUB, [tA_R, tB_R], [qkb_R])
        tt(a, x1, cv, MUL, [src_R, CR], [tA_R]); tt(b2, x0, sv, MUL, [src_R, CR], [tB_R])
        tt(ov[:, :, :, 1, :], a, b2, ADD, [tA_R, tB_R], [qkb_R])

    def attend(kT, qT, vT, kts, q0, N, scale, reads, ob, avoid=()):
        sb = [None] * len(kts)

        def S(i):
            b = nb()
            while b == ob or b in avoid:
                b = nb()
            sb[i] = b
            mm(pb[b][:, 0:N], kT(kts[i]), qT, True, True, reads, [pb_R[b]])
        S(0)
        for i in range(len(kts)):
            if i + 1 < len(kts):
                S(i + 1)
            p = pt_i[0] % 3
            pt_i[0] += 1
            act(PT[p][:, 0:N], pb[sb[i]][:, 0:N], AF.Exp, [pb_R[sb[i]]], [PT_R[p]], scale=scale)
            mm(pb[ob][:, 0:N], vT(kts[i]), PT[p][:, 0:N], i == 0, i == len(kts) - 1, reads + [PT_R[p]], [pb_R[ob]])

    def obank():
        b = nb()
        return b

    def attend_multi(jobs, finishers, LA=2):
        flat = [(ji, i) for ji, jb in enumerate(jobs) for i in range(len(jb["kts"]))]
        sbank = {}
        ob_of = {}
        live = []
        gobs = {}

        def alloc():
            b = nb()
            while b in live or b in sbank.values():
                b = nb()
            return b

        def S(idx):
            ji, i = flat[idx]
            jb = jobs[ji]
            b = alloc()
            sbank[idx] = b
            mm(pb[b][:, 0:jb["N"]], jb["kT"](jb["kts"][i]), jb["qT"], True, True, jb["reads"], [pb_R[b]])
        for idx in range(min(LA, len(flat))):
            S(idx)
        for idx, (ji, i) in enumerate(flat):
            if idx + LA < len(flat):
                S(idx + LA)
            jb = jobs[ji]
            N = jb["N"]
            if i == 0:
                ob_of[ji] = alloc()
                live.append(ob_of[ji])
            ob = ob_of[ji]
            p = pt_i[0] % 3
            pt_i[0] += 1
            sb_ = sbank.pop(idx)
            act(PT[p][:, 0:N], pb[sb_][:, 0:N], AF.Exp, [pb_R[sb_]], [PT_R[p]], scale=jb["scale"])
            last = (i == len(jb["kts"]) - 1)
            mm(pb[ob][:, 0:N], jb["vT"](jb["kts"][i]), PT[p][:, 0:N], i == 0, last, jb["reads"] + [PT_R[p]], [pb_R[ob]])
            if last:
                gid = jb["gid"]
                gobs.setdefault(gid, []).append(ob)
                if ji + 1 >= len(jobs) or jobs[ji + 1]["gid"] != gid:
                    finishers[gid](gobs[gid])
                    for o in gobs[gid]:
                        live.remove(o)

    for g in GROUPS:
        S_GRP = (g == 1)
        NKT = 12 if S_GRP else 8
        KOFF = 4 if S_GRP else 0
        NK = NKT * 128
        if S_GRP:
            seqs = [(0, 1024, list(range(12)))]
        else:
            seqs = [(s * 256, 256, [2 * s, 2 * s + 1]) for s in range(4)]
        for t in range(8):
            st_, sr = nstg()
            kb.dma("sp", st_[:], x_d[g, t * 128:(t + 1) * 128, :], writes=[sr])
            for hh in range(2):
                b = nb()
                for kk in range(4):
                    kc = hh * 4 + kk
                    tr(pb[b][:, kk * 128:(kk + 1) * 128], st_[:, kc * 128:(kc + 1) * 128], identf[:], [sr], [pb_R[b]])
                cp(xT[:, hh * 4:(hh + 1) * 4, t * 128:(t + 1) * 128], pb[b].rearrange("p (k c) -> p k c", k=4), [pb_R[b]],
                   [xT_R[k][t // 4] for k in range(hh * 4, hh * 4 + 4)], eng=("act" if hh else "dve"))

        for l in range(NLAYERS):
            load_layer_params(l)
            norm_mod(A1, 0, l, g)
            if STOP == 'N':
                continue
            BR_OFF = 22528
            BR = arena[:, BR_OFF:BR_OFF + 8192].rearrange("p (j b n) -> p j b n", j=4, b=2)
            BR_R = [Res(f"BR{j}") for j in range(4)]
            OUTS = not S_GRP

            slot, sr = new_slot()
            W = slot[:, 0:4096].rearrange("p (k c) -> p k c", k=8)
            wsrc = win_d[l].rearrange("(k p) c -> p k c", p=128)
            for g2_ in range(2):
                for kv_ in range(2):
                    load_w(W[:, :, g2_ * 128 + kv_ * 64:g2_ * 128 + kv_ * 64 + 64], wsrc[:, :, (kv_ * 2 + g2_) * 64:(kv_ * 2 + g2_) * 64 + 64], sr)
            load_w(W[:, :, 256:512], wsrc[:, :, 256:512], sr)
            barrier(kb)
            QT = av(0, 2048).rearrange("p (b n) -> p b n", b=2); QT_R = Res("QT_A")
            KT = av(2048, 1536); KT_R = Res("KT_A")
            VA = av(3584, 4608).rearrange("p (k c) -> p k c", c=384); VA_R = Res("VA")
            kb.op("dve", lambda e: e.memset(VA[:, 0:NKT, :], 1.0), writes=[VA_R])
            if S_GRP:
                st_, sr2 = nstg()
                kb.dma("sp", st_[:, 0:512].rearrange("p (k f) -> p k f", k=4), cak_d[l].rearrange("(k p) f -> p k f", p=128), writes=[sr2])
                b = nb()
                for kt in range(4):
                    tr(pb[b][:, kt * 128:(kt + 1) * 128], st_[:, kt * 128:(kt + 1) * 128], identf[:], [sr2], [pb_R[b]])
                cp(KT[:, 0:512], pb[b], [pb_R[b]], [KT_R])
                kb.dma("pool", VA[:, 0:4, 64:128], cav_d[l].rearrange("(k p) f -> p k f", p=128)[:, :, 0:64], writes=[VA_R])
                kb.dma("pool", VA[:, 0:4, 256:320], cav_d[l].rearrange("(k p) f -> p k f", p=128)[:, :, 64:128], writes=[VA_R])
            if STOP == 'A0':
                continue
            def tile_gen(t):
                TS_ = TSET[t % 2]; tA_ = TS_['tA_']; tA_R_ = TS_['tA_R_']; tB_ = TS_['tB_']; tB_R_ = TS_['tB_R_']; tC_ = TS_['tC_']; tC_R_ = TS_['tC_R_']; sm_ = TS_['sm_']; sm_R_ = TS_['sm_R_']; qkb_ = TS_['qkb_']; qkb_R_ = TS_['qkb_R_']; krpad_ = TS_['krpad_']; krpad_R_ = TS_['krpad_R_']; zgT_ = TS_['zgT_']; zgT_R_ = TS_['zgT_R_']
                zb = t % 2
                PSBp = PSB[:, (t % 2) * 512:(t % 2) * 512 + 512]
                z = pb[zb]; zr = pb_R[zb]
                for kc in range(8):
                    mm(z, hT[:, kc, t * 128:(t + 1) * 128], W[:, kc, :], kc == 0, kc == 7, [hT_R[t // 4], sr], [zr])
                if STOP == 'A2a':
                    return
                yield
                act(tC_[:, 0:384], z[:, 0:384], AF.Square, [zr], [tC_R_])
                kb.op("dve", lambda e, sm_=sm_, tC_=tC_: e.reduce_sum(out=sm_[:, 0:6], in_=tC_[:, 0:384].rearrange("p (h d) -> p h d", h=6), axis=AX.X), reads=[tC_R_], writes=[sm_R_])
                if STOP == 'A2b':
                    return
                act(sm_[:, 0:6], sm_[:, 0:6], AF.Ln, [sm_R_, CR], [sm_R_], scale=1.0 / 64, bias=epsT[:])
                act(sm_[:, 0:6], sm_[:, 0:6], AF.Exp, [sm_R_], [sm_R_], scale=-0.5)
                if STOP == 'A2c':
                    return
                yield
                tt(tC_[:, 0:384].rearrange("p (h d) -> p h d", h=6), z[:, 0:384].rearrange("p (h d) -> p h d", h=6),
                   bc(sm_[:, 0:6].unsqueeze(2), [128, 6, 64]), MUL, [zr, sm_R_], [tC_R_])
                tt(tC_[:, 0:384], tC_[:, 0:384], gA[:], MUL, [tC_R_, LP], [tC_R_])
                kt = KOFF + t
                if STOP == 'A2':
                    return
                yield
                cp(VA[:, kt, 64:128], z[:, 384:448], [zr], [VA_R], eng="act")
                cp(VA[:, kt, 256:320], z[:, 448:512], [zr], [VA_R], eng="act")
                if S_GRP:
                    rope_tm(tC_[:, 0:384], tC_R_, 6, 16, cosA[:, t, :], sinA[:, t, :], qkb_[:, 0:384], tA_, tA_R_, tB_, tB_R_, qkb_R_)
                else:
                    cp(qkb_[:, 0:384], tC_[:, 0:384], [tC_R_], [qkb_R_])
                    o_, or_ = nstg()
                    cp(o_[:, 0:128], tC_[:, 256:384], [tC_R_], [or_], eng="act")
                    cp(o_[:, 128:256], z[:, 384:512], [zr], [or_], eng="act")
                    s_, tt_ = t // 2, (t % 2) * 128
                    kb.dma("sp", nak_d[s_, l, tt_:tt_ + 128, :], o_[:, 0:128], reads=[or_])
                    kb.dma("sp", nav_d[s_, l, tt_:tt_ + 128, :], o_[:, 128:256], reads=[or_])
                if STOP == 'A3':
                    return
                yield
                for g2 in range(2):
                    tr(PSBp[:, g2 * 128:(g2 + 1) * 128], qkb_[:, g2 * 128:(g2 + 1) * 128], identb[:], [qkb_R_], [pb_R[7]])
                tr(PSBp[:, 256:384], qkb_[:, 256:384], identb[:], [qkb_R_], [pb_R[7]])
                if STOP == 'A4':
                    return
                yield
                if STOP != 'A6':
                    cp(QT[:, :, t * 128:(t + 1) * 128], PSBp[:, 0:256].rearrange("p (b n) -> p b n", b=2), [pb_R[7]], [QT_R])
                if STOP != 'A5':
                    cp(KT[:, kt * 128:(kt + 1) * 128], PSBp[:, 256:384], [pb_R[7], QT_R], [KT_R], eng="act")
                yield
            gens_ = [tile_gen(t_) for t_ in range(8)]
            for t0_ in range(0, 8, 2):
                act_ = [gens_[t0_], gens_[t0_ + 1]]
                while act_:
                    for g_ in list(act_):
                        try:
                            next(g_)
                        except StopIteration:
                            act_.remove(g_)
            if STOP in ('A1', 'A2', 'A2a', 'A2b', 'A2c', 'A3', 'A4', 'A5', 'A6'):
                continue
            jobs = []; fins = {}
            for (q0, qlen, kts) in seqs:
                for qc in range(0, qlen, 512):
                    N = min(512, qlen - qc)
                    for kv in range(2):
                        for g2 in range(2):
                            rows = slice(kv * 64, (kv + 1) * 64)
                            orow = g2 * 64
                            srow = 64 - orow
                            if g2 == 0:
                                vfn = lambda kt, kv=kv: VA[:, kt, kv * 192 + 64:kv * 192 + 192]
                            else:
                                vfn = lambda kt, kv=kv: VA[:, kt, kv * 192:kv * 192 + 128]
                            gid = len(jobs)

                            def fin(obs, orow=orow, srow=srow, N=N, kv=kv, c0=q0 + qc):
                                ob = obs[0]
                                recipf(tA[srow:srow + 64, 0:N], pb[ob][srow:srow + 64, 0:N], [pb_R[ob]], [tA_R])
                                tt(BR[orow:orow + 64, 0, kv, c0:c0 + N], pb[ob][orow:orow + 64, 0:N], tA[srow:srow + 64, 0:N], MUL,
                                   [pb_R[ob], tA_R], [BR_R[0]])
                            fins[gid] = fin
                            jobs.append(dict(kT=(lambda kt, rows=rows: KT[rows, kt * 128:(kt + 1) * 128]), qT=QT[rows, g2, q0 + qc:q0 + qc + N], vT=vfn,
                                             kts=kts, N=N, scale=0.125, reads=[KT_R, QT_R, VA_R], gid=gid))
            attend_multi(jobs, fins)
            if STOP == 'A':
                continue
            slot, sr = new_slot()
            W = slot[:, 0:6144].rearrange("p (k c) -> p k c", k=8)
            load_w(W, win_d[l].rearrange("(k p) c -> p k c", p=128)[:, :, 1312:2080], sr)
            barrier(kb)
            QT = av(0, 2048).rearrange("p (b n) -> p b n", b=2); QT_R = Res("QT_C")
            KC = av(2048, 6144).rearrange("p (v b n) -> p v b n", v=2, b=2); KC_R = Res("KT_C")
            VC_OFF = 8192
            VC = av(VC_OFF, 4608).rearrange("p (k c) -> p k c", c=384); VC_R = Res("VC")
            kb.op("dve", lambda e: e.memset(VC[:, 0:NKT, :], 1.0), writes=[VC_R])

            def kc_store(src, kcols):
                for v in range(2):
                    for bb in range(2):
                        ts(KC[:, v, bb, kcols], src[:, bb, :], pm[:, v:v + 1], MUL, [pb_R[7], pb_R[6], CR], [KC_R])
            if S_GRP:
                for bb in range(2):
                    st_, sr2 = nstg()
                    kb.dma("sp", st_[:, 0:512].rearrange("p (k f) -> p k f", k=4), cck_d[l].rearrange("(k p) f -> p k f", p=128)[:, :, bb * 128:(bb + 1) * 128], writes=[sr2])
                    for kt in range(4):
                        tr(pb[6][:, kt * 128:(kt + 1) * 128], st_[:, kt * 128:(kt + 1) * 128], identf[:], [sr2], [pb_R[6]])
                    for v in range(2):
                        ts(KC[:, v, bb, 0:512], pb[6], pm[:, v:v + 1], MUL, [pb_R[6], CR], [KC_R])
                for (d0, d1, s0, s1) in VDST:
                    kb.dma("pool", VC[:, 0:4, d0:d1], ccv_d[l].rearrange("(k p) f -> p k f", p=128)[:, :, s0:s1], writes=[VC_R])
            def tile_gen(t):
                TS_ = TSET[t % 2]; tA_ = TS_['tA_']; tA_R_ = TS_['tA_R_']; tB_ = TS_['tB_']; tB_R_ = TS_['tB_R_']; tC_ = TS_['tC_']; tC_R_ = TS_['tC_R_']; sm_ = TS_['sm_']; sm_R_ = TS_['sm_R_']; qkb_ = TS_['qkb_']; qkb_R_ = TS_['qkb_R_']; krpad_ = TS_['krpad_']; krpad_R_ = TS_['krpad_R_']; zgT_ = TS_['zgT_']; zgT_R_ = TS_['zgT_R_']
                zb = 2 * (t % 2)
                PSBp = PSB[:, (t % 2) * 512:(t % 2) * 512 + 512]
                zr = [pb_R[zb], pb_R[zb + 1]]
                z = PS[:, zb * 512:zb * 512 + 768]
                for bi in range(2):
                    c0, c1 = bi * 512, min(768, bi * 512 + 512)
                    for kc in range(8):
                        mm(PS[:, (zb + bi) * 512:(zb + bi) * 512 + c1 - c0], hT[:, kc, t * 128:(t + 1) * 128], W[:, kc, c0:c1], kc == 0, kc == 7,
                           [hT_R[t // 4], sr], [pb_R[zb + bi]])
                yield
                kt = KOFF + t
                for (d0, d1, s0, s1) in VDST:
                    cp(VC[:, kt, d0:d1], z[:, 512 + s0:512 + s1], zr, [VC_R], eng="act")
                if S_GRP:
                    rope_tm(z[:, 0:512], zr[0], 16, 8, cosC[:, t, :], sinC[:, t, :], qkb_[:, 0:512], tA_, tA_R_, tB_, tB_R_, qkb_R_)
                else:
                    cp(qkb_[:, 0:512], z[:, 0:512], zr, [qkb_R_])
                    o_, or_ = nstg()
                    cp(o_[:, 0:512], z[:, 256:768], zr, [or_], eng="act")
                    s_, tt_ = t // 2, (t % 2) * 128
                    kb.dma("sp", nck_d[s_, l, tt_:tt_ + 128, :], o_[:, 0:256], reads=[or_])
                    kb.dma("sp", ncv_d[s_, l, tt_:tt_ + 128, :], o_[:, 256:512], reads=[or_])
                yield
                for i in range(4):
                    tr(PSBp[:, i * 128:(i + 1) * 128], qkb_[:, i * 128:(i + 1) * 128], identb[:], [qkb_R_], [pb_R[7]])
                yield
                cp(QT[:, :, t * 128:(t + 1) * 128], PSBp[:, 0:256].rearrange("p (b n) -> p b n", b=2), [pb_R[7]], [QT_R], eng="act")
                kc_store(PSBp[:, 256:512].rearrange("p (b n) -> p b n", b=2), slice(kt * 128, (kt + 1) * 128))
                yield
            gens_ = [tile_gen(t_) for t_ in range(8)]
            for t0_ in range(0, 8, 2):
                act_ = [gens_[t0_], gens_[t0_ + 1]]
                while act_:
                    for g_ in list(act_):
                        try:
                            next(g_)
                        except StopIteration:
                            act_.remove(g_)
            OCR = tC; OCR_R = tC_R
            jobs = []; fins = {}
            for (q0, qlen, kts) in seqs:
                for qc in range(0, qlen, 512):
                    N = min(512, qlen - qc)
                    for bb in range(2):
                        for hf in range(2):
                            h = bb * 2 + hf
                            rows = slice(hf * 64, (hf + 1) * 64)
                            orow = hf * 64
                            srow = 64 - orow
                            vfn = lambda kt, h=h: VC[:, kt, VOFFS[h]:VOFFS[h] + 128]
                            gid = len(jobs)

                            def fin(obs, orow=orow, srow=srow, N=N, bb=bb, hf=hf, c0=q0 + qc):
                                o1, o2 = obs
                                recipf(tA[srow:srow + 64, 0:N], pb[o1][srow:srow + 64, 0:N], [pb_R[o1]], [tA_R])
                                recipf(tB[srow:srow + 64, 0:N], pb[o2][srow:srow + 64, 0:N], [pb_R[o2]], [tB_R])
                                tt(tA[orow:orow + 64, 512:512 + N], pb[o1][orow:orow + 64, 0:N], tA[srow:srow + 64, 0:N], MUL, [pb_R[o1], tA_R], [tA_R])
                                tt(tB[orow:orow + 64, 512:512 + N], pb[o2][orow:orow + 64, 0:N], tB[srow:srow + 64, 0:N], MUL, [pb_R[o2], tB_R], [tB_R])
                                stt(OCR[orow:orow + 64, 0:N], tB[orow:orow + 64, 512:512 + N], neglam[orow:orow + 64, l:l + 1], tA[orow:orow + 64, 512:512 + N], MUL, ADD,
                                    [tA_R, tB_R, MR], [OCR_R])
                                if hf == 1:
                                    act(PT[0][:, 0:N], OCR[:, 0:N], AF.Square, [OCR_R], [PT_R[0]])
                                    b = nb()
                                    while b in obs:
                                        b = nb()
                                    mm(pb[b][:, 0:N], bd64[:], PT[0][:, 0:N], True, True, [PT_R[0], CR], [pb_R[b]])
                                    act(rstd[:, 0:N], pb[b][:, 0:N], AF.Ln, [pb_R[b], CR], [rstd_R], scale=1.0 / 64, bias=epsT[:])
                                    act(rstd[:, 0:N], rstd[:, 0:N], AF.Exp, [rstd_R], [rstd_R], scale=-0.5)
                                    tt(OCR[:, 0:N], OCR[:, 0:N], rstd[:, 0:N], MUL, [OCR_R, rstd_R], [OCR_R])
                                    act(BR[:, 2, bb, c0:c0 + N], OCR[:, 0:N], AF.Copy, [OCR_R, MR], [BR_R[2]], scale=ocs[:, l:l + 1])
                            fins[gid] = fin
                            for j in range(2):
                                jobs.append(dict(kT=(lambda kt, rows=rows, j=j, bb=bb: KC[rows, j, bb, kt * 128:(kt + 1) * 128]), qT=QT[rows, bb, q0 + qc:q0 + qc + N],
                                                 vT=vfn, kts=kts, N=N, scale=32 ** -0.5, reads=[KC_R, QT_R, VC_R], gid=gid))
            attend_multi(jobs, fins)

            if STOP == 'C':
                continue
            slot, sr = new_slot()
            W = slot[:, 0:2816].rearrange("p (k c) -> p k c", k=8)
            load_w(W, win_d[l].rearrange("(k p) c -> p k c", p=128)[:, :, 2080:2432], sr)
            UQ = slot[:, 2816:3584].rearrange("p (k c) -> p k c", k=2)
            UQS = slot[:, 3584:4352].rearrange("p (k c) -> p k c", k=2)
            UKV = slot[:, 4352:4864]
            load_w(UQ[:, 0, :], wuq_d[l, 0:128, :], sr); load_w(UQ[0:64, 1, :], wuq_d[l, 128:192, :], sr)
            load_w(UKV, wukv_d[l], sr)
            if S_GRP:
                with nc.allow_non_contiguous_dma(reason="rope column swap"):
                    for (kc_, r0, r1, pr) in ((0, 0, 128, 128), (1, 128, 192, 64)):
                        src = wuq_d[l, r0:r1, :].rearrange("p (h c) -> p h c", h=4)[:, :, 64:96].rearrange("p h (r f i) -> p h r f i", r=2, f=2)
                        dst = UQS[0:pr, kc_, :].rearrange("p (h c) -> p h c", h=4)[:, :, 64:96].rearrange("p h (r f i) -> p h r f i", r=2, f=2)
                        for f in range(2):
                            for r_ in range(2):
                                load_w(dst[:, :, r_, f, :], src[:, :, r_, 1 - f, :], sr)
            barrier(kb)
            DQN = av(0, 2048).rearrange("p (b n) -> p b n", b=2); DQN_R = Res("DQN")
            CKT = av(2048, 1536); CKT_R = Res("CKT")
            KD_ = av(3584, 6144).rearrange("p (h n) -> p h n", h=4); KD_R = Res("KT_D")
            DQT = av(9728, 4096).rearrange("p (h n) -> p h n", h=4); DQT_R = Res("DQT")
            VD_OFF = 13824
            VD = av(VD_OFF, 4608).rearrange("p (k c) -> p k c", c=384); VD_R = Res("VD")
            kb.op("dve", lambda e: e.memset(VD[:, 0:NKT, :], 1.0), writes=[VD_R])
            if S_GRP:
                st_, sr2 = nstg()
                kb.dma("sp", st_[:, 0:512].rearrange("p (k f) -> p k f", k=4), cdc_d[l].rearrange("(k p) f -> p k f", p=128), writes=[sr2])
                for kt in range(4):
                    tr(pb[6][:, kt * 128:(kt + 1) * 128], st_[:, kt * 128:(kt + 1) * 128], identf[:], [sr2], [pb_R[6]])
                cp(CKT[:, 0:512], pb[6], [pb_R[6]], [CKT_R])
                stgkr = st_[:, 512:896].rearrange("p (k f) -> p k f", k=4)
                kb.op("dve", lambda e, stgkr=stgkr: e.memset(stgkr[:, :, 0:64], 0.0), writes=[sr2])
                kb.dma("sp", stgkr[:, :, 64:96], cdr_d[l].rearrange("(k p) f -> p k f", p=128), writes=[sr2])
                b = nb()
                for kt in range(4):
                    mm(pb[b][0:96, kt * 128:(kt + 1) * 128], stgkr[:, kt, :], identf[:], True, True, [sr2, CR], [pb_R[b]])
                cp(KD_[64:96, :, 0:512], bc(pb[b][64:96, :].unsqueeze(1), [32, 4, 512]), [pb_R[b]], [KD_R])
            def tile_gen(t):
                TS_ = TSET[t % 2]; tA_ = TS_['tA_']; tA_R_ = TS_['tA_R_']; tB_ = TS_['tB_']; tB_R_ = TS_['tB_R_']; tC_ = TS_['tC_']; tC_R_ = TS_['tC_R_']; sm_ = TS_['sm_']; sm_R_ = TS_['sm_R_']; qkb_ = TS_['qkb_']; qkb_R_ = TS_['qkb_R_']; krpad_ = TS_['krpad_']; krpad_R_ = TS_['krpad_R_']; zgT_ = TS_['zgT_']; zgT_R_ = TS_['zgT_R_']
                zb = t % 2
                PSBp = PSB[:, (t % 2) * 512:(t % 2) * 512 + 512]
                z = pb[zb]; zr = pb_R[zb]
                for kc in range(8):
                    mm(z[:, 0:352], hT[:, kc, t * 128:(t + 1) * 128], W[:, kc, :], kc == 0, kc == 7, [hT_R[t // 4], sr], [zr])
                yield
                act(tA_[:, 0:320], z[:, 0:320], AF.Square, [zr], [tA_R_])
                kb.op("dve", lambda e, sm_=sm_, tA_=tA_: e.reduce_sum(out=sm_[:, 8:9], in_=tA_[:, 0:192], axis=AX.X), reads=[tA_R_], writes=[sm_R_])
                kb.op("dve", lambda e, sm_=sm_, tA_=tA_: e.reduce_sum(out=sm_[:, 9:10], in_=tA_[:, 192:320], axis=AX.X), reads=[tA_R_], writes=[sm_R_])
                act(sm_[:, 8:9], sm_[:, 8:9], AF.Ln, [sm_R_, CR], [sm_R_], scale=1.0 / 192, bias=epsT[:])
                act(sm_[:, 9:10], sm_[:, 9:10], AF.Ln, [sm_R_, CR], [sm_R_], scale=1.0 / 128, bias=epsT[:])
                act(sm_[:, 8:10], sm_[:, 8:10], AF.Exp, [sm_R_], [sm_R_], scale=-0.5)
                yield
                stt(qkb_[:, 0:192], z[:, 0:192], sm_[:, 8:9], gDQ[:], MUL, MUL, [zr, sm_R_, LP], [qkb_R_])
                stt(tB_[:, 0:128], z[:, 192:320], sm_[:, 9:10], gKV[:], MUL, MUL, [zr, sm_R_, LP], [tB_R_])
                cp(qkb_[:, 192:320], tB_[:, 0:128], [tB_R_], [qkb_R_])
                yield
                kt = KOFF + t
                if S_GRP:
                    xv = z[:, 320:352].rearrange("p (r h i) -> p r h i", r=2, h=2)
                    ov = krpad_[:, 64:96].rearrange("p (r h i) -> p r h i", r=2, h=2)
                    cv = cosC[:, t, :].rearrange("p (r i) -> p r i", r=2); sv = sinC[:, t, :].rearrange("p (r i) -> p r i", r=2)
                    a = tC_[:, 0:16].rearrange("p (r i) -> p r i", r=2); b2 = tC_[:, 16:32].rearrange("p (r i) -> p r i", r=2)
                    tt(a, xv[:, :, 0, :], cv, MUL, [zr, CR], [tC_R_]); tt(b2, xv[:, :, 1, :], sv, MUL, [zr, CR], [tC_R_])
                    tt(ov[:, :, 0, :], a, b2, SUB, [tC_R_], [krpad_R_])
                    tt(a, xv[:, :, 1, :], cv, MUL, [zr, CR], [tC_R_]); tt(b2, xv[:, :, 0, :], sv, MUL, [zr, CR], [tC_R_])
                    tt(ov[:, :, 1, :], a, b2, ADD, [tC_R_], [krpad_R_])
                else:
                    cp(krpad_[:, 64:96], z[:, 320:352], [zr], [krpad_R_])
                    o_, or_ = nstg()
                    cp(o_[:, 0:128], tB_[:, 0:128], [tB_R_], [or_], eng="act")
                    cp(o_[:, 128:160], z[:, 320:352], [zr], [or_], eng="act")
                    s_, tt_ = t // 2, (t % 2) * 128
                    kb.dma("sp", nckv_d[s_, l, tt_:tt_ + 128, :], o_[:, 0:128], reads=[or_])
                    kb.dma("sp", nkr_d[s_, l, tt_:tt_ + 128, :], o_[:, 128:160], reads=[or_])
                yield
                tr(PSBp[:, 0:128], qkb_[:, 0:128], identb[:], [qkb_R_], [pb_R[7]])
                tr(PSBp[0:64, 128:256], qkb_[:, 128:192], identb[:], [qkb_R_], [pb_R[7]])
                tr(PSBp[:, 256:384], qkb_[:, 192:320], identb[:], [qkb_R_], [pb_R[7]])
                yield
                cp(DQN[:, 0, t * 128:(t + 1) * 128], PSBp[:, 0:128], [pb_R[7]], [DQN_R])
                cp(DQN[0:64, 1, t * 128:(t + 1) * 128], PSBp[0:64, 128:256], [pb_R[7]], [DQN_R])
                cp(CKT[:, kt * 128:(kt + 1) * 128], PSBp[:, 256:384], [pb_R[7]], [CKT_R], eng="act")
                b = 2 + (t % 2)
                mm(pb[b][0:96, 0:128], krpad_[:], identb[:], True, True, [krpad_R_, CR], [pb_R[b]])
                cp(KD_[64:96, :, kt * 128:(kt + 1) * 128], bc(pb[b][64:96, 0:128].unsqueeze(1), [32, 4, 128]), [pb_R[b]], [KD_R])
                yield
            gens_ = [tile_gen(t_) for t_ in range(8)]
            for t0_ in range(0, 8, 2):
                act_ = [gens_[t0_], gens_[t0_ + 1]]
                while act_:
                    for g_ in list(act_):
                        try:
                            next(g_)
                        except StopIteration:
                            act_.remove(g_)
            for h in range(4):
                for c0 in range(0, NK, 512):
                    b = nb()
                    mm(pb[b][0:64, :], UKV[:, h * 128:h * 128 + 64], CKT[:, c0:c0 + 512], True, True, [sr, CKT_R], [pb_R[b]])
                    cp(KD_[0:64, h, c0:c0 + 512], pb[b][0:64, :], [pb_R[b]], [KD_R], eng=("act" if h % 2 else "dve"))
            for kt in range(NKT):
                b = nb()
                mm(pb[b][:, 0:256].rearrange("p (h e) -> p h e", h=4), CKT[:, kt * 128:(kt + 1) * 128], UKV.rearrange("p (h c) -> p h c", h=4)[:, :, 64:128], True, True,
                   [sr, CKT_R], [pb_R[b]])
                for (d0, d1, s0, s1) in VDST:
                    cp(VD[:, kt, d0:d1], pb[b][:, s0:s1], [pb_R[b]], [VD_R], eng=("act" if kt % 2 else "dve"))
            for h in range(4):
                for qc in range(0, 1024, 512):
                    b = nb()
                    mm(pb[b][0:96, :], UQ[:, 0, h * 96:(h + 1) * 96], DQN[:, 0, qc:qc + 512], True, False, [sr, DQN_R], [pb_R[b]])
                    mm(pb[b][0:96, :], UQ[0:64, 1, h * 96:(h + 1) * 96], DQN[0:64, 1, qc:qc + 512], False, True, [sr, DQN_R], [pb_R[b]])
                    if S_GRP:
                        b2_ = nb()
                        mm(pb[b2_][0:96, :], UQS[:, 0, h * 96:(h + 1) * 96], DQN[:, 0, qc:qc + 512], True, False, [sr, DQN_R], [pb_R[b2_]])
                        mm(pb[b2_][0:96, :], UQS[0:64, 1, h * 96:(h + 1) * 96], DQN[0:64, 1, qc:qc + 512], False, True, [sr, DQN_R], [pb_R[b2_]])
                        cp(DQT[0:64, h, qc:qc + 512], pb[b][0:64, :], [pb_R[b]], [DQT_R], eng="act")
                        tt(tA[64:96, 0:512], pb[b][64:96, :], cosD[64:96, qc:qc + 512], MUL, [pb_R[b], CR], [tA_R])
                        tt(tB[64:96, 0:512], pb[b2_][64:96, :], sinD[64:96, qc:qc + 512], MUL, [pb_R[b2_], CR], [tB_R])
                        tt(DQT[64:96, h, qc:qc + 512], tA[64:96, 0:512], tB[64:96, 0:512], ADD, [tA_R, tB_R], [DQT_R])
                    else:
                        cp(DQT[0:96, h, qc:qc + 512], pb[b][0:96, :], [pb_R[b]], [DQT_R], eng=("act" if h % 2 else "dve"))
            jobs = []; fins = {}
            for (q0, qlen, kts) in seqs:
                for qc in range(0, qlen, 512):
                    N = min(512, qlen - qc)
                    for h in range(4):
                        bb, hf = h // 2, h % 2
                        orow = hf * 64
                        srow = 64 - orow
                        vfn = lambda kt, h=h: VD[:, kt, VOFFS[h]:VOFFS[h] + 128]
                        gid = len(jobs)

                        def fin(obs, orow=orow, srow=srow, N=N, bb=bb, c0=q0 + qc):
                            ob = obs[0]
                            recipf(tA[srow:srow + 64, 0:N], pb[ob][srow:srow + 64, 0:N], [pb_R[ob]], [tA_R])
                            tt(BR[orow:orow + 64, 3, bb, c0:c0 + N], pb[ob][orow:orow + 64, 0:N], tA[srow:srow + 64, 0:N], MUL,
                               [pb_R[ob], tA_R], [BR_R[3]])
                        fins[gid] = fin
                        jobs.append(dict(kT=(lambda kt, h=h: KD_[0:96, h, kt * 128:(kt + 1) * 128]), qT=DQT[0:96, h, q0 + qc:q0 + qc + N], vT=vfn,
                                         kts=kts, N=N, scale=96 ** -0.5, reads=[KD_R, DQT_R, VD_R], gid=gid))
            attend_multi(jobs, fins)
            if STOP == 'D':
                continue
            slot, sr = new_slot()
            W = slot[:, 0:6144].rearrange("p (k c) -> p k c", k=8)
            G = slot[:, 6144:6656].rearrange("p (k c) -> p k c", k=8)
            load_w(W, win_d[l].rearrange("(k p) c -> p k c", p=128)[:, :, 512:1280], sr)
            load_w(G[:, :, 0:16], win_d[l].rearrange("(k p) c -> p k c", p=128)[:, :, 1280:1296], sr)
            load_w(G[:, :, 32:48], win_d[l].rearrange("(k p) c -> p k c", p=128)[:, :, 1296:1312], sr)
            barrier(kb)
            BT = av(0, 12288).rearrange("p (t d c) -> p t d c", t=8, d=2); BT_R = Res("BT")
            KDc = av(12288, 2048).rearrange("p (t d c) -> p t d c", t=8, d=2); KDc_R = Res("KDc")
            VB = av(14336, 2048).rearrange("p (t c) -> p t c", t=8); VB_R = Res("VB")
            GR = av(16384, 2048).rearrange("p (t c) -> p t c", t=8); GR_R = Res("GR")
            SIN = av(18432, 4096).rearrange("p (t d c) -> p t d c", t=8, d=2); SIN_R = Res("SIN")
            Lsp = tC; Lsp_R = tC_R
            for t in range(8):
                TS_ = TSET[t % 2]; tA_ = TS_['tA_']; tA_R_ = TS_['tA_R_']; tB_ = TS_['tB_']; tB_R_ = TS_['tB_R_']; tC_ = TS_['tC_']; tC_R_ = TS_['tC_R_']; sm_ = TS_['sm_']; sm_R_ = TS_['sm_R_']; qkb_ = TS_['qkb_']; qkb_R_ = TS_['qkb_R_']; krpad_ = TS_['krpad_']; krpad_R_ = TS_['krpad_R_']; zgT_ = TS_['zgT_']; zgT_R_ = TS_['zgT_R_']
                zb = nb()
                while zb >= 4:
                    zb = nb()
                zr = [pb_R[zb], pb_R[zb + 1]]
                z = PS[:, zb * 512:zb * 512 + 768]
                for bi in range(2):
                    c0, c1 = bi * 512, min(768, bi * 512 + 512)
                    for kc in range(8):
                        mm(PS[:, (zb + bi) * 512:(zb + bi) * 512 + c1 - c0], hT[:, kc, t * 128:(t + 1) * 128], W[:, kc, c0:c1], kc == 0, kc == 7,
                           [hT_R[t // 4], sr], [pb_R[zb + bi]])
                if BCUT == 1:
                    continue
                for d in range(2):
                    for kc in range(8):
                        mm(pb[5][0:16, d * 128:(d + 1) * 128], G[:, kc, d * 32:d * 32 + 16], hT[:, kc, t * 128:(t + 1) * 128], kc == 0, kc == 7, [hT_R[t // 4], sr], [pb_R[5]])
                cp(zgT_, pb[5][0:16, 0:256], [pb_R[5]], [zgT_R_])
                if BCUT == 2:
                    continue
                for d in range(2):
                    mm(pb[5][:, 256 + d * 128:384 + d * 128], zgT_[0:16, d * 128:(d + 1) * 128], GW[0:16, d * 128:(d + 1) * 128], True, False, [zgT_R_, LP], [pb_R[5]])
                    mm(pb[5][:, 256 + d * 128:384 + d * 128], onesrow[0:1, :], GBias[0:1, d * 128:(d + 1) * 128], False, True, [CR, LP], [pb_R[5]])
                if BCUT == 3:
                    continue
                act(tA_[:, 0:256], pb[5][:, 256:512], AF.Exp, [pb_R[5]], [tA_R_], scale=-1.0)
                act(tC_[:, 0:256], tA_[:, 0:256], AF.Ln, [tA_R_, CR], [tC_R_], bias=onescol[:])
                if BCUT == 4:
                    continue
                for i, (m_, d) in enumerate(((0, 0), (1, 0), (2, 1), (3, 1))):
                    mm(pb[6][:, i * 128:(i + 1) * 128], tri[:, m_, :], tC_[:, d * 128:(d + 1) * 128], True, True, [tC_R_, CR], [pb_R[6]])
                if BCUT == 5:
                    continue
                for d in range(2):
                    mm(pb[5][:, 2 * d:2 + 2 * d], tC_[:, d * 128:(d + 1) * 128], onesf[:, 0:2], True, True, [tC_R_, CR], [pb_R[5]])
                act(GL[:, t, :], pb[5][:, 0:4].rearrange("p (d two) -> p d two", two=2)[:, :, 0], AF.Exp, [pb_R[5]], [GL_R], scale=-1.0 / 16)
                if BCUT == 6:
                    continue
                Ea = tA_; Eb = tB_
                act(Ea[:, 0:512], pb[6], AF.Exp, [pb_R[6]], [tA_R_], scale=-1.0 / 16)
                act(Eb[:, 0:256].rearrange("p (a c) -> p a c", a=2), pb[6].rearrange("p (a b c) -> p a b c", a=2, b=2)[:, :, 0, :], AF.Exp, [pb_R[6]], [tB_R_], scale=1.0 / 16)
                if BCUT == 7:
                    continue
                zq, zk = z[:, 0:128], z[:, 128:256]
                stt(qkb_[:, 0:128], zq, 32 ** -0.5, Ea[:, 0:128], MUL, MUL, zr + [tA_R_], [qkb_R_])
                stt(qkb_[:, 128:256], zq, 32 ** -0.5, Ea[:, 256:384], MUL, MUL, zr + [tA_R_], [qkb_R_])
                tt(qkb_[:, 256:384], zk, Eb[:, 0:128], MUL, zr + [tB_R_], [qkb_R_])
                tt(qkb_[:, 384:512], zk, Eb[:, 128:256], MUL, zr + [tB_R_], [qkb_R_])
                tt(KDc[:, t, 0, :], zk, Ea[:, 128:256], MUL, zr + [tA_R_], [KDc_R])
                tt(KDc[:, t, 1, :], zk, Ea[:, 384:512], MUL, zr + [tA_R_], [KDc_R])
                if BCUT == 8:
                    continue
                cp(VB[:, t, :], z[:, 256:512], zr, [VB_R], eng="act")
                act(GR[:, t, :], z[:, 512:768], AF.Silu, zr, [GR_R])
                if BCUT == 9:
                    continue
                for i in range(4):
                    tr(PSB[:, i * 128:(i + 1) * 128], qkb_[:, i * 128:(i + 1) * 128], identb[:], [qkb_R_], [pb_R[7]])
                if BCUT == 10:
                    continue
                for d in range(2):
                    cp(BT[:, t, d, 0:128], PSB[:, 256 + d * 128:384 + d * 128], [pb_R[7]], [BT_R], eng="act")
                    cp(BT[:, t, d, 128:256], PSB[:, d * 128:(d + 1) * 128], [pb_R[7]], [BT_R], eng="act")
                    tt(BT[:, t, d, 256:768].rearrange("p (h n) -> p h n", h=4), bc(PSB[:, d * 128:(d + 1) * 128].unsqueeze(1), [128, 4, 128]),
                       bc(hm[:].unsqueeze(2), [128, 4, 128]), MUL, [pb_R[7], CR], [BT_R])
            if STOP.startswith('B1'):
                continue
            hm3 = bc(hm[:].unsqueeze(2), [128, 4, 64])
            for si, (q0, qlen, kts) in enumerate(seqs):
                t0, nt = q0 // 128, qlen // 128
                for d in range(2):
                    Sf, Sr = Sst[d], Sst_R[d]
                    if S_GRP:
                        kb.dma("sp", Sf[:], (sbf_d if d == 0 else sbb_d)[l], writes=[Sr])
                    else:
                        kb.op("dve", lambda e, Sf=Sf: e.memset(Sf[:], 0.0), writes=[Sr])
                    order = range(t0, t0 + nt) if d == 0 else range(t0 + nt - 1, t0 - 1, -1)
                    for t in order:
                        tt(SIN[:, t, d, :].rearrange("p (h e) -> p h e", h=4), bc(Sf[:].unsqueeze(1), [128, 4, 64]), hm3, MUL, [Sr, CR], [SIN_R])
                        b = nb()
                        mm(pb[b][:, 0:256], KDc[:, t, d, :], VB[:, t, :], True, True, [KDc_R, VB_R], [pb_R[b]])
                        tt(tA[:, 0:256].rearrange("p (h e) -> p h e", h=4), pb[b][:, 0:256].rearrange("p (h e) -> p h e", h=4), hm3, MUL, [pb_R[b], CR], [tA_R])
                        kb.op("dve", lambda e: e.reduce_sum(out=tB[:, 0:64], in_=tA[:, 0:256].rearrange("p (h e) -> p e h", h=4), axis=AX.X), reads=[tA_R], writes=[tB_R])
                        stt(Sf[:], Sf[:], GL[:, t, d:d + 1], tB[:, 0:64], MUL, ADD, [Sr, GL_R, tB_R], [Sr])
                    if not S_GRP:
                        kb.dma("sp", (nbf_d if d == 0 else nbb_d)[si, l], Sf[:], reads=[Sr])
            if STOP == 'B2':
                continue
            for t in range(8):
                ob = nb()
                mm(pb[ob][:, 0:256], BT[:, t, 0, 128:256], SIN[:, t, 0, :], True, False, [BT_R, SIN_R], [pb_R[ob]])
                mm(pb[ob][:, 0:256], BT[:, t, 1, 128:256], SIN[:, t, 1, :], False, False, [BT_R, SIN_R], [pb_R[ob]])
                for d in range(2):
                    ab = nb()
                    while ab == ob:
                        ab = nb()
                    mm(pb[ab], BT[:, t, d, 0:128], BT[:, t, d, 256:768], True, True, [BT_R], [pb_R[ab]])
                    p = pt_i[0] % 3
                    pt_i[0] += 1
                    tt(PT[p][:].rearrange("p (h n) -> p h n", h=4), pb[ab].rearrange("p (h n) -> p h n", h=4),
                       bc(tri[:, (0 if d == 0 else 2), :].unsqueeze(1), [128, 4, 128]), MUL, [pb_R[ab], CR], [PT_R[p]])
                    for h in range(4):
                        mm(pb[ob][:, h * 64:(h + 1) * 64], PT[p][:, h * 128:(h + 1) * 128], VB[:, t, h * 64:(h + 1) * 64], False, (d == 1 and h == 3),
                           [PT_R[p], VB_R], [pb_R[ob]])
                o = pb[ob][:, 0:256]
                act(tA[:, 0:256], o, AF.Square, [pb_R[ob]], [tA_R])
                kb.op("dve", lambda e: e.reduce_sum(out=sm[:, 16:20], in_=tA[:, 0:256].rearrange("p (h d) -> p h d", h=4), axis=AX.X), reads=[tA_R], writes=[sm_R])
                act(sm[:, 16:20], sm[:, 16:20], AF.Ln, [sm_R, CR], [sm_R], scale=1.0 / 64, bias=epsT[:])
                act(sm[:, 16:20], sm[:, 16:20], AF.Exp, [sm_R], [sm_R], scale=-0.5)
                tt(tA[:, 0:256].rearrange("p (h d) -> p h d", h=4), o.rearrange("p (h d) -> p h d", h=4), bc(sm[:, 16:20].unsqueeze(2), [128, 4, 64]), MUL,
                   [pb_R[ob], sm_R], [tA_R])
                tt(tA[:, 0:256], tA[:, 0:256], gBO[:], MUL, [tA_R, LP], [tA_R])
                tt(qkb[:, 0:256], tA[:, 0:256], GR[:, t, :], MUL, [tA_R, GR_R], [qkb_R])
                for i in range(2):
                    tr(PSB[:, i * 128:(i + 1) * 128], qkb[:, i * 128:(i + 1) * 128], identb[:], [qkb_R], [pb_R[7]])
                cp(BR[:, 1, :, t * 128:(t + 1) * 128], PSB[:, 0:256].rearrange("p (b n) -> p b n", b=2), [pb_R[7]], [BR_R[1]])

            if STOP == 'B':
                continue
            bslot, bsr = new_slot()
            BW = bslot[:, 0:8192].rearrange("p (j k c) -> p j k c", j=4, k=2)
            for j in range(4):
                load_w(BW[:, j], wbr_d[l, j].rearrange("(k p) c -> p k c", p=128), bsr)
            barrier(kb)
            if DEBUG and l == 0 and g == GROUPS[0]:
                for jb in range(8):
                    o_, or_ = nstg()
                    cp(o_[:], BR[:, jb // 2, jb % 2, :], BR_R, [or_])
                    kb.dma("sp", dbg_br[:, jb, :], o_[:], reads=[or_])
            MG = av(0, 8192).rearrange("p (k n) -> p k n", k=8)
            for mp in range(4):
                slot, sr = new_slot(avoid=bslot)
                GWT = slot[:, 0:8192].rearrange("p (j k c) -> p j k c", j=4, k=8)
                for j in range(4):
                    c0 = 2432 + j * 1024 + mp * 256
                    load_w(GWT[:, j], win_d[l].rearrange("(k p) c -> p k c", p=128)[:, :, c0:c0 + 256], sr)
                for mi in range(2):
                    m = mp * 2 + mi
                    for h in range(2):
                        hs = slice(h * 512, (h + 1) * 512)
                        for j in range(4):
                            b1 = nb(); b2_ = nb()
                            for kc in range(8):
                                mm(pb[b1], GWT[:, j, kc, mi * 128:(mi + 1) * 128], hT[:, kc, hs], kc == 0, kc == 7, [sr, hT_R[h]], [pb_R[b1]])
                            for kc in range(2):
                                mm(pb[b2_], BW[:, j, kc, m * 128:(m + 1) * 128], BR[:, j, kc, hs], kc == 0, kc == 1, [bsr, BR_R[j]], [pb_R[b2_]])
                            act(tA[:, 0:512], pb[b1], AF.Sigmoid, [pb_R[b1]], [tA_R])
                            if j == 0:
                                tt(tC[:, 0:512], pb[b2_], tA[:, 0:512], MUL, [pb_R[b2_], tA_R], [tC_R])
                            else:
                                tt(tB[:, 0:512], pb[b2_], tA[:, 0:512], MUL, [pb_R[b2_], tA_R], [tB_R])
                                if j < 3:
                                    tt(tC[:, 0:512], tC[:, 0:512], tB[:, 0:512], ADD, [tC_R, tB_R], [tC_R])
                                else:
                                    tt(MG[:, m, hs], tC[:, 0:512], tB[:, 0:512], ADD, [tC_R, tB_R], [MG_R[h]])
            if STOP == 'M':
                continue
            slot, sr = new_slot()
            WO = slot[:, 0:8192].rearrange("p (k c) -> p k c", k=8)
            load_w(WO, wout_d[l].rearrange("(k p) c -> p k c", p=128), sr)
            for m in range(8):
                for h in range(2):
                    hs = slice(h * 512, (h + 1) * 512)
                    b = nb()
                    for kc in range(8):
                        mm(pb[b], WO[:, kc, m * 128:(m + 1) * 128], MG[:, kc, hs], kc == 0, kc == 7, [sr, MG_R[h]], [pb_R[b]])
                    stt(xT[:, m, hs], pb[b], modT[:, l, 16 + m, g:g + 1], xT[:, m, hs], MUL, ADD, [pb_R[b], MR, xT_R[m][h]], [xT_R[m][h]])
            if DEBUG and l == 0 and g == GROUPS[0]:
                for kc in range(8):
                    kb.dma("sp", dbg_x1[:, kc, :], xT[:, kc, :], reads=[xT_R[kc][0], xT_R[kc][1]])
                    o_, or_ = nstg()
                    cp(o_[:], MG[:, kc, :], MG_R, [or_])
                    kb.dma("sp", dbg_mg[:, kc, :], o_[:], reads=[or_])
            norm_mod(A2, 24, l, g)
            AT = av(8192, 22528).rearrange("p (f n) -> p f n", f=22); AT_R = [Res("AT0"), Res("AT1")]
            U_ = stg[0]; U_R = stg_R[0]; Cc = stg[1]; Cc_R = stg_R[1]
            nsq = 1 if S_GRP else 4
            sl = 1024 // nsq
            for f0 in range(0, 22, 4):
                nf = min(4, 22 - f0)
                slot, sr = new_slot()
                UW = slot[:, 0:4096].rearrange("p (k c) -> p k c", k=8)
                GW2 = slot[:, 4096:8192].rearrange("p (k c) -> p k c", k=8)
                load_w(UW[:, :, 0:nf * 128], wfu_d[l].rearrange("(k p) c -> p k c", p=128)[:, :, f0 * 128:(f0 + nf) * 128], sr)
                load_w(GW2[:, :, 0:nf * 128], wfg_d[l].rearrange("(k p) c -> p k c", p=128)[:, :, f0 * 128:(f0 + nf) * 128], sr)
                for fi in range(nf):
                    f = f0 + fi
                    gb_ = []
                    for h in range(2):
                        hs = slice(h * 512, (h + 1) * 512)
                        bu = nb(); bg = nb()
                        gb_.append(bg)
                        for kc in range(8):
                            mm(pb[bu], UW[:, kc, fi * 128:(fi + 1) * 128], hT[:, kc, hs], kc == 0, kc == 7, [sr, hT_R[h]], [pb_R[bu]])
                        for kc in range(8):
                            mm(pb[bg], GW2[:, kc, fi * 128:(fi + 1) * 128], hT[:, kc, hs], kc == 0, kc == 7, [sr, hT_R[h]], [pb_R[bg]])
                        cp(U_[:, hs], pb[bu], [pb_R[bu]], [U_R], eng="act")
                    act(Cc[:], U_[:], AF.Identity, [U_R, LP], [Cc_R], scale=cwT[:, 1, f:f + 1], bias=cbT[:, f:f + 1])
                    Uv = U_[:].rearrange("p (s n) -> p s n", s=nsq); Cv = Cc[:].rearrange("p (s n) -> p s n", s=nsq)
                    stt(Cv[:, :, 1:sl], Uv[:, :, 0:sl - 1], cwT[:, 0, f:f + 1], Cv[:, :, 1:sl], MUL, ADD, [U_R, Cc_R, LP], [Cc_R])
                    stt(Cv[:, :, 0:sl - 1], Uv[:, :, 1:sl], cwT[:, 2, f:f + 1], Cv[:, :, 0:sl - 1], MUL, ADD, [U_R, Cc_R, LP], [Cc_R])
                    act(tA[:], Cc[:], AF.Gelu_apprx_tanh, [Cc_R], [tA_R])
                    for h in range(2):
                        hs = slice(h * 512, (h + 1) * 512)
                        tt(AT[:, f, hs], tA[:, hs], pb[gb_[h]], MUL, [tA_R, pb_R[gb_[h]]], [AT_R[h]])
            for m0 in range(0, 8, 2):
                slot, sr = new_slot()
                WD = slot[:, 0:5632].rearrange("p (f c) -> p f c", f=22)
                load_w(WD, wfd_d[l].rearrange("(f p) c -> p f c", p=128)[:, :, m0 * 128:(m0 + 2) * 128], sr)
                for mi in range(2):
                    m = m0 + mi
                    for h in range(2):
                        hs = slice(h * 512, (h + 1) * 512)
                        b = nb()
                        for f in range(22):
                            mm(pb[b], WD[:, f, mi * 128:(mi + 1) * 128], AT[:, f, hs], f == 0, f == 21, [sr, AT_R[h]], [pb_R[b]])
                        stt(xT[:, m, hs], pb[b], modT[:, l, 40 + m, g:g + 1], xT[:, m, hs], MUL, ADD, [pb_R[b], MR, xT_R[m][h]], [xT_R[m][h]])

            if DEBUG and l == 0 and g == GROUPS[0]:
                for kc in range(8):
                    kb.dma("sp", dbg_x2[:, kc, :], xT[:, kc, :], reads=[xT_R[kc][0], xT_R[kc][1]])
        barrier(kb)
        kb.dma("sp", fgB, fg_d.partition_broadcast(128), writes=[fgB_R])
        for t in range(8):
            b0 = nb()
            while b0 >= 5:
                b0 = nb()
            for kc in range(8):
                bi = b0 + kc // 4
                tr(PS[:, bi * 512 + (kc % 4) * 128: bi * 512 + (kc % 4 + 1) * 128], xT[:, kc, t * 128:(t + 1) * 128], identf[:], [xT_R[kc][t // 4]], [pb_R[bi]])
            yp = PS[:, b0 * 512:b0 * 512 + 1024]
            yr = [pb_R[b0], pb_R[b0 + 1]]
            act(tA[:], yp, AF.Square, yr, [tA_R])
            kb.op("dve", lambda e: e.reduce_sum(out=sm[:, 24:25], in_=tA[:], axis=AX.X), reads=[tA_R], writes=[sm_R])
            act(sm[:, 24:25], sm[:, 24:25], AF.Ln, [sm_R, CR], [sm_R], scale=1.0 / 1024, bias=epsT[:])
            act(sm[:, 24:25], sm[:, 24:25], AF.Exp, [sm_R], [sm_R], scale=-0.5)
            o_, or_ = nstg()
            stt(o_[:], yp, sm[:, 24:25], fgB, MUL, MUL, yr + [sm_R, fgB_R], [or_])
            kb.dma("sp", y_d[g, t * 128:(t + 1) * 128, :], o_[:], reads=[or_])
    return kb


def _consts():
    c = {}
    c["k_ident"] = np.eye(128, dtype=np.float32)
    bd = np.zeros((128, 128), np.float32); bd[:64, :64] = 1; bd[64:, 64:] = 1
    c["k_bd64"] = bd
    s = np.arange(128)[:, None]; t = np.arange(128)[None, :]
    c["k_tri"] = np.stack([(s <= t), (s > t), (s >= t), (s < t)]).astype(np.float32)
    hmk = np.zeros((128, 4), np.float32)
    for h in range(4):
        hmk[h * 32:(h + 1) * 32, h] = 1
    c["k_hm"] = hmk
    pmk = np.zeros((128, 2), np.float32)
    for p in range(128):
        pmk[p, (p // 32) % 2] = 1
    c["k_pm"] = pmk
    tok = np.arange(1024)
    row = (tok // 64).astype(np.float32); col = (tok % 64).astype(np.float32)

    def tab(half):
        inv = (10000.0 ** (-np.arange(half, dtype=np.float32) / half)).astype(np.float32)
        ar = row[:, None] * inv[None, :]; ac = col[:, None] * inv[None, :]
        return (np.concatenate([np.cos(ar), np.cos(ac)], 1).astype(np.float32), np.concatenate([np.sin(ar), np.sin(ac)], 1).astype(np.float32))
    c["k_cosA"], c["k_sinA"] = tab(16)
    cC, sC = tab(8)
    c["k_cosC"], c["k_sinC"] = cC, sC
    cr, cc_ = cC[:, 0:8], cC[:, 8:16]; sr_, sc_ = sC[:, 0:8], sC[:, 8:16]
    c["k_cosD"] = np.ascontiguousarray(np.concatenate([cr, cr, cc_, cc_], 1).T)
    c["k_sinD"] = np.ascontiguousarray(np.concatenate([-sr_, sr_, -sc_, sc_], 1).T)
    return c


WNAMES = ["w_mod", "b_mod", "norm1_g", "norm2_g", "w_in", "a_qnorm_g", "a_knorm_g", "b_gate_w_fwd", "b_gate_b_fwd", "b_gate_w_bwd",
          "b_gate_b_bwd", "b_onorm_g", "c_lq1", "c_lk1", "c_lq2", "c_lk2", "c_onorm_g", "d_qnorm_g", "d_w_uq", "d_kvnorm_g", "d_w_ukv",
          "w_branch", "w_out", "w_ffu", "w_ffg", "conv_w", "conv_b", "w_ffd", "final_g"]


def in_map(inp, c, consts, wts):
    m = dict(wts)
    m.update(consts)
    f = lambda a: np.ascontiguousarray(a, dtype=np.float32)
    m["x"] = f(np.stack([inp["x_prompt"][4 * c:4 * c + 4].reshape(1024, 1024), inp["x_sample"][c]]))
    m["cvec"] = f(np.stack([inp["c_ctx"], inp["c"][c]]))
    m["ca_k"] = f(inp["cache_a_k"][c].reshape(4, 512, 128)); m["ca_v"] = f(inp["cache_a_v"][c].reshape(4, 512, 128))
    m["sb_f"] = f(inp["state_b_fwd"][c].reshape(4, 128, 64)); m["sb_b"] = f(inp["state_b_bwd"][c].reshape(4, 128, 64))
    m["cc_k"] = f(inp["cache_c_k"][c].reshape(4, 512, 256)); m["cc_v"] = f(inp["cache_c_v"][c].reshape(4, 512, 256))
    m["cd_ckv"] = f(inp["cache_d_ckv"][c]); m["cd_kr"] = f(inp["cache_d_krope"][c])
    return m


def assemble(R):
    y_p = np.concatenate([r["y"][0].reshape(4, 256, 1024) for r in R], 0)
    y_s = np.stack([r["y"][1] for r in R], 0)

    def cat(name, shp):
        return np.concatenate([r[name].reshape((4,) + shp) for r in R], 0)
    return (y_p, y_s, cat("nak", (4, 256, 2, 64)), cat("nav", (4, 256, 2, 64)), cat("nbf", (4, 4, 32, 64)), cat("nbb", (4, 4, 32, 64)),
            cat("nck", (4, 256, 4, 2, 32)), cat("ncv", (4, 256, 4, 64)), cat("nckv", (4, 256, 128)), cat("nkr", (4, 256, 32)))


def kernel(**inp):
    inp = {k: np.asarray(v) for k, v in inp.items()}
    kb = build()
    nc = kb.finish()
    consts = _consts()
    wts = {k: np.ascontiguousarray(inp[k], dtype=np.float32) for k in WNAMES}
    in_maps = [in_map(inp, c, consts, wts) for c in range(8)]
    res = run_bass_kernel_spmd(nc, in_maps, core_ids=list(range(8)))
    return assemble(res.results)
```
